# Optimizing a Trainium2 kernel written in Bass

```python
import math
import jax, jax.numpy as jnp
from jax import lax
import numpy as np

D_MODEL = 1024
BATCH = 16
SEQ = 256
DEPTH = 4
DEC_BATCH = 2
DEC_SEQ = 4096
PAST_LEN = 512

GRID_W = 64
HEAD_DIM = 64
A_HEADS = 6
A_KV_HEADS = 2
A_GROUP = A_HEADS // A_KV_HEADS
A_WIDTH = A_HEADS * HEAD_DIM
WINDOW = 128
BLK = 128
B_HEADS = 4
B_QK_DIM = 32
B_V_DIM = 2 * B_QK_DIM
B_WIDTH = B_HEADS * B_V_DIM
C_HEADS = 6
C_HEAD_DIM = 64
C_WIDTH = C_HEADS * C_HEAD_DIM
C_W_RANK = 64
C_A_RANK = 64
C_G_RANK = 128
MIX_W = A_WIDTH + B_WIDTH + C_WIDTH
D_FF = -(-8 * D_MODEL // (3 * 256)) * 256
IN_SPLITS = (A_WIDTH, A_KV_HEADS * HEAD_DIM, A_KV_HEADS * HEAD_DIM,
             B_WIDTH, B_WIDTH, B_WIDTH,
             3 * C_WIDTH, 2 * C_W_RANK, 2 * C_A_RANK, C_G_RANK)
IN_COLS = sum(IN_SPLITS)
ROPE_THETA = 10000.0
NORM_EPS = 1e-6
GN_EPS = 64e-5
DECAY_SCALE = 0.606531
NEG_INF = -1e30

kernel_name = "hybrid_prefix_diffusion_step"


def rmsnorm(x, g):
    xf = x.astype(jnp.float32)
    y = xf * lax.rsqrt(jnp.mean(xf * xf, axis=-1, keepdims=True) + NORM_EPS)
    return (y * g.astype(jnp.float32)).astype(x.dtype)


def rope_1d(x, pos):
    d = x.shape[-1]
    inv = ROPE_THETA ** (-jnp.arange(0, d, 2, dtype=jnp.float32) / d)
    ang = pos.astype(jnp.float32)[:, None] * inv[None, :]
    ang = jnp.concatenate([ang, ang], axis=-1)
    shape = (x.shape[1],) + (1,) * (x.ndim - 3) + (d,)
    cos = jnp.cos(ang).reshape(shape).astype(x.dtype)
    sin = jnp.sin(ang).reshape(shape).astype(x.dtype)
    x1, x2 = jnp.split(x, 2, axis=-1)
    return x * cos + jnp.concatenate([-x2, x1], axis=-1) * sin


def axial_rope(x):
    n_rows = x.shape[1] // GRID_W
    rows = jnp.repeat(jnp.arange(n_rows), GRID_W)
    cols = jnp.tile(jnp.arange(GRID_W), n_rows)
    xr, xc = jnp.split(x, 2, axis=-1)
    return jnp.concatenate([rope_1d(xr, rows), rope_1d(xc, cols)], axis=-1)


def short_conv3(x, w):
    xp = jnp.pad(x, ((0, 0), (1, 1), (0, 0)))
    return xp[:, :-2] * w[0] + xp[:, 1:-1] * w[1] + xp[:, 2:] * w[2]


def adaln(cvec, w, b):
    m = jax.nn.silu(cvec) @ w + b
    return jnp.split(m[:, None, :], 6, axis=-1)


def swiglu(h, w1, w3, w2):
    return (jax.nn.silu(h @ w1) * (h @ w3)) @ w2


def to_blocks(q):
    b, t = q.shape[:2]
    return jnp.moveaxis(q.reshape((b, t // BLK, BLK) + q.shape[2:]), 1, 0)


def from_blocks(o):
    nb, b = o.shape[:2]
    return jnp.moveaxis(o, 0, 1).reshape((b, nb * BLK) + o.shape[3:])


def sink_attend(q, k, v, sink, mask):
    s = jnp.einsum('bqhgd,bshd->bhgqs', q, k, preferred_element_type=jnp.float32) * (HEAD_DIM ** -0.5)
    if mask is not None:
        s = jnp.where(mask, s, NEG_INF)
    sk = jnp.broadcast_to(sink.astype(jnp.float32)[None, :, :, None, None], s.shape[:-1] + (1,))
    p = jax.nn.softmax(jnp.concatenate([s, sk], axis=-1), axis=-1)[..., :-1]
    return jnp.einsum('bhgqs,bshd->bqhgd', p.astype(v.dtype), v)


def window_attn_context(q, k, v, sink):
    return from_blocks(lax.map(lambda qb: sink_attend(qb, k, v, sink, None), to_blocks(q)))


def window_attn_latent(q, k, v, kc, vc, sink):
    t = q.shape[1]
    c_len = kc.shape[1]
    pad = ((0, 0), (BLK, BLK), (0, 0), (0, 0))
    kp = jnp.pad(k, pad)
    vp = jnp.pad(v, pad)
    qi = jnp.arange(BLK)[:, None]
    kj = jnp.arange(3 * BLK)[None, :]
    ctx_mask = jnp.ones((BLK, c_len), dtype=bool)

    def blk(args):
        qb, b = args
        start = b * BLK
        kw = lax.dynamic_slice_in_dim(kp, start, 3 * BLK, axis=1)
        vw = lax.dynamic_slice_in_dim(vp, start, 3 * BLK, axis=1)
        pos = start - BLK + kj
        win = (jnp.abs(kj - BLK - qi) <= WINDOW) & (pos >= 0) & (pos < t)
        mask = jnp.concatenate([win, ctx_mask], axis=1)
        return sink_attend(qb, jnp.concatenate([kw, kc], axis=1), jnp.concatenate([vw, vc], axis=1), sink, mask)

    return from_blocks(lax.map(blk, (to_blocks(q), jnp.arange(t // BLK))))


def diff_lambda(lp, lam_init):
    lp = lp.astype(jnp.float32)
    return jnp.exp(jnp.sum(lp[0] * lp[1])) - jnp.exp(jnp.sum(lp[2] * lp[3])) + lam_init


def diff_attend(q, k, v, lam):
    def blk(qb):
        s = jnp.einsum('bqhmd,bshmd->bhmqs', qb, k, preferred_element_type=jnp.float32) * (B_QK_DIM ** -0.5)
        p = jax.nn.softmax(s, axis=-1)
        a = p[:, :, 0] - lam * p[:, :, 1]
        return jnp.einsum('bhqs,bshd->bqhd', a.astype(v.dtype), v)
    return from_blocks(lax.map(blk, to_blocks(q)))


def wkv_scan(s0, r, w, k, v, kk, a, reverse):
    xs = tuple(jnp.moveaxis(t, 1, 0) for t in (r, w, k, v, kk, a))

    def step(s, inp):
        rt, wt, kt, vt, kkt, at = inp
        sa = jnp.einsum('bhij,bhj->bhi', s, -kkt)
        s = s * wt[:, :, None, :] + sa[..., None] * (kkt * at)[:, :, None, :] + vt[..., None] * kt[:, :, None, :]
        return s, jnp.einsum('bhij,bhj->bhi', s, rt)

    s, y = lax.scan(step, s0, xs, reverse=reverse)
    return s, jnp.moveaxis(y, 0, 1)


def rwkv_mix(rkv, cw, ca, cg, p, s_f, s_b):
    b, t = rkv.shape[:2]
    f32 = jnp.float32
    hs = lambda z: z.reshape(z.shape[:-1] + (C_HEADS, C_HEAD_DIM))
    r, k, v = [hs(z.astype(f32)) for z in jnp.split(rkv, 3, axis=-1)]
    cw = cw.astype(f32).reshape(b, t, 2, C_W_RANK)
    ca = ca.astype(f32).reshape(b, t, 2, C_A_RANK)
    w = jnp.exp(-DECAY_SCALE * jax.nn.sigmoid(p['c_w0'] + jnp.einsum('btdr,drc->btdc', jnp.tanh(cw), p['c_w2'])))
    a = jax.nn.sigmoid(p['c_a0'] + jnp.einsum('btdr,drc->btdc', ca, p['c_a2']))
    w, a = hs(w), hs(a)
    kk = k * hs(p['c_kk']).astype(f32)
    kk = kk / jnp.maximum(jnp.sqrt(jnp.sum(kk * kk, axis=-1, keepdims=True)), 1e-12)
    kd = k[:, :, None] * (1.0 + (a - 1.0) * hs(p['c_ka']).astype(f32))
    s_f, y_f = wkv_scan(s_f, r, w[:, :, 0], kd[:, :, 0], v, kk, a[:, :, 0], False)
    s_b, y_b = wkv_scan(s_b, r, w[:, :, 1], kd[:, :, 1], v, kk, a[:, :, 1], True)
    bonus = jnp.sum(jnp.sum(r[:, :, None] * kd * p['c_rk'].astype(f32), axis=-1, keepdims=True) * v[:, :, None], axis=2)
    y = y_f + y_b + bonus
    mu = jnp.mean(y, axis=-1, keepdims=True)
    var = jnp.mean(jnp.square(y - mu), axis=-1, keepdims=True)
    y = ((y - mu) * lax.rsqrt(var + GN_EPS)).reshape(b, t, C_WIDTH) * p['c_lnx_g'] + p['c_lnx_b']
    g = jax.nn.sigmoid(cg) @ p['c_g2']
    return (y * g).astype(rkv.dtype), s_f, s_b


def project(h, p, latent):
    b, t = h.shape[:2]
    z = h @ p['w_in']
    idx = np.cumsum(IN_SPLITS)[:-1].tolist()
    aq, ak, av, bq, bk, bv, rkv, cw, ca, cg = jnp.split(z, idx, axis=-1)
    aq = aq.reshape(b, t, A_KV_HEADS, A_GROUP, HEAD_DIM)
    ak = ak.reshape(b, t, A_KV_HEADS, HEAD_DIM)
    av = av.reshape(b, t, A_KV_HEADS, HEAD_DIM)
    bq = bq.reshape(b, t, B_HEADS, 2, B_QK_DIM)
    bk = bk.reshape(b, t, B_HEADS, 2, B_QK_DIM)
    bv = bv.reshape(b, t, B_HEADS, B_V_DIM)
    if latent:
        aq, ak, bq, bk = axial_rope(aq), axial_rope(ak), axial_rope(bq), axial_rope(bk)
    rkv = short_conv3(rkv, p['c_conv'])
    return aq, ak, av, bq, bk, bv, rkv, cw, ca, cg


def merge_out(a_o, b_o, c_o, p, lam_init):
    b, t = c_o.shape[:2]
    b_o = rmsnorm(b_o, p['b_subln_g']) * (1.0 - lam_init)
    m = jnp.concatenate([a_o.reshape(b, t, A_WIDTH), b_o.reshape(b, t, B_WIDTH), c_o], axis=-1)
    return m @ p['w_out']


def mixers_context(h, p, l):
    aq, ak, av, bq, bk, bv, rkv, cw, ca, cg = project(h, p, False)
    lam_init = 0.8 - 0.6 * math.exp(-0.3 * l)
    lam = diff_lambda(p['b_lambda'], lam_init)
    a_o = window_attn_context(aq, ak, av, p['a_sink'].reshape(A_KV_HEADS, A_GROUP))
    b_o = diff_attend(bq, bk, bv, lam)
    s0 = jnp.zeros((h.shape[0], C_HEADS, C_HEAD_DIM, C_HEAD_DIM), jnp.float32)
    c_o, s_f, s_b = rwkv_mix(rkv, cw, ca, cg, p, s0, s0)
    return merge_out(a_o, b_o, c_o, p, lam_init), (ak, av, bk, bv, s_f, s_b)


def mixers_latent(h, p, l, ctx):
    kc_a, vc_a, kc_b, vc_b, s_f, s_b = ctx
    aq, ak, av, bq, bk, bv, rkv, cw, ca, cg = project(h, p, True)
    lam_init = 0.8 - 0.6 * math.exp(-0.3 * l)
    lam = diff_lambda(p['b_lambda'], lam_init)
    a_o = window_attn_latent(aq, ak, av, kc_a, vc_a, p['a_sink'].reshape(A_KV_HEADS, A_GROUP))
    b_o = diff_attend(bq, jnp.concatenate([bk, kc_b], axis=1), jnp.concatenate([bv, vc_b], axis=1), lam)
    c_o, _, _ = rwkv_mix(rkv, cw, ca, cg, p, s_f.astype(jnp.float32), s_b.astype(jnp.float32))
    return merge_out(a_o, b_o, c_o, p, lam_init)


def block(x, cvec, p, mixer):
    sh1, sc1, g1, sh2, sc2, g2 = adaln(cvec, p['ada_w'], p['ada_b'])
    h = rmsnorm(x, p['norm1_g']) * (1 + sc1) + sh1
    m, extra = mixer(h)
    x = x + g1 * m
    h = rmsnorm(x, p['norm2_g']) * (1 + sc2) + sh2
    x = x + g2 * swiglu(h, p['ffn_w1'], p['ffn_w3'], p['ffn_w2'])
    return x, extra


def setup_inputs(seed: int = 0) -> dict:
    key = jax.random.key(seed)
    ks = iter(jax.random.split(key, 40))

    def nrm(shape, scale):
        return jax.random.normal(next(ks), shape, jnp.float32) * scale

    L = DEPTH
    return {
        "x_prompt": nrm((BATCH, SEQ, D_MODEL), 1.0),
        "x_sample": nrm((DEC_BATCH, DEC_SEQ, D_MODEL), 1.0),
        "cache_a_k": nrm((DEC_BATCH, L, PAST_LEN, A_KV_HEADS, HEAD_DIM), 1.0),
        "cache_a_v": nrm((DEC_BATCH, L, PAST_LEN, A_KV_HEADS, HEAD_DIM), 1.0),
        "cache_b_k": nrm((DEC_BATCH, L, PAST_LEN, B_HEADS, 2, B_QK_DIM), 1.0),
        "cache_b_v": nrm((DEC_BATCH, L, PAST_LEN, B_HEADS, B_V_DIM), 1.0),
        "state_c_fwd": nrm((DEC_BATCH, L, C_HEADS, C_HEAD_DIM, C_HEAD_DIM), 0.3),
        "state_c_bwd": nrm((DEC_BATCH, L, C_HEADS, C_HEAD_DIM, C_HEAD_DIM), 0.3),
        "c": nrm((DEC_BATCH, D_MODEL), 1.0),
        "c_ctx": nrm((D_MODEL,), 1.0),
        "ada_w": nrm((L, D_MODEL, 6 * D_MODEL), 0.5 * D_MODEL ** -0.5),
        "ada_b": nrm((L, 6 * D_MODEL), 0.02),
        "norm1_g": 1.0 + nrm((L, D_MODEL), 0.02),
        "norm2_g": 1.0 + nrm((L, D_MODEL), 0.02),
        "w_in": nrm((L, D_MODEL, IN_COLS), D_MODEL ** -0.5),
        "a_sink": nrm((L, A_HEADS), 0.5),
        "b_lambda": nrm((L, 4, B_QK_DIM), 0.1),
        "b_subln_g": 1.0 + nrm((L, B_V_DIM), 0.02),
        "c_conv": jnp.array([0.25, 1.0, 0.25], jnp.float32)[None, :, None] + nrm((L, 3, 3 * C_WIDTH), 0.1),
        "c_w0": nrm((L, 2, C_WIDTH), 0.5),
        "c_w2": nrm((L, 2, C_W_RANK, C_WIDTH), 0.1),
        "c_a0": nrm((L, 2, C_WIDTH), 0.5),
        "c_a2": nrm((L, 2, C_A_RANK, C_WIDTH), 0.1),
        "c_g2": nrm((L, C_G_RANK, C_WIDTH), C_G_RANK ** -0.5),
        "c_kk": 1.0 + nrm((L, C_WIDTH), 0.1),
        "c_ka": 1.0 + nrm((L, C_WIDTH), 0.1),
        "c_rk": nrm((L, C_HEADS, C_HEAD_DIM), 0.1),
        "c_lnx_g": 1.0 + nrm((L, C_WIDTH), 0.02),
        "c_lnx_b": nrm((L, C_WIDTH), 0.02),
        "w_out": nrm((L, MIX_W, D_MODEL), MIX_W ** -0.5),
        "ffn_w1": nrm((L, D_MODEL, D_FF), D_MODEL ** -0.5),
        "ffn_w3": nrm((L, D_MODEL, D_FF), D_MODEL ** -0.5),
        "ffn_w2": nrm((L, D_FF, D_MODEL), D_FF ** -0.5),
        "final_g": 1.0 + nrm((D_MODEL,), 0.02),
    }


def reference(x_prompt, x_sample, cache_a_k, cache_a_v, cache_b_k, cache_b_v, state_c_fwd, state_c_bwd,
              c, c_ctx, ada_w, ada_b, norm1_g, norm2_g, w_in, a_sink, b_lambda, b_subln_g,
              c_conv, c_w0, c_w2, c_a0, c_a2, c_g2, c_kk, c_ka, c_rk, c_lnx_g, c_lnx_b,
              w_out, ffn_w1, ffn_w3, ffn_w2, final_g):
    weights = dict(ada_w=ada_w, ada_b=ada_b, norm1_g=norm1_g, norm2_g=norm2_g, w_in=w_in, a_sink=a_sink,
                   b_lambda=b_lambda, b_subln_g=b_subln_g, c_conv=c_conv, c_w0=c_w0, c_w2=c_w2, c_a0=c_a0,
                   c_a2=c_a2, c_g2=c_g2, c_kk=c_kk, c_ka=c_ka, c_rk=c_rk, c_lnx_g=c_lnx_g, c_lnx_b=c_lnx_b,
                   w_out=w_out, ffn_w1=ffn_w1, ffn_w3=ffn_w3, ffn_w2=ffn_w2)
    xp = x_prompt
    xs = x_sample
    new_ak, new_av, new_bk, new_bv, new_sf, new_sb = [], [], [], [], [], []
    for l in range(DEPTH):
        p = {name: w[l] for name, w in weights.items()}
        xp, ctx_new = block(xp, c_ctx[None], p, lambda h: mixers_context(h, p, l))
        ak, av, bk, bv, sf, sb = ctx_new
        new_ak.append(ak)
        new_av.append(av)
        new_bk.append(bk)
        new_bv.append(bv)
        new_sf.append(sf.astype(xp.dtype))
        new_sb.append(sb.astype(xp.dtype))
        ctx_cached = (cache_a_k[:, l], cache_a_v[:, l], cache_b_k[:, l], cache_b_v[:, l],
                      state_c_fwd[:, l], state_c_bwd[:, l])
        xs, _ = block(xs, c, p, lambda h: (mixers_latent(h, p, l, ctx_cached), None))
    y_prompt = rmsnorm(xp, final_g)
    y_sample = rmsnorm(xs, final_g)
    new_a_k = jnp.stack(new_ak, axis=1)
    new_a_v = jnp.stack(new_av, axis=1)
    new_b_k = jnp.stack(new_bk, axis=1)
    new_b_v = jnp.stack(new_bv, axis=1)
    new_c_fwd = jnp.stack(new_sf, axis=1)
    new_c_bwd = jnp.stack(new_sb, axis=1)
    return (y_prompt, y_sample, new_a_k, new_a_v, new_b_k, new_b_v, new_c_fwd, new_c_bwd)
```

```python
import math
from contextlib import ExitStack
import numpy as np
import concourse.bass as bass
import concourse.mybir as mybir
from concourse.bass_utils import run_bass_kernel_spmd

F32 = mybir.dt.float32
AF = mybir.ActivationFunctionType
ALU = mybir.AluOpType
AX = mybir.AxisListType

D = 1024
L = 4
TP = 256
TS = 4096
PAST = 512
NPL = 2
DFF = 2816
INC = 2944
DEC = 0.606531
EPS = 1e-6
GN_EPS = 64e-5
NCORES = 8


class Buf:
    __slots__ = ("w", "r")

    def __init__(self):
        self.w = []
        self.r = []


class TT:
    def __init__(self, h):
        self.h = h
        self.b = Buf()

    def __getitem__(self, k):
        return self.h[k]


class Prog:
    RING = 12

    def __init__(self, nc, stack):
        self.nc = nc
        self.eng = {"pe": nc.tensor, "act": nc.scalar, "dve": nc.vector, "pool": nc.gpsimd, "sp": nc.sync}
        self.semh = {}
        self.cnt = {}
        self.seen = {e: {} for e in self.eng}
        for e in ("pe", "act", "dve", "pool"):
            self.semh["S_" + e] = stack.enter_context(nc.semaphore("S_" + e))
            self.cnt[e] = 0
        self.dq = {}
        for q in ("sp", "pool"):
            ring = []
            for k in range(self.RING):
                key = "D_%s_%d" % (q, k)
                self.semh[key] = stack.enter_context(nc.semaphore(key))
                ring.append([key, 0])
            self.dq[q] = [ring, 0]
        self.n = 0

    def _waits(self, e, reads, writes):
        need = {}
        for b in reads:
            for (key, val, te) in b.w:
                if need.get(key, 0) < val:
                    need[key] = val
        for b in writes:
            for (key, val, te) in b.w:
                if te != e and need.get(key, 0) < val:
                    need[key] = val
            for (key, val, te) in b.r:
                if te != e and need.get(key, 0) < val:
                    need[key] = val
        seen = self.seen[e]
        for key, val in need.items():
            if seen.get(key, 0) < val:
                self.eng[e].wait_ge(self.semh[key], val)
                seen[key] = val
                self.n += 1

    def _record(self, tok, reads, writes):
        for b in writes:
            b.w = [tok]
            b.r = []
        for b in reads:
            b.r = [t for t in b.r if t[0] != tok[0]]
            b.r.append(tok)

    def op(self, e, fn, reads=(), writes=()):
        reads = [t.b for t in reads]
        writes = [t.b for t in writes]
        self._waits(e, reads, writes)
        ins = fn(self.eng[e])
        self.cnt[e] += 1
        ins.then_inc(self.semh["S_" + e], 1)
        self._record(("S_" + e, self.cnt[e], e), reads, writes)
        self.n += 1

    def dma(self, q, out_ap, in_ap, reads=(), writes=()):
        reads = [t.b for t in reads]
        writes = [t.b for t in writes]
        self._waits(q, reads, writes)
        ring, rr = self.dq[q]
        slot = ring[rr % self.RING]
        self.dq[q][1] = rr + 1
        key, prev = slot
        if prev > 0 and self.seen[q].get(key, 0) < prev:
            self.eng[q].wait_ge(self.semh[key], prev)
            self.seen[q][key] = prev
        ins = self.eng[q].dma_start(out=out_ap, in_=in_ap)
        ins.then_inc(self.semh[key], 16)
        slot[1] = prev + 16
        self._record((key, prev + 16, None), reads, writes)
        self.n += 2

    def barrier(self):
        targets = {}
        for e in ("pe", "act", "dve", "pool"):
            if self.cnt[e] > 0:
                targets["S_" + e] = self.cnt[e]
        for q in self.dq:
            for key, val in self.dq[q][0]:
                if val > 0:
                    targets[key] = val
        for e in self.eng:
            seen = self.seen[e]
            for key, val in targets.items():
                if key == "S_" + e:
                    continue
                if seen.get(key, 0) < val:
                    self.eng[e].wait_ge(self.semh[key], val)
                    seen[key] = val
                    self.n += 1


def _rope_tables(dh, T, grid_w=64):
    half = dh // 2
    t = np.arange(T)
    rows = t // grid_w
    cols = t % grid_w
    inv = 10000.0 ** (-np.arange(0, half, 2, dtype=np.float32) / half)
    cos = np.zeros((dh, T), np.float32)
    sin = np.zeros((dh, T), np.float32)
    perm = np.zeros((dh, dh), np.float32)
    q = half // 2
    for ax, pos in enumerate((rows, cols)):
        ang = pos[None, :].astype(np.float32) * inv[:, None]
        base = ax * half
        for i in range(q):
            cos[base + i] = np.cos(ang[i])
            cos[base + q + i] = np.cos(ang[i])
            sin[base + i] = -np.sin(ang[i])
            sin[base + q + i] = np.sin(ang[i])
            perm[base + q + i, base + i] = 1.0
            perm[base + i, base + q + i] = 1.0
    return cos, sin, perm


def _consts():
    c = {}
    c["ident"] = np.eye(128, dtype=np.float32)
    c["ones"] = np.ones((128, 512), np.float32)
    cosA, sinA, pA = _rope_tables(64, TS)
    cosB, sinB, pB = _rope_tables(32, TS)
    c["cosA"] = np.tile(cosA, (2, 1))
    c["sinA"] = np.tile(sinA, (2, 1))
    c["cosB"] = np.tile(cosB, (4, 1))
    c["sinB"] = np.tile(sinB, (4, 1))
    PA = np.zeros((128, 128), np.float32)
    PB = np.zeros((128, 128), np.float32)
    for i in range(2):
        PA[i * 64:(i + 1) * 64, i * 64:(i + 1) * 64] = pA
    for i in range(4):
        PB[i * 32:(i + 1) * 32, i * 32:(i + 1) * 32] = pB
    c["permA"] = PA
    c["permB"] = PB
    p = np.arange(128)[:, None]
    f = np.arange(128)[None, :]
    c["mprev"] = (p >= f).astype(np.float32)
    c["mnext"] = (p <= f).astype(np.float32)
    p = np.arange(64)[:, None]
    f = np.arange(64)[None, :]
    rep = lambda m: np.tile(m.astype(np.float32), (1, 6))
    c["m_lt"] = rep(p < f)
    c["m_le"] = rep(p <= f)
    c["m_gt"] = rep(p > f)
    c["m_ge"] = rep(p >= f)
    c["n_lt"] = -rep(p < f)
    c["n_le"] = -rep(p <= f)
    c["n_gt"] = -rep(p > f)
    c["n_ge"] = -rep(p >= f)
    c["i6"] = rep(p == f)
    rs = np.ones((64, 6 * 512), np.float32)
    rs[:, ::64] = 0.0
    c["rstart"] = rs
    return c


CONST_SHAPES = {k: v.shape for k, v in _consts().items()}

WEIGHT_SHAPES = dict(
    ada_w=(L, D, 6 * D), ada_b=(L, 6 * D), norm1_g=(L, D), norm2_g=(L, D), w_in=(L, D, INC), a_sink=(L, 6),
    b_lambda=(L, 4, 32), b_subln_g=(L, 64), c_conv=(L, 3, 1152), c_w0=(L, 2, 384), c_w2=(L, 2, 64, 384),
    c_a0=(L, 2, 384), c_a2=(L, 2, 64, 384), c_g2=(L, 128, 384), c_kk=(L, 384), c_ka=(L, 384), c_rk=(L, 6, 64),
    c_lnx_g=(L, 384), c_lnx_b=(L, 384), w_out=(L, D, D), ffn_w1=(L, D, DFF), ffn_w3=(L, D, DFF),
    ffn_w2=(L, DFF, D), final_g=(D,), c_ctx=(D,))

CORE_IN_SHAPES = dict(
    xp=(NPL * TP, D), xs=(TS, D), cak=(L, PAST, 128), cav=(L, PAST, 128), cbk=(L, PAST, 256), cbv=(L, PAST, 256),
    scf=(L, 6, 64, 64), scb=(L, 6, 64, 64), cs=(D,))

OUT_SHAPES = dict(
    yp=(NPL * TP, D), ys=(TS, D), nak=(NPL, L, TP, 128), nav=(NPL, L, TP, 128), nbk=(NPL, L, TP, 256),
    nbv=(NPL, L, TP, 256), ncf=(NPL, L, 6, 64, 64), ncb=(NPL, L, 6, 64, 64))


def build(nlayers=L, debug=False, do_sample=True):
    nc = bass.Bass("TRN2", target_bir_lowering=False)
    I = {}
    for k, s in list(WEIGHT_SHAPES.items()) + list(CORE_IN_SHAPES.items()) + list(CONST_SHAPES.items()):
        I[k] = nc.dram_tensor(k, list(s), F32, kind="ExternalInput").ap()
    O = {k: nc.dram_tensor(k, list(s), F32, kind="ExternalOutput").ap() for k, s in OUT_SHAPES.items()}
    skind = "ExternalOutput" if debug else "Internal"
    streams = []
    for sname, tt in (("p", NPL * TP), ("s", TS)):
        S = {"T": tt, "name": sname}
        for nm, shp in (("XT", [D, tt]), ("QA", [6, 64, tt]), ("KA", [2, 64, tt]), ("QB", [8, 32, tt]),
                        ("KB", [8, 32, tt]), ("VA", [tt, 128]), ("VB", [tt, 256]), ("RKV", [18, 64, tt]),
                        ("CW", [2, 64, tt]), ("CA", [2, 64, tt]), ("CG", [128, tt]), ("M", [tt, D]),
                        ("YF", [tt, 384])):
            S[nm] = nc.dram_tensor("%s_%s" % (nm, sname), shp, F32, kind=skind).ap()
        streams.append(S)
    SP_, SS_ = streams
    if not do_sample:
        streams = [SP_]
    seqs = [(SP_, 0, TP, False, 0), (SP_, TP, TP, False, 1)]
    if do_sample:
        seqs.append((SS_, 0, TS, True, -1))

    with ExitStack() as glob:
        p = Prog(nc, glob)

        uid = [0]

        def sb(st, name, shape):
            uid[0] += 1
            return TT(st.enter_context(nc.sbuf_tensor("s%d_%s" % (uid[0], name), list(shape), F32)))

        ps = [TT(glob.enter_context(nc.psum_tensor("ps%d" % i, [128, 512], F32))) for i in range(8)]
        psi = [0]

        def nps():
            t = ps[2 + psi[0] % 6]
            psi[0] += 1
            return t

        def mm(out, lhsT, rhs, start, stop, r, w):
            p.op("pe", lambda e: e.matmul(out, lhsT, rhs, start=start, stop=stop), reads=r, writes=w)

        def tr(out, in_, idn, r, w):
            p.op("pe", lambda e: e.transpose(out, in_, idn), reads=r, writes=w)

        def ld(out, in_, w, r=()):
            p.dma("sp", out, in_, reads=r, writes=w)

        def stq(out, in_, r):
            p.dma("pool", out, in_, reads=r)

        def act(out, in_, func, r, w, **kw):
            p.op("act", lambda e: e.activation(out=out, in_=in_, func=func, **kw), reads=r, writes=w)

        def tt_(eng, out, in0, in1, op, r, w):
            p.op(eng, lambda e: e.tensor_tensor(out=out, in0=in0, in1=in1, op=op), reads=r, writes=w)

        def ts_(eng, out, in0, s1, s2, op0, op1, r, w):
            if s2 is None:
                p.op(eng, lambda e: e.tensor_single_scalar(out=out, in_=in0, scalar=s1, op=op0), reads=r, writes=w)
            else:
                p.op(eng, lambda e: e.tensor_scalar(out=out, in0=in0, scalar1=s1, scalar2=s2, op0=op0, op1=op1), reads=r, writes=w)

        def stt(out, in0, scalar, in1, op0, op1, r, w):
            p.op("dve", lambda e: e.scalar_tensor_tensor(out=out, in0=in0, scalar=scalar, in1=in1, op0=op0, op1=op1), reads=r, writes=w)

        def cp(eng, out, in_, r, w):
            if eng == "act":
                p.op("act", lambda e: e.copy(out=out, in_=in_), reads=r, writes=w)
            else:
                p.op(eng, lambda e: e.tensor_copy(out=out, in_=in_), reads=r, writes=w)

        def rsqrt_(out, in_, scale, bias, r, w, tmpw):
            act(out, in_, AF.Sqrt, r, w, scale=scale, bias=bias)
            p.op("dve", lambda e: e.reciprocal(out=out, in_=out), reads=w, writes=w)

        ident = sb(glob, "ident", [128, 128])
        ones = sb(glob, "ones", [128, 512])
        epsc = sb(glob, "epsc", [128, 2])
        ld(ident[:], I["ident"][:, :], [ident])
        ld(ones[:], I["ones"][:, :], [ones])
        p.op("pool", lambda e: e.memset(epsc[:, 0:1], EPS), writes=[epsc])
        p.op("pool", lambda e: e.memset(epsc[:, 1:2], GN_EPS), writes=[epsc])

        stgc = sb(glob, "stgc", [128, 512])

        def load_cols(st, src2d, rows, w, name):
            dst = sb(st, name, [w, rows])
            for r0 in range(0, rows, 128):
                rr = min(128, rows - r0)
                ld(stgc[0:rr, 0:w], src2d[r0:r0 + rr, :], [stgc])
                pt = nps()
                tr(pt[0:w, 0:rr], stgc[0:rr, 0:w], ident[0:rr, 0:rr], [stgc, ident], [pt])
                cp("act", dst[:, r0:r0 + rr], pt[0:w, 0:rr], [pt], [dst])
            return dst

        def bcast_row(st, src_row, n, name, dst=None):
            if dst is None:
                dst = sb(st, name, [128, n])
            ld(stgc[0:1, 0:n], src_row, [stgc])
            pt = nps()
            mm(pt[:, 0:n], ones[0:1, 0:128], stgc[0:1, 0:n], True, True, [ones, stgc], [pt])
            cp("act", dst[:, 0:n], pt[:, 0:n], [pt], [dst])
            return dst

        adab = load_cols(glob, I["ada_b"].rearrange("l (j p) -> (l j) p", p=128), L * 48, 128, "adab")
        g1c = load_cols(glob, I["norm1_g"].rearrange("l (j p) -> (l j) p", p=128), L * 8, 128, "g1c")
        g2c = load_cols(glob, I["norm2_g"].rearrange("l (j p) -> (l j) p", p=128), L * 8, 128, "g2c")
        gfc = load_cols(glob, I["final_g"].rearrange("(j p) -> j p", p=128), 8, 128, "gfc")
        cct = load_cols(glob, I["c_ctx"].rearrange("(j p) -> j p", p=128), 8, 128, "cct")
        cst = load_cols(glob, I["cs"].rearrange("(j p) -> j p", p=128), 8, 128, "cst")
        convc = load_cols(glob, I["c_conv"].rearrange("l k (t c) -> (l k t) c", c=64), L * 3 * 18, 64, "convc")
        w0c = load_cols(glob, I["c_w0"].rearrange("l d (h c) -> (l d h) c", c=64), L * 12, 64, "w0c")
        a0c = load_cols(glob, I["c_a0"].rearrange("l d (h c) -> (l d h) c", c=64), L * 12, 64, "a0c")
        kkc = load_cols(glob, I["c_kk"].rearrange("l (h c) -> (l h) c", c=64), L * 6, 64, "kkc")
        kac = load_cols(glob, I["c_ka"].rearrange("l (h c) -> (l h) c", c=64), L * 6, 64, "kac")
        rkc_ = load_cols(glob, I["c_rk"].rearrange("l h c -> (l h) c"), L * 6, 64, "rkc")

        mod = sb(glob, "mod", [128, L * 48, 2])
        modA1 = sb(glob, "modA1", [128, L * 8, 2])
        modA2 = sb(glob, "modA2", [128, L * 8, 2])
        with ExitStack() as st:
            silc = sb(st, "silc", [128, 8, 2])
            act(silc[:, :, 0], cct[:, :], AF.Silu, [cct], [silc])
            act(silc[:, :, 1], cst[:, :], AF.Silu, [cst], [silc])
            pcs = [sb(st, "adapc%d" % i, [128, 8, 512]) for i in range(2)]
            k_ = 0
            for l in range(nlayers):
                wv = I["ada_w"][l].rearrange("(c p) f -> p c f", p=128)
                for pc in range(12):
                    pt_ = pcs[k_ % 2]
                    k_ += 1
                    ld(pt_[:], wv[:, :, pc * 512:(pc + 1) * 512], [pt_])
                    for jb in range(4):
                        pq = nps()
                        for k in range(8):
                            mm(pq[:, 0:2], pt_[:, k, jb * 128:(jb + 1) * 128], silc[:, k, :], k == 0, k == 7, [pt_, silc], [pq])
                        j = l * 48 + pc * 4 + jb
                        ts_("dve", mod[:, j, :], pq[:, 0:2], adab[:, j:j + 1], None, ALU.add, ALU.bypass, [pq, adab], [mod])
                stt(modA1[:, l * 8:(l + 1) * 8, :], mod[:, l * 48 + 8:l * 48 + 16, :], 1.0,
                    g1c[:, l * 8:(l + 1) * 8].unsqueeze(2).to_broadcast([128, 8, 2]), ALU.add, ALU.mult, [mod, g1c], [modA1])
                stt(modA2[:, l * 8:(l + 1) * 8, :], mod[:, l * 48 + 32:l * 48 + 40, :], 1.0,
                    g2c[:, l * 8:(l + 1) * 8].unsqueeze(2).to_broadcast([128, 8, 2]), ALU.add, ALU.mult, [mod, g2c], [modA2])
            p.barrier()

        def MOD(l, which, c, cv):
            j = l * 48 + which * 8 + c
            return mod[:, j, cv:cv + 1]

        with ExitStack() as st:
            xin = [sb(st, "xin%d" % i, [128, D]) for i in range(2)]
            xo = [sb(st, "xo%d" % i, [128, 8, 128]) for i in range(2)]
            k_ = 0
            for S, src in ((SP_, I["xp"]), (SS_, I["xs"])):
                if S is SS_ and not do_sample:
                    continue
                for b in range(S["T"] // 128):
                    a_ = xin[k_ % 2]
                    o_ = xo[k_ % 2]
                    k_ += 1
                    ld(a_[:], src[b * 128:(b + 1) * 128, :], [a_])
                    for half in range(2):
                        pt = nps()
                        for c in range(4):
                            cc = half * 4 + c
                            tr(pt[:, c * 128:(c + 1) * 128], a_[:, cc * 128:(cc + 1) * 128], ident[:, :], [a_, ident], [pt])
                        cp("act" if half == 0 else "dve", o_[:, half * 4:(half + 1) * 4, :],
                           pt[:, :].rearrange("p (c t) -> p c t", t=128), [pt], [o_])
                    stq(S["XT"].rearrange("(c p) t -> p c t", p=128)[:, :, b * 128:(b + 1) * 128], o_[:], [o_])
            p.barrier()

        for l in range(nlayers):
            lam_init = 0.8 - 0.6 * math.exp(-0.3 * l)
            with ExitStack() as st:
                win = sb(st, "win", [128, 8, INC])
                wv = I["w_in"][l].rearrange("(c p) f -> p c f", p=128)
                for k in range(8):
                    ld(win[:, k, :], wv[:, k, :], [win])
                xt = sb(st, "xt", [128, 8, 512])
                sq = sb(st, "sq", [128, 8, 512])
                h = sb(st, "h", [128, 8, 512])
                rstd = sb(st, "rstd", [128, 512])
                stg = [sb(st, "stg%d" % i, [128, 512]) for i in range(3)]
                zs = sb(st, "zs", [128, 512])
                t2 = sb(st, "t2", [128, 512])
                cosA = sb(st, "cosA", [128, 512]); sinA = sb(st, "sinA", [128, 512])
                cosB = sb(st, "cosB", [128, 512]); sinB = sb(st, "sinB", [128, 512])
                permA = sb(st, "permA", [128, 128]); permB = sb(st, "permB", [128, 128])
                ld(permA[:], I["permA"][:, :], [permA]); ld(permB[:], I["permB"][:, :], [permB])
                vst = [sb(st, "vst%d" % i, [128, 768]) for i in range(2)]
                sgi = [0]
                for S in streams:
                    latent = S is SS_
                    cv = 1 if latent else 0
                    T = S["T"]
                    XTv = S["XT"].rearrange("(c p) t -> p c t", p=128)
                    for t0 in range(0, T, 512):
                        ld(xt[:], XTv[:, :, t0:t0 + 512], [xt])
                        if latent:
                            ld(cosA[:], I["cosA"][:, t0:t0 + 512], [cosA]); ld(sinA[:], I["sinA"][:, t0:t0 + 512], [sinA])
                            ld(cosB[:], I["cosB"][:, t0:t0 + 512], [cosB]); ld(sinB[:], I["sinB"][:, t0:t0 + 512], [sinB])
                        act(sq[:], xt[:], AF.Square, [xt], [sq])
                        pss = nps()
                        for c in range(8):
                            mm(pss[:, :], ones[:, 0:128], sq[:, c, :], c == 0, c == 7, [ones, sq], [pss])
                        rsqrt_(rstd[:], pss[:, :], 1.0 / D, epsc[:, 0:1], [pss, epsc], [rstd], None)
                        for c in range(8):
                            stt(h[:, c, :], xt[:, c, :], modA1[:, l * 8 + c, cv:cv + 1], rstd[:], ALU.mult, ALU.mult, [xt, modA1, rstd], [h])
                            act(h[:, c, :], h[:, c, :], AF.Identity, [h, mod], [h], bias=MOD(l, 0, c, cv), scale=1.0)
                        for j in range(23):
                            if j in (4, 9, 10):
                                continue
                            pz = nps()
                            for k in range(8):
                                mm(pz[:, :], win[:, k, j * 128:(j + 1) * 128], h[:, k, :], k == 0, k == 7, [win, h], [pz])
                            sg = stg[sgi[0] % 3]
                            sgi[0] += 1
                            rope = latent and j in (0, 1, 2, 3, 5, 6, 7, 8)
                            if rope:
                                isA = j <= 3
                                cs_, sn_, pm_ = (cosA, sinA, permA) if isA else (cosB, sinB, permB)
                                cp("act", zs[:], pz[:, :], [pz], [zs])
                                pr = nps()
                                mm(pr[:, :], pm_[:, :], zs[:], True, True, [pm_, zs], [pr])
                                tt_("dve", t2[:], pr[:, :], sn_[:], ALU.mult, [pr, sn_], [t2])
                                tt_("pool", sg[:], zs[:], cs_[:], ALU.mult, [zs, cs_], [sg])
                                tt_("pool", sg[:], sg[:], t2[:], ALU.add, [sg, t2], [sg])
                            elif j == 20:
                                act(sg[:], pz[:, :], AF.Tanh, [pz], [sg])
                            elif j == 22:
                                act(sg[:], pz[:, :], AF.Sigmoid, [pz], [sg])
                            else:
                                cp("act" if j % 2 == 0 else "dve", sg[:], pz[:, :], [pz], [sg])
                            sl = slice(t0, t0 + 512)
                            if j <= 2:
                                for hh in range(2):
                                    stq(S["QA"][2 * j + hh, :, sl], sg[hh * 64:(hh + 1) * 64, :], [sg])
                            elif j == 3:
                                for hh in range(2):
                                    stq(S["KA"][hh, :, sl], sg[hh * 64:(hh + 1) * 64, :], [sg])
                            elif j in (5, 6):
                                for q4 in range(4):
                                    stq(S["QB"][(j - 5) * 4 + q4, :, sl], sg[q4 * 32:(q4 + 1) * 32, :], [sg])
                            elif j in (7, 8):
                                for q4 in range(4):
                                    stq(S["KB"][(j - 7) * 4 + q4, :, sl], sg[q4 * 32:(q4 + 1) * 32, :], [sg])
                            elif 11 <= j <= 19:
                                for hh in range(2):
                                    stq(S["RKV"][(j - 11) * 2 + hh, :, sl], sg[hh * 64:(hh + 1) * 64, :], [sg])
                            elif j == 20:
                                for hh in range(2):
                                    stq(S["CW"][hh, :, sl], sg[hh * 64:(hh + 1) * 64, :], [sg])
                            elif j == 21:
                                for hh in range(2):
                                    stq(S["CA"][hh, :, sl], sg[hh * 64:(hh + 1) * 64, :], [sg])
                            else:
                                stq(S["CG"][:, sl], sg[:, :], [sg])
                        for i in range(4):
                            vs_ = vst[i % 2]
                            groups = [(512, 128, 0), (1152, 256, 128)]
                            if not latent:
                                groups += [(384, 128, 384), (896, 256, 512)]
                            pv = nps()
                            pk = nps()
                            for (c0, wd, o0) in groups:
                                pp, oo = (pv, o0) if o0 < 384 else (pk, o0 - 384)
                                for k in range(8):
                                    mm(pp[:, oo:oo + wd], h[:, k, i * 128:(i + 1) * 128], win[:, k, c0:c0 + wd], k == 0, k == 7, [h, win], [pp])
                            cp("act", vs_[:, 0:384], pv[:, 0:384], [pv], [vs_])
                            r0 = t0 + i * 128
                            stq(S["VA"][r0:r0 + 128, :], vs_[:, 0:128], [vs_])
                            stq(S["VB"][r0:r0 + 128, :], vs_[:, 128:384], [vs_])
                            if not latent:
                                cp("dve", vs_[:, 384:768], pk[:, 0:384], [pk], [vs_])
                                bl = r0 // TP
                                rr = r0 % TP
                                stq(O["nav"][bl, l, rr:rr + 128, :], vs_[:, 0:128], [vs_])
                                stq(O["nbv"][bl, l, rr:rr + 128, :], vs_[:, 128:384], [vs_])
                                stq(O["nak"][bl, l, rr:rr + 128, :], vs_[:, 384:512], [vs_])
                                stq(O["nbk"][bl, l, rr:rr + 128, :], vs_[:, 512:768], [vs_])
                p.barrier()

            with ExitStack() as st:
                sinkb = bcast_row(st, I["a_sink"][l:l + 1, :], 6, "sinkb")
                act(sinkb[:, 0:6], sinkb[:, 0:6], AF.Exp, [sinkb], [sinkb])
                mprev = sb(st, "mprev", [128, 128]); mnext = sb(st, "mnext", [128, 128])
                ld(mprev[:], I["mprev"][:, :], [mprev]); ld(mnext[:], I["mnext"][:, :], [mnext])
                kT = sb(st, "kT", [64, TS + PAST])
                qT = sb(st, "qT", [64, 3, TS])
                vt = sb(st, "vt", [128, (TS + PAST) // 128, 65])
                p.op("pool", lambda e: e.memset(vt[:, :, 64:65], 1.0), writes=[vt])
                cstg = sb(st, "cstg", [128, 4, 64])
                pT = [sb(st, "pT%d" % i, [128, 384]) for i in range(3)]
                ao = [sb(st, "ao%d" % i, [128, 3, 64]) for i in range(2)]
                den = sb(st, "den", [128, 3])
                pti = [0]
                for (S, o, T, latent, bl) in seqs:
                    nb = T // 128
                    for g in range(2):
                        ld(kT[:, 0:T], S["KA"][g, :, o:o + T], [kT])
                        ld(qT[:, :, 0:T], S["QA"][3 * g:3 * g + 3, :, o:o + T].rearrange("m d t -> d m t"), [qT])
                        ld(vt[:, 0:nb, 0:64], S["VA"][o:o + T, g * 64:(g + 1) * 64].rearrange("(n p) d -> p n d", p=128), [vt])
                        nctx = 0
                        if latent:
                            nctx = PAST // 128
                            ld(cstg[:], I["cak"][l, :, g * 64:(g + 1) * 64].rearrange("(n p) d -> p n d", p=128), [cstg])
                            pt = nps()
                            for n_ in range(4):
                                tr(pt[0:64, n_ * 128:(n_ + 1) * 128], cstg[:, n_, :], ident[:, :], [cstg, ident], [pt])
                            cp("act", kT[:, T:T + PAST], pt[0:64, :], [pt], [kT])
                            ld(vt[:, nb:nb + 4, 0:64], I["cav"][l, :, g * 64:(g + 1) * 64].rearrange("(n p) d -> p n d", p=128), [vt])
                        for b in range(nb):
                            if latent:
                                tiles = [(kb, (mprev if kb == b - 1 else (mnext if kb == b + 1 else None)))
                                         for kb in (b - 1, b, b + 1) if 0 <= kb < nb]
                                tiles += [(nb + c_, None) for c_ in range(nctx)]
                            else:
                                tiles = [(kb, None) for kb in range(nb)]
                            po = ps[b % 2]
                            for ti, (kb, msk) in enumerate(tiles):
                                pss = nps()
                                mm(pss[:, 0:384], kT[:, kb * 128:(kb + 1) * 128], qT[:, :, b * 128:(b + 1) * 128], True, True, [kT, qT], [pss])
                                pt_ = pT[pti[0] % 3]
                                pti[0] += 1
                                act(pt_[:], pss[:, 0:384], AF.Exp, [pss], [pt_], scale=0.125)
                                if msk is not None:
                                    tt_("pool", pt_[:, :].rearrange("p (m q) -> p m q", q=128), pt_[:, :].rearrange("p (m q) -> p m q", q=128),
                                        msk[:, :].unsqueeze(1).to_broadcast([128, 3, 128]), ALU.mult, [pt_, msk], [pt_])
                                for m in range(3):
                                    mm(po[:, m * 65:(m + 1) * 65], pt_[:, m * 128:(m + 1) * 128], vt[:, kb, :], ti == 0 and m == 0, ti == len(tiles) - 1 and m == 2, [pt_, vt], [po])
                            pov = po[:, 0:195].rearrange("p (m d) -> p m d", d=65)
                            tt_("dve", den[:, :].unsqueeze(2), pov[:, :, 64:65], sinkb[:, 3 * g:3 * g + 3].unsqueeze(2), ALU.add, [po, sinkb], [den])
                            p.op("dve", lambda e: e.reciprocal(out=den[:, :], in_=den[:, :]), reads=[den], writes=[den])
                            a_ = ao[b % 2]
                            tt_("dve", a_[:], pov[:, :, 0:64], den[:, :].unsqueeze(2).to_broadcast([128, 3, 64]), ALU.mult, [po, den], [a_])
                            r0 = o + b * 128
                            stq(S["M"][r0:r0 + 128, g * 192:(g + 1) * 192], a_[:].rearrange("p m d -> p (m d)"), [a_])
                p.barrier()

            with ExitStack() as st:
                lamr = bcast_row(st, I["b_lambda"][l:l + 1].rearrange("o a d -> o (a d)"), 128, "lamr")
                lam2 = sb(st, "lam2", [128, 2, 32])
                lv = lamr[:, :].rearrange("p (a b d) -> p a b d", a=2, b=2)
                tt_("dve", lam2[:], lv[:, :, 0, :], lv[:, :, 1, :], ALU.mult, [lamr], [lam2])
                lam1 = sb(st, "lam1", [128, 2])
                p.op("dve", lambda e: e.tensor_reduce(out=lam1[:, :], in_=lam2[:], axis=AX.X, op=ALU.add), reads=[lam2], writes=[lam1])
                act(lam1[:, :], lam1[:, :], AF.Exp, [lam1], [lam1])
                nlam = sb(st, "nlam", [128, 1])
                tt_("dve", nlam[:, :], lam1[:, 1:2], lam1[:, 0:1], ALU.subtract, [lam1], [nlam])
                ts_("dve", nlam[:, :], nlam[:, :], -lam_init, None, ALU.add, ALU.bypass, [nlam], [nlam])
                gsub = bcast_row(st, I["b_subln_g"][l:l + 1, :], 64, "gsub")
                ts_("dve", gsub[:, :], gsub[:, :], 1.0 - lam_init, None, ALU.mult, ALU.bypass, [gsub], [gsub])
                kT = sb(st, "kTb", [32, 2, TS + PAST])
                qT = sb(st, "qTb", [32, 2, TS])
                vt = sb(st, "vtb", [128, (TS + PAST) // 128, 65])
                p.op("pool", lambda e: e.memset(vt[:, :, 64:65], 1.0), writes=[vt])
                cstg = sb(st, "cstgb", [128, 4, 32])
                pT = [sb(st, "pTb%d" % i, [128, 512]) for i in range(3)]
                o1 = sb(st, "o1", [128, 4, 64]); o2 = sb(st, "o2", [128, 4, 64]); o3 = sb(st, "o3", [128, 4, 64])
                rd = sb(st, "rd", [128, 2, 4]); ssq = sb(st, "ssq", [128, 4])
                bo = [sb(st, "bo%d" % i, [128, 4, 64]) for i in range(2)]
                pti = [0]
                boi = [0]
                for (S, o, T, latent, bl) in seqs:
                    nk = T // 128 + (PAST // 128 if latent else 0)
                    nown = T // 128
                    qn = min(512, T)
                    for hd in range(4):
                        ld(vt[:, 0:nown, 0:64], S["VB"][o:o + T, hd * 64:(hd + 1) * 64].rearrange("(n p) d -> p n d", p=128), [vt])
                        ld(kT[:, :, 0:T], S["KB"][2 * hd:2 * hd + 2, :, o:o + T].rearrange("m d t -> d m t"), [kT])
                        ld(qT[:, :, 0:T], S["QB"][2 * hd:2 * hd + 2, :, o:o + T].rearrange("m d t -> d m t"), [qT])
                        if latent:
                            ld(vt[:, nown:nk, 0:64], I["cbv"][l, :, hd * 64:(hd + 1) * 64].rearrange("(n p) d -> p n d", p=128), [vt])
                            for mp in range(2):
                                c0 = hd * 64 + mp * 32
                                ld(cstg[:], I["cbk"][l, :, c0:c0 + 32].rearrange("(n p) d -> p n d", p=128), [cstg])
                                pt = nps()
                                for n_ in range(4):
                                    tr(pt[0:32, n_ * 128:(n_ + 1) * 128], cstg[:, n_, :], ident[:, :], [cstg, ident], [pt])
                                cp("act", kT[:, mp, T:T + PAST], pt[0:32, :], [pt], [kT])
                        for q0 in range(0, T, qn):
                            nqb = qn // 128
                            pos = [ps[0], ps[1]]
                            for mp in range(2):
                                for kt in range(nk):
                                    pss = nps()
                                    mm(pss[:, 0:qn], kT[:, mp, kt * 128:(kt + 1) * 128], qT[:, mp, q0:q0 + qn], True, True, [kT, qT], [pss])
                                    pt_ = pT[pti[0] % 3]
                                    pti[0] += 1
                                    act(pt_[:, 0:qn], pss[:, 0:qn], AF.Exp, [pss], [pt_], scale=32.0 ** -0.5)
                                    for qb in range(nqb):
                                        mm(pos[mp][:, qb * 65:(qb + 1) * 65], pt_[:, qb * 128:(qb + 1) * 128], vt[:, kt, :], kt == 0 and qb == 0, kt == nk - 1 and qb == nqb - 1, [pt_, vt], [pos[mp]])
                            v1 = pos[0][:, 0:nqb * 65].rearrange("p (q d) -> p q d", d=65)
                            v2 = pos[1][:, 0:nqb * 65].rearrange("p (q d) -> p q d", d=65)
                            p.op("dve", lambda e: e.reciprocal(out=rd[:, 0, 0:nqb].unsqueeze(2), in_=v1[:, :, 64:65]), reads=[pos[0]], writes=[rd])
                            p.op("dve", lambda e: e.reciprocal(out=rd[:, 1, 0:nqb].unsqueeze(2), in_=v2[:, :, 64:65]), reads=[pos[1]], writes=[rd])
                            tt_("dve", o1[:, 0:nqb, :], v1[:, :, 0:64], rd[:, 0, 0:nqb].unsqueeze(2).to_broadcast([128, nqb, 64]), ALU.mult, [pos[0], rd], [o1])
                            tt_("dve", o2[:, 0:nqb, :], v2[:, :, 0:64], rd[:, 1, 0:nqb].unsqueeze(2).to_broadcast([128, nqb, 64]), ALU.mult, [pos[1], rd], [o2])
                            stt(o1[:, 0:nqb, :], o2[:, 0:nqb, :], nlam[:, 0:1], o1[:, 0:nqb, :], ALU.mult, ALU.add, [o1, o2, nlam], [o1])
                            tt_("pool", o3[:, 0:nqb, :], o1[:, 0:nqb, :], o1[:, 0:nqb, :], ALU.mult, [o1], [o3])
                            p.op("dve", lambda e: e.tensor_reduce(out=ssq[:, 0:nqb], in_=o3[:, 0:nqb, :], axis=AX.X, op=ALU.add), reads=[o3], writes=[ssq])
                            rsqrt_(ssq[:, 0:nqb], ssq[:, 0:nqb], 1.0 / 64, epsc[:, 0:1], [ssq, epsc], [ssq], None)
                            b_ = bo[boi[0] % 2]
                            boi[0] += 1
                            tt_("dve", b_[:, 0:nqb, :], o1[:, 0:nqb, :], ssq[:, 0:nqb].unsqueeze(2).to_broadcast([128, nqb, 64]), ALU.mult, [o1, ssq], [b_])
                            tt_("pool", b_[:, 0:nqb, :], b_[:, 0:nqb, :], gsub[:, 0:64].unsqueeze(1).to_broadcast([128, nqb, 64]), ALU.mult, [b_, gsub], [b_])
                            r0 = o + q0
                            stq(S["M"][r0:r0 + qn, 384 + hd * 64:384 + (hd + 1) * 64].rearrange("(q p) d -> p q d", p=128), b_[:, 0:nqb, :], [b_])
                p.barrier()

            with ExitStack() as st:
                cw2 = sb(st, "cw2", [64, 2, 384]); ca2 = sb(st, "ca2", [64, 2, 384]); cg2 = sb(st, "cg2", [128, 384])
                ld(cw2[:], I["c_w2"][l].rearrange("d r c -> r d c"), [cw2])
                ld(ca2[:], I["c_a2"][l].rearrange("d r c -> r d c"), [ca2])
                ld(cg2[:], I["c_g2"][l], [cg2])
                lnxg = bcast_row(st, I["c_lnx_g"][l:l + 1, :], 384, "lnxg")
                lnxb = bcast_row(st, I["c_lnx_b"][l:l + 1, :], 384, "lnxb")
                MK = {}
                for nm in ("m_lt", "m_le", "m_gt", "m_ge", "n_lt", "n_le", "n_gt", "n_ge", "i6"):
                    MK[nm] = sb(st, nm, [64, 384])
                    ld(MK[nm][:], I[nm][:, :], [MK[nm]])
                rstart = sb(st, "rstart", [64, 768])
                ld(rstart[:], I["rstart"][:, 0:768], [rstart])
                NSG = 128
                rkvx = sb(st, "rkvx", [64, 18, NSG + 2])
                rkv = sb(st, "rkv", [64, 18, NSG])
                cwt = sb(st, "cwt", [64, NSG]); cat = sb(st, "cat", [64, NSG]); cgt = sb(st, "cgt", [128, NSG])
                sig = sb(st, "sig", [64, 6, NSG]); a_ = sb(st, "a_", [64, 6, NSG])
                kk = sb(st, "kk", [64, 6, NSG]); kd = sb(st, "kd", [64, 6, NSG]); bb = sb(st, "bb", [64, 6, NSG])
                rkc = sb(st, "rkcs", [64, 6, NSG])
                Lc = sb(st, "Lc", [64, 6, NSG]); Lx = sb(st, "Lx", [64, 6, NSG]); Ld = sb(st, "Ld", [64, 6, NSG])
                E = sb(st, "E", [64, 6, NSG])
                Kt_ = sb(st, "Kt_", [64, 6, NSG]); Rt_ = sb(st, "Rt_", [64, 6, NSG]); Dh = sb(st, "Dh", [64, 6, NSG])
                Bh = sb(st, "Bh", [64, 6, NSG]); Dg = sb(st, "Dg", [64, 6, NSG]); nBg = sb(st, "nBg", [64, 6, NSG])
                gC = sb(st, "gC", [64, 6, NSG // 64])
                tmp = sb(st, "tmpc", [64, 6, NSG])
                H = sb(st, "H", [64, 6, 64])
                hst = sb(st, "hst", [64, 6, 64])
                Vt = sb(st, "Vt", [64, 384]); Dgt = sb(st, "Dgt", [64, 384]); nBgt = sb(st, "nBgt", [64, 384])
                P0 = sb(st, "P0", [64, 384]); PT0 = sb(st, "PT0", [64, 384]); P1 = sb(st, "P1", [64, 384]); PT1 = sb(st, "PT1", [64, 384])
                X0 = sb(st, "X0", [64, 384]); X1 = sb(st, "X1", [64, 384])
                AdT = sb(st, "AdT", [64, 384]); BdT = sb(st, "BdT", [64, 384]); nBbT = sb(st, "nBbT", [64, 384])
                Wb = sb(st, "Wb", [64, 384]); Zb = sb(st, "Zb", [64, 384])
                coef = sb(st, "coef", [64, 6]); yb = sb(st, "yb", [64, 384]); yf = sb(st, "yf", [64, 384]); y2 = sb(st, "y2", [64, 384])
                gst = sb(st, "gst", [64, 6]); gs2 = sb(st, "gs2", [64, 6])
                hv = lambda t, h_: t[:, h_ * 64:(h_ + 1) * 64]

                for (S, o, T, latent, bl) in seqs:
                    nsg = (T + NSG - 1) // NSG
                    for d in range(2):
                        if latent:
                            src = I["scf" if d == 0 else "scb"][l]
                            ld(hst[:], src.rearrange("h i j -> i h j"), [hst])
                            pt = nps()
                            for h_ in range(6):
                                tr(pt[0:64, h_ * 64:(h_ + 1) * 64], hst[:, h_, :], ident[0:64, 0:64], [hst, ident], [pt])
                            cp("act", H[:].rearrange("p h i -> p (h i)"), pt[0:64, 0:384], [pt], [H])
                        else:
                            p.op("pool", lambda e: e.memset(H[:], 0.0), writes=[H])
                        segs = list(range(nsg)) if d == 0 else list(range(nsg - 1, -1, -1))
                        for sg_ in segs:
                            ts0 = sg_ * NSG
                            n = min(NSG, T - ts0)
                            nch = n // 64
                            lo = max(ts0 - 1, 0)
                            hi = min(ts0 + n + 1, T)
                            if ts0 == 0:
                                p.op("pool", lambda e: e.memset(rkvx[:, :, 0:1], 0.0), writes=[rkvx])
                            if ts0 + n == T:
                                p.op("pool", lambda e: e.memset(rkvx[:, :, n + 1:n + 2], 0.0), writes=[rkvx])
                            ld(rkvx[:, :, lo - (ts0 - 1):hi - (ts0 - 1)], S["RKV"][:, :, o + lo:o + hi].rearrange("c d t -> d c t"), [rkvx])
                            ld(cwt[:, 0:n], S["CW"][d, :, o + ts0:o + ts0 + n], [cwt])
                            ld(cat[:, 0:n], S["CA"][d, :, o + ts0:o + ts0 + n], [cat])
                            if d == 1:
                                ld(cgt[:, 0:n], S["CG"][:, o + ts0:o + ts0 + n], [cgt])
                            for c in range(18):
                                cb = (l * 3) * 18 + c
                                ts_("dve", rkv[:, c, 0:n], rkvx[:, c, 1:n + 1], convc[:, cb + 18:cb + 19], None, ALU.mult, ALU.bypass, [rkvx, convc], [rkv])
                                stt(rkv[:, c, 0:n], rkvx[:, c, 0:n], convc[:, cb:cb + 1], rkv[:, c, 0:n], ALU.mult, ALU.add, [rkvx, convc, rkv], [rkv])
                                stt(rkv[:, c, 0:n], rkvx[:, c, 2:n + 2], convc[:, cb + 36:cb + 37], rkv[:, c, 0:n], ALU.mult, ALU.add, [rkvx, convc, rkv], [rkv])
                            r_ = lambda h_: rkv[:, h_, 0:n]
                            k_ = lambda h_: rkv[:, 6 + h_, 0:n]
                            v_ = lambda h_: rkv[:, 12 + h_, 0:n]
                            for h_ in range(6):
                                ci = (l * 2 + d) * 6 + h_
                                pw = nps()
                                mm(pw[0:64, 0:n], cw2[:, d, h_ * 64:(h_ + 1) * 64], cwt[:, 0:n], True, True, [cw2, cwt], [pw])
                                act(sig[:, h_, 0:n], pw[0:64, 0:n], AF.Sigmoid, [pw, w0c], [sig], bias=w0c[:, ci:ci + 1], scale=1.0)
                                pa = nps()
                                mm(pa[0:64, 0:n], ca2[:, d, h_ * 64:(h_ + 1) * 64], cat[:, 0:n], True, True, [ca2, cat], [pa])
                                act(a_[:, h_, 0:n], pa[0:64, 0:n], AF.Sigmoid, [pa, a0c], [a_], bias=a0c[:, ci:ci + 1], scale=1.0)
                                ts_("dve", kk[:, h_, 0:n], k_(h_), kkc[:, l * 6 + h_:l * 6 + h_ + 1], None, ALU.mult, ALU.bypass, [rkv, kkc], [kk])
                                tt_("pool", tmp[:, h_, 0:n], kk[:, h_, 0:n], kk[:, h_, 0:n], ALU.mult, [kk], [tmp])
                                pk_ = nps()
                                mm(pk_[0:64, 0:n], ones[0:64, 0:64], tmp[:, h_, 0:n], True, True, [ones, tmp], [pk_])
                                act(tmp[:, h_, 0:n], pk_[0:64, 0:n], AF.Sqrt, [pk_], [tmp])
                                ts_("dve", tmp[:, h_, 0:n], tmp[:, h_, 0:n], 1e-12, None, ALU.max, ALU.bypass, [tmp], [tmp])
                                p.op("dve", lambda e: e.reciprocal(out=tmp[:, h_, 0:n], in_=tmp[:, h_, 0:n]), reads=[tmp], writes=[tmp])
                                tt_("dve", kk[:, h_, 0:n], kk[:, h_, 0:n], tmp[:, h_, 0:n], ALU.mult, [kk, tmp], [kk])
                                ts_("dve", kd[:, h_, 0:n], a_[:, h_, 0:n], -1.0, kac[:, l * 6 + h_:l * 6 + h_ + 1], ALU.add, ALU.mult, [a_, kac], [kd])
                                stt(kd[:, h_, 0:n], kd[:, h_, 0:n], 1.0, k_(h_), ALU.add, ALU.mult, [kd, rkv], [kd])
                                stt(rkc[:, h_, 0:n], r_(h_), rkc_[:, l * 6 + h_:l * 6 + h_ + 1], kd[:, h_, 0:n], ALU.mult, ALU.mult, [rkv, rkc_, kd], [rkc])
                            tt_("pool", bb[:, :, 0:n], kk[:, :, 0:n], a_[:, :, 0:n], ALU.mult, [kk, a_], [bb])
                            if n == NSG:
                                p.op("dve", lambda e: e.tensor_tensor_scan(out=Lc[:].rearrange("p h t -> p (h t)"), data0=rstart[:, :],
                                                                            data1=sig[:].rearrange("p h t -> p (h t)"), initial=0.0, op0=ALU.mult, op1=ALU.add),
                                     reads=[rstart, sig], writes=[Lc])
                            else:
                                for h_ in range(6):
                                    p.op("dve", lambda e: e.tensor_tensor_scan(out=Lc[:, h_, 0:n], data0=rstart[:, 0:n], data1=sig[:, h_, 0:n],
                                                                                initial=0.0, op0=ALU.mult, op1=ALU.add), reads=[rstart, sig], writes=[Lc])
                            tt_("pool", Lx[:, :, 0:n], Lc[:, :, 0:n], sig[:, :, 0:n], ALU.subtract, [Lc, sig], [Lx])
                            c4 = lambda t: t[:, :, 0:n].rearrange("p h (c s) -> p h c s", s=64)
                            Ltot = c4(Lc)[:, :, :, 63:64]
                            tt_("dve", c4(Ld), Ltot.to_broadcast([64, 6, nch, 64]), c4(Lc), ALU.subtract, [Lc], [Ld])
                            act(gC[:, :, 0:nch].unsqueeze(3), Ltot, AF.Exp, [Lc], [gC], scale=-DEC)
                            if d == 0:
                                act(E[:, :, 0:n], Lx[:, :, 0:n], AF.Exp, [Lx], [E], scale=-DEC)
                                tt_("dve", Kt_[:, :, 0:n], kk[:, :, 0:n], E[:, :, 0:n], ALU.mult, [kk, E], [Kt_])
                                act(E[:, :, 0:n], Lc[:, :, 0:n], AF.Exp, [Lc], [E], scale=-DEC)
                                tt_("dve", Rt_[:, :, 0:n], rkv[:, 0:6, 0:n], E[:, :, 0:n], ALU.mult, [rkv, E], [Rt_])
                                act(E[:, :, 0:n], Lc[:, :, 0:n], AF.Exp, [Lc], [E], scale=DEC)
                            else:
                                act(E[:, :, 0:n], Ld[:, :, 0:n], AF.Exp, [Ld], [E], scale=-DEC)
                                tt_("dve", Kt_[:, :, 0:n], kk[:, :, 0:n], E[:, :, 0:n], ALU.mult, [kk, E], [Kt_])
                                tt_("dve", c4(tmp), Ltot.to_broadcast([64, 6, nch, 64]), c4(Lx), ALU.subtract, [Lc, Lx], [tmp])
                                act(E[:, :, 0:n], tmp[:, :, 0:n], AF.Exp, [tmp], [E], scale=-DEC)
                                tt_("dve", Rt_[:, :, 0:n], rkv[:, 0:6, 0:n], E[:, :, 0:n], ALU.mult, [rkv, E], [Rt_])
                                act(E[:, :, 0:n], tmp[:, :, 0:n], AF.Exp, [tmp], [E], scale=DEC)
                            tt_("dve", Dh[:, :, 0:n], kd[:, :, 0:n], E[:, :, 0:n], ALU.mult, [kd, E], [Dh])
                            tt_("pool", Bh[:, :, 0:n], bb[:, :, 0:n], E[:, :, 0:n], ALU.mult, [bb, E], [Bh])
                            act(E[:, :, 0:n], (Ld if d == 0 else Lx)[:, :, 0:n], AF.Exp, [Ld, Lx], [E], scale=-DEC)
                            tt_("dve", Dg[:, :, 0:n], kd[:, :, 0:n], E[:, :, 0:n], ALU.mult, [kd, E], [Dg])
                            stt(nBg[:, :, 0:n], bb[:, :, 0:n], -1.0, E[:, :, 0:n], ALU.mult, ALU.mult, [bb, E], [nBg])
                            if d == 0:
                                nmAT, mAT, nmA, mBT, nmBT = MK["n_lt"], MK["m_lt"], MK["n_gt"], MK["m_le"], MK["n_le"]
                            else:
                                nmAT, mAT, nmA, mBT, nmBT = MK["n_gt"], MK["m_gt"], MK["n_lt"], MK["m_ge"], MK["n_ge"]
                            chunks = list(range(nch)) if d == 0 else list(range(nch - 1, -1, -1))
                            for c in chunks:
                                cs = slice(c * 64, (c + 1) * 64)
                                for (src_t, srcidx, dst_t) in ((rkv, 12, Vt), (Dg, 0, Dgt), (nBg, 0, nBgt)):
                                    pt = nps()
                                    for h_ in range(6):
                                        tr(pt[0:64, h_ * 64:(h_ + 1) * 64], src_t[:, srcidx + h_, cs], ident[0:64, 0:64], [src_t, ident], [pt])
                                    cp("act", dst_t[:, :], pt[0:64, 0:384], [pt], [dst_t])
                                pAbT, pBbT, pAdT, pBdT, pAb = nps(), nps(), nps(), nps(), nps()
                                for h_ in range(6):
                                    hs = slice(h_ * 64, (h_ + 1) * 64)
                                    mm(pAbT[0:64, hs], Bh[:, h_, cs], Kt_[:, h_, cs], True, True, [Bh, Kt_], [pAbT])
                                    mm(pBbT[0:64, hs], Bh[:, h_, cs], Rt_[:, h_, cs], True, True, [Bh, Rt_], [pBbT])
                                    mm(pAdT[0:64, hs], Dh[:, h_, cs], Kt_[:, h_, cs], True, True, [Dh, Kt_], [pAdT])
                                    mm(pBdT[0:64, hs], Dh[:, h_, cs], Rt_[:, h_, cs], True, True, [Dh, Rt_], [pBdT])
                                    mm(pAb[0:64, hs], Kt_[:, h_, cs], Bh[:, h_, cs], True, True, [Kt_, Bh], [pAb])
                                tt_("dve", PT0[:, :], pAbT[0:64, 0:384], nmAT[:, :], ALU.mult, [pAbT, nmAT], [PT0])
                                tt_("dve", P0[:, :], pAb[0:64, 0:384], nmA[:, :], ALU.mult, [pAb, nmA], [P0])
                                tt_("pool", X0[:, :], PT0[:, :], MK["i6"][:, :], ALU.add, [PT0, MK["i6"]], [X0])
                                tt_("dve", AdT[:, :], pAdT[0:64, 0:384], mAT[:, :], ALU.mult, [pAdT, mAT], [AdT])
                                tt_("dve", BdT[:, :], pBdT[0:64, 0:384], mBT[:, :], ALU.mult, [pBdT, mBT], [BdT])
                                tt_("dve", nBbT[:, :], pBbT[0:64, 0:384], nmBT[:, :], ALU.mult, [pBbT, nmBT], [nBbT])
                                Pc, PTc, Pn, PTn, Xc, Xn = P0, PT0, P1, PT1, X0, X1
                                for lev in range(1, 6):
                                    pP, pPT, pX = nps(), nps(), nps()
                                    for h_ in range(6):
                                        hs = slice(h_ * 64, (h_ + 1) * 64)
                                        mm(pP[0:64, hs], hv(PTc, h_), hv(Pc, h_), True, True, [PTc, Pc], [pP])
                                    cp("act", Pn[:, :], pP[0:64, 0:384], [pP], [Pn])
                                    if lev < 5:
                                        for h_ in range(6):
                                            hs = slice(h_ * 64, (h_ + 1) * 64)
                                            mm(pPT[0:64, hs], hv(Pc, h_), hv(PTc, h_), True, True, [PTc, Pc], [pPT])
                                        cp("dve", PTn[:, :], pPT[0:64, 0:384], [pPT], [PTn])
                                    for h_ in range(6):
                                        hs = slice(h_ * 64, (h_ + 1) * 64)
                                        mm(pX[0:64, hs], hv(Pn, h_), hv(Xc, h_), True, True, [Pn, Xc], [pX])
                                    tt_("dve", Xn[:, :], pX[0:64, 0:384], Xc[:, :], ALU.add, [pX, Xc], [Xn])
                                    Pc, Pn = Pn, Pc
                                    PTc, PTn = PTn, PTc
                                    Xc, Xn = Xn, Xc
                                XT_ = Xc
                                Hf = H[:].rearrange("p h i -> p (h i)")
                                pW = nps()
                                for h_ in range(6):
                                    hs = slice(h_ * 64, (h_ + 1) * 64)
                                    mm(pW[0:64, hs], Kt_[:, h_, cs], H[:, h_, :], True, False, [Kt_, H], [pW])
                                    mm(pW[0:64, hs], hv(AdT, h_), hv(Vt, h_), False, True, [AdT, Vt], [pW])
                                cp("act", Wb[:, :], pW[0:64, 0:384], [pW], [Wb])
                                pZ = nps()
                                for h_ in range(6):
                                    hs = slice(h_ * 64, (h_ + 1) * 64)
                                    mm(pZ[0:64, hs], hv(XT_, h_), hv(Wb, h_), True, True, [XT_, Wb], [pZ])
                                cp("act", Zb[:, :], pZ[0:64, 0:384], [pZ], [Zb])
                                pY, pC, pH = nps(), nps(), nps()
                                for h_ in range(6):
                                    hs = slice(h_ * 64, (h_ + 1) * 64)
                                    mm(pY[0:64, hs], Rt_[:, h_, cs], H[:, h_, :], True, False, [Rt_, H], [pY])
                                    mm(pY[0:64, hs], hv(BdT, h_), hv(Vt, h_), False, False, [BdT, Vt], [pY])
                                    mm(pY[0:64, hs], hv(nBbT, h_), hv(Zb, h_), False, True, [nBbT, Zb], [pY])
                                    mm(pC[0:64, h_:h_ + 1], rkc[:, h_, cs], ones[0:64, 0:1], True, True, [rkc, ones], [pC])
                                    mm(pH[0:64, hs], hv(Dgt, h_), hv(Vt, h_), True, False, [Dgt, Vt], [pH])
                                    mm(pH[0:64, hs], hv(nBgt, h_), hv(Zb, h_), False, True, [nBgt, Zb], [pH])
                                cp("act", coef[:, :], pC[0:64, 0:6], [pC], [coef])
                                v3 = lambda t: t[:, :].rearrange("p (h i) -> p h i", i=64)
                                tt_("pool", v3(y2), v3(Vt), coef[:, :].unsqueeze(2).to_broadcast([64, 6, 64]), ALU.mult, [Vt, coef], [y2])
                                tt_("dve", yb[:, :], pY[0:64, 0:384], y2[:, :], ALU.add, [pY, y2], [yb])
                                tt_("pool", H[:], H[:], gC[:, :, c:c + 1].to_broadcast([64, 6, 64]), ALU.mult, [H, gC], [H])
                                tt_("dve", Hf, Hf, pH[0:64, 0:384], ALU.add, [H, pH], [H])
                                r0 = o + ts0 + c * 64
                                if d == 0:
                                    stq(S["YF"][r0:r0 + 64, :], yb[:, :], [yb])
                                else:
                                    ld(yf[:, :], S["YF"][r0:r0 + 64, :], [yf])
                                    tt_("pool", yb[:, :], yb[:, :], yf[:, :], ALU.add, [yb, yf], [yb])
                                    p.op("dve", lambda e: e.tensor_reduce(out=gst[:, :], in_=v3(yb), axis=AX.X, op=ALU.add), reads=[yb], writes=[gst])
                                    ts_("dve", gst[:, :], gst[:, :], -1.0 / 64, None, ALU.mult, ALU.bypass, [gst], [gst])
                                    tt_("dve", v3(yb), v3(yb), gst[:, :].unsqueeze(2).to_broadcast([64, 6, 64]), ALU.add, [yb, gst], [yb])
                                    tt_("pool", y2[:, :], yb[:, :], yb[:, :], ALU.mult, [yb], [y2])
                                    p.op("dve", lambda e: e.tensor_reduce(out=gs2[:, :], in_=v3(y2), axis=AX.X, op=ALU.add), reads=[y2], writes=[gs2])
                                    rsqrt_(gs2[:, :], gs2[:, :], 1.0 / 64, epsc[0:64, 1:2], [gs2, epsc], [gs2], None)
                                    tt_("dve", v3(yb), v3(yb), gs2[:, :].unsqueeze(2).to_broadcast([64, 6, 64]), ALU.mult, [yb, gs2], [yb])
                                    tt_("pool", yb[:, :], yb[:, :], lnxg[0:64, :], ALU.mult, [yb, lnxg], [yb])
                                    tt_("pool", yb[:, :], yb[:, :], lnxb[0:64, :], ALU.add, [yb, lnxb], [yb])
                                    pg = nps()
                                    mm(pg[0:64, 0:384], cgt[:, cs], cg2[:, :], True, True, [cgt, cg2], [pg])
                                    tt_("dve", y2[:, :], yb[:, :], pg[0:64, 0:384], ALU.mult, [yb, pg], [y2])
                                    stq(S["M"][r0:r0 + 64, 640:1024], y2[:, :], [y2])
                        if not latent:
                            pt = nps()
                            for h_ in range(6):
                                tr(pt[0:64, h_ * 64:(h_ + 1) * 64], H[:, h_, :], ident[0:64, 0:64], [H, ident], [pt])
                            cp("act", hst[:].rearrange("p h j -> p (h j)"), pt[0:64, 0:384], [pt], [hst])
                            stq(O["ncf" if d == 0 else "ncb"][bl, l].rearrange("h i j -> i h j"), hst[:], [hst])
                p.barrier()

            with ExitStack() as st:
                wout = sb(st, "wout", [128, 8, D])
                wv = I["w_out"][l].rearrange("(c p) f -> p c f", p=128)
                for k in range(8):
                    ld(wout[:, k, :], wv[:, k, :], [wout])
                xt = sb(st, "xt2", [128, 8, 512])
                mT = sb(st, "mT", [128, 8, 512])
                sq = sb(st, "sq2", [128, 8, 512])
                rstd = sb(st, "rstd2", [128, 512])
                mtok = [sb(st, "mtok%d" % i, [128, D]) for i in range(2)]
                actb = sb(st, "actb", [128, 22, 512])
                w13 = [sb(st, "w13_%d" % i, [128, 8, 2, 256]) for i in range(2)]
                w2b = [sb(st, "w2b_%d" % i, [128, 22, 128]) for i in range(2)]
                sgl = sb(st, "sgl", [128, 512])
                w1v = I["ffn_w1"][l].rearrange("(c p) f -> p c f", p=128)
                w3v = I["ffn_w3"][l].rearrange("(c p) f -> p c f", p=128)
                w2v = I["ffn_w2"][l].rearrange("(c p) f -> p c f", p=128)
                wi = [0]
                for S in streams:
                    latent = S is SS_
                    cv = 1 if latent else 0
                    T = S["T"]
                    XTv = S["XT"].rearrange("(c p) t -> p c t", p=128)
                    for t0 in range(0, T, 512):
                        ld(xt[:], XTv[:, :, t0:t0 + 512], [xt])
                        for i in range(4):
                            mk = mtok[i % 2]
                            ld(mk[:], S["M"][t0 + i * 128:t0 + (i + 1) * 128, :], [mk])
                            for half in range(2):
                                pt = nps()
                                for c in range(4):
                                    cc = half * 4 + c
                                    tr(pt[:, c * 128:(c + 1) * 128], mk[:, cc * 128:(cc + 1) * 128], ident[:, :], [mk, ident], [pt])
                                cp("act" if half == 0 else "dve", mT[:, half * 4:(half + 1) * 4, i * 128:(i + 1) * 128],
                                   pt[:, :].rearrange("p (c t) -> p c t", t=128), [pt], [mT])
                        for dc in range(8):
                            po = nps()
                            for k in range(8):
                                mm(po[:, :], wout[:, k, dc * 128:(dc + 1) * 128], mT[:, k, :], k == 0, k == 7, [wout, mT], [po])
                            stt(xt[:, dc, :], po[:, :], MOD(l, 2, dc, cv), xt[:, dc, :], ALU.mult, ALU.add, [po, mod, xt], [xt])
                        act(sq[:], xt[:], AF.Square, [xt], [sq])
                        pss = nps()
                        for c in range(8):
                            mm(pss[:, :], ones[:, 0:128], sq[:, c, :], c == 0, c == 7, [ones, sq], [pss])
                        rsqrt_(rstd[:], pss[:, :], 1.0 / D, epsc[:, 0:1], [pss, epsc], [rstd], None)
                        hh = mT
                        for c in range(8):
                            stt(hh[:, c, :], xt[:, c, :], modA2[:, l * 8 + c, cv:cv + 1], rstd[:], ALU.mult, ALU.mult, [xt, modA2, rstd], [hh])
                            act(hh[:, c, :], hh[:, c, :], AF.Identity, [hh, mod], [hh], bias=MOD(l, 3, c, cv), scale=1.0)
                        for fp in range(11):
                            wt = w13[wi[0] % 2]
                            wi[0] += 1
                            ld(wt[:, :, 0, :], w1v[:, :, fp * 256:(fp + 1) * 256], [wt])
                            ld(wt[:, :, 1, :], w3v[:, :, fp * 256:(fp + 1) * 256], [wt])
                            for f2 in range(2):
                                fc = fp * 2 + f2
                                p1, p3 = nps(), nps()
                                for k in range(8):
                                    mm(p1[:, :], wt[:, k, 0, f2 * 128:(f2 + 1) * 128], hh[:, k, :], k == 0, k == 7, [wt, hh], [p1])
                                for k in range(8):
                                    mm(p3[:, :], wt[:, k, 1, f2 * 128:(f2 + 1) * 128], hh[:, k, :], k == 0, k == 7, [wt, hh], [p3])
                                act(sgl[:], p1[:, :], AF.Silu, [p1], [sgl])
                                tt_("dve", actb[:, fc, :], sgl[:], p3[:, :], ALU.mult, [sgl, p3], [actb])
                        for dc in range(8):
                            w2t = w2b[dc % 2]
                            ld(w2t[:], w2v[:, :, dc * 128:(dc + 1) * 128], [w2t])
                            po = nps()
                            for fc in range(22):
                                mm(po[:, :], w2t[:, fc, :], actb[:, fc, :], fc == 0, fc == 21, [w2t, actb], [po])
                            stt(xt[:, dc, :], po[:, :], MOD(l, 5, dc, cv), xt[:, dc, :], ALU.mult, ALU.add, [po, mod, xt], [xt])
                        stq(XTv[:, :, t0:t0 + 512], xt[:], [xt])
                p.barrier()

        with ExitStack() as st:
            xt = sb(st, "xtf", [128, 8, 512])
            sq = sb(st, "sqf", [128, 8, 512])
            rstd = sb(st, "rstdf", [128, 512])
            yo = [sb(st, "yo%d" % i, [128, D]) for i in range(2)]
            for S, dst in ((SP_, O["yp"]), (SS_, O["ys"])):
                if S is SS_ and not do_sample:
                    continue
                T = S["T"]
                XTv = S["XT"].rearrange("(c p) t -> p c t", p=128)
                for t0 in range(0, T, 512):
                    ld(xt[:], XTv[:, :, t0:t0 + 512], [xt])
                    act(sq[:], xt[:], AF.Square, [xt], [sq])
                    pss = nps()
                    for c in range(8):
                        mm(pss[:, :], ones[:, 0:128], sq[:, c, :], c == 0, c == 7, [ones, sq], [pss])
                    rsqrt_(rstd[:], pss[:, :], 1.0 / D, epsc[:, 0:1], [pss, epsc], [rstd], None)
                    for c in range(8):
                        stt(sq[:, c, :], xt[:, c, :], gfc[:, c:c + 1], rstd[:], ALU.mult, ALU.mult, [xt, gfc, rstd], [sq])
                    for i in range(4):
                        y_ = yo[i % 2]
                        for half in range(2):
                            pt = nps()
                            for c in range(4):
                                cc = half * 4 + c
                                tr(pt[:, c * 128:(c + 1) * 128], sq[:, cc, i * 128:(i + 1) * 128], ident[:, :], [sq, ident], [pt])
                            cp("act" if half == 0 else "dve", y_[:, half * 512:(half + 1) * 512], pt[:, :], [pt], [y_])
                        stq(dst[t0 + i * 128:t0 + (i + 1) * 128, :], y_[:], [y_])
            p.barrier()
    return nc


def make_in_maps(inputs):
    f = lambda a: np.ascontiguousarray(np.asarray(a, dtype=np.float32))
    consts = _consts()
    shared = {k: f(inputs[k]) for k in WEIGHT_SHAPES}
    shared.update(consts)
    maps = []
    for c in range(NCORES):
        b = c % 2
        m = dict(shared)
        m["xp"] = f(inputs["x_prompt"][NPL * c:NPL * (c + 1)]).reshape(NPL * TP, D)
        m["xs"] = f(inputs["x_sample"][b])
        m["cak"] = f(inputs["cache_a_k"][b]).reshape(L, PAST, 128)
        m["cav"] = f(inputs["cache_a_v"][b]).reshape(L, PAST, 128)
        m["cbk"] = f(inputs["cache_b_k"][b]).reshape(L, PAST, 256)
        m["cbv"] = f(inputs["cache_b_v"][b]).reshape(L, PAST, 256)
        m["scf"] = f(inputs["state_c_fwd"][b])
        m["scb"] = f(inputs["state_c_bwd"][b])
        m["cs"] = f(inputs["c"][b])
        maps.append(m)
    return maps


def kernel(**inputs):
    nc = build()
    maps = make_in_maps(inputs)
    res = run_bass_kernel_spmd(nc, maps, core_ids=list(range(NCORES))).results
    yp = np.concatenate([r["yp"].reshape(NPL, TP, D) for r in res], axis=0)
    ys = np.stack([res[0]["ys"], res[1]["ys"]], axis=0)
    nak = np.concatenate([r["nak"].reshape(NPL, L, TP, 2, 64) for r in res], axis=0)
    nav = np.concatenate([r["nav"].reshape(NPL, L, TP, 2, 64) for r in res], axis=0)
    nbk = np.concatenate([r["nbk"].reshape(NPL, L, TP, 4, 2, 32) for r in res], axis=0)
    nbv = np.concatenate([r["nbv"].reshape(NPL, L, TP, 4, 64) for r in res], axis=0)
    ncf = np.concatenate([r["ncf"] for r in res], axis=0)
    ncb = np.concatenate([r["ncb"] for r in res], axis=0)
    return tuple(np.ascontiguousarray(a.astype(np.float32)) for a in (yp, ys, nak, nav, nbk, nbv, ncf, ncb))
```

```python
import math
from contextlib import ExitStack
import numpy as np
import concourse.bass as bass
import concourse.mybir as mybir
from concourse.bass_utils import run_bass_kernel_spmd

F32 = mybir.dt.float32
BF16 = mybir.dt.bfloat16
AF = mybir.ActivationFunctionType
ALU = mybir.AluOpType
AX = mybir.AxisListType

D = 1024
L = 4
TP = 256
TS = 4096
PAST = 512
NPL = 2
DFF = 2816
INC = 2944
DEC = 0.606531
EPS = 1e-6
GN_EPS = 64e-5
NCORES = 8


class Buf:
    __slots__ = ("w", "r")

    def __init__(self):
        self.w = []
        self.r = []


class TT:
    def __init__(self, h):
        self.h = h
        self.b = Buf()

    def __getitem__(self, k):
        return self.h[k]


class Prog:
    RING = 12

    def __init__(self, nc, stack):
        self.nc = nc
        self.eng = {"pe": nc.tensor, "act": nc.scalar, "dve": nc.vector, "pool": nc.gpsimd, "sp": nc.sync}
        self.semh = {}
        self.cnt = {}
        self.seen = {e: {} for e in self.eng}
        for e in ("pe", "act", "dve", "pool"):
            self.semh["S_" + e] = stack.enter_context(nc.semaphore("S_" + e))
            self.cnt[e] = 0
        self.dq = {}
        for q in ("sp", "pool"):
            ring = []
            for k in range(self.RING):
                key = "D_%s_%d" % (q, k)
                self.semh[key] = stack.enter_context(nc.semaphore(key))
                ring.append([key, 0])
            self.dq[q] = [ring, 0]
        self.n = 0

    def _waits(self, e, reads, writes):
        need = {}
        for b in reads:
            for (key, val, te) in b.w:
                if need.get(key, 0) < val:
                    need[key] = val
        for b in writes:
            for (key, val, te) in b.w:
                if te != e and need.get(key, 0) < val:
                    need[key] = val
            for (key, val, te) in b.r:
                if te != e and need.get(key, 0) < val:
                    need[key] = val
        seen = self.seen[e]
        for key, val in need.items():
            if seen.get(key, 0) < val:
                self.eng[e].wait_ge(self.semh[key], val)
                seen[key] = val
                self.n += 1

    def _record(self, tok, reads, writes):
        for b in writes:
            b.w = [tok]
            b.r = []
        for b in reads:
            b.r = [t for t in b.r if t[0] != tok[0]]
            b.r.append(tok)

    def op(self, e, fn, reads=(), writes=()):
        reads = [t.b for t in reads]
        writes = [t.b for t in writes]
        self._waits(e, reads, writes)
        ins = fn(self.eng[e])
        self.cnt[e] += 1
        ins.then_inc(self.semh["S_" + e], 1)
        self._record(("S_" + e, self.cnt[e], e), reads, writes)
        self.n += 1

    def dma(self, q, out_ap, in_ap, reads=(), writes=()):
        reads = [t.b for t in reads]
        writes = [t.b for t in writes]
        self._waits(q, reads, writes)
        ring, rr = self.dq[q]
        slot = ring[rr % self.RING]
        self.dq[q][1] = rr + 1
        key, prev = slot
        if prev > 0 and self.seen[q].get(key, 0) < prev:
            self.eng[q].wait_ge(self.semh[key], prev)
            self.seen[q][key] = prev
        ins = self.eng[q].dma_start(out=out_ap, in_=in_ap)
        ins.then_inc(self.semh[key], 16)
        slot[1] = prev + 16
        self._record((key, prev + 16, None), reads, writes)
        self.n += 2

    def barrier(self):
        targets = {}
        for e in ("pe", "act", "dve", "pool"):
            if self.cnt[e] > 0:
                targets["S_" + e] = self.cnt[e]
        for q in self.dq:
            for key, val in self.dq[q][0]:
                if val > 0:
                    targets[key] = val
        for e in self.eng:
            seen = self.seen[e]
            for key, val in targets.items():
                if key == "S_" + e:
                    continue
                if seen.get(key, 0) < val:
                    self.eng[e].wait_ge(self.semh[key], val)
                    seen[key] = val
                    self.n += 1


def _rope_tables(dh, T, grid_w=64):
    half = dh // 2
    t = np.arange(T)
    rows = t // grid_w
    cols = t % grid_w
    inv = 10000.0 ** (-np.arange(0, half, 2, dtype=np.float32) / half)
    cos = np.zeros((dh, T), np.float32)
    sin = np.zeros((dh, T), np.float32)
    perm = np.zeros((dh, dh), np.float32)
    q = half // 2
    for ax, pos in enumerate((rows, cols)):
        ang = pos[None, :].astype(np.float32) * inv[:, None]
        base = ax * half
        for i in range(q):
            cos[base + i] = np.cos(ang[i])
            cos[base + q + i] = np.cos(ang[i])
            sin[base + i] = -np.sin(ang[i])
            sin[base + q + i] = np.sin(ang[i])
            perm[base + q + i, base + i] = 1.0
            perm[base + i, base + q + i] = 1.0
    return cos, sin, perm


def _consts():
    c = {}
    c["ident"] = np.eye(128, dtype=np.float32)
    c["ones"] = np.ones((128, 512), np.float32)
    cosA, sinA, pA = _rope_tables(64, TS)
    cosB, sinB, pB = _rope_tables(32, TS)
    c["cosA"] = np.tile(cosA, (2, 1))
    c["sinA"] = np.tile(sinA, (2, 1))
    c["cosB"] = np.tile(cosB, (4, 1))
    c["sinB"] = np.tile(sinB, (4, 1))
    PA = np.zeros((128, 128), np.float32)
    PB = np.zeros((128, 128), np.float32)
    for i in range(2):
        PA[i * 64:(i + 1) * 64, i * 64:(i + 1) * 64] = pA
    for i in range(4):
        PB[i * 32:(i + 1) * 32, i * 32:(i + 1) * 32] = pB
    c["permA"] = PA
    c["permB"] = PB
    p = np.arange(128)[:, None]
    f = np.arange(128)[None, :]
    c["mprev"] = (p >= f).astype(np.float32)
    c["mnext"] = (p <= f).astype(np.float32)
    p = np.arange(64)[:, None]
    f = np.arange(64)[None, :]
    rep = lambda m: np.tile(m.astype(np.float32), (1, 6))
    c["m_lt"] = rep(p < f)
    c["m_le"] = rep(p <= f)
    c["m_gt"] = rep(p > f)
    c["m_ge"] = rep(p >= f)
    c["n_lt"] = -rep(p < f)
    c["n_le"] = -rep(p <= f)
    c["n_gt"] = -rep(p > f)
    c["n_ge"] = -rep(p >= f)
    c["i6"] = rep(p == f)
    rs = np.ones((64, 6 * 512), np.float32)
    rs[:, ::64] = 0.0
    c["rstart"] = rs
    return c


CONST_SHAPES = {k: v.shape for k, v in _consts().items()}

WEIGHT_SHAPES = dict(
    ada_w=(L, D, 6 * D), ada_b=(L, 6 * D), norm1_g=(L, D), norm2_g=(L, D), w_in=(L, D, INC), a_sink=(L, 6),
    b_lambda=(L, 4, 32), b_subln_g=(L, 64), c_conv=(L, 3, 1152), c_w0=(L, 2, 384), c_w2=(L, 2, 64, 384),
    c_a0=(L, 2, 384), c_a2=(L, 2, 64, 384), c_g2=(L, 128, 384), c_kk=(L, 384), c_ka=(L, 384), c_rk=(L, 6, 64),
    c_lnx_g=(L, 384), c_lnx_b=(L, 384), w_out=(L, D, D), ffn_w1=(L, D, DFF), ffn_w3=(L, D, DFF),
    ffn_w2=(L, DFF, D), final_g=(D,), c_ctx=(D,))

CORE_IN_SHAPES = dict(
    xp=(NPL * TP, D), xs=(TS, D), cak=(L, PAST, 128), cav=(L, PAST, 128), cbk=(L, PAST, 256), cbv=(L, PAST, 256),
    scf=(L, 6, 64, 64), scb=(L, 6, 64, 64), cs=(D,))

OUT_SHAPES = dict(
    yp=(NPL * TP, D), ys=(TS, D), nak=(NPL, L, TP, 128), nav=(NPL, L, TP, 128), nbk=(NPL, L, TP, 256),
    nbv=(NPL, L, TP, 256), ncf=(NPL, L, 6, 64, 64), ncb=(NPL, L, 6, 64, 64))


def build(nlayers=L, debug=False, do_sample=True):
    nc = bass.Bass("TRN2", target_bir_lowering=False)
    I = {}
    for k, s in list(WEIGHT_SHAPES.items()) + list(CORE_IN_SHAPES.items()) + list(CONST_SHAPES.items()):
        I[k] = nc.dram_tensor(k, list(s), F32, kind="ExternalInput").ap()
    O = {k: nc.dram_tensor(k, list(s), F32, kind="ExternalOutput").ap() for k, s in OUT_SHAPES.items()}
    skind = "ExternalOutput" if debug else "Internal"
    streams = []
    for sname, tt in (("p", NPL * TP), ("s", TS)):
        S = {"T": tt, "name": sname}
        for nm, shp in (("XT", [D, tt]), ("QA", [6, 64, tt]), ("KA", [2, 64, tt]), ("QB", [8, 32, tt]),
                        ("KB", [8, 32, tt]), ("VA", [tt, 128]), ("VB", [tt, 256]), ("RKV", [18, 64, tt]),
                        ("CW", [2, 64, tt]), ("CA", [2, 64, tt]), ("CG", [128, tt]), ("M", [tt, D]),
                        ("YF", [tt, 384]), ("YB", [tt, 384])):
            S[nm] = nc.dram_tensor("%s_%s" % (nm, sname), shp, F32, kind=skind).ap()
        streams.append(S)
    SP_, SS_ = streams
    if not do_sample:
        streams = [SP_]
    seqs = [(SP_, 0, TP, False, 0), (SP_, TP, TP, False, 1)]
    if do_sample:
        seqs.append((SS_, 0, TS, True, -1))

    with ExitStack() as glob:
        p = Prog(nc, glob)

        uid = [0]

        def sb(st, name, shape, dt=F32):
            uid[0] += 1
            return TT(st.enter_context(nc.sbuf_tensor("s%d_%s" % (uid[0], name), list(shape), dt)))

        ps = [TT(glob.enter_context(nc.psum_tensor("ps%d" % i, [128, 512], F32))) for i in range(8)]
        psi = [0]

        def nps():
            t = ps[2 + psi[0] % 6]
            psi[0] += 1
            return t

        def mm(out, lhsT, rhs, start, stop, r, w):
            p.op("pe", lambda e: e.matmul(out, lhsT, rhs, start=start, stop=stop), reads=r, writes=w)

        def tr(out, in_, idn, r, w):
            p.op("pe", lambda e: e.transpose(out, in_, idn), reads=r, writes=w)

        def ld(out, in_, w, r=()):
            p.dma("sp", out, in_, reads=r, writes=w)

        def stq(out, in_, r):
            p.dma("pool", out, in_, reads=r)

        def act(out, in_, func, r, w, **kw):
            p.op("act", lambda e: e.activation(out=out, in_=in_, func=func, **kw), reads=r, writes=w)

        def tt_(eng, out, in0, in1, op, r, w):
            p.op(eng, lambda e: e.tensor_tensor(out=out, in0=in0, in1=in1, op=op), reads=r, writes=w)

        def ts_(eng, out, in0, s1, s2, op0, op1, r, w):
            if s2 is None:
                p.op(eng, lambda e: e.tensor_single_scalar(out=out, in_=in0, scalar=s1, op=op0), reads=r, writes=w)
            else:
                p.op(eng, lambda e: e.tensor_scalar(out=out, in0=in0, scalar1=s1, scalar2=s2, op0=op0, op1=op1), reads=r, writes=w)

        def stt(out, in0, scalar, in1, op0, op1, r, w):
            p.op("dve", lambda e: e.scalar_tensor_tensor(out=out, in0=in0, scalar=scalar, in1=in1, op0=op0, op1=op1), reads=r, writes=w)

        def cp(eng, out, in_, r, w):
            if eng == "act":
                p.op("act", lambda e: e.copy(out=out, in_=in_), reads=r, writes=w)
            else:
                p.op(eng, lambda e: e.tensor_copy(out=out, in_=in_), reads=r, writes=w)

        def rsqrt_(out, in_, scale, bias, r, w, tmpw):
            act(out, in_, AF.Sqrt, r, w, scale=scale, bias=bias)
            p.op("dve", lambda e: e.reciprocal(out=out, in_=out), reads=w, writes=w)

        ident = sb(glob, "ident", [128, 128])
        ones = sb(glob, "ones", [128, 512])
        epsc = sb(glob, "epsc", [128, 2])
        ld(ident[:], I["ident"][:, :], [ident])
        ld(ones[:], I["ones"][:, :], [ones])
        p.op("pool", lambda e: e.memset(epsc[:, 0:1], EPS), writes=[epsc])
        p.op("pool", lambda e: e.memset(epsc[:, 1:2], GN_EPS), writes=[epsc])

        stgc = sb(glob, "stgc", [128, 512])

        def load_cols(st, src2d, rows, w, name):
            dst = sb(st, name, [w, rows])
            for r0 in range(0, rows, 128):
                rr = min(128, rows - r0)
                ld(stgc[0:rr, 0:w], src2d[r0:r0 + rr, :], [stgc])
                pt = nps()
                tr(pt[0:w, 0:rr], stgc[0:rr, 0:w], ident[0:rr, 0:rr], [stgc, ident], [pt])
                cp("act", dst[:, r0:r0 + rr], pt[0:w, 0:rr], [pt], [dst])
            return dst

        def bcast_row(st, src_row, n, name, dst=None):
            if dst is None:
                dst = sb(st, name, [128, n])
            ld(stgc[0:1, 0:n], src_row, [stgc])
            pt = nps()
            mm(pt[:, 0:n], ones[0:1, 0:128], stgc[0:1, 0:n], True, True, [ones, stgc], [pt])
            cp("act", dst[:, 0:n], pt[:, 0:n], [pt], [dst])
            return dst

        adab = load_cols(glob, I["ada_b"].rearrange("l (j p) -> (l j) p", p=128), L * 48, 128, "adab")
        g1c = load_cols(glob, I["norm1_g"].rearrange("l (j p) -> (l j) p", p=128), L * 8, 128, "g1c")
        g2c = load_cols(glob, I["norm2_g"].rearrange("l (j p) -> (l j) p", p=128), L * 8, 128, "g2c")
        gfc = load_cols(glob, I["final_g"].rearrange("(j p) -> j p", p=128), 8, 128, "gfc")
        cct = load_cols(glob, I["c_ctx"].rearrange("(j p) -> j p", p=128), 8, 128, "cct")
        cst = load_cols(glob, I["cs"].rearrange("(j p) -> j p", p=128), 8, 128, "cst")
        convc = load_cols(glob, I["c_conv"].rearrange("l k (t c) -> (l k t) c", c=64), L * 3 * 18, 64, "convc")
        w0c = load_cols(glob, I["c_w0"].rearrange("l d (h c) -> (l d h) c", c=64), L * 12, 64, "w0c")
        a0c = load_cols(glob, I["c_a0"].rearrange("l d (h c) -> (l d h) c", c=64), L * 12, 64, "a0c")
        kkc = load_cols(glob, I["c_kk"].rearrange("l (h c) -> (l h) c", c=64), L * 6, 64, "kkc")
        kac = load_cols(glob, I["c_ka"].rearrange("l (h c) -> (l h) c", c=64), L * 6, 64, "kac")
        rkc_ = load_cols(glob, I["c_rk"].rearrange("l h c -> (l h) c"), L * 6, 64, "rkc")

        mod = sb(glob, "mod", [128, L * 48, 2])
        modA1 = sb(glob, "modA1", [128, L * 8, 2])
        modA2 = sb(glob, "modA2", [128, L * 8, 2])
        with ExitStack() as st:
            silc = sb(st, "silc", [128, 8, 2])
            act(silc[:, :, 0], cct[:, :], AF.Silu, [cct], [silc])
            act(silc[:, :, 1], cst[:, :], AF.Silu, [cst], [silc])
            pcs = [sb(st, "adapc%d" % i, [128, 8, 512]) for i in range(2)]
            k_ = 0
            for l in range(nlayers):
                wv = I["ada_w"][l].rearrange("(c p) f -> p c f", p=128)
                for pc in range(12):
                    pt_ = pcs[k_ % 2]
                    k_ += 1
                    ld(pt_[:], wv[:, :, pc * 512:(pc + 1) * 512], [pt_])
                    for jb in range(4):
                        pq = nps()
                        for k in range(8):
                            mm(pq[:, 0:2], pt_[:, k, jb * 128:(jb + 1) * 128], silc[:, k, :], k == 0, k == 7, [pt_, silc], [pq])
                        j = l * 48 + pc * 4 + jb
                        ts_("dve", mod[:, j, :], pq[:, 0:2], adab[:, j:j + 1], None, ALU.add, ALU.bypass, [pq, adab], [mod])
                stt(modA1[:, l * 8:(l + 1) * 8, :], mod[:, l * 48 + 8:l * 48 + 16, :], 1.0,
                    g1c[:, l * 8:(l + 1) * 8].unsqueeze(2).to_broadcast([128, 8, 2]), ALU.add, ALU.mult, [mod, g1c], [modA1])
                stt(modA2[:, l * 8:(l + 1) * 8, :], mod[:, l * 48 + 32:l * 48 + 40, :], 1.0,
                    g2c[:, l * 8:(l + 1) * 8].unsqueeze(2).to_broadcast([128, 8, 2]), ALU.add, ALU.mult, [mod, g2c], [modA2])
            p.barrier()

        def MOD(l, which, c, cv):
            j = l * 48 + which * 8 + c
            return mod[:, j, cv:cv + 1]

        with ExitStack() as st:
            xin = [sb(st, "xin%d" % i, [128, D]) for i in range(2)]
            xo = [sb(st, "xo%d" % i, [128, 8, 128]) for i in range(2)]
            k_ = 0
            for S, src in ((SP_, I["xp"]), (SS_, I["xs"])):
                if S is SS_ and not do_sample:
                    continue
                for b in range(S["T"] // 128):
                    a_ = xin[k_ % 2]
                    o_ = xo[k_ % 2]
                    k_ += 1
                    ld(a_[:], src[b * 128:(b + 1) * 128, :], [a_])
                    for half in range(2):
                        pt = nps()
                        for c in range(4):
                            cc = half * 4 + c
                            tr(pt[:, c * 128:(c + 1) * 128], a_[:, cc * 128:(cc + 1) * 128], ident[:, :], [a_, ident], [pt])
                        cp("act" if half == 0 else "dve", o_[:, half * 4:(half + 1) * 4, :],
                           pt[:, :].rearrange("p (c t) -> p c t", t=128), [pt], [o_])
                    stq(S["XT"].rearrange("(c p) t -> p c t", p=128)[:, :, b * 128:(b + 1) * 128], o_[:], [o_])
            p.barrier()

        for l in range(nlayers):
            lam_init = 0.8 - 0.6 * math.exp(-0.3 * l)
            with ExitStack() as st:
                win = sb(st, "win", [128, 8, INC])
                wv = I["w_in"][l].rearrange("(c p) f -> p c f", p=128)
                for k in range(8):
                    ld(win[:, k, :], wv[:, k, :], [win])
                xt = sb(st, "xt", [128, 8, 512])
                sq = sb(st, "sq", [128, 8, 512])
                h = sb(st, "h", [128, 8, 512])
                rstd = sb(st, "rstd", [128, 512])
                stg = [sb(st, "stg%d" % i, [128, 512]) for i in range(3)]
                zs = sb(st, "zs", [128, 512])
                t2 = sb(st, "t2", [128, 512])
                cosA = sb(st, "cosA", [128, 512]); sinA = sb(st, "sinA", [128, 512])
                cosB = sb(st, "cosB", [128, 512]); sinB = sb(st, "sinB", [128, 512])
                permA = sb(st, "permA", [128, 128]); permB = sb(st, "permB", [128, 128])
                ld(permA[:], I["permA"][:, :], [permA]); ld(permB[:], I["permB"][:, :], [permB])
                vst = [sb(st, "vst%d" % i, [128, 768]) for i in range(2)]
                sgi = [0]
                for S in streams:
                    latent = S is SS_
                    cv = 1 if latent else 0
                    T = S["T"]
                    XTv = S["XT"].rearrange("(c p) t -> p c t", p=128)
                    for t0 in range(0, T, 512):
                        ld(xt[:], XTv[:, :, t0:t0 + 512], [xt])
                        if latent:
                            ld(cosA[:], I["cosA"][:, t0:t0 + 512], [cosA]); ld(sinA[:], I["sinA"][:, t0:t0 + 512], [sinA])
                            ld(cosB[:], I["cosB"][:, t0:t0 + 512], [cosB]); ld(sinB[:], I["sinB"][:, t0:t0 + 512], [sinB])
                        act(sq[:], xt[:], AF.Square, [xt], [sq])
                        pss = nps()
                        for c in range(8):
                            mm(pss[:, :], ones[:, 0:128], sq[:, c, :], c == 0, c == 7, [ones, sq], [pss])
                        rsqrt_(rstd[:], pss[:, :], 1.0 / D, epsc[:, 0:1], [pss, epsc], [rstd], None)
                        for c in range(8):
                            stt(h[:, c, :], xt[:, c, :], modA1[:, l * 8 + c, cv:cv + 1], rstd[:], ALU.mult, ALU.mult, [xt, modA1, rstd], [h])
                            act(h[:, c, :], h[:, c, :], AF.Identity, [h, mod], [h], bias=MOD(l, 0, c, cv), scale=1.0)
                        for j in range(23):
                            if j in (4, 9, 10):
                                continue
                            pz = nps()
                            for k in range(8):
                                mm(pz[:, :], win[:, k, j * 128:(j + 1) * 128], h[:, k, :], k == 0, k == 7, [win, h], [pz])
                            sg = stg[sgi[0] % 3]
                            sgi[0] += 1
                            rope = latent and j in (0, 1, 2, 3, 5, 6, 7, 8)
                            if rope:
                                isA = j <= 3
                                cs_, sn_, pm_ = (cosA, sinA, permA) if isA else (cosB, sinB, permB)
                                cp("act", zs[:], pz[:, :], [pz], [zs])
                                pr = nps()
                                mm(pr[:, :], pm_[:, :], zs[:], True, True, [pm_, zs], [pr])
                                tt_("dve", t2[:], pr[:, :], sn_[:], ALU.mult, [pr, sn_], [t2])
                                tt_("pool", sg[:], zs[:], cs_[:], ALU.mult, [zs, cs_], [sg])
                                tt_("pool", sg[:], sg[:], t2[:], ALU.add, [sg, t2], [sg])
                            elif j == 20:
                                act(sg[:], pz[:, :], AF.Tanh, [pz], [sg])
                            elif j == 22:
                                act(sg[:], pz[:, :], AF.Sigmoid, [pz], [sg])
                            else:
                                cp("act" if j % 2 == 0 else "dve", sg[:], pz[:, :], [pz], [sg])
                            sl = slice(t0, t0 + 512)
                            if j <= 2:
                                for hh in range(2):
                                    stq(S["QA"][2 * j + hh, :, sl], sg[hh * 64:(hh + 1) * 64, :], [sg])
                            elif j == 3:
                                for hh in range(2):
                                    stq(S["KA"][hh, :, sl], sg[hh * 64:(hh + 1) * 64, :], [sg])
                            elif j in (5, 6):
                                for q4 in range(4):
                                    stq(S["QB"][(j - 5) * 4 + q4, :, sl], sg[q4 * 32:(q4 + 1) * 32, :], [sg])
                            elif j in (7, 8):
                                for q4 in range(4):
                                    stq(S["KB"][(j - 7) * 4 + q4, :, sl], sg[q4 * 32:(q4 + 1) * 32, :], [sg])
                            elif 11 <= j <= 19:
                                for hh in range(2):
                                    stq(S["RKV"][(j - 11) * 2 + hh, :, sl], sg[hh * 64:(hh + 1) * 64, :], [sg])
                            elif j == 20:
                                for hh in range(2):
                                    stq(S["CW"][hh, :, sl], sg[hh * 64:(hh + 1) * 64, :], [sg])
                            elif j == 21:
                                for hh in range(2):
                                    stq(S["CA"][hh, :, sl], sg[hh * 64:(hh + 1) * 64, :], [sg])
                            else:
                                stq(S["CG"][:, sl], sg[:, :], [sg])
                        for i in range(4):
                            vs_ = vst[i % 2]
                            groups = [(512, 128, 0), (1152, 256, 128)]
                            if not latent:
                                groups += [(384, 128, 384), (896, 256, 512)]
                            pv = nps()
                            pk = nps()
                            for (c0, wd, o0) in groups:
                                pp, oo = (pv, o0) if o0 < 384 else (pk, o0 - 384)
                                for k in range(8):
                                    mm(pp[:, oo:oo + wd], h[:, k, i * 128:(i + 1) * 128], win[:, k, c0:c0 + wd], k == 0, k == 7, [h, win], [pp])
                            cp("act", vs_[:, 0:384], pv[:, 0:384], [pv], [vs_])
                            r0 = t0 + i * 128
                            stq(S["VA"][r0:r0 + 128, :], vs_[:, 0:128], [vs_])
                            stq(S["VB"][r0:r0 + 128, :], vs_[:, 128:384], [vs_])
                            if not latent:
                                cp("dve", vs_[:, 384:768], pk[:, 0:384], [pk], [vs_])
                                bl = r0 // TP
                                rr = r0 % TP
                                stq(O["nav"][bl, l, rr:rr + 128, :], vs_[:, 0:128], [vs_])
                                stq(O["nbv"][bl, l, rr:rr + 128, :], vs_[:, 128:384], [vs_])
                                stq(O["nak"][bl, l, rr:rr + 128, :], vs_[:, 384:512], [vs_])
                                stq(O["nbk"][bl, l, rr:rr + 128, :], vs_[:, 512:768], [vs_])
                p.barrier()

            with ExitStack() as st:
                sinkb = bcast_row(st, I["a_sink"][l:l + 1, :], 6, "sinkb")
                act(sinkb[:, 0:6], sinkb[:, 0:6], AF.Exp, [sinkb], [sinkb])
                mprev = sb(st, "mprev", [128, 128]); mnext = sb(st, "mnext", [128, 128])
                ld(mprev[:], I["mprev"][:, :], [mprev]); ld(mnext[:], I["mnext"][:, :], [mnext])
                kT = sb(st, "kT", [64, TS + PAST])
                qT = sb(st, "qT", [64, 3, TS])
                vt = sb(st, "vt", [128, (TS + PAST) // 128, 65])
                p.op("pool", lambda e: e.memset(vt[:, :, 64:65], 1.0), writes=[vt])
                cstg = sb(st, "cstg", [128, 4, 64])
                pT = [sb(st, "pT%d" % i, [128, 384]) for i in range(3)]
                ao = [sb(st, "ao%d" % i, [128, 3, 64]) for i in range(2)]
                den = sb(st, "den", [128, 3])
                pti = [0]
                for (S, o, T, latent, bl) in seqs:
                    nb = T // 128
                    for g in range(2):
                        ld(kT[:, 0:T], S["KA"][g, :, o:o + T], [kT])
                        ld(qT[:, :, 0:T], S["QA"][3 * g:3 * g + 3, :, o:o + T].rearrange("m d t -> d m t"), [qT])
                        ld(vt[:, 0:nb, 0:64], S["VA"][o:o + T, g * 64:(g + 1) * 64].rearrange("(n p) d -> p n d", p=128), [vt])
                        nctx = 0
                        if latent:
                            nctx = PAST // 128
                            ld(cstg[:], I["cak"][l, :, g * 64:(g + 1) * 64].rearrange("(n p) d -> p n d", p=128), [cstg])
                            pt = nps()
                            for n_ in range(4):
                                tr(pt[0:64, n_ * 128:(n_ + 1) * 128], cstg[:, n_, :], ident[:, :], [cstg, ident], [pt])
                            cp("act", kT[:, T:T + PAST], pt[0:64, :], [pt], [kT])
                            ld(vt[:, nb:nb + 4, 0:64], I["cav"][l, :, g * 64:(g + 1) * 64].rearrange("(n p) d -> p n d", p=128), [vt])
                        for b in range(nb):
                            if latent:
                                tiles = [(kb, (mprev if kb == b - 1 else (mnext if kb == b + 1 else None)))
                                         for kb in (b - 1, b, b + 1) if 0 <= kb < nb]
                                tiles += [(nb + c_, None) for c_ in range(nctx)]
                            else:
                                tiles = [(kb, None) for kb in range(nb)]
                            po = ps[b % 2]
                            for ti, (kb, msk) in enumerate(tiles):
                                pss = nps()
                                mm(pss[:, 0:384], kT[:, kb * 128:(kb + 1) * 128], qT[:, :, b * 128:(b + 1) * 128], True, True, [kT, qT], [pss])
                                pt_ = pT[pti[0] % 3]
                                pti[0] += 1
                                act(pt_[:], pss[:, 0:384], AF.Exp, [pss], [pt_], scale=0.125)
                                if msk is not None:
                                    tt_("pool", pt_[:, :].rearrange("p (m q) -> p m q", q=128), pt_[:, :].rearrange("p (m q) -> p m q", q=128),
                                        msk[:, :].unsqueeze(1).to_broadcast([128, 3, 128]), ALU.mult, [pt_, msk], [pt_])
                                for m in range(3):
                                    mm(po[:, m * 65:(m + 1) * 65], pt_[:, m * 128:(m + 1) * 128], vt[:, kb, :], ti == 0 and m == 0, ti == len(tiles) - 1 and m == 2, [pt_, vt], [po])
                            pov = po[:, 0:195].rearrange("p (m d) -> p m d", d=65)
                            tt_("dve", den[:, :].unsqueeze(2), pov[:, :, 64:65], sinkb[:, 3 * g:3 * g + 3].unsqueeze(2), ALU.add, [po, sinkb], [den])
                            p.op("dve", lambda e: e.reciprocal(out=den[:, :], in_=den[:, :]), reads=[den], writes=[den])
                            a_ = ao[b % 2]
                            tt_("dve", a_[:], pov[:, :, 0:64], den[:, :].unsqueeze(2).to_broadcast([128, 3, 64]), ALU.mult, [po, den], [a_])
                            r0 = o + b * 128
                            stq(S["M"][r0:r0 + 128, g * 192:(g + 1) * 192], a_[:].rearrange("p m d -> p (m d)"), [a_])
                p.barrier()

            with ExitStack() as st:
                lamr = bcast_row(st, I["b_lambda"][l:l + 1].rearrange("o a d -> o (a d)"), 128, "lamr")
                lam2 = sb(st, "lam2", [128, 2, 32])
                lv = lamr[:, :].rearrange("p (a b d) -> p a b d", a=2, b=2)
                tt_("dve", lam2[:], lv[:, :, 0, :], lv[:, :, 1, :], ALU.mult, [lamr], [lam2])
                lam1 = sb(st, "lam1", [128, 2])
                p.op("dve", lambda e: e.tensor_reduce(out=lam1[:, :], in_=lam2[:], axis=AX.X, op=ALU.add), reads=[lam2], writes=[lam1])
                act(lam1[:, :], lam1[:, :], AF.Exp, [lam1], [lam1])
                nlam = sb(st, "nlam", [128, 1])
                tt_("dve", nlam[:, :], lam1[:, 1:2], lam1[:, 0:1], ALU.subtract, [lam1], [nlam])
                ts_("dve", nlam[:, :], nlam[:, :], -lam_init, None, ALU.add, ALU.bypass, [nlam], [nlam])
                gsub = bcast_row(st, I["b_subln_g"][l:l + 1, :], 64, "gsub")
                ts_("dve", gsub[:, :], gsub[:, :], 1.0 - lam_init, None, ALU.mult, ALU.bypass, [gsub], [gsub])
                kT = sb(st, "kTb", [32, 2, TS + PAST], BF16)
                qT = sb(st, "qTb", [32, 2, TS], BF16)
                vt = sb(st, "vtb", [128, (TS + PAST) // 128, 65], BF16)
                p.op("pool", lambda e: e.memset(vt[:, :, 64:65], 1.0), writes=[vt])
                stf = sb(st, "stfb", [32, 2, TS])
                vtf = sb(st, "vtfb", [128, (TS + PAST) // 128, 64])
                cstg = sb(st, "cstgb", [128, 4, 32])
                pT = [sb(st, "pTb%d" % i, [128, 512], BF16) for i in range(6)]
                o1 = sb(st, "o1", [128, 4, 64]); o2 = sb(st, "o2", [128, 4, 64]); o3 = sb(st, "o3", [128, 4, 64])
                rd = sb(st, "rd", [128, 2, 4]); ssq = sb(st, "ssq", [128, 4])
                bo = [sb(st, "bo%d" % i, [128, 4, 64]) for i in range(2)]
                pti = [0]
                boi = [0]
                for (S, o, T, latent, bl) in seqs:
                    nk = T // 128 + (PAST // 128 if latent else 0)
                    nown = T // 128
                    qn = min(512, T)
                    for hd in range(4):
                        ld(vtf[:, 0:nown, :], S["VB"][o:o + T, hd * 64:(hd + 1) * 64].rearrange("(n p) d -> p n d", p=128), [vtf])
                        if latent:
                            ld(vtf[:, nown:nk, :], I["cbv"][l, :, hd * 64:(hd + 1) * 64].rearrange("(n p) d -> p n d", p=128), [vtf])
                        cp("pool", vt[:, 0:nk, 0:64], vtf[:, 0:nk, :], [vtf], [vt])
                        ld(stf[:, :, 0:T], S["KB"][2 * hd:2 * hd + 2, :, o:o + T].rearrange("m d t -> d m t"), [stf])
                        cp("pool", kT[:, :, 0:T], stf[:, :, 0:T], [stf], [kT])
                        ld(stf[:, :, 0:T], S["QB"][2 * hd:2 * hd + 2, :, o:o + T].rearrange("m d t -> d m t"), [stf])
                        cp("pool", qT[:, :, 0:T], stf[:, :, 0:T], [stf], [qT])
                        if latent:
                            for mp in range(2):
                                c0 = hd * 64 + mp * 32
                                ld(cstg[:], I["cbk"][l, :, c0:c0 + 32].rearrange("(n p) d -> p n d", p=128), [cstg])
                                pt = nps()
                                for n_ in range(4):
                                    tr(pt[0:32, n_ * 128:(n_ + 1) * 128], cstg[:, n_, :], ident[:, :], [cstg, ident], [pt])
                                cp("act", kT[:, mp, T:T + PAST], pt[0:32, :], [pt], [kT])
                        for q0 in range(0, T, qn):
                            nqb = qn // 128
                            pos = [ps[0], ps[1]]
                            for mp in range(2):
                                for kt in range(nk):
                                    pss = nps()
                                    mm(pss[:, 0:qn], kT[:, mp, kt * 128:(kt + 1) * 128], qT[:, mp, q0:q0 + qn], True, True, [kT, qT], [pss])
                                    pt_ = pT[pti[0] % 6]
                                    pti[0] += 1
                                    act(pt_[:, 0:qn], pss[:, 0:qn], AF.Exp, [pss], [pt_], scale=32.0 ** -0.5)
                                    for qb in range(nqb):
                                        mm(pos[mp][:, qb * 65:(qb + 1) * 65], pt_[:, qb * 128:(qb + 1) * 128], vt[:, kt, :], kt == 0 and qb == 0, kt == nk - 1 and qb == nqb - 1, [pt_, vt], [pos[mp]])
                            v1 = pos[0][:, 0:nqb * 65].rearrange("p (q d) -> p q d", d=65)
                            v2 = pos[1][:, 0:nqb * 65].rearrange("p (q d) -> p q d", d=65)
                            p.op("dve", lambda e: e.reciprocal(out=rd[:, 0, 0:nqb].unsqueeze(2), in_=v1[:, :, 64:65]), reads=[pos[0]], writes=[rd])
                            p.op("dve", lambda e: e.reciprocal(out=rd[:, 1, 0:nqb].unsqueeze(2), in_=v2[:, :, 64:65]), reads=[pos[1]], writes=[rd])
                            tt_("dve", o1[:, 0:nqb, :], v1[:, :, 0:64], rd[:, 0, 0:nqb].unsqueeze(2).to_broadcast([128, nqb, 64]), ALU.mult, [pos[0], rd], [o1])
                            tt_("dve", o2[:, 0:nqb, :], v2[:, :, 0:64], rd[:, 1, 0:nqb].unsqueeze(2).to_broadcast([128, nqb, 64]), ALU.mult, [pos[1], rd], [o2])
                            stt(o1[:, 0:nqb, :], o2[:, 0:nqb, :], nlam[:, 0:1], o1[:, 0:nqb, :], ALU.mult, ALU.add, [o1, o2, nlam], [o1])
                            tt_("pool", o3[:, 0:nqb, :], o1[:, 0:nqb, :], o1[:, 0:nqb, :], ALU.mult, [o1], [o3])
                            p.op("dve", lambda e: e.tensor_reduce(out=ssq[:, 0:nqb], in_=o3[:, 0:nqb, :], axis=AX.X, op=ALU.add), reads=[o3], writes=[ssq])
                            rsqrt_(ssq[:, 0:nqb], ssq[:, 0:nqb], 1.0 / 64, epsc[:, 0:1], [ssq, epsc], [ssq], None)
                            b_ = bo[boi[0] % 2]
                            boi[0] += 1
                            tt_("dve", b_[:, 0:nqb, :], o1[:, 0:nqb, :], ssq[:, 0:nqb].unsqueeze(2).to_broadcast([128, nqb, 64]), ALU.mult, [o1, ssq], [b_])
                            tt_("pool", b_[:, 0:nqb, :], b_[:, 0:nqb, :], gsub[:, 0:64].unsqueeze(1).to_broadcast([128, nqb, 64]), ALU.mult, [b_, gsub], [b_])
                            r0 = o + q0
                            stq(S["M"][r0:r0 + qn, 384 + hd * 64:384 + (hd + 1) * 64].rearrange("(q p) d -> p q d", p=128), b_[:, 0:nqb, :], [b_])
                p.barrier()

            with ExitStack() as st:
                cw2 = sb(st, "cw2", [64, 2, 384]); ca2 = sb(st, "ca2", [64, 2, 384])
                ld(cw2[:], I["c_w2"][l].rearrange("d r c -> r d c"), [cw2])
                ld(ca2[:], I["c_a2"][l].rearrange("d r c -> r d c"), [ca2])
                MK = {}
                for nm in ("m_lt", "m_le", "m_gt", "m_ge", "n_lt", "n_le", "n_gt", "n_ge", "i6"):
                    MK[nm] = sb(st, nm, [64, 384])
                    ld(MK[nm][:], I[nm][:, :], [MK[nm]])
                rstart = sb(st, "rstart", [64, 768])
                ld(rstart[:], I["rstart"][:, 0:768], [rstart])
                NSG = 128
                rkvx = sb(st, "rkvx", [64, 18, NSG + 2])
                sig = sb(st, "sig", [64, 6, NSG]); a_ = sb(st, "a_", [64, 6, NSG])
                kk = sb(st, "kk", [64, 6, NSG]); kd = sb(st, "kd", [64, 6, NSG]); bb = sb(st, "bb", [64, 6, NSG])
                Lc = sb(st, "Lc", [64, 6, NSG]); Lx = sb(st, "Lx", [64, 6, NSG]); Ld = sb(st, "Ld", [64, 6, NSG])
                E = sb(st, "E", [64, 6, NSG]); tmp = sb(st, "tmpc", [64, 6, NSG])
                hv = lambda t, h_: t[:, h_ * 64:(h_ + 1) * 64]
                v3 = lambda t: t[:, :].rearrange("p (h i) -> p h i", i=64)
                CB = []
                for ci_ in range(2):
                    Bd = {}
                    for nm in ("rkv",):
                        Bd[nm] = sb(st, "%s%d" % (nm, ci_), [64, 18, NSG])
                    for nm in ("cwt", "cat"):
                        Bd[nm] = sb(st, "%s%d" % (nm, ci_), [64, NSG])
                    for nm in ("rkc", "Kt_", "Rt_", "Dh", "Bh", "Dg", "nBg"):
                        Bd[nm] = sb(st, "%s%d" % (nm, ci_), [64, 6, NSG])
                    Bd["gC"] = sb(st, "gC%d" % ci_, [64, 6, NSG // 64])
                    Bd["H"] = sb(st, "H%d" % ci_, [64, 6, 64]); Bd["hst"] = sb(st, "hst%d" % ci_, [64, 6, 64])
                    for nm in ("Vt", "Dgt", "nBgt", "Pa", "Pb", "PTa", "PTb", "Xa", "Xb", "AdT", "BdT", "nBbT", "Wb", "Zb", "yb", "y2"):
                        Bd[nm] = sb(st, "%s%d" % (nm, ci_), [64, 384])
                    Bd["coef"] = sb(st, "coef%d" % ci_, [64, 6])
                    CB.append(Bd)

                def chain(S, o, T, latent, bl, d, Bd):
                    rkv, cwt, cat, rkc = Bd["rkv"], Bd["cwt"], Bd["cat"], Bd["rkc"]
                    Kt_, Rt_, Dh, Bh, Dg, nBg, gC, H, hst = (Bd[k] for k in ("Kt_", "Rt_", "Dh", "Bh", "Dg", "nBg", "gC", "H", "hst"))
                    Vt, Dgt, nBgt, AdT, BdT, nBbT, Wb, Zb, yb, y2, coef = (Bd[k] for k in ("Vt", "Dgt", "nBgt", "AdT", "BdT", "nBbT", "Wb", "Zb", "yb", "y2", "coef"))
                    Pq = [Bd["Pa"], Bd["Pb"]]; PTq = [Bd["PTa"], Bd["PTb"]]; Xq = [Bd["Xa"], Bd["Xb"]]
                    nsg = (T + NSG - 1) // NSG
                    YD = S["YF" if d == 0 else "YB"]
                    if latent:
                        src = I["scf" if d == 0 else "scb"][l]
                        ld(hst[:], src.rearrange("h i j -> i h j"), [hst])
                        pt = nps()
                        for h_ in range(6):
                            tr(pt[0:64, h_ * 64:(h_ + 1) * 64], hst[:, h_, :], ident[0:64, 0:64], [hst, ident], [pt])
                        cp("act", H[:].rearrange("p h i -> p (h i)"), pt[0:64, 0:384], [pt], [H])
                    else:
                        p.op("pool", lambda e: e.memset(H[:], 0.0), writes=[H])
                    segs = list(range(nsg)) if d == 0 else list(range(nsg - 1, -1, -1))
                    for sg_ in segs:
                        ts0 = sg_ * NSG
                        n = min(NSG, T - ts0)
                        nch = n // 64
                        lo = max(ts0 - 1, 0)
                        hi = min(ts0 + n + 1, T)
                        if ts0 == 0:
                            p.op("pool", lambda e: e.memset(rkvx[:, :, 0:1], 0.0), writes=[rkvx])
                        if ts0 + n == T:
                            p.op("pool", lambda e: e.memset(rkvx[:, :, n + 1:n + 2], 0.0), writes=[rkvx])
                        ld(rkvx[:, :, lo - (ts0 - 1):hi - (ts0 - 1)], S["RKV"][:, :, o + lo:o + hi].rearrange("c d t -> d c t"), [rkvx])
                        ld(cwt[:, 0:n], S["CW"][d, :, o + ts0:o + ts0 + n], [cwt])
                        ld(cat[:, 0:n], S["CA"][d, :, o + ts0:o + ts0 + n], [cat])
                        for c in range(18):
                            cb = (l * 3) * 18 + c
                            ts_("dve", rkv[:, c, 0:n], rkvx[:, c, 1:n + 1], convc[:, cb + 18:cb + 19], None, ALU.mult, ALU.bypass, [rkvx, convc], [rkv])
                            stt(rkv[:, c, 0:n], rkvx[:, c, 0:n], convc[:, cb:cb + 1], rkv[:, c, 0:n], ALU.mult, ALU.add, [rkvx, convc, rkv], [rkv])
                            stt(rkv[:, c, 0:n], rkvx[:, c, 2:n + 2], convc[:, cb + 36:cb + 37], rkv[:, c, 0:n], ALU.mult, ALU.add, [rkvx, convc, rkv], [rkv])
                        r_ = lambda h_: rkv[:, h_, 0:n]
                        k_ = lambda h_: rkv[:, 6 + h_, 0:n]
                        for h_ in range(6):
                            ci = (l * 2 + d) * 6 + h_
                            pw = nps()
                            mm(pw[0:64, 0:n], cw2[:, d, h_ * 64:(h_ + 1) * 64], cwt[:, 0:n], True, True, [cw2, cwt], [pw])
                            act(sig[:, h_, 0:n], pw[0:64, 0:n], AF.Sigmoid, [pw, w0c], [sig], bias=w0c[:, ci:ci + 1], scale=1.0)
                            pa = nps()
                            mm(pa[0:64, 0:n], ca2[:, d, h_ * 64:(h_ + 1) * 64], cat[:, 0:n], True, True, [ca2, cat], [pa])
                            act(a_[:, h_, 0:n], pa[0:64, 0:n], AF.Sigmoid, [pa, a0c], [a_], bias=a0c[:, ci:ci + 1], scale=1.0)
                            ts_("dve", kk[:, h_, 0:n], k_(h_), kkc[:, l * 6 + h_:l * 6 + h_ + 1], None, ALU.mult, ALU.bypass, [rkv, kkc], [kk])
                            tt_("pool", tmp[:, h_, 0:n], kk[:, h_, 0:n], kk[:, h_, 0:n], ALU.mult, [kk], [tmp])
                            pk_ = nps()
                            mm(pk_[0:64, 0:n], ones[0:64, 0:64], tmp[:, h_, 0:n], True, True, [ones, tmp], [pk_])
                            act(tmp[:, h_, 0:n], pk_[0:64, 0:n], AF.Sqrt, [pk_], [tmp])
                            ts_("dve", tmp[:, h_, 0:n], tmp[:, h_, 0:n], 1e-12, None, ALU.max, ALU.bypass, [tmp], [tmp])
                            p.op("dve", lambda e: e.reciprocal(out=tmp[:, h_, 0:n], in_=tmp[:, h_, 0:n]), reads=[tmp], writes=[tmp])
                            tt_("dve", kk[:, h_, 0:n], kk[:, h_, 0:n], tmp[:, h_, 0:n], ALU.mult, [kk, tmp], [kk])
                            ts_("dve", kd[:, h_, 0:n], a_[:, h_, 0:n], -1.0, kac[:, l * 6 + h_:l * 6 + h_ + 1], ALU.add, ALU.mult, [a_, kac], [kd])
                            stt(kd[:, h_, 0:n], kd[:, h_, 0:n], 1.0, k_(h_), ALU.add, ALU.mult, [kd, rkv], [kd])
                            stt(rkc[:, h_, 0:n], r_(h_), rkc_[:, l * 6 + h_:l * 6 + h_ + 1], kd[:, h_, 0:n], ALU.mult, ALU.mult, [rkv, rkc_, kd], [rkc])
                        tt_("pool", bb[:, :, 0:n], kk[:, :, 0:n], a_[:, :, 0:n], ALU.mult, [kk, a_], [bb])
                        if n == NSG:
                            p.op("dve", lambda e: e.tensor_tensor_scan(out=Lc[:].rearrange("p h t -> p (h t)"), data0=rstart[:, :],
                                                                        data1=sig[:].rearrange("p h t -> p (h t)"), initial=0.0, op0=ALU.mult, op1=ALU.add),
                                 reads=[rstart, sig], writes=[Lc])
                        else:
                            for h_ in range(6):
                                p.op("dve", lambda e: e.tensor_tensor_scan(out=Lc[:, h_, 0:n], data0=rstart[:, 0:n], data1=sig[:, h_, 0:n],
                                                                            initial=0.0, op0=ALU.mult, op1=ALU.add), reads=[rstart, sig], writes=[Lc])
                        tt_("pool", Lx[:, :, 0:n], Lc[:, :, 0:n], sig[:, :, 0:n], ALU.subtract, [Lc, sig], [Lx])
                        c4 = lambda t: t[:, :, 0:n].rearrange("p h (c s) -> p h c s", s=64)
                        Ltot = c4(Lc)[:, :, :, 63:64]
                        tt_("dve", c4(Ld), Ltot.to_broadcast([64, 6, nch, 64]), c4(Lc), ALU.subtract, [Lc], [Ld])
                        act(gC[:, :, 0:nch].unsqueeze(3), Ltot, AF.Exp, [Lc], [gC], scale=-DEC)
                        if d == 0:
                            act(E[:, :, 0:n], Lx[:, :, 0:n], AF.Exp, [Lx], [E], scale=-DEC)
                            tt_("dve", Kt_[:, :, 0:n], kk[:, :, 0:n], E[:, :, 0:n], ALU.mult, [kk, E], [Kt_])
                            act(E[:, :, 0:n], Lc[:, :, 0:n], AF.Exp, [Lc], [E], scale=-DEC)
                            tt_("dve", Rt_[:, :, 0:n], rkv[:, 0:6, 0:n], E[:, :, 0:n], ALU.mult, [rkv, E], [Rt_])
                            act(E[:, :, 0:n], Lc[:, :, 0:n], AF.Exp, [Lc], [E], scale=DEC)
                        else:
                            act(E[:, :, 0:n], Ld[:, :, 0:n], AF.Exp, [Ld], [E], scale=-DEC)
                            tt_("dve", Kt_[:, :, 0:n], kk[:, :, 0:n], E[:, :, 0:n], ALU.mult, [kk, E], [Kt_])
                            tt_("dve", c4(tmp), Ltot.to_broadcast([64, 6, nch, 64]), c4(Lx), ALU.subtract, [Lc, Lx], [tmp])
                            act(E[:, :, 0:n], tmp[:, :, 0:n], AF.Exp, [tmp], [E], scale=-DEC)
                            tt_("dve", Rt_[:, :, 0:n], rkv[:, 0:6, 0:n], E[:, :, 0:n], ALU.mult, [rkv, E], [Rt_])
                            act(E[:, :, 0:n], tmp[:, :, 0:n], AF.Exp, [tmp], [E], scale=DEC)
                        tt_("dve", Dh[:, :, 0:n], kd[:, :, 0:n], E[:, :, 0:n], ALU.mult, [kd, E], [Dh])
                        tt_("pool", Bh[:, :, 0:n], bb[:, :, 0:n], E[:, :, 0:n], ALU.mult, [bb, E], [Bh])
                        act(E[:, :, 0:n], (Ld if d == 0 else Lx)[:, :, 0:n], AF.Exp, [Ld, Lx], [E], scale=-DEC)
                        tt_("dve", Dg[:, :, 0:n], kd[:, :, 0:n], E[:, :, 0:n], ALU.mult, [kd, E], [Dg])
                        stt(nBg[:, :, 0:n], bb[:, :, 0:n], -1.0, E[:, :, 0:n], ALU.mult, ALU.mult, [bb, E], [nBg])
                        if d == 0:
                            nmAT, mAT, nmA, mBT, nmBT = MK["n_lt"], MK["m_lt"], MK["n_gt"], MK["m_le"], MK["n_le"]
                        else:
                            nmAT, mAT, nmA, mBT, nmBT = MK["n_gt"], MK["m_gt"], MK["n_lt"], MK["m_ge"], MK["n_ge"]
                        yield
                        chunks = list(range(nch)) if d == 0 else list(range(nch - 1, -1, -1))
                        for c in chunks:
                            cs = slice(c * 64, (c + 1) * 64)
                            for (src_t, srcidx, dst_t) in ((rkv, 12, Vt), (Dg, 0, Dgt), (nBg, 0, nBgt)):
                                pt = nps()
                                for h_ in range(6):
                                    tr(pt[0:64, h_ * 64:(h_ + 1) * 64], src_t[:, srcidx + h_, cs], ident[0:64, 0:64], [src_t, ident], [pt])
                                cp("act", dst_t[:, :], pt[0:64, 0:384], [pt], [dst_t])
                            pAbT, pBbT, pAdT, pBdT, pAb = nps(), nps(), nps(), nps(), nps()
                            for h_ in range(6):
                                hs = slice(h_ * 64, (h_ + 1) * 64)
                                mm(pAbT[0:64, hs], Bh[:, h_, cs], Kt_[:, h_, cs], True, True, [Bh, Kt_], [pAbT])
                                mm(pBbT[0:64, hs], Bh[:, h_, cs], Rt_[:, h_, cs], True, True, [Bh, Rt_], [pBbT])
                                mm(pAdT[0:64, hs], Dh[:, h_, cs], Kt_[:, h_, cs], True, True, [Dh, Kt_], [pAdT])
                                mm(pBdT[0:64, hs], Dh[:, h_, cs], Rt_[:, h_, cs], True, True, [Dh, Rt_], [pBdT])
                                mm(pAb[0:64, hs], Kt_[:, h_, cs], Bh[:, h_, cs], True, True, [Kt_, Bh], [pAb])
                            tt_("dve", PTq[0][:, :], pAbT[0:64, 0:384], nmAT[:, :], ALU.mult, [pAbT, nmAT], [PTq[0]])
                            tt_("dve", Pq[0][:, :], pAb[0:64, 0:384], nmA[:, :], ALU.mult, [pAb, nmA], [Pq[0]])
                            tt_("pool", Xq[0][:, :], PTq[0][:, :], MK["i6"][:, :], ALU.add, [PTq[0], MK["i6"]], [Xq[0]])
                            tt_("dve", AdT[:, :], pAdT[0:64, 0:384], mAT[:, :], ALU.mult, [pAdT, mAT], [AdT])
                            tt_("dve", BdT[:, :], pBdT[0:64, 0:384], mBT[:, :], ALU.mult, [pBdT, mBT], [BdT])
                            tt_("dve", nBbT[:, :], pBbT[0:64, 0:384], nmBT[:, :], ALU.mult, [pBbT, nmBT], [nBbT])
                            yield
                            for k in range(1, 6):
                                Pc, PTc = Pq[(k - 1) % 2], PTq[(k - 1) % 2]
                                Pn, PTn = Pq[k % 2], PTq[k % 2]
                                pP = nps()
                                for h_ in range(6):
                                    hs = slice(h_ * 64, (h_ + 1) * 64)
                                    mm(pP[0:64, hs], hv(PTc, h_), hv(Pc, h_), True, True, [PTc, Pc], [pP])
                                if k < 5:
                                    pPT = nps()
                                    for h_ in range(6):
                                        hs = slice(h_ * 64, (h_ + 1) * 64)
                                        mm(pPT[0:64, hs], hv(Pc, h_), hv(PTc, h_), True, True, [PTc, Pc], [pPT])
                                if k >= 2:
                                    Xo, Xn = Xq[k % 2], Xq[(k - 1) % 2]
                                    pX = nps()
                                    for h_ in range(6):
                                        hs = slice(h_ * 64, (h_ + 1) * 64)
                                        mm(pX[0:64, hs], hv(Pc, h_), hv(Xo, h_), True, True, [Pc, Xo], [pX])
                                    tt_("dve", Xn[:, :], pX[0:64, 0:384], Xo[:, :], ALU.add, [pX, Xo], [Xn])
                                cp("act", Pn[:, :], pP[0:64, 0:384], [pP], [Pn])
                                if k < 5:
                                    cp("dve", PTn[:, :], pPT[0:64, 0:384], [pPT], [PTn])
                                yield
                            P5, X4, X5 = Pq[1], Xq[0], Xq[1]
                            pX = nps()
                            for h_ in range(6):
                                hs = slice(h_ * 64, (h_ + 1) * 64)
                                mm(pX[0:64, hs], hv(P5, h_), hv(X4, h_), True, True, [P5, X4], [pX])
                            pW = nps()
                            for h_ in range(6):
                                hs = slice(h_ * 64, (h_ + 1) * 64)
                                mm(pW[0:64, hs], Kt_[:, h_, cs], H[:, h_, :], True, False, [Kt_, H], [pW])
                                mm(pW[0:64, hs], hv(AdT, h_), hv(Vt, h_), False, True, [AdT, Vt], [pW])
                            tt_("dve", X5[:, :], pX[0:64, 0:384], X4[:, :], ALU.add, [pX, X4], [X5])
                            cp("act", Wb[:, :], pW[0:64, 0:384], [pW], [Wb])
                            yield
                            pZ = nps()
                            for h_ in range(6):
                                hs = slice(h_ * 64, (h_ + 1) * 64)
                                mm(pZ[0:64, hs], hv(X5, h_), hv(Wb, h_), True, True, [X5, Wb], [pZ])
                            cp("act", Zb[:, :], pZ[0:64, 0:384], [pZ], [Zb])
                            yield
                            Hf = H[:].rearrange("p h i -> p (h i)")
                            pY, pC, pH = nps(), nps(), nps()
                            for h_ in range(6):
                                hs = slice(h_ * 64, (h_ + 1) * 64)
                                mm(pY[0:64, hs], Rt_[:, h_, cs], H[:, h_, :], True, False, [Rt_, H], [pY])
                                mm(pY[0:64, hs], hv(BdT, h_), hv(Vt, h_), False, False, [BdT, Vt], [pY])
                                mm(pY[0:64, hs], hv(nBbT, h_), hv(Zb, h_), False, True, [nBbT, Zb], [pY])
                                mm(pC[0:64, h_:h_ + 1], rkc[:, h_, cs], ones[0:64, 0:1], True, True, [rkc, ones], [pC])
                                mm(pH[0:64, hs], hv(Dgt, h_), hv(Vt, h_), True, False, [Dgt, Vt], [pH])
                                mm(pH[0:64, hs], hv(nBgt, h_), hv(Zb, h_), False, True, [nBgt, Zb], [pH])
                            cp("act", coef[:, :], pC[0:64, 0:6], [pC], [coef])
                            tt_("pool", v3(y2), v3(Vt), coef[:, :].unsqueeze(2).to_broadcast([64, 6, 64]), ALU.mult, [Vt, coef], [y2])
                            tt_("dve", yb[:, :], pY[0:64, 0:384], y2[:, :], ALU.add, [pY, y2], [yb])
                            tt_("pool", H[:], H[:], gC[:, :, c:c + 1].to_broadcast([64, 6, 64]), ALU.mult, [H, gC], [H])
                            tt_("dve", Hf, Hf, pH[0:64, 0:384], ALU.add, [H, pH], [H])
                            r0 = o + ts0 + c * 64
                            stq(YD[r0:r0 + 64, :], yb[:, :], [yb])
                            yield
                    if not latent:
                        pt = nps()
                        for h_ in range(6):
                            tr(pt[0:64, h_ * 64:(h_ + 1) * 64], H[:, h_, :], ident[0:64, 0:64], [H, ident], [pt])
                        cp("act", hst[:].rearrange("p h j -> p (h j)"), pt[0:64, 0:384], [pt], [hst])
                        stq(O["ncf" if d == 0 else "ncb"][bl, l].rearrange("h i j -> i h j"), hst[:], [hst])

                for (S, o, T, latent, bl) in seqs:
                    alive = [chain(S, o, T, latent, bl, 0, CB[0]), chain(S, o, T, latent, bl, 1, CB[1])]
                    while alive:
                        for g_ in list(alive):
                            try:
                                next(g_)
                            except StopIteration:
                                alive.remove(g_)
                p.barrier()

            with ExitStack() as st:
                cg2 = sb(st, "cg2", [128, 384])
                ld(cg2[:], I["c_g2"][l], [cg2])
                lnxg = bcast_row(st, I["c_lnx_g"][l:l + 1, :], 384, "lnxg")
                lnxb = bcast_row(st, I["c_lnx_b"][l:l + 1, :], 384, "lnxb")
                yfb = [sb(st, "yfb%d" % i, [128, 384]) for i in range(2)]
                ybb = [sb(st, "ybb%d" % i, [128, 384]) for i in range(2)]
                y2b = [sb(st, "y2b%d" % i, [128, 384]) for i in range(2)]
                cgb = [sb(st, "cgb%d" % i, [128, 128]) for i in range(2)]
                gsa = sb(st, "gsa", [128, 6]); gsb = sb(st, "gsb", [128, 6])
                w3 = lambda t: t[:, :].rearrange("p (h i) -> p h i", i=64)
                ti_ = 0
                for S in streams:
                    for r0 in range(0, S["T"], 128):
                        yf_, yb_, y2_, cg_ = yfb[ti_ % 2], ybb[ti_ % 2], y2b[ti_ % 2], cgb[ti_ % 2]
                        ti_ += 1
                        ld(yf_[:], S["YF"][r0:r0 + 128, :], [yf_])
                        ld(yb_[:], S["YB"][r0:r0 + 128, :], [yb_])
                        ld(cg_[:], S["CG"][:, r0:r0 + 128], [cg_])
                        tt_("pool", yb_[:, :], yb_[:, :], yf_[:, :], ALU.add, [yb_, yf_], [yb_])
                        p.op("dve", lambda e: e.tensor_reduce(out=gsa[:, :], in_=w3(yb_), axis=AX.X, op=ALU.add), reads=[yb_], writes=[gsa])
                        ts_("dve", gsa[:, :], gsa[:, :], -1.0 / 64, None, ALU.mult, ALU.bypass, [gsa], [gsa])
                        tt_("dve", w3(yb_), w3(yb_), gsa[:, :].unsqueeze(2).to_broadcast([128, 6, 64]), ALU.add, [yb_, gsa], [yb_])
                        tt_("pool", y2_[:, :], yb_[:, :], yb_[:, :], ALU.mult, [yb_], [y2_])
                        p.op("dve", lambda e: e.tensor_reduce(out=gsb[:, :], in_=w3(y2_), axis=AX.X, op=ALU.add), reads=[y2_], writes=[gsb])
                        rsqrt_(gsb[:, :], gsb[:, :], 1.0 / 64, epsc[:, 1:2], [gsb, epsc], [gsb], None)
                        tt_("dve", w3(yb_), w3(yb_), gsb[:, :].unsqueeze(2).to_broadcast([128, 6, 64]), ALU.mult, [yb_, gsb], [yb_])
                        tt_("pool", yb_[:, :], yb_[:, :], lnxg[:, :], ALU.mult, [yb_, lnxg], [yb_])
                        tt_("pool", yb_[:, :], yb_[:, :], lnxb[:, :], ALU.add, [yb_, lnxb], [yb_])
                        pg = nps()
                        mm(pg[:, 0:384], cg_[:, :], cg2[:, :], True, True, [cg_, cg2], [pg])
                        tt_("dve", y2_[:, :], yb_[:, :], pg[:, 0:384], ALU.mult, [yb_, pg], [y2_])
                        stq(S["M"][r0:r0 + 128, 640:1024], y2_[:, :], [y2_])
                p.barrier()

            with ExitStack() as st:
                wout = sb(st, "wout", [128, 8, D])
                wv = I["w_out"][l].rearrange("(c p) f -> p c f", p=128)
                for k in range(8):
                    ld(wout[:, k, :], wv[:, k, :], [wout])
                xt = sb(st, "xt2", [128, 8, 512])
                mT = sb(st, "mT", [128, 8, 512])
                sq = sb(st, "sq2", [128, 8, 512])
                rstd = sb(st, "rstd2", [128, 512])
                mtok = [sb(st, "mtok%d" % i, [128, D]) for i in range(2)]
                actb = sb(st, "actb", [128, 22, 512])
                w13 = [sb(st, "w13_%d" % i, [128, 8, 2, 256]) for i in range(2)]
                w2b = [sb(st, "w2b_%d" % i, [128, 22, 128]) for i in range(2)]
                sgl = sb(st, "sgl", [128, 512])
                w1v = I["ffn_w1"][l].rearrange("(c p) f -> p c f", p=128)
                w3v = I["ffn_w3"][l].rearrange("(c p) f -> p c f", p=128)
                w2v = I["ffn_w2"][l].rearrange("(c p) f -> p c f", p=128)
                wi = [0]
                for S in streams:
                    latent = S is SS_
                    cv = 1 if latent else 0
                    T = S["T"]
                    XTv = S["XT"].rearrange("(c p) t -> p c t", p=128)
                    for t0 in range(0, T, 512):
                        ld(xt[:], XTv[:, :, t0:t0 + 512], [xt])
                        for i in range(4):
                            mk = mtok[i % 2]
                            ld(mk[:], S["M"][t0 + i * 128:t0 + (i + 1) * 128, :], [mk])
                            for half in range(2):
                                pt = nps()
                                for c in range(4):
                                    cc = half * 4 + c
                                    tr(pt[:, c * 128:(c + 1) * 128], mk[:, cc * 128:(cc + 1) * 128], ident[:, :], [mk, ident], [pt])
                                cp("act" if half == 0 else "dve", mT[:, half * 4:(half + 1) * 4, i * 128:(i + 1) * 128],
                                   pt[:, :].rearrange("p (c t) -> p c t", t=128), [pt], [mT])
                        for dc in range(8):
                            po = nps()
                            for k in range(8):
                                mm(po[:, :], wout[:, k, dc * 128:(dc + 1) * 128], mT[:, k, :], k == 0, k == 7, [wout, mT], [po])
                            stt(xt[:, dc, :], po[:, :], MOD(l, 2, dc, cv), xt[:, dc, :], ALU.mult, ALU.add, [po, mod, xt], [xt])
                        act(sq[:], xt[:], AF.Square, [xt], [sq])
                        pss = nps()
                        for c in range(8):
                            mm(pss[:, :], ones[:, 0:128], sq[:, c, :], c == 0, c == 7, [ones, sq], [pss])
                        rsqrt_(rstd[:], pss[:, :], 1.0 / D, epsc[:, 0:1], [pss, epsc], [rstd], None)
                        hh = mT
                        for c in range(8):
                            stt(hh[:, c, :], xt[:, c, :], modA2[:, l * 8 + c, cv:cv + 1], rstd[:], ALU.mult, ALU.mult, [xt, modA2, rstd], [hh])
                            act(hh[:, c, :], hh[:, c, :], AF.Identity, [hh, mod], [hh], bias=MOD(l, 3, c, cv), scale=1.0)
                        for fp in range(11):
                            wt = w13[wi[0] % 2]
                            wi[0] += 1
                            ld(wt[:, :, 0, :], w1v[:, :, fp * 256:(fp + 1) * 256], [wt])
                            ld(wt[:, :, 1, :], w3v[:, :, fp * 256:(fp + 1) * 256], [wt])
                            for f2 in range(2):
                                fc = fp * 2 + f2
                                p1, p3 = nps(), nps()
                                for k in range(8):
                                    mm(p1[:, :], wt[:, k, 0, f2 * 128:(f2 + 1) * 128], hh[:, k, :], k == 0, k == 7, [wt, hh], [p1])
                                for k in range(8):
                                    mm(p3[:, :], wt[:, k, 1, f2 * 128:(f2 + 1) * 128], hh[:, k, :], k == 0, k == 7, [wt, hh], [p3])
                                act(sgl[:], p1[:, :], AF.Silu, [p1], [sgl])
                                tt_("dve", actb[:, fc, :], sgl[:], p3[:, :], ALU.mult, [sgl, p3], [actb])
                        for dc in range(8):
                            w2t = w2b[dc % 2]
                            ld(w2t[:], w2v[:, :, dc * 128:(dc + 1) * 128], [w2t])
                            po = nps()
                            for fc in range(22):
                                mm(po[:, :], w2t[:, fc, :], actb[:, fc, :], fc == 0, fc == 21, [w2t, actb], [po])
                            stt(xt[:, dc, :], po[:, :], MOD(l, 5, dc, cv), xt[:, dc, :], ALU.mult, ALU.add, [po, mod, xt], [xt])
                        stq(XTv[:, :, t0:t0 + 512], xt[:], [xt])
                p.barrier()

        with ExitStack() as st:
            xt = sb(st, "xtf", [128, 8, 512])
            sq = sb(st, "sqf", [128, 8, 512])
            rstd = sb(st, "rstdf", [128, 512])
            yo = [sb(st, "yo%d" % i, [128, D]) for i in range(2)]
            for S, dst in ((SP_, O["yp"]), (SS_, O["ys"])):
                if S is SS_ and not do_sample:
                    continue
                T = S["T"]
                XTv = S["XT"].rearrange("(c p) t -> p c t", p=128)
                for t0 in range(0, T, 512):
                    ld(xt[:], XTv[:, :, t0:t0 + 512], [xt])
                    act(sq[:], xt[:], AF.Square, [xt], [sq])
                    pss = nps()
                    for c in range(8):
                        mm(pss[:, :], ones[:, 0:128], sq[:, c, :], c == 0, c == 7, [ones, sq], [pss])
                    rsqrt_(rstd[:], pss[:, :], 1.0 / D, epsc[:, 0:1], [pss, epsc], [rstd], None)
                    for c in range(8):
                        stt(sq[:, c, :], xt[:, c, :], gfc[:, c:c + 1], rstd[:], ALU.mult, ALU.mult, [xt, gfc, rstd], [sq])
                    for i in range(4):
                        y_ = yo[i % 2]
                        for half in range(2):
                            pt = nps()
                            for c in range(4):
                                cc = half * 4 + c
                                tr(pt[:, c * 128:(c + 1) * 128], sq[:, cc, i * 128:(i + 1) * 128], ident[:, :], [sq, ident], [pt])
                            cp("act" if half == 0 else "dve", y_[:, half * 512:(half + 1) * 512], pt[:, :], [pt], [y_])
                        stq(dst[t0 + i * 128:t0 + (i + 1) * 128, :], y_[:], [y_])
            p.barrier()
    return nc


def make_in_maps(inputs):
    f = lambda a: np.ascontiguousarray(np.asarray(a, dtype=np.float32))
    consts = _consts()
    shared = {k: f(inputs[k]) for k in WEIGHT_SHAPES}
    shared.update(consts)
    maps = []
    for c in range(NCORES):
        b = c % 2
        m = dict(shared)
        m["xp"] = f(inputs["x_prompt"][NPL * c:NPL * (c + 1)]).reshape(NPL * TP, D)
        m["xs"] = f(inputs["x_sample"][b])
        m["cak"] = f(inputs["cache_a_k"][b]).reshape(L, PAST, 128)
        m["cav"] = f(inputs["cache_a_v"][b]).reshape(L, PAST, 128)
        m["cbk"] = f(inputs["cache_b_k"][b]).reshape(L, PAST, 256)
        m["cbv"] = f(inputs["cache_b_v"][b]).reshape(L, PAST, 256)
        m["scf"] = f(inputs["state_c_fwd"][b])
        m["scb"] = f(inputs["state_c_bwd"][b])
        m["cs"] = f(inputs["c"][b])
        maps.append(m)
    return maps


def kernel(**inputs):
    nc = build()
    maps = make_in_maps(inputs)
    res = run_bass_kernel_spmd(nc, maps, core_ids=list(range(NCORES))).results
    yp = np.concatenate([r["yp"].reshape(NPL, TP, D) for r in res], axis=0)
    ys = np.stack([res[0]["ys"], res[1]["ys"]], axis=0)
    nak = np.concatenate([r["nak"].reshape(NPL, L, TP, 2, 64) for r in res], axis=0)
    nav = np.concatenate([r["nav"].reshape(NPL, L, TP, 2, 64) for r in res], axis=0)
    nbk = np.concatenate([r["nbk"].reshape(NPL, L, TP, 4, 2, 32) for r in res], axis=0)
    nbv = np.concatenate([r["nbv"].reshape(NPL, L, TP, 4, 64) for r in res], axis=0)
    ncf = np.concatenate([r["ncf"] for r in res], axis=0)
    ncb = np.concatenate([r["ncb"] for r in res], axis=0)
    return tuple(np.ascontiguousarray(a.astype(np.float32)) for a in (yp, ys, nak, nav, nbk, nbv, ncf, ncb))
```

```python
import math
from contextlib import ExitStack
import numpy as np
import concourse.bass as bass
import concourse.mybir as mybir
from concourse.bass_utils import run_bass_kernel_spmd

F32 = mybir.dt.float32
BF16 = mybir.dt.bfloat16
AF = mybir.ActivationFunctionType
ALU = mybir.AluOpType
AX = mybir.AxisListType

D = 1024
L = 4
TP = 256
TS = 4096
PAST = 512
NPL = 2
DFF = 2816
INC = 2944
DEC = 0.606531
EPS = 1e-6
GN_EPS = 64e-5
NCORES = 8


class Buf:
    __slots__ = ("w", "r")

    def __init__(self):
        self.w = []
        self.r = []


class TT:
    def __init__(self, h):
        self.h = h
        self.b = Buf()

    def __getitem__(self, k):
        return self.h[k]


class Prog:
    RING = 12

    def __init__(self, nc, stack):
        self.nc = nc
        self.eng = {"pe": nc.tensor, "act": nc.scalar, "dve": nc.vector, "pool": nc.gpsimd, "sp": nc.sync}
        self.semh = {}
        self.cnt = {}
        self.seen = {e: {} for e in self.eng}
        for e in ("pe", "act", "dve", "pool"):
            self.semh["S_" + e] = stack.enter_context(nc.semaphore("S_" + e))
            self.cnt[e] = 0
        self.dq = {}
        for q in ("sp", "pool"):
            ring = []
            for k in range(self.RING):
                key = "D_%s_%d" % (q, k)
                self.semh[key] = stack.enter_context(nc.semaphore(key))
                ring.append([key, 0])
            self.dq[q] = [ring, 0]
        self.n = 0

    def _waits(self, e, reads, writes):
        need = {}
        for b in reads:
            for (key, val, te) in b.w:
                if need.get(key, 0) < val:
                    need[key] = val
        for b in writes:
            for (key, val, te) in b.w:
                if te != e and need.get(key, 0) < val:
                    need[key] = val
            for (key, val, te) in b.r:
                if te != e and need.get(key, 0) < val:
                    need[key] = val
        seen = self.seen[e]
        for key, val in need.items():
            if seen.get(key, 0) < val:
                self.eng[e].wait_ge(self.semh[key], val)
                seen[key] = val
                self.n += 1

    def _record(self, tok, reads, writes):
        for b in writes:
            b.w = [tok]
            b.r = []
        for b in reads:
            b.r = [t for t in b.r if t[0] != tok[0]]
            b.r.append(tok)

    def op(self, e, fn, reads=(), writes=()):
        reads = [t.b for t in reads]
        writes = [t.b for t in writes]
        self._waits(e, reads, writes)
        ins = fn(self.eng[e])
        self.cnt[e] += 1
        ins.then_inc(self.semh["S_" + e], 1)
        self._record(("S_" + e, self.cnt[e], e), reads, writes)
        self.n += 1

    def dma(self, q, out_ap, in_ap, reads=(), writes=()):
        reads = [t.b for t in reads]
        writes = [t.b for t in writes]
        self._waits(q, reads, writes)
        ring, rr = self.dq[q]
        slot = ring[rr % self.RING]
        self.dq[q][1] = rr + 1
        key, prev = slot
        if prev > 0 and self.seen[q].get(key, 0) < prev:
            self.eng[q].wait_ge(self.semh[key], prev)
            self.seen[q][key] = prev
        ins = self.eng[q].dma_start(out=out_ap, in_=in_ap)
        ins.then_inc(self.semh[key], 16)
        slot[1] = prev + 16
        self._record((key, prev + 16, None), reads, writes)
        self.n += 2

    def barrier(self):
        targets = {}
        for e in ("pe", "act", "dve", "pool"):
            if self.cnt[e] > 0:
                targets["S_" + e] = self.cnt[e]
        for q in self.dq:
            for key, val in self.dq[q][0]:
                if val > 0:
                    targets[key] = val
        for e in self.eng:
            seen = self.seen[e]
            for key, val in targets.items():
                if key == "S_" + e:
                    continue
                if seen.get(key, 0) < val:
                    self.eng[e].wait_ge(self.semh[key], val)
                    seen[key] = val
                    self.n += 1


def _rope_tables(dh, T, grid_w=64):
    half = dh // 2
    t = np.arange(T)
    rows = t // grid_w
    cols = t % grid_w
    inv = 10000.0 ** (-np.arange(0, half, 2, dtype=np.float32) / half)
    cos = np.zeros((dh, T), np.float32)
    sin = np.zeros((dh, T), np.float32)
    perm = np.zeros((dh, dh), np.float32)
    q = half // 2
    for ax, pos in enumerate((rows, cols)):
        ang = pos[None, :].astype(np.float32) * inv[:, None]
        base = ax * half
        for i in range(q):
            cos[base + i] = np.cos(ang[i])
            cos[base + q + i] = np.cos(ang[i])
            sin[base + i] = -np.sin(ang[i])
            sin[base + q + i] = np.sin(ang[i])
            perm[base + q + i, base + i] = 1.0
            perm[base + i, base + q + i] = 1.0
    return cos, sin, perm


def _consts():
    c = {}
    c["ident"] = np.eye(128, dtype=np.float32)
    c["ones"] = np.ones((128, 512), np.float32)
    cosA, sinA, pA = _rope_tables(64, TS)
    cosB, sinB, pB = _rope_tables(32, TS)
    c["cosA"] = np.tile(cosA, (2, 1))
    c["sinA"] = np.tile(sinA, (2, 1))
    c["cosB"] = np.tile(cosB, (4, 1))
    c["sinB"] = np.tile(sinB, (4, 1))
    PA = np.zeros((128, 128), np.float32)
    PB = np.zeros((128, 128), np.float32)
    for i in range(2):
        PA[i * 64:(i + 1) * 64, i * 64:(i + 1) * 64] = pA
    for i in range(4):
        PB[i * 32:(i + 1) * 32, i * 32:(i + 1) * 32] = pB
    c["permA"] = PA
    c["permB"] = PB
    p = np.arange(128)[:, None]
    f = np.arange(128)[None, :]
    c["mprev"] = (p >= f).astype(np.float32)
    c["mnext"] = (p <= f).astype(np.float32)
    p = np.arange(64)[:, None]
    f = np.arange(64)[None, :]
    rep = lambda m: np.tile(m.astype(np.float32), (1, 6))
    c["m_lt"] = rep(p < f)
    c["m_le"] = rep(p <= f)
    c["m_gt"] = rep(p > f)
    c["m_ge"] = rep(p >= f)
    c["n_lt"] = -rep(p < f)
    c["n_le"] = -rep(p <= f)
    c["n_gt"] = -rep(p > f)
    c["n_ge"] = -rep(p >= f)
    c["i6"] = rep(p == f)
    rs = np.ones((64, 6 * 512), np.float32)
    rs[:, ::64] = 0.0
    c["rstart"] = rs
    return c


CONST_SHAPES = {k: v.shape for k, v in _consts().items()}

WEIGHT_SHAPES = dict(
    ada_w=(L, D, 6 * D), ada_b=(L, 6 * D), norm1_g=(L, D), norm2_g=(L, D), w_in=(L, D, INC), a_sink=(L, 6),
    b_lambda=(L, 4, 32), b_subln_g=(L, 64), c_conv=(L, 3, 1152), c_w0=(L, 2, 384), c_w2=(L, 2, 64, 384),
    c_a0=(L, 2, 384), c_a2=(L, 2, 64, 384), c_g2=(L, 128, 384), c_kk=(L, 384), c_ka=(L, 384), c_rk=(L, 6, 64),
    c_lnx_g=(L, 384), c_lnx_b=(L, 384), w_out=(L, D, D), ffn_w1=(L, D, DFF), ffn_w3=(L, D, DFF),
    ffn_w2=(L, DFF, D), final_g=(D,), c_ctx=(D,))

CORE_IN_SHAPES = dict(
    xp=(NPL * TP, D), xs=(TS, D), cak=(L, PAST, 128), cav=(L, PAST, 128), cbk=(L, PAST, 256), cbv=(L, PAST, 256),
    scf=(L, 6, 64, 64), scb=(L, 6, 64, 64), cs=(D,))

OUT_SHAPES = dict(
    yp=(NPL * TP, D), ys=(TS, D), nak=(NPL, L, TP, 128), nav=(NPL, L, TP, 128), nbk=(NPL, L, TP, 256),
    nbv=(NPL, L, TP, 256), ncf=(NPL, L, 6, 64, 64), ncb=(NPL, L, 6, 64, 64))


def build(nlayers=L, debug=False, do_sample=True):
    nc = bass.Bass("TRN2", target_bir_lowering=False)
    I = {}
    for k, s in list(WEIGHT_SHAPES.items()) + list(CORE_IN_SHAPES.items()) + list(CONST_SHAPES.items()):
        I[k] = nc.dram_tensor(k, list(s), F32, kind="ExternalInput").ap()
    O = {k: nc.dram_tensor(k, list(s), F32, kind="ExternalOutput").ap() for k, s in OUT_SHAPES.items()}
    skind = "ExternalOutput" if debug else "Internal"
    streams = []
    for sname, tt in (("p", NPL * TP), ("s", TS)):
        S = {"T": tt, "name": sname}
        for nm, shp in (("XT", [D, tt]), ("QA", [6, 64, tt]), ("KA", [2, 64, tt]), ("QB", [8, 32, tt]),
                        ("KB", [8, 32, tt]), ("VA", [tt, 128]), ("VB", [tt, 256]), ("RKV", [18, 64, tt]),
                        ("CW", [2, 64, tt]), ("CA", [2, 64, tt]), ("CG", [128, tt]), ("M", [tt, D]),
                        ("YF", [tt, 384]), ("YB", [tt, 384]), ("RKVC", [18, 64, tt])):
            S[nm] = nc.dram_tensor("%s_%s" % (nm, sname), shp, F32, kind=skind).ap()
        streams.append(S)
    SP_, SS_ = streams
    if not do_sample:
        streams = [SP_]
    seqs = [(SP_, 0, TP, False, 0), (SP_, TP, TP, False, 1)]
    if do_sample:
        seqs.append((SS_, 0, TS, True, -1))

    with ExitStack() as glob:
        p = Prog(nc, glob)

        uid = [0]

        def sb(st, name, shape, dt=F32):
            uid[0] += 1
            return TT(st.enter_context(nc.sbuf_tensor("s%d_%s" % (uid[0], name), list(shape), dt)))

        ps = [TT(glob.enter_context(nc.psum_tensor("ps%d" % i, [128, 512], F32))) for i in range(8)]
        psi = [0]

        def nps():
            t = ps[2 + psi[0] % 6]
            psi[0] += 1
            return t

        def mm(out, lhsT, rhs, start, stop, r, w):
            p.op("pe", lambda e: e.matmul(out, lhsT, rhs, start=start, stop=stop), reads=r, writes=w)

        def tr(out, in_, idn, r, w):
            p.op("pe", lambda e: e.transpose(out, in_, idn), reads=r, writes=w)

        def ld(out, in_, w, r=()):
            p.dma("sp", out, in_, reads=r, writes=w)

        def stq(out, in_, r):
            p.dma("pool", out, in_, reads=r)

        def act(out, in_, func, r, w, **kw):
            p.op("act", lambda e: e.activation(out=out, in_=in_, func=func, **kw), reads=r, writes=w)

        def tt_(eng, out, in0, in1, op, r, w):
            p.op(eng, lambda e: e.tensor_tensor(out=out, in0=in0, in1=in1, op=op), reads=r, writes=w)

        def ts_(eng, out, in0, s1, s2, op0, op1, r, w):
            if s2 is None:
                p.op(eng, lambda e: e.tensor_single_scalar(out=out, in_=in0, scalar=s1, op=op0), reads=r, writes=w)
            else:
                p.op(eng, lambda e: e.tensor_scalar(out=out, in0=in0, scalar1=s1, scalar2=s2, op0=op0, op1=op1), reads=r, writes=w)

        def stt(out, in0, scalar, in1, op0, op1, r, w):
            p.op("dve", lambda e: e.scalar_tensor_tensor(out=out, in0=in0, scalar=scalar, in1=in1, op0=op0, op1=op1), reads=r, writes=w)

        def cp(eng, out, in_, r, w):
            if eng == "act":
                p.op("act", lambda e: e.copy(out=out, in_=in_), reads=r, writes=w)
            else:
                p.op(eng, lambda e: e.tensor_copy(out=out, in_=in_), reads=r, writes=w)

        def rsqrt_(out, in_, scale, bias, r, w, tmpw):
            act(out, in_, AF.Sqrt, r, w, scale=scale, bias=bias)
            p.op("dve", lambda e: e.reciprocal(out=out, in_=out), reads=w, writes=w)

        ident = sb(glob, "ident", [128, 128])
        ones = sb(glob, "ones", [128, 512])
        epsc = sb(glob, "epsc", [128, 2])
        ld(ident[:], I["ident"][:, :], [ident])
        ld(ones[:], I["ones"][:, :], [ones])
        p.op("pool", lambda e: e.memset(epsc[:, 0:1], EPS), writes=[epsc])
        p.op("pool", lambda e: e.memset(epsc[:, 1:2], GN_EPS), writes=[epsc])

        stgc = sb(glob, "stgc", [128, 512])

        def load_cols(st, src2d, rows, w, name):
            dst = sb(st, name, [w, rows])
            for r0 in range(0, rows, 128):
                rr = min(128, rows - r0)
                ld(stgc[0:rr, 0:w], src2d[r0:r0 + rr, :], [stgc])
                pt = nps()
                tr(pt[0:w, 0:rr], stgc[0:rr, 0:w], ident[0:rr, 0:rr], [stgc, ident], [pt])
                cp("act", dst[:, r0:r0 + rr], pt[0:w, 0:rr], [pt], [dst])
            return dst

        def bcast_row(st, src_row, n, name, dst=None):
            if dst is None:
                dst = sb(st, name, [128, n])
            ld(stgc[0:1, 0:n], src_row, [stgc])
            pt = nps()
            mm(pt[:, 0:n], ones[0:1, 0:128], stgc[0:1, 0:n], True, True, [ones, stgc], [pt])
            cp("act", dst[:, 0:n], pt[:, 0:n], [pt], [dst])
            return dst

        adab = load_cols(glob, I["ada_b"].rearrange("l (j p) -> (l j) p", p=128), L * 48, 128, "adab")
        g1c = load_cols(glob, I["norm1_g"].rearrange("l (j p) -> (l j) p", p=128), L * 8, 128, "g1c")
        g2c = load_cols(glob, I["norm2_g"].rearrange("l (j p) -> (l j) p", p=128), L * 8, 128, "g2c")
        gfc = load_cols(glob, I["final_g"].rearrange("(j p) -> j p", p=128), 8, 128, "gfc")
        cct = load_cols(glob, I["c_ctx"].rearrange("(j p) -> j p", p=128), 8, 128, "cct")
        cst = load_cols(glob, I["cs"].rearrange("(j p) -> j p", p=128), 8, 128, "cst")
        convc = load_cols(glob, I["c_conv"].rearrange("l k (t c) -> (l k t) c", c=64), L * 3 * 18, 64, "convc")
        w0c = load_cols(glob, I["c_w0"].rearrange("l d (h c) -> (l d h) c", c=64), L * 12, 64, "w0c")
        a0c = load_cols(glob, I["c_a0"].rearrange("l d (h c) -> (l d h) c", c=64), L * 12, 64, "a0c")
        kkc = load_cols(glob, I["c_kk"].rearrange("l (h c) -> (l h) c", c=64), L * 6, 64, "kkc")
        kac = load_cols(glob, I["c_ka"].rearrange("l (h c) -> (l h) c", c=64), L * 6, 64, "kac")
        rkc_ = load_cols(glob, I["c_rk"].rearrange("l h c -> (l h) c"), L * 6, 64, "rkc")

        mod = sb(glob, "mod", [128, L * 48, 2])
        modA1 = sb(glob, "modA1", [128, L * 8, 2])
        modA2 = sb(glob, "modA2", [128, L * 8, 2])
        with ExitStack() as st:
            silc = sb(st, "silc", [128, 8, 2])
            act(silc[:, :, 0], cct[:, :], AF.Silu, [cct], [silc])
            act(silc[:, :, 1], cst[:, :], AF.Silu, [cst], [silc])
            pcs = [sb(st, "adapc%d" % i, [128, 8, 512]) for i in range(2)]
            k_ = 0
            for l in range(nlayers):
                wv = I["ada_w"][l].rearrange("(c p) f -> p c f", p=128)
                for pc in range(12):
                    pt_ = pcs[k_ % 2]
                    k_ += 1
                    ld(pt_[:], wv[:, :, pc * 512:(pc + 1) * 512], [pt_])
                    for jb in range(4):
                        pq = nps()
                        for k in range(8):
                            mm(pq[:, 0:2], pt_[:, k, jb * 128:(jb + 1) * 128], silc[:, k, :], k == 0, k == 7, [pt_, silc], [pq])
                        j = l * 48 + pc * 4 + jb
                        ts_("dve", mod[:, j, :], pq[:, 0:2], adab[:, j:j + 1], None, ALU.add, ALU.bypass, [pq, adab], [mod])
                stt(modA1[:, l * 8:(l + 1) * 8, :], mod[:, l * 48 + 8:l * 48 + 16, :], 1.0,
                    g1c[:, l * 8:(l + 1) * 8].unsqueeze(2).to_broadcast([128, 8, 2]), ALU.add, ALU.mult, [mod, g1c], [modA1])
                stt(modA2[:, l * 8:(l + 1) * 8, :], mod[:, l * 48 + 32:l * 48 + 40, :], 1.0,
                    g2c[:, l * 8:(l + 1) * 8].unsqueeze(2).to_broadcast([128, 8, 2]), ALU.add, ALU.mult, [mod, g2c], [modA2])
            p.barrier()

        def MOD(l, which, c, cv):
            j = l * 48 + which * 8 + c
            return mod[:, j, cv:cv + 1]

        with ExitStack() as st:
            xin = [sb(st, "xin%d" % i, [128, D]) for i in range(2)]
            xo = [sb(st, "xo%d" % i, [128, 8, 128]) for i in range(2)]
            k_ = 0
            for S, src in ((SP_, I["xp"]), (SS_, I["xs"])):
                if S is SS_ and not do_sample:
                    continue
                for b in range(S["T"] // 128):
                    a_ = xin[k_ % 2]
                    o_ = xo[k_ % 2]
                    k_ += 1
                    ld(a_[:], src[b * 128:(b + 1) * 128, :], [a_])
                    for half in range(2):
                        pt = nps()
                        for c in range(4):
                            cc = half * 4 + c
                            tr(pt[:, c * 128:(c + 1) * 128], a_[:, cc * 128:(cc + 1) * 128], ident[:, :], [a_, ident], [pt])
                        cp("act" if half == 0 else "dve", o_[:, half * 4:(half + 1) * 4, :],
                           pt[:, :].rearrange("p (c t) -> p c t", t=128), [pt], [o_])
                    stq(S["XT"].rearrange("(c p) t -> p c t", p=128)[:, :, b * 128:(b + 1) * 128], o_[:], [o_])
            p.barrier()

        for l in range(nlayers):
            lam_init = 0.8 - 0.6 * math.exp(-0.3 * l)
            with ExitStack() as st:
                win = sb(st, "win", [128, 8, INC])
                wv = I["w_in"][l].rearrange("(c p) f -> p c f", p=128)
                for k in range(8):
                    ld(win[:, k, :], wv[:, k, :], [win])
                xt = sb(st, "xt", [128, 8, 512])
                sq = sb(st, "sq", [128, 8, 512])
                h = sb(st, "h", [128, 8, 512])
                rstd = sb(st, "rstd", [128, 512])
                stg = [sb(st, "stg%d" % i, [128, 512]) for i in range(3)]
                zs = sb(st, "zs", [128, 512])
                t2 = sb(st, "t2", [128, 512])
                cosA = sb(st, "cosA", [128, 512]); sinA = sb(st, "sinA", [128, 512])
                cosB = sb(st, "cosB", [128, 512]); sinB = sb(st, "sinB", [128, 512])
                permA = sb(st, "permA", [128, 128]); permB = sb(st, "permB", [128, 128])
                ld(permA[:], I["permA"][:, :], [permA]); ld(permB[:], I["permB"][:, :], [permB])
                vst = [sb(st, "vst%d" % i, [128, 768]) for i in range(2)]
                sgi = [0]
                for S in streams:
                    latent = S is SS_
                    cv = 1 if latent else 0
                    T = S["T"]
                    XTv = S["XT"].rearrange("(c p) t -> p c t", p=128)
                    for t0 in range(0, T, 512):
                        ld(xt[:], XTv[:, :, t0:t0 + 512], [xt])
                        if latent:
                            ld(cosA[:], I["cosA"][:, t0:t0 + 512], [cosA]); ld(sinA[:], I["sinA"][:, t0:t0 + 512], [sinA])
                            ld(cosB[:], I["cosB"][:, t0:t0 + 512], [cosB]); ld(sinB[:], I["sinB"][:, t0:t0 + 512], [sinB])
                        act(sq[:], xt[:], AF.Square, [xt], [sq])
                        pss = nps()
                        for c in range(8):
                            mm(pss[:, :], ones[:, 0:128], sq[:, c, :], c == 0, c == 7, [ones, sq], [pss])
                        rsqrt_(rstd[:], pss[:, :], 1.0 / D, epsc[:, 0:1], [pss, epsc], [rstd], None)
                        for c in range(8):
                            stt(h[:, c, :], xt[:, c, :], modA1[:, l * 8 + c, cv:cv + 1], rstd[:], ALU.mult, ALU.mult, [xt, modA1, rstd], [h])
                            act(h[:, c, :], h[:, c, :], AF.Identity, [h, mod], [h], bias=MOD(l, 0, c, cv), scale=1.0)
                        for j in range(23):
                            if j in (4, 9, 10):
                                continue
                            pz = nps()
                            for k in range(8):
                                mm(pz[:, :], win[:, k, j * 128:(j + 1) * 128], h[:, k, :], k == 0, k == 7, [win, h], [pz])
                            sg = stg[sgi[0] % 3]
                            sgi[0] += 1
                            rope = latent and j in (0, 1, 2, 3, 5, 6, 7, 8)
                            if rope:
                                isA = j <= 3
                                cs_, sn_, pm_ = (cosA, sinA, permA) if isA else (cosB, sinB, permB)
                                cp("act", zs[:], pz[:, :], [pz], [zs])
                                pr = nps()
                                mm(pr[:, :], pm_[:, :], zs[:], True, True, [pm_, zs], [pr])
                                tt_("dve", t2[:], pr[:, :], sn_[:], ALU.mult, [pr, sn_], [t2])
                                tt_("pool", sg[:], zs[:], cs_[:], ALU.mult, [zs, cs_], [sg])
                                tt_("pool", sg[:], sg[:], t2[:], ALU.add, [sg, t2], [sg])
                            elif j == 20:
                                act(sg[:], pz[:, :], AF.Tanh, [pz], [sg])
                            elif j == 22:
                                act(sg[:], pz[:, :], AF.Sigmoid, [pz], [sg])
                            else:
                                cp("act" if j % 2 == 0 else "dve", sg[:], pz[:, :], [pz], [sg])
                            sl = slice(t0, t0 + 512)
                            if j <= 2:
                                for hh in range(2):
                                    stq(S["QA"][2 * j + hh, :, sl], sg[hh * 64:(hh + 1) * 64, :], [sg])
                            elif j == 3:
                                for hh in range(2):
                                    stq(S["KA"][hh, :, sl], sg[hh * 64:(hh + 1) * 64, :], [sg])
                            elif j in (5, 6):
                                for q4 in range(4):
                                    stq(S["QB"][(j - 5) * 4 + q4, :, sl], sg[q4 * 32:(q4 + 1) * 32, :], [sg])
                            elif j in (7, 8):
                                for q4 in range(4):
                                    stq(S["KB"][(j - 7) * 4 + q4, :, sl], sg[q4 * 32:(q4 + 1) * 32, :], [sg])
                            elif 11 <= j <= 19:
                                for hh in range(2):
                                    stq(S["RKV"][(j - 11) * 2 + hh, :, sl], sg[hh * 64:(hh + 1) * 64, :], [sg])
                            elif j == 20:
                                for hh in range(2):
                                    stq(S["CW"][hh, :, sl], sg[hh * 64:(hh + 1) * 64, :], [sg])
                            elif j == 21:
                                for hh in range(2):
                                    stq(S["CA"][hh, :, sl], sg[hh * 64:(hh + 1) * 64, :], [sg])
                            else:
                                stq(S["CG"][:, sl], sg[:, :], [sg])
                        for i in range(4):
                            vs_ = vst[i % 2]
                            groups = [(512, 128, 0), (1152, 256, 128)]
                            if not latent:
                                groups += [(384, 128, 384), (896, 256, 512)]
                            pv = nps()
                            pk = nps()
                            for (c0, wd, o0) in groups:
                                pp, oo = (pv, o0) if o0 < 384 else (pk, o0 - 384)
                                for k in range(8):
                                    mm(pp[:, oo:oo + wd], h[:, k, i * 128:(i + 1) * 128], win[:, k, c0:c0 + wd], k == 0, k == 7, [h, win], [pp])
                            cp("act", vs_[:, 0:384], pv[:, 0:384], [pv], [vs_])
                            r0 = t0 + i * 128
                            stq(S["VA"][r0:r0 + 128, :], vs_[:, 0:128], [vs_])
                            stq(S["VB"][r0:r0 + 128, :], vs_[:, 128:384], [vs_])
                            if not latent:
                                cp("dve", vs_[:, 384:768], pk[:, 0:384], [pk], [vs_])
                                bl = r0 // TP
                                rr = r0 % TP
                                stq(O["nav"][bl, l, rr:rr + 128, :], vs_[:, 0:128], [vs_])
                                stq(O["nbv"][bl, l, rr:rr + 128, :], vs_[:, 128:384], [vs_])
                                stq(O["nak"][bl, l, rr:rr + 128, :], vs_[:, 384:512], [vs_])
                                stq(O["nbk"][bl, l, rr:rr + 128, :], vs_[:, 512:768], [vs_])
                p.barrier()

            with ExitStack() as st:
                sinkb = bcast_row(st, I["a_sink"][l:l + 1, :], 6, "sinkb")
                act(sinkb[:, 0:6], sinkb[:, 0:6], AF.Exp, [sinkb], [sinkb])
                mprev = sb(st, "mprev", [128, 128]); mnext = sb(st, "mnext", [128, 128])
                ld(mprev[:], I["mprev"][:, :], [mprev]); ld(mnext[:], I["mnext"][:, :], [mnext])
                kT = sb(st, "kT", [64, TS + PAST])
                qT = sb(st, "qT", [64, 3, TS])
                vt = sb(st, "vt", [128, (TS + PAST) // 128, 65])
                p.op("pool", lambda e: e.memset(vt[:, :, 64:65], 1.0), writes=[vt])
                cstg = sb(st, "cstg", [128, 4, 64])
                pT = [sb(st, "pT%d" % i, [128, 384]) for i in range(3)]
                ao = [sb(st, "ao%d" % i, [128, 3, 64]) for i in range(2)]
                den = sb(st, "den", [128, 3])
                pti = [0]
                for (S, o, T, latent, bl) in seqs:
                    nb = T // 128
                    for g in range(2):
                        ld(kT[:, 0:T], S["KA"][g, :, o:o + T], [kT])
                        ld(qT[:, :, 0:T], S["QA"][3 * g:3 * g + 3, :, o:o + T].rearrange("m d t -> d m t"), [qT])
                        ld(vt[:, 0:nb, 0:64], S["VA"][o:o + T, g * 64:(g + 1) * 64].rearrange("(n p) d -> p n d", p=128), [vt])
                        nctx = 0
                        if latent:
                            nctx = PAST // 128
                            ld(cstg[:], I["cak"][l, :, g * 64:(g + 1) * 64].rearrange("(n p) d -> p n d", p=128), [cstg])
                            pt = nps()
                            for n_ in range(4):
                                tr(pt[0:64, n_ * 128:(n_ + 1) * 128], cstg[:, n_, :], ident[:, :], [cstg, ident], [pt])
                            cp("act", kT[:, T:T + PAST], pt[0:64, :], [pt], [kT])
                            ld(vt[:, nb:nb + 4, 0:64], I["cav"][l, :, g * 64:(g + 1) * 64].rearrange("(n p) d -> p n d", p=128), [vt])
                        for b in range(nb):
                            if latent:
                                tiles = [(kb, (mprev if kb == b - 1 else (mnext if kb == b + 1 else None)))
                                         for kb in (b - 1, b, b + 1) if 0 <= kb < nb]
                                tiles += [(nb + c_, None) for c_ in range(nctx)]
                            else:
                                tiles = [(kb, None) for kb in range(nb)]
                            po = ps[b % 2]
                            for ti, (kb, msk) in enumerate(tiles):
                                pss = nps()
                                mm(pss[:, 0:384], kT[:, kb * 128:(kb + 1) * 128], qT[:, :, b * 128:(b + 1) * 128], True, True, [kT, qT], [pss])
                                pt_ = pT[pti[0] % 3]
                                pti[0] += 1
                                act(pt_[:], pss[:, 0:384], AF.Exp, [pss], [pt_], scale=0.125)
                                if msk is not None:
                                    tt_("pool", pt_[:, :].rearrange("p (m q) -> p m q", q=128), pt_[:, :].rearrange("p (m q) -> p m q", q=128),
                                        msk[:, :].unsqueeze(1).to_broadcast([128, 3, 128]), ALU.mult, [pt_, msk], [pt_])
                                for m in range(3):
                                    mm(po[:, m * 65:(m + 1) * 65], pt_[:, m * 128:(m + 1) * 128], vt[:, kb, :], ti == 0 and m == 0, ti == len(tiles) - 1 and m == 2, [pt_, vt], [po])
                            pov = po[:, 0:195].rearrange("p (m d) -> p m d", d=65)
                            tt_("dve", den[:, :].unsqueeze(2), pov[:, :, 64:65], sinkb[:, 3 * g:3 * g + 3].unsqueeze(2), ALU.add, [po, sinkb], [den])
                            p.op("dve", lambda e: e.reciprocal(out=den[:, :], in_=den[:, :]), reads=[den], writes=[den])
                            a_ = ao[b % 2]
                            tt_("dve", a_[:], pov[:, :, 0:64], den[:, :].unsqueeze(2).to_broadcast([128, 3, 64]), ALU.mult, [po, den], [a_])
                            r0 = o + b * 128
                            stq(S["M"][r0:r0 + 128, g * 192:(g + 1) * 192], a_[:].rearrange("p m d -> p (m d)"), [a_])
                p.barrier()

            with ExitStack() as st:
                lamr = bcast_row(st, I["b_lambda"][l:l + 1].rearrange("o a d -> o (a d)"), 128, "lamr")
                lam2 = sb(st, "lam2", [128, 2, 32])
                lv = lamr[:, :].rearrange("p (a b d) -> p a b d", a=2, b=2)
                tt_("dve", lam2[:], lv[:, :, 0, :], lv[:, :, 1, :], ALU.mult, [lamr], [lam2])
                lam1 = sb(st, "lam1", [128, 2])
                p.op("dve", lambda e: e.tensor_reduce(out=lam1[:, :], in_=lam2[:], axis=AX.X, op=ALU.add), reads=[lam2], writes=[lam1])
                act(lam1[:, :], lam1[:, :], AF.Exp, [lam1], [lam1])
                nlam = sb(st, "nlam", [128, 1])
                tt_("dve", nlam[:, :], lam1[:, 1:2], lam1[:, 0:1], ALU.subtract, [lam1], [nlam])
                ts_("dve", nlam[:, :], nlam[:, :], -lam_init, None, ALU.add, ALU.bypass, [nlam], [nlam])
                gsub = bcast_row(st, I["b_subln_g"][l:l + 1, :], 64, "gsub")
                ts_("dve", gsub[:, :], gsub[:, :], 1.0 - lam_init, None, ALU.mult, ALU.bypass, [gsub], [gsub])
                kT = sb(st, "kTb", [32, 2, TS + PAST], BF16)
                qT = sb(st, "qTb", [32, 2, TS], BF16)
                vt = sb(st, "vtb", [128, (TS + PAST) // 128, 65], BF16)
                p.op("pool", lambda e: e.memset(vt[:, :, 64:65], 1.0), writes=[vt])
                stf = sb(st, "stfb", [32, 2, TS])
                vtf = sb(st, "vtfb", [128, (TS + PAST) // 128, 64])
                cstg = sb(st, "cstgb", [128, 4, 32])
                pT = [sb(st, "pTb%d" % i, [128, 512], BF16) for i in range(6)]
                o1 = sb(st, "o1", [128, 4, 64]); o2 = sb(st, "o2", [128, 4, 64]); o3 = sb(st, "o3", [128, 4, 64])
                rd = sb(st, "rd", [128, 2, 4]); ssq = sb(st, "ssq", [128, 4])
                bo = [sb(st, "bo%d" % i, [128, 4, 64]) for i in range(2)]
                pti = [0]
                boi = [0]
                for (S, o, T, latent, bl) in seqs:
                    nk = T // 128 + (PAST // 128 if latent else 0)
                    nown = T // 128
                    qn = min(512, T)
                    for hd in range(4):
                        ld(vtf[:, 0:nown, :], S["VB"][o:o + T, hd * 64:(hd + 1) * 64].rearrange("(n p) d -> p n d", p=128), [vtf])
                        if latent:
                            ld(vtf[:, nown:nk, :], I["cbv"][l, :, hd * 64:(hd + 1) * 64].rearrange("(n p) d -> p n d", p=128), [vtf])
                        cp("pool", vt[:, 0:nk, 0:64], vtf[:, 0:nk, :], [vtf], [vt])
                        ld(stf[:, :, 0:T], S["KB"][2 * hd:2 * hd + 2, :, o:o + T].rearrange("m d t -> d m t"), [stf])
                        cp("pool", kT[:, :, 0:T], stf[:, :, 0:T], [stf], [kT])
                        ld(stf[:, :, 0:T], S["QB"][2 * hd:2 * hd + 2, :, o:o + T].rearrange("m d t -> d m t"), [stf])
                        cp("pool", qT[:, :, 0:T], stf[:, :, 0:T], [stf], [qT])
                        if latent:
                            for mp in range(2):
                                c0 = hd * 64 + mp * 32
                                ld(cstg[:], I["cbk"][l, :, c0:c0 + 32].rearrange("(n p) d -> p n d", p=128), [cstg])
                                pt = nps()
                                for n_ in range(4):
                                    tr(pt[0:32, n_ * 128:(n_ + 1) * 128], cstg[:, n_, :], ident[:, :], [cstg, ident], [pt])
                                cp("act", kT[:, mp, T:T + PAST], pt[0:32, :], [pt], [kT])
                        for q0 in range(0, T, qn):
                            nqb = qn // 128
                            pos = [ps[0], ps[1]]
                            for mp in range(2):
                                for kt in range(nk):
                                    pss = nps()
                                    mm(pss[:, 0:qn], kT[:, mp, kt * 128:(kt + 1) * 128], qT[:, mp, q0:q0 + qn], True, True, [kT, qT], [pss])
                                    pt_ = pT[pti[0] % 6]
                                    pti[0] += 1
                                    act(pt_[:, 0:qn], pss[:, 0:qn], AF.Exp, [pss], [pt_], scale=32.0 ** -0.5)
                                    for qb in range(nqb):
                                        mm(pos[mp][:, qb * 65:(qb + 1) * 65], pt_[:, qb * 128:(qb + 1) * 128], vt[:, kt, :], kt == 0 and qb == 0, kt == nk - 1 and qb == nqb - 1, [pt_, vt], [pos[mp]])
                            v1 = pos[0][:, 0:nqb * 65].rearrange("p (q d) -> p q d", d=65)
                            v2 = pos[1][:, 0:nqb * 65].rearrange("p (q d) -> p q d", d=65)
                            p.op("dve", lambda e: e.reciprocal(out=rd[:, 0, 0:nqb].unsqueeze(2), in_=v1[:, :, 64:65]), reads=[pos[0]], writes=[rd])
                            p.op("dve", lambda e: e.reciprocal(out=rd[:, 1, 0:nqb].unsqueeze(2), in_=v2[:, :, 64:65]), reads=[pos[1]], writes=[rd])
                            tt_("dve", o1[:, 0:nqb, :], v1[:, :, 0:64], rd[:, 0, 0:nqb].unsqueeze(2).to_broadcast([128, nqb, 64]), ALU.mult, [pos[0], rd], [o1])
                            tt_("dve", o2[:, 0:nqb, :], v2[:, :, 0:64], rd[:, 1, 0:nqb].unsqueeze(2).to_broadcast([128, nqb, 64]), ALU.mult, [pos[1], rd], [o2])
                            stt(o1[:, 0:nqb, :], o2[:, 0:nqb, :], nlam[:, 0:1], o1[:, 0:nqb, :], ALU.mult, ALU.add, [o1, o2, nlam], [o1])
                            tt_("pool", o3[:, 0:nqb, :], o1[:, 0:nqb, :], o1[:, 0:nqb, :], ALU.mult, [o1], [o3])
                            p.op("dve", lambda e: e.tensor_reduce(out=ssq[:, 0:nqb], in_=o3[:, 0:nqb, :], axis=AX.X, op=ALU.add), reads=[o3], writes=[ssq])
                            rsqrt_(ssq[:, 0:nqb], ssq[:, 0:nqb], 1.0 / 64, epsc[:, 0:1], [ssq, epsc], [ssq], None)
                            b_ = bo[boi[0] % 2]
                            boi[0] += 1
                            tt_("dve", b_[:, 0:nqb, :], o1[:, 0:nqb, :], ssq[:, 0:nqb].unsqueeze(2).to_broadcast([128, nqb, 64]), ALU.mult, [o1, ssq], [b_])
                            tt_("pool", b_[:, 0:nqb, :], b_[:, 0:nqb, :], gsub[:, 0:64].unsqueeze(1).to_broadcast([128, nqb, 64]), ALU.mult, [b_, gsub], [b_])
                            r0 = o + q0
                            stq(S["M"][r0:r0 + qn, 384 + hd * 64:384 + (hd + 1) * 64].rearrange("(q p) d -> p q d", p=128), b_[:, 0:nqb, :], [b_])
                p.barrier()

            with ExitStack() as st:
                CT = 512
                rx = [sb(st, "rx%d" % i, [64, 18, CT + 2]) for i in range(2)]
                ro = [sb(st, "ro%d" % i, [64, 18, CT]) for i in range(2)]
                ti_ = 0
                for (S, o, T, latent, bl) in seqs:
                    for ts0 in range(0, T, CT):
                        n = min(CT, T - ts0)
                        x_, o_ = rx[ti_ % 2], ro[ti_ % 2]
                        ti_ += 1
                        lo = max(ts0 - 1, 0)
                        hi = min(ts0 + n + 1, T)
                        if ts0 == 0:
                            p.op("pool", lambda e: e.memset(x_[:, :, 0:1], 0.0), writes=[x_])
                        if ts0 + n == T:
                            p.op("pool", lambda e: e.memset(x_[:, :, n + 1:n + 2], 0.0), writes=[x_])
                        ld(x_[:, :, lo - (ts0 - 1):hi - (ts0 - 1)], S["RKV"][:, :, o + lo:o + hi].rearrange("c d t -> d c t"), [x_])
                        for c in range(18):
                            cb = (l * 3) * 18 + c
                            act(o_[:, c, 0:n], x_[:, c, 1:n + 1], AF.Copy, [x_, convc], [o_], scale=convc[:, cb + 18:cb + 19])
                            stt(o_[:, c, 0:n], x_[:, c, 0:n], convc[:, cb:cb + 1], o_[:, c, 0:n], ALU.mult, ALU.add, [x_, convc, o_], [o_])
                            stt(o_[:, c, 0:n], x_[:, c, 2:n + 2], convc[:, cb + 36:cb + 37], o_[:, c, 0:n], ALU.mult, ALU.add, [x_, convc, o_], [o_])
                        stq(S["RKVC"][:, :, o + ts0:o + ts0 + n].rearrange("c d t -> d c t"), o_[:, :, 0:n], [o_])
                p.barrier()

            with ExitStack() as st:
                cw2 = sb(st, "cw2", [64, 2, 384]); ca2 = sb(st, "ca2", [64, 2, 384])
                ld(cw2[:], I["c_w2"][l].rearrange("d r c -> r d c"), [cw2])
                ld(ca2[:], I["c_a2"][l].rearrange("d r c -> r d c"), [ca2])
                MK = {}
                for nm in ("m_lt", "m_le", "m_gt", "m_ge", "n_lt", "n_le", "n_gt", "n_ge", "i6"):
                    MK[nm] = sb(st, nm, [64, 384])
                    ld(MK[nm][:], I[nm][:, :], [MK[nm]])
                rstart = sb(st, "rstart", [64, 768])
                ld(rstart[:], I["rstart"][:, 0:768], [rstart])
                NSG = 128
                onesb = sb(st, "onesb", [64, 2], BF16)
                p.op("pool", lambda e: e.memset(onesb[:], 1.0), writes=[onesb])
                KKB = sb(st, "KKB", [64, 6, NSG]); KAB = sb(st, "KAB", [64, 6, NSG]); RKB = sb(st, "RKB", [64, 6, NSG])
                for (dst_, src_) in ((KKB, kkc), (KAB, kac), (RKB, rkc_)):
                    cp("pool", dst_[:], src_[:, l * 6:(l + 1) * 6].unsqueeze(2).to_broadcast([64, 6, NSG]), [src_], [dst_])
                sig = sb(st, "sig", [64, 6, NSG]); a_ = sb(st, "a_", [64, 6, NSG])
                kk = sb(st, "kk", [64, 6, NSG]); kd = sb(st, "kd", [64, 6, NSG]); bb = sb(st, "bb", [64, 6, NSG])
                Lc = sb(st, "Lc", [64, 6, NSG]); Lx = sb(st, "Lx", [64, 6, NSG]); Ld = sb(st, "Ld", [64, 6, NSG])
                E = sb(st, "E", [64, 6, NSG]); tmp = sb(st, "tmpc", [64, 6, NSG])
                hv = lambda t, h_: t[:, h_ * 64:(h_ + 1) * 64]
                v3 = lambda t: t[:, :].rearrange("p (h i) -> p h i", i=64)
                CB = []
                for ci_ in range(2):
                    Bd = {}
                    for nm in ("rkv",):
                        Bd[nm] = sb(st, "%s%d" % (nm, ci_), [64, 18, NSG])
                    for nm in ("cwt", "cat"):
                        Bd[nm] = sb(st, "%s%d" % (nm, ci_), [64, NSG])
                    for nm in ("rkc", "Kt_", "Rt_", "Dh", "Bh"):
                        Bd[nm] = sb(st, "%s%d" % (nm, ci_), [64, 6, NSG], BF16)
                    for nm in ("Dg", "nBg"):
                        Bd[nm] = sb(st, "%s%d" % (nm, ci_), [64, 6, NSG])
                    Bd["Hb"] = sb(st, "Hb%d" % ci_, [64, 6, 64], BF16)
                    Bd["gC"] = sb(st, "gC%d" % ci_, [64, 6, NSG // 64])
                    Bd["H"] = sb(st, "H%d" % ci_, [64, 6, 64]); Bd["hst"] = sb(st, "hst%d" % ci_, [64, 6, 64])
                    for nm in ("Vt", "Dgt", "nBgt", "Pa", "Pb", "PTa", "PTb", "Xa", "Xb", "AdT", "BdT", "nBbT", "Wb", "Zb"):
                        Bd[nm] = sb(st, "%s%d" % (nm, ci_), [64, 384], BF16)
                    for nm in ("yb", "y2", "Vf"):
                        Bd[nm] = sb(st, "%s%d" % (nm, ci_), [64, 384])
                    Bd["coef"] = sb(st, "coef%d" % ci_, [64, 6])
                    CB.append(Bd)

                def chain(S, o, T, latent, bl, d, Bd):
                    rkv, cwt, cat, rkc = Bd["rkv"], Bd["cwt"], Bd["cat"], Bd["rkc"]
                    Kt_, Rt_, Dh, Bh, Dg, nBg, gC, H, hst = (Bd[k] for k in ("Kt_", "Rt_", "Dh", "Bh", "Dg", "nBg", "gC", "H", "hst"))
                    Hb, Vf = Bd["Hb"], Bd["Vf"]
                    Vt, Dgt, nBgt, AdT, BdT, nBbT, Wb, Zb, yb, y2, coef = (Bd[k] for k in ("Vt", "Dgt", "nBgt", "AdT", "BdT", "nBbT", "Wb", "Zb", "yb", "y2", "coef"))
                    Pq = [Bd["Pa"], Bd["Pb"]]; PTq = [Bd["PTa"], Bd["PTb"]]; Xq = [Bd["Xa"], Bd["Xb"]]
                    nsg = (T + NSG - 1) // NSG
                    YD = S["YF" if d == 0 else "YB"]
                    if latent:
                        src = I["scf" if d == 0 else "scb"][l]
                        ld(hst[:], src.rearrange("h i j -> i h j"), [hst])
                        pt = nps()
                        for h_ in range(6):
                            tr(pt[0:64, h_ * 64:(h_ + 1) * 64], hst[:, h_, :], ident[0:64, 0:64], [hst, ident], [pt])
                        cp("act", H[:].rearrange("p h i -> p (h i)"), pt[0:64, 0:384], [pt], [H])
                    else:
                        p.op("pool", lambda e: e.memset(H[:], 0.0), writes=[H])
                    cp("pool", Hb[:], H[:], [H], [Hb])
                    segs = list(range(nsg)) if d == 0 else list(range(nsg - 1, -1, -1))
                    for sg_ in segs:
                        ts0 = sg_ * NSG
                        n = min(NSG, T - ts0)
                        nch = n // 64
                        assert n == NSG
                        ld(rkv[:, :, 0:n], S["RKVC"][:, :, o + ts0:o + ts0 + n].rearrange("c d t -> d c t"), [rkv])
                        ld(cwt[:, 0:n], S["CW"][d, :, o + ts0:o + ts0 + n], [cwt])
                        ld(cat[:, 0:n], S["CA"][d, :, o + ts0:o + ts0 + n], [cat])
                        for h_ in range(6):
                            ci = (l * 2 + d) * 6 + h_
                            pw = nps()
                            mm(pw[0:64, 0:n], cw2[:, d, h_ * 64:(h_ + 1) * 64], cwt[:, 0:n], True, True, [cw2, cwt], [pw])
                            act(sig[:, h_, 0:n], pw[0:64, 0:n], AF.Sigmoid, [pw, w0c], [sig], bias=w0c[:, ci:ci + 1], scale=1.0)
                            pa = nps()
                            mm(pa[0:64, 0:n], ca2[:, d, h_ * 64:(h_ + 1) * 64], cat[:, 0:n], True, True, [ca2, cat], [pa])
                            act(a_[:, h_, 0:n], pa[0:64, 0:n], AF.Sigmoid, [pa, a0c], [a_], bias=a0c[:, ci:ci + 1], scale=1.0)
                        fl = lambda t: t[:].rearrange("p h t -> p (h t)")
                        tt_("dve", kk[:], rkv[:, 6:12, :], KKB[:], ALU.mult, [rkv, KKB], [kk])
                        tt_("pool", tmp[:], kk[:], kk[:], ALU.mult, [kk], [tmp])
                        for hf in range(2):
                            pk_ = nps()
                            mm(pk_[0:64, 0:384], ones[0:64, 0:64], fl(tmp)[:, hf * 384:(hf + 1) * 384], True, True, [ones, tmp], [pk_])
                            act(fl(Ld)[:, hf * 384:(hf + 1) * 384], pk_[0:64, 0:384], AF.Sqrt, [pk_], [Ld])
                        ts_("pool", Ld[:], Ld[:], 1e-12, None, ALU.max, ALU.bypass, [Ld], [Ld])
                        p.op("dve", lambda e: e.reciprocal(out=Ld[:], in_=Ld[:]), reads=[Ld], writes=[Ld])
                        tt_("dve", kk[:], kk[:], Ld[:], ALU.mult, [kk, Ld], [kk])
                        stt(kd[:], a_[:], -1.0, KAB[:], ALU.add, ALU.mult, [a_, KAB], [kd])
                        stt(kd[:], kd[:], 1.0, rkv[:, 6:12, :], ALU.add, ALU.mult, [kd, rkv], [kd])
                        tt_("pool", tmp[:], rkv[:, 0:6, :], RKB[:], ALU.mult, [rkv, RKB], [tmp])
                        tt_("pool", rkc[:], tmp[:], kd[:], ALU.mult, [tmp, kd], [rkc])
                        tt_("pool", bb[:, :, 0:n], kk[:, :, 0:n], a_[:, :, 0:n], ALU.mult, [kk, a_], [bb])
                        if n == NSG:
                            p.op("dve", lambda e: e.tensor_tensor_scan(out=Lc[:].rearrange("p h t -> p (h t)"), data0=rstart[:, :],
                                                                        data1=sig[:].rearrange("p h t -> p (h t)"), initial=0.0, op0=ALU.mult, op1=ALU.add),
                                 reads=[rstart, sig], writes=[Lc])
                        else:
                            for h_ in range(6):
                                p.op("dve", lambda e: e.tensor_tensor_scan(out=Lc[:, h_, 0:n], data0=rstart[:, 0:n], data1=sig[:, h_, 0:n],
                                                                            initial=0.0, op0=ALU.mult, op1=ALU.add), reads=[rstart, sig], writes=[Lc])
                        tt_("pool", Lx[:, :, 0:n], Lc[:, :, 0:n], sig[:, :, 0:n], ALU.subtract, [Lc, sig], [Lx])
                        c4 = lambda t: t[:, :, 0:n].rearrange("p h (c s) -> p h c s", s=64)
                        Ltot = c4(Lc)[:, :, :, 63:64]
                        tt_("dve", c4(Ld), Ltot.to_broadcast([64, 6, nch, 64]), c4(Lc), ALU.subtract, [Lc], [Ld])
                        act(gC[:, :, 0:nch].unsqueeze(3), Ltot, AF.Exp, [Lc], [gC], scale=-DEC)
                        if d == 0:
                            act(E[:, :, 0:n], Lx[:, :, 0:n], AF.Exp, [Lx], [E], scale=-DEC)
                            tt_("dve", Kt_[:, :, 0:n], kk[:, :, 0:n], E[:, :, 0:n], ALU.mult, [kk, E], [Kt_])
                            act(E[:, :, 0:n], Lc[:, :, 0:n], AF.Exp, [Lc], [E], scale=-DEC)
                            tt_("dve", Rt_[:, :, 0:n], rkv[:, 0:6, 0:n], E[:, :, 0:n], ALU.mult, [rkv, E], [Rt_])
                            act(E[:, :, 0:n], Lc[:, :, 0:n], AF.Exp, [Lc], [E], scale=DEC)
                        else:
                            act(E[:, :, 0:n], Ld[:, :, 0:n], AF.Exp, [Ld], [E], scale=-DEC)
                            tt_("dve", Kt_[:, :, 0:n], kk[:, :, 0:n], E[:, :, 0:n], ALU.mult, [kk, E], [Kt_])
                            tt_("dve", c4(tmp), Ltot.to_broadcast([64, 6, nch, 64]), c4(Lx), ALU.subtract, [Lc, Lx], [tmp])
                            act(E[:, :, 0:n], tmp[:, :, 0:n], AF.Exp, [tmp], [E], scale=-DEC)
                            tt_("dve", Rt_[:, :, 0:n], rkv[:, 0:6, 0:n], E[:, :, 0:n], ALU.mult, [rkv, E], [Rt_])
                            act(E[:, :, 0:n], tmp[:, :, 0:n], AF.Exp, [tmp], [E], scale=DEC)
                        tt_("dve", Dh[:, :, 0:n], kd[:, :, 0:n], E[:, :, 0:n], ALU.mult, [kd, E], [Dh])
                        tt_("pool", Bh[:, :, 0:n], bb[:, :, 0:n], E[:, :, 0:n], ALU.mult, [bb, E], [Bh])
                        act(E[:, :, 0:n], (Ld if d == 0 else Lx)[:, :, 0:n], AF.Exp, [Ld, Lx], [E], scale=-DEC)
                        tt_("dve", Dg[:, :, 0:n], kd[:, :, 0:n], E[:, :, 0:n], ALU.mult, [kd, E], [Dg])
                        stt(nBg[:, :, 0:n], bb[:, :, 0:n], -1.0, E[:, :, 0:n], ALU.mult, ALU.mult, [bb, E], [nBg])
                        if d == 0:
                            nmAT, mAT, nmA, mBT, nmBT = MK["n_lt"], MK["m_lt"], MK["n_gt"], MK["m_le"], MK["n_le"]
                        else:
                            nmAT, mAT, nmA, mBT, nmBT = MK["n_gt"], MK["m_gt"], MK["n_lt"], MK["m_ge"], MK["n_ge"]
                        yield
                        chunks = list(range(nch)) if d == 0 else list(range(nch - 1, -1, -1))
                        for c in chunks:
                            cs = slice(c * 64, (c + 1) * 64)
                            for (src_t, srcidx, dst_t) in ((rkv, 12, Vt), (Dg, 0, Dgt), (nBg, 0, nBgt)):
                                pt = nps()
                                for h_ in range(6):
                                    tr(pt[0:64, h_ * 64:(h_ + 1) * 64], src_t[:, srcidx + h_, cs], ident[0:64, 0:64], [src_t, ident], [pt])
                                cp("act", dst_t[:, :], pt[0:64, 0:384], [pt], [dst_t])
                                if dst_t is Vt:
                                    cp("dve", Vf[:, :], pt[0:64, 0:384], [pt], [Vf])
                            pAbT, pBbT, pAdT, pBdT, pAb = nps(), nps(), nps(), nps(), nps()
                            for h_ in range(6):
                                hs = slice(h_ * 64, (h_ + 1) * 64)
                                mm(pAbT[0:64, hs], Bh[:, h_, cs], Kt_[:, h_, cs], True, True, [Bh, Kt_], [pAbT])
                                mm(pBbT[0:64, hs], Bh[:, h_, cs], Rt_[:, h_, cs], True, True, [Bh, Rt_], [pBbT])
                                mm(pAdT[0:64, hs], Dh[:, h_, cs], Kt_[:, h_, cs], True, True, [Dh, Kt_], [pAdT])
                                mm(pBdT[0:64, hs], Dh[:, h_, cs], Rt_[:, h_, cs], True, True, [Dh, Rt_], [pBdT])
                                mm(pAb[0:64, hs], Kt_[:, h_, cs], Bh[:, h_, cs], True, True, [Kt_, Bh], [pAb])
                            tt_("dve", PTq[0][:, :], pAbT[0:64, 0:384], nmAT[:, :], ALU.mult, [pAbT, nmAT], [PTq[0]])
                            tt_("dve", Pq[0][:, :], pAb[0:64, 0:384], nmA[:, :], ALU.mult, [pAb, nmA], [Pq[0]])
                            tt_("pool", Xq[0][:, :], PTq[0][:, :], MK["i6"][:, :], ALU.add, [PTq[0], MK["i6"]], [Xq[0]])
                            tt_("dve", AdT[:, :], pAdT[0:64, 0:384], mAT[:, :], ALU.mult, [pAdT, mAT], [AdT])
                            tt_("dve", BdT[:, :], pBdT[0:64, 0:384], mBT[:, :], ALU.mult, [pBdT, mBT], [BdT])
                            tt_("dve", nBbT[:, :], pBbT[0:64, 0:384], nmBT[:, :], ALU.mult, [pBbT, nmBT], [nBbT])
                            yield
                            for k in range(1, 6):
                                Pc, PTc = Pq[(k - 1) % 2], PTq[(k - 1) % 2]
                                Pn, PTn = Pq[k % 2], PTq[k % 2]
                                pP = nps()
                                for h_ in range(6):
                                    hs = slice(h_ * 64, (h_ + 1) * 64)
                                    mm(pP[0:64, hs], hv(PTc, h_), hv(Pc, h_), True, True, [PTc, Pc], [pP])
                                if k < 5:
                                    pPT = nps()
                                    for h_ in range(6):
                                        hs = slice(h_ * 64, (h_ + 1) * 64)
                                        mm(pPT[0:64, hs], hv(Pc, h_), hv(PTc, h_), True, True, [PTc, Pc], [pPT])
                                if k >= 2:
                                    Xo, Xn = Xq[k % 2], Xq[(k - 1) % 2]
                                    pX = nps()
                                    for h_ in range(6):
                                        hs = slice(h_ * 64, (h_ + 1) * 64)
                                        mm(pX[0:64, hs], hv(Pc, h_), hv(Xo, h_), True, True, [Pc, Xo], [pX])
                                    tt_("dve", Xn[:, :], pX[0:64, 0:384], Xo[:, :], ALU.add, [pX, Xo], [Xn])
                                cp("act", Pn[:, :], pP[0:64, 0:384], [pP], [Pn])
                                if k < 5:
                                    cp("dve", PTn[:, :], pPT[0:64, 0:384], [pPT], [PTn])
                                yield
                            P5, X4, X5 = Pq[1], Xq[0], Xq[1]
                            pX = nps()
                            for h_ in range(6):
                                hs = slice(h_ * 64, (h_ + 1) * 64)
                                mm(pX[0:64, hs], hv(P5, h_), hv(X4, h_), True, True, [P5, X4], [pX])
                            pW = nps()
                            for h_ in range(6):
                                hs = slice(h_ * 64, (h_ + 1) * 64)
                                mm(pW[0:64, hs], Kt_[:, h_, cs], Hb[:, h_, :], True, False, [Kt_, Hb], [pW])
                                mm(pW[0:64, hs], hv(AdT, h_), hv(Vt, h_), False, True, [AdT, Vt], [pW])
                            tt_("dve", X5[:, :], pX[0:64, 0:384], X4[:, :], ALU.add, [pX, X4], [X5])
                            cp("act", Wb[:, :], pW[0:64, 0:384], [pW], [Wb])
                            yield
                            pZ = nps()
                            for h_ in range(6):
                                hs = slice(h_ * 64, (h_ + 1) * 64)
                                mm(pZ[0:64, hs], hv(X5, h_), hv(Wb, h_), True, True, [X5, Wb], [pZ])
                            cp("act", Zb[:, :], pZ[0:64, 0:384], [pZ], [Zb])
                            yield
                            Hf = H[:].rearrange("p h i -> p (h i)")
                            pY, pC, pH = nps(), nps(), nps()
                            for h_ in range(6):
                                hs = slice(h_ * 64, (h_ + 1) * 64)
                                mm(pY[0:64, hs], Rt_[:, h_, cs], Hb[:, h_, :], True, False, [Rt_, Hb], [pY])
                                mm(pY[0:64, hs], hv(BdT, h_), hv(Vt, h_), False, False, [BdT, Vt], [pY])
                                mm(pY[0:64, hs], hv(nBbT, h_), hv(Zb, h_), False, True, [nBbT, Zb], [pY])
                                mm(pC[0:64, h_:h_ + 1], rkc[:, h_, cs], onesb[:, 0:1], True, True, [rkc, onesb], [pC])
                                mm(pH[0:64, hs], hv(Dgt, h_), hv(Vt, h_), True, False, [Dgt, Vt], [pH])
                                mm(pH[0:64, hs], hv(nBgt, h_), hv(Zb, h_), False, True, [nBgt, Zb], [pH])
                            cp("act", coef[:, :], pC[0:64, 0:6], [pC], [coef])
                            tt_("pool", v3(y2), v3(Vf), coef[:, :].unsqueeze(2).to_broadcast([64, 6, 64]), ALU.mult, [Vf, coef], [y2])
                            tt_("dve", yb[:, :], pY[0:64, 0:384], y2[:, :], ALU.add, [pY, y2], [yb])
                            tt_("pool", H[:], H[:], gC[:, :, c:c + 1].to_broadcast([64, 6, 64]), ALU.mult, [H, gC], [H])
                            tt_("dve", Hf, Hf, pH[0:64, 0:384], ALU.add, [H, pH], [H])
                            cp("pool", Hb[:], H[:], [H], [Hb])
                            r0 = o + ts0 + c * 64
                            stq(YD[r0:r0 + 64, :], yb[:, :], [yb])
                            yield
                    if not latent:
                        pt = nps()
                        for h_ in range(6):
                            tr(pt[0:64, h_ * 64:(h_ + 1) * 64], H[:, h_, :], ident[0:64, 0:64], [H, ident], [pt])
                        cp("act", hst[:].rearrange("p h j -> p (h j)"), pt[0:64, 0:384], [pt], [hst])
                        stq(O["ncf" if d == 0 else "ncb"][bl, l].rearrange("h i j -> i h j"), hst[:], [hst])

                for (S, o, T, latent, bl) in seqs:
                    alive = [chain(S, o, T, latent, bl, 0, CB[0]), chain(S, o, T, latent, bl, 1, CB[1])]
                    while alive:
                        for g_ in list(alive):
                            try:
                                next(g_)
                            except StopIteration:
                                alive.remove(g_)
                p.barrier()

            with ExitStack() as st:
                cg2 = sb(st, "cg2", [128, 384])
                ld(cg2[:], I["c_g2"][l], [cg2])
                lnxg = bcast_row(st, I["c_lnx_g"][l:l + 1, :], 384, "lnxg")
                lnxb = bcast_row(st, I["c_lnx_b"][l:l + 1, :], 384, "lnxb")
                yfb = [sb(st, "yfb%d" % i, [128, 384]) for i in range(2)]
                ybb = [sb(st, "ybb%d" % i, [128, 384]) for i in range(2)]
                y2b = [sb(st, "y2b%d" % i, [128, 384]) for i in range(2)]
                cgb = [sb(st, "cgb%d" % i, [128, 128]) for i in range(2)]
                gsa = sb(st, "gsa", [128, 6]); gsb = sb(st, "gsb", [128, 6])
                w3 = lambda t: t[:, :].rearrange("p (h i) -> p h i", i=64)
                ti_ = 0
                for S in streams:
                    for r0 in range(0, S["T"], 128):
                        yf_, yb_, y2_, cg_ = yfb[ti_ % 2], ybb[ti_ % 2], y2b[ti_ % 2], cgb[ti_ % 2]
                        ti_ += 1
                        ld(yf_[:], S["YF"][r0:r0 + 128, :], [yf_])
                        ld(yb_[:], S["YB"][r0:r0 + 128, :], [yb_])
                        ld(cg_[:], S["CG"][:, r0:r0 + 128], [cg_])
                        tt_("pool", yb_[:, :], yb_[:, :], yf_[:, :], ALU.add, [yb_, yf_], [yb_])
                        p.op("dve", lambda e: e.tensor_reduce(out=gsa[:, :], in_=w3(yb_), axis=AX.X, op=ALU.add), reads=[yb_], writes=[gsa])
                        ts_("dve", gsa[:, :], gsa[:, :], -1.0 / 64, None, ALU.mult, ALU.bypass, [gsa], [gsa])
                        tt_("dve", w3(yb_), w3(yb_), gsa[:, :].unsqueeze(2).to_broadcast([128, 6, 64]), ALU.add, [yb_, gsa], [yb_])
                        tt_("pool", y2_[:, :], yb_[:, :], yb_[:, :], ALU.mult, [yb_], [y2_])
                        p.op("dve", lambda e: e.tensor_reduce(out=gsb[:, :], in_=w3(y2_), axis=AX.X, op=ALU.add), reads=[y2_], writes=[gsb])
                        rsqrt_(gsb[:, :], gsb[:, :], 1.0 / 64, epsc[:, 1:2], [gsb, epsc], [gsb], None)
                        tt_("dve", w3(yb_), w3(yb_), gsb[:, :].unsqueeze(2).to_broadcast([128, 6, 64]), ALU.mult, [yb_, gsb], [yb_])
                        tt_("pool", yb_[:, :], yb_[:, :], lnxg[:, :], ALU.mult, [yb_, lnxg], [yb_])
                        tt_("pool", yb_[:, :], yb_[:, :], lnxb[:, :], ALU.add, [yb_, lnxb], [yb_])
                        pg = nps()
                        mm(pg[:, 0:384], cg_[:, :], cg2[:, :], True, True, [cg_, cg2], [pg])
                        tt_("dve", y2_[:, :], yb_[:, :], pg[:, 0:384], ALU.mult, [yb_, pg], [y2_])
                        stq(S["M"][r0:r0 + 128, 640:1024], y2_[:, :], [y2_])
                p.barrier()

            with ExitStack() as st:
                wout = sb(st, "wout", [128, 8, D])
                wv = I["w_out"][l].rearrange("(c p) f -> p c f", p=128)
                for k in range(8):
                    ld(wout[:, k, :], wv[:, k, :], [wout])
                xt = sb(st, "xt2", [128, 8, 512])
                mT = sb(st, "mT", [128, 8, 512])
                sq = sb(st, "sq2", [128, 8, 512])
                rstd = sb(st, "rstd2", [128, 512])
                mtok = [sb(st, "mtok%d" % i, [128, D]) for i in range(2)]
                actb = sb(st, "actb", [128, 22, 512])
                w13 = [sb(st, "w13_%d" % i, [128, 8, 2, 256]) for i in range(2)]
                w2b = [sb(st, "w2b_%d" % i, [128, 22, 128]) for i in range(2)]
                sgl = sb(st, "sgl", [128, 512])
                w1v = I["ffn_w1"][l].rearrange("(c p) f -> p c f", p=128)
                w3v = I["ffn_w3"][l].rearrange("(c p) f -> p c f", p=128)
                w2v = I["ffn_w2"][l].rearrange("(c p) f -> p c f", p=128)
                wi = [0]
                for S in streams:
                    latent = S is SS_
                    cv = 1 if latent else 0
                    T = S["T"]
                    XTv = S["XT"].rearrange("(c p) t -> p c t", p=128)
                    for t0 in range(0, T, 512):
                        ld(xt[:], XTv[:, :, t0:t0 + 512], [xt])
                        for i in range(4):
                            mk = mtok[i % 2]
                            ld(mk[:], S["M"][t0 + i * 128:t0 + (i + 1) * 128, :], [mk])
                            for half in range(2):
                                pt = nps()
                                for c in range(4):
                                    cc = half * 4 + c
                                    tr(pt[:, c * 128:(c + 1) * 128], mk[:, cc * 128:(cc + 1) * 128], ident[:, :], [mk, ident], [pt])
                                cp("act" if half == 0 else "dve", mT[:, half * 4:(half + 1) * 4, i * 128:(i + 1) * 128],
                                   pt[:, :].rearrange("p (c t) -> p c t", t=128), [pt], [mT])
                        for dc in range(8):
                            po = nps()
                            for k in range(8):
                                mm(po[:, :], wout[:, k, dc * 128:(dc + 1) * 128], mT[:, k, :], k == 0, k == 7, [wout, mT], [po])
                            stt(xt[:, dc, :], po[:, :], MOD(l, 2, dc, cv), xt[:, dc, :], ALU.mult, ALU.add, [po, mod, xt], [xt])
                        act(sq[:], xt[:], AF.Square, [xt], [sq])
                        pss = nps()
                        for c in range(8):
                            mm(pss[:, :], ones[:, 0:128], sq[:, c, :], c == 0, c == 7, [ones, sq], [pss])
                        rsqrt_(rstd[:], pss[:, :], 1.0 / D, epsc[:, 0:1], [pss, epsc], [rstd], None)
                        hh = mT
                        for c in range(8):
                            stt(hh[:, c, :], xt[:, c, :], modA2[:, l * 8 + c, cv:cv + 1], rstd[:], ALU.mult, ALU.mult, [xt, modA2, rstd], [hh])
                            act(hh[:, c, :], hh[:, c, :], AF.Identity, [hh, mod], [hh], bias=MOD(l, 3, c, cv), scale=1.0)
                        for fp in range(11):
                            wt = w13[wi[0] % 2]
                            wi[0] += 1
                            ld(wt[:, :, 0, :], w1v[:, :, fp * 256:(fp + 1) * 256], [wt])
                            ld(wt[:, :, 1, :], w3v[:, :, fp * 256:(fp + 1) * 256], [wt])
                            for f2 in range(2):
                                fc = fp * 2 + f2
                                p1, p3 = nps(), nps()
                                for k in range(8):
                                    mm(p1[:, :], wt[:, k, 0, f2 * 128:(f2 + 1) * 128], hh[:, k, :], k == 0, k == 7, [wt, hh], [p1])
                                for k in range(8):
                                    mm(p3[:, :], wt[:, k, 1, f2 * 128:(f2 + 1) * 128], hh[:, k, :], k == 0, k == 7, [wt, hh], [p3])
                                act(sgl[:], p1[:, :], AF.Silu, [p1], [sgl])
                                tt_("dve", actb[:, fc, :], sgl[:], p3[:, :], ALU.mult, [sgl, p3], [actb])
                        for dc in range(8):
                            w2t = w2b[dc % 2]
                            ld(w2t[:], w2v[:, :, dc * 128:(dc + 1) * 128], [w2t])
                            po = nps()
                            for fc in range(22):
                                mm(po[:, :], w2t[:, fc, :], actb[:, fc, :], fc == 0, fc == 21, [w2t, actb], [po])
                            stt(xt[:, dc, :], po[:, :], MOD(l, 5, dc, cv), xt[:, dc, :], ALU.mult, ALU.add, [po, mod, xt], [xt])
                        stq(XTv[:, :, t0:t0 + 512], xt[:], [xt])
                p.barrier()

        with ExitStack() as st:
            xt = sb(st, "xtf", [128, 8, 512])
            sq = sb(st, "sqf", [128, 8, 512])
            rstd = sb(st, "rstdf", [128, 512])
            yo = [sb(st, "yo%d" % i, [128, D]) for i in range(2)]
            for S, dst in ((SP_, O["yp"]), (SS_, O["ys"])):
                if S is SS_ and not do_sample:
                    continue
                T = S["T"]
                XTv = S["XT"].rearrange("(c p) t -> p c t", p=128)
                for t0 in range(0, T, 512):
                    ld(xt[:], XTv[:, :, t0:t0 + 512], [xt])
                    act(sq[:], xt[:], AF.Square, [xt], [sq])
                    pss = nps()
                    for c in range(8):
                        mm(pss[:, :], ones[:, 0:128], sq[:, c, :], c == 0, c == 7, [ones, sq], [pss])
                    rsqrt_(rstd[:], pss[:, :], 1.0 / D, epsc[:, 0:1], [pss, epsc], [rstd], None)
                    for c in range(8):
                        stt(sq[:, c, :], xt[:, c, :], gfc[:, c:c + 1], rstd[:], ALU.mult, ALU.mult, [xt, gfc, rstd], [sq])
                    for i in range(4):
                        y_ = yo[i % 2]
                        for half in range(2):
                            pt = nps()
                            for c in range(4):
                                cc = half * 4 + c
                                tr(pt[:, c * 128:(c + 1) * 128], sq[:, cc, i * 128:(i + 1) * 128], ident[:, :], [sq, ident], [pt])
                            cp("act" if half == 0 else "dve", y_[:, half * 512:(half + 1) * 512], pt[:, :], [pt], [y_])
                        stq(dst[t0 + i * 128:t0 + (i + 1) * 128, :], y_[:], [y_])
            p.barrier()
    return nc


def make_in_maps(inputs):
    f = lambda a: np.ascontiguousarray(np.asarray(a, dtype=np.float32))
    consts = _consts()
    shared = {k: f(inputs[k]) for k in WEIGHT_SHAPES}
    shared.update(consts)
    maps = []
    for c in range(NCORES):
        b = c % 2
        m = dict(shared)
        m["xp"] = f(inputs["x_prompt"][NPL * c:NPL * (c + 1)]).reshape(NPL * TP, D)
        m["xs"] = f(inputs["x_sample"][b])
        m["cak"] = f(inputs["cache_a_k"][b]).reshape(L, PAST, 128)
        m["cav"] = f(inputs["cache_a_v"][b]).reshape(L, PAST, 128)
        m["cbk"] = f(inputs["cache_b_k"][b]).reshape(L, PAST, 256)
        m["cbv"] = f(inputs["cache_b_v"][b]).reshape(L, PAST, 256)
        m["scf"] = f(inputs["state_c_fwd"][b])
        m["scb"] = f(inputs["state_c_bwd"][b])
        m["cs"] = f(inputs["c"][b])
        maps.append(m)
    return maps


def kernel(**inputs):
    nc = build()
    maps = make_in_maps(inputs)
    res = run_bass_kernel_spmd(nc, maps, core_ids=list(range(NCORES))).results
    yp = np.concatenate([r["yp"].reshape(NPL, TP, D) for r in res], axis=0)
    ys = np.stack([res[0]["ys"], res[1]["ys"]], axis=0)
    nak = np.concatenate([r["nak"].reshape(NPL, L, TP, 2, 64) for r in res], axis=0)
    nav = np.concatenate([r["nav"].reshape(NPL, L, TP, 2, 64) for r in res], axis=0)
    nbk = np.concatenate([r["nbk"].reshape(NPL, L, TP, 4, 2, 32) for r in res], axis=0)
    nbv = np.concatenate([r["nbv"].reshape(NPL, L, TP, 4, 64) for r in res], axis=0)
    ncf = np.concatenate([r["ncf"] for r in res], axis=0)
    ncb = np.concatenate([r["ncb"] for r in res], axis=0)
    return tuple(np.ascontiguousarray(a.astype(np.float32)) for a in (yp, ys, nak, nav, nbk, nbv, ncf, ncb))
```

```python
import math
from contextlib import ExitStack
import numpy as np
import concourse.bass as bass
import concourse.mybir as mybir
from concourse.bass_utils import run_bass_kernel_spmd

F32 = mybir.dt.float32
BF16 = mybir.dt.bfloat16
AF = mybir.ActivationFunctionType
ALU = mybir.AluOpType
AX = mybir.AxisListType

D = 1024
L = 4
TP = 256
TS = 4096
PAST = 512
NPL = 2
DFF = 2816
INC = 2944
DEC = 0.606531
EPS = 1e-6
GN_EPS = 64e-5
NCORES = 8


class Buf:
    __slots__ = ("w", "r")

    def __init__(self):
        self.w = []
        self.r = []


class TT:
    def __init__(self, h):
        self.h = h
        self.b = Buf()

    def __getitem__(self, k):
        return self.h[k]


class Prog:
    RING = 12

    def __init__(self, nc, stack):
        self.nc = nc
        self.eng = {"pe": nc.tensor, "act": nc.scalar, "dve": nc.vector, "pool": nc.gpsimd, "sp": nc.sync}
        self.semh = {}
        self.cnt = {}
        self.seen = {e: {} for e in self.eng}
        for e in ("pe", "act", "dve", "pool"):
            self.semh["S_" + e] = stack.enter_context(nc.semaphore("S_" + e))
            self.cnt[e] = 0
        self.dq = {}
        for q in ("sp", "pool"):
            ring = []
            for k in range(self.RING):
                key = "D_%s_%d" % (q, k)
                self.semh[key] = stack.enter_context(nc.semaphore(key))
                ring.append([key, 0])
            self.dq[q] = [ring, 0]
        self.n = 0

    def _waits(self, e, reads, writes):
        need = {}
        for b in reads:
            for (key, val, te) in b.w:
                if need.get(key, 0) < val:
                    need[key] = val
        for b in writes:
            for (key, val, te) in b.w:
                if te != e and need.get(key, 0) < val:
                    need[key] = val
            for (key, val, te) in b.r:
                if te != e and need.get(key, 0) < val:
                    need[key] = val
        seen = self.seen[e]
        for key, val in need.items():
            if seen.get(key, 0) < val:
                self.eng[e].wait_ge(self.semh[key], val)
                seen[key] = val
                self.n += 1

    def _record(self, tok, reads, writes):
        for b in writes:
            b.w = [tok]
            b.r = []
        for b in reads:
            b.r = [t for t in b.r if t[0] != tok[0]]
            b.r.append(tok)

    def op(self, e, fn, reads=(), writes=()):
        reads = [t.b for t in reads]
        writes = [t.b for t in writes]
        self._waits(e, reads, writes)
        ins = fn(self.eng[e])
        self.cnt[e] += 1
        ins.then_inc(self.semh["S_" + e], 1)
        self._record(("S_" + e, self.cnt[e], e), reads, writes)
        self.n += 1

    def dma(self, q, out_ap, in_ap, reads=(), writes=()):
        reads = [t.b for t in reads]
        writes = [t.b for t in writes]
        self._waits(q, reads, writes)
        ring, rr = self.dq[q]
        slot = ring[rr % self.RING]
        self.dq[q][1] = rr + 1
        key, prev = slot
        if prev > 0 and self.seen[q].get(key, 0) < prev:
            self.eng[q].wait_ge(self.semh[key], prev)
            self.seen[q][key] = prev
        ins = self.eng[q].dma_start(out=out_ap, in_=in_ap)
        ins.then_inc(self.semh[key], 16)
        slot[1] = prev + 16
        self._record((key, prev + 16, None), reads, writes)
        self.n += 2

    def barrier(self):
        targets = {}
        for e in ("pe", "act", "dve", "pool"):
            if self.cnt[e] > 0:
                targets["S_" + e] = self.cnt[e]
        for q in self.dq:
            for key, val in self.dq[q][0]:
                if val > 0:
                    targets[key] = val
        for e in self.eng:
            seen = self.seen[e]
            for key, val in targets.items():
                if key == "S_" + e:
                    continue
                if seen.get(key, 0) < val:
                    self.eng[e].wait_ge(self.semh[key], val)
                    seen[key] = val
                    self.n += 1


def _rope_tables(dh, T, grid_w=64):
    half = dh // 2
    t = np.arange(T)
    rows = t // grid_w
    cols = t % grid_w
    inv = 10000.0 ** (-np.arange(0, half, 2, dtype=np.float32) / half)
    cos = np.zeros((dh, T), np.float32)
    sin = np.zeros((dh, T), np.float32)
    perm = np.zeros((dh, dh), np.float32)
    q = half // 2
    for ax, pos in enumerate((rows, cols)):
        ang = pos[None, :].astype(np.float32) * inv[:, None]
        base = ax * half
        for i in range(q):
            cos[base + i] = np.cos(ang[i])
            cos[base + q + i] = np.cos(ang[i])
            sin[base + i] = -np.sin(ang[i])
            sin[base + q + i] = np.sin(ang[i])
            perm[base + q + i, base + i] = 1.0
            perm[base + i, base + q + i] = 1.0
    return cos, sin, perm


def _consts():
    c = {}
    c["ident"] = np.eye(128, dtype=np.float32)
    c["ones"] = np.ones((128, 512), np.float32)
    cosA, sinA, pA = _rope_tables(64, TS)
    cosB, sinB, pB = _rope_tables(32, TS)
    c["cosA"] = np.tile(cosA, (2, 1))
    c["sinA"] = np.tile(sinA, (2, 1))
    c["cosB"] = np.tile(cosB, (4, 1))
    c["sinB"] = np.tile(sinB, (4, 1))
    PA = np.zeros((128, 128), np.float32)
    PB = np.zeros((128, 128), np.float32)
    for i in range(2):
        PA[i * 64:(i + 1) * 64, i * 64:(i + 1) * 64] = pA
    for i in range(4):
        PB[i * 32:(i + 1) * 32, i * 32:(i + 1) * 32] = pB
    c["permA"] = PA
    c["permB"] = PB
    p = np.arange(128)[:, None]
    f = np.arange(128)[None, :]
    c["mprev"] = (p >= f).astype(np.float32)
    c["mnext"] = (p <= f).astype(np.float32)
    p = np.arange(64)[:, None]
    f = np.arange(64)[None, :]
    rep = lambda m: np.tile(m.astype(np.float32), (1, 6))
    c["m_lt"] = rep(p < f)
    c["m_le"] = rep(p <= f)
    c["m_gt"] = rep(p > f)
    c["m_ge"] = rep(p >= f)
    c["n_lt"] = -rep(p < f)
    c["n_le"] = -rep(p <= f)
    c["n_gt"] = -rep(p > f)
    c["n_ge"] = -rep(p >= f)
    c["i6"] = rep(p == f)
    rs = np.ones((64, 6 * 512), np.float32)
    rs[:, ::64] = 0.0
    c["rstart"] = rs
    return c


CONST_SHAPES = {k: v.shape for k, v in _consts().items()}

WEIGHT_SHAPES = dict(
    ada_w=(L, D, 6 * D), ada_b=(L, 6 * D), norm1_g=(L, D), norm2_g=(L, D), w_in=(L, D, INC), a_sink=(L, 6),
    b_lambda=(L, 4, 32), b_subln_g=(L, 64), c_conv=(L, 3, 1152), c_w0=(L, 2, 384), c_w2=(L, 2, 64, 384),
    c_a0=(L, 2, 384), c_a2=(L, 2, 64, 384), c_g2=(L, 128, 384), c_kk=(L, 384), c_ka=(L, 384), c_rk=(L, 6, 64),
    c_lnx_g=(L, 384), c_lnx_b=(L, 384), w_out=(L, D, D), ffn_w1=(L, D, DFF), ffn_w3=(L, D, DFF),
    ffn_w2=(L, DFF, D), final_g=(D,), c_ctx=(D,))

CORE_IN_SHAPES = dict(
    xp=(NPL * TP, D), xs=(TS, D), cak=(L, PAST, 128), cav=(L, PAST, 128), cbk=(L, PAST, 256), cbv=(L, PAST, 256),
    scf=(L, 6, 64, 64), scb=(L, 6, 64, 64), cs=(D,))

OUT_SHAPES = dict(
    yp=(NPL * TP, D), ys=(TS, D), nak=(NPL, L, TP, 128), nav=(NPL, L, TP, 128), nbk=(NPL, L, TP, 256),
    nbv=(NPL, L, TP, 256), ncf=(NPL, L, 6, 64, 64), ncb=(NPL, L, 6, 64, 64))


def build(nlayers=L, debug=False, do_sample=True):
    nc = bass.Bass("TRN2", target_bir_lowering=False)
    I = {}
    for k, s in list(WEIGHT_SHAPES.items()) + list(CORE_IN_SHAPES.items()) + list(CONST_SHAPES.items()):
        I[k] = nc.dram_tensor(k, list(s), F32, kind="ExternalInput").ap()
    O = {k: nc.dram_tensor(k, list(s), F32, kind="ExternalOutput").ap() for k, s in OUT_SHAPES.items()}
    skind = "ExternalOutput" if debug else "Internal"
    streams = []
    for sname, tt in (("p", NPL * TP), ("s", TS)):
        S = {"T": tt, "name": sname}
        for nm, shp in (("XT", [D, tt]), ("QA", [6, 64, tt]), ("KA", [2, 64, tt]), ("QB", [8, 32, tt]),
                        ("KB", [8, 32, tt]), ("VA", [tt, 128]), ("VB", [tt, 256]), ("RKV", [18, 64, tt]),
                        ("CW", [2, 64, tt]), ("CA", [2, 64, tt]), ("CG", [128, tt]), ("M", [tt, D]),
                        ("YF", [tt, 384]), ("YB", [tt, 384]), ("RKVC", [18, 64, tt])):
            S[nm] = nc.dram_tensor("%s_%s" % (nm, sname), shp, F32, kind=skind).ap()
        streams.append(S)
    SP_, SS_ = streams
    if not do_sample:
        streams = [SP_]
    seqs = [(SP_, 0, TP, False, 0), (SP_, TP, TP, False, 1)]
    if do_sample:
        seqs.append((SS_, 0, TS, True, -1))

    with ExitStack() as glob:
        p = Prog(nc, glob)

        uid = [0]

        def sb(st, name, shape, dt=F32):
            uid[0] += 1
            return TT(st.enter_context(nc.sbuf_tensor("s%d_%s" % (uid[0], name), list(shape), dt)))

        ps = [TT(glob.enter_context(nc.psum_tensor("ps%d" % i, [128, 512], F32))) for i in range(8)]
        psi = [0]

        def nps():
            t = ps[2 + psi[0] % 6]
            psi[0] += 1
            return t

        def mm(out, lhsT, rhs, start, stop, r, w):
            p.op("pe", lambda e: e.matmul(out, lhsT, rhs, start=start, stop=stop), reads=r, writes=w)

        def tr(out, in_, idn, r, w):
            p.op("pe", lambda e: e.transpose(out, in_, idn), reads=r, writes=w)

        def ld(out, in_, w, r=()):
            p.dma("sp", out, in_, reads=r, writes=w)

        def stq(out, in_, r):
            p.dma("pool", out, in_, reads=r)

        def act(out, in_, func, r, w, **kw):
            p.op("act", lambda e: e.activation(out=out, in_=in_, func=func, **kw), reads=r, writes=w)

        def tt_(eng, out, in0, in1, op, r, w):
            p.op(eng, lambda e: e.tensor_tensor(out=out, in0=in0, in1=in1, op=op), reads=r, writes=w)

        def ts_(eng, out, in0, s1, s2, op0, op1, r, w):
            if s2 is None:
                p.op(eng, lambda e: e.tensor_single_scalar(out=out, in_=in0, scalar=s1, op=op0), reads=r, writes=w)
            else:
                p.op(eng, lambda e: e.tensor_scalar(out=out, in0=in0, scalar1=s1, scalar2=s2, op0=op0, op1=op1), reads=r, writes=w)

        def stt(out, in0, scalar, in1, op0, op1, r, w):
            p.op("dve", lambda e: e.scalar_tensor_tensor(out=out, in0=in0, scalar=scalar, in1=in1, op0=op0, op1=op1), reads=r, writes=w)

        def cp(eng, out, in_, r, w):
            if eng == "act":
                p.op("act", lambda e: e.copy(out=out, in_=in_), reads=r, writes=w)
            else:
                p.op(eng, lambda e: e.tensor_copy(out=out, in_=in_), reads=r, writes=w)

        def rsqrt_(out, in_, scale, bias, r, w, tmpw):
            act(out, in_, AF.Sqrt, r, w, scale=scale, bias=bias)
            p.op("dve", lambda e: e.reciprocal(out=out, in_=out), reads=w, writes=w)

        ident = sb(glob, "ident", [128, 128])
        ones = sb(glob, "ones", [128, 512])
        epsc = sb(glob, "epsc", [128, 2])
        ld(ident[:], I["ident"][:, :], [ident])
        ld(ones[:], I["ones"][:, :], [ones])
        p.op("pool", lambda e: e.memset(epsc[:, 0:1], EPS), writes=[epsc])
        p.op("pool", lambda e: e.memset(epsc[:, 1:2], GN_EPS), writes=[epsc])

        stgc = sb(glob, "stgc", [128, 512])

        def load_cols(st, src2d, rows, w, name):
            dst = sb(st, name, [w, rows])
            for r0 in range(0, rows, 128):
                rr = min(128, rows - r0)
                ld(stgc[0:rr, 0:w], src2d[r0:r0 + rr, :], [stgc])
                pt = nps()
                tr(pt[0:w, 0:rr], stgc[0:rr, 0:w], ident[0:rr, 0:rr], [stgc, ident], [pt])
                cp("act", dst[:, r0:r0 + rr], pt[0:w, 0:rr], [pt], [dst])
            return dst

        def bcast_row(st, src_row, n, name, dst=None):
            if dst is None:
                dst = sb(st, name, [128, n])
            ld(stgc[0:1, 0:n], src_row, [stgc])
            pt = nps()
            mm(pt[:, 0:n], ones[0:1, 0:128], stgc[0:1, 0:n], True, True, [ones, stgc], [pt])
            cp("act", dst[:, 0:n], pt[:, 0:n], [pt], [dst])
            return dst

        adab = load_cols(glob, I["ada_b"].rearrange("l (j p) -> (l j) p", p=128), L * 48, 128, "adab")
        g1c = load_cols(glob, I["norm1_g"].rearrange("l (j p) -> (l j) p", p=128), L * 8, 128, "g1c")
        g2c = load_cols(glob, I["norm2_g"].rearrange("l (j p) -> (l j) p", p=128), L * 8, 128, "g2c")
        gfc = load_cols(glob, I["final_g"].rearrange("(j p) -> j p", p=128), 8, 128, "gfc")
        cct = load_cols(glob, I["c_ctx"].rearrange("(j p) -> j p", p=128), 8, 128, "cct")
        cst = load_cols(glob, I["cs"].rearrange("(j p) -> j p", p=128), 8, 128, "cst")
        convc = load_cols(glob, I["c_conv"].rearrange("l k (t c) -> (l k t) c", c=64), L * 3 * 18, 64, "convc")
        w0c = load_cols(glob, I["c_w0"].rearrange("l d (h c) -> (l d h) c", c=64), L * 12, 64, "w0c")
        a0c = load_cols(glob, I["c_a0"].rearrange("l d (h c) -> (l d h) c", c=64), L * 12, 64, "a0c")
        kkc = load_cols(glob, I["c_kk"].rearrange("l (h c) -> (l h) c", c=64), L * 6, 64, "kkc")
        kac = load_cols(glob, I["c_ka"].rearrange("l (h c) -> (l h) c", c=64), L * 6, 64, "kac")
        rkc_ = load_cols(glob, I["c_rk"].rearrange("l h c -> (l h) c"), L * 6, 64, "rkc")

        mod = sb(glob, "mod", [128, L * 48, 2])
        modA1 = sb(glob, "modA1", [128, L * 8, 2])
        modA2 = sb(glob, "modA2", [128, L * 8, 2])
        with ExitStack() as st:
            silc = sb(st, "silc", [128, 8, 2])
            act(silc[:, :, 0], cct[:, :], AF.Silu, [cct], [silc])
            act(silc[:, :, 1], cst[:, :], AF.Silu, [cst], [silc])
            pcs = [sb(st, "adapc%d" % i, [128, 8, 512]) for i in range(2)]
            k_ = 0
            for l in range(nlayers):
                wv = I["ada_w"][l].rearrange("(c p) f -> p c f", p=128)
                for pc in range(12):
                    pt_ = pcs[k_ % 2]
                    k_ += 1
                    ld(pt_[:], wv[:, :, pc * 512:(pc + 1) * 512], [pt_])
                    for jb in range(4):
                        pq = nps()
                        for k in range(8):
                            mm(pq[:, 0:2], pt_[:, k, jb * 128:(jb + 1) * 128], silc[:, k, :], k == 0, k == 7, [pt_, silc], [pq])
                        j = l * 48 + pc * 4 + jb
                        ts_("dve", mod[:, j, :], pq[:, 0:2], adab[:, j:j + 1], None, ALU.add, ALU.bypass, [pq, adab], [mod])
                stt(modA1[:, l * 8:(l + 1) * 8, :], mod[:, l * 48 + 8:l * 48 + 16, :], 1.0,
                    g1c[:, l * 8:(l + 1) * 8].unsqueeze(2).to_broadcast([128, 8, 2]), ALU.add, ALU.mult, [mod, g1c], [modA1])
                stt(modA2[:, l * 8:(l + 1) * 8, :], mod[:, l * 48 + 32:l * 48 + 40, :], 1.0,
                    g2c[:, l * 8:(l + 1) * 8].unsqueeze(2).to_broadcast([128, 8, 2]), ALU.add, ALU.mult, [mod, g2c], [modA2])
            p.barrier()

        def MOD(l, which, c, cv):
            j = l * 48 + which * 8 + c
            return mod[:, j, cv:cv + 1]

        with ExitStack() as st:
            xin = [sb(st, "xin%d" % i, [128, D]) for i in range(2)]
            xo = [sb(st, "xo%d" % i, [128, 8, 128]) for i in range(2)]
            k_ = 0
            for S, src in ((SP_, I["xp"]), (SS_, I["xs"])):
                if S is SS_ and not do_sample:
                    continue
                for b in range(S["T"] // 128):
                    a_ = xin[k_ % 2]
                    o_ = xo[k_ % 2]
                    k_ += 1
                    ld(a_[:], src[b * 128:(b + 1) * 128, :], [a_])
                    for half in range(2):
                        pt = nps()
                        for c in range(4):
                            cc = half * 4 + c
                            tr(pt[:, c * 128:(c + 1) * 128], a_[:, cc * 128:(cc + 1) * 128], ident[:, :], [a_, ident], [pt])
                        cp("act" if half == 0 else "dve", o_[:, half * 4:(half + 1) * 4, :],
                           pt[:, :].rearrange("p (c t) -> p c t", t=128), [pt], [o_])
                    stq(S["XT"].rearrange("(c p) t -> p c t", p=128)[:, :, b * 128:(b + 1) * 128], o_[:], [o_])
            p.barrier()

        for l in range(nlayers):
            lam_init = 0.8 - 0.6 * math.exp(-0.3 * l)
            with ExitStack() as st:
                win = sb(st, "win", [128, 8, INC], BF16)
                wstg = [sb(st, "wstg%d" % i, [128, INC]) for i in range(2)]
                wv = I["w_in"][l].rearrange("(c p) f -> p c f", p=128)
                for k in range(8):
                    ws_ = wstg[k % 2]
                    ld(ws_[:], wv[:, k, :], [ws_])
                    cp("pool" if k % 2 == 0 else "dve", win[:, k, :], ws_[:], [ws_], [win])
                xt = sb(st, "xt", [128, 8, 512])
                sq = sb(st, "sq", [128, 8, 512])
                h = sb(st, "h", [128, 8, 512], BF16)
                hf = sb(st, "hf", [128, 8, 512])
                rstd = sb(st, "rstd", [128, 512])
                stg = [sb(st, "stg%d" % i, [128, 512]) for i in range(3)]
                zs = sb(st, "zs", [128, 512])
                t2 = sb(st, "t2", [128, 512])
                cosA = sb(st, "cosA", [128, 512]); sinA = sb(st, "sinA", [128, 512])
                cosB = sb(st, "cosB", [128, 512]); sinB = sb(st, "sinB", [128, 512])
                permA = sb(st, "permA", [128, 128]); permB = sb(st, "permB", [128, 128])
                ld(permA[:], I["permA"][:, :], [permA]); ld(permB[:], I["permB"][:, :], [permB])
                vst = [sb(st, "vst%d" % i, [128, 768]) for i in range(2)]
                sgi = [0]
                for S in streams:
                    latent = S is SS_
                    cv = 1 if latent else 0
                    T = S["T"]
                    XTv = S["XT"].rearrange("(c p) t -> p c t", p=128)
                    for t0 in range(0, T, 512):
                        ld(xt[:], XTv[:, :, t0:t0 + 512], [xt])
                        if latent:
                            ld(cosA[:], I["cosA"][:, t0:t0 + 512], [cosA]); ld(sinA[:], I["sinA"][:, t0:t0 + 512], [sinA])
                            ld(cosB[:], I["cosB"][:, t0:t0 + 512], [cosB]); ld(sinB[:], I["sinB"][:, t0:t0 + 512], [sinB])
                        act(sq[:], xt[:], AF.Square, [xt], [sq])
                        pss = nps()
                        for c in range(8):
                            mm(pss[:, :], ones[:, 0:128], sq[:, c, :], c == 0, c == 7, [ones, sq], [pss])
                        rsqrt_(rstd[:], pss[:, :], 1.0 / D, epsc[:, 0:1], [pss, epsc], [rstd], None)
                        for c in range(8):
                            stt(hf[:, c, :], xt[:, c, :], modA1[:, l * 8 + c, cv:cv + 1], rstd[:], ALU.mult, ALU.mult, [xt, modA1, rstd], [hf])
                            act(h[:, c, :], hf[:, c, :], AF.Identity, [hf, mod], [h], bias=MOD(l, 0, c, cv), scale=1.0)
                        for j in range(23):
                            if j in (4, 9, 10):
                                continue
                            pz = nps()
                            for k in range(8):
                                mm(pz[:, :], win[:, k, j * 128:(j + 1) * 128], h[:, k, :], k == 0, k == 7, [win, h], [pz])
                            sg = stg[sgi[0] % 3]
                            sgi[0] += 1
                            rope = latent and j in (0, 1, 2, 3, 5, 6, 7, 8)
                            if rope:
                                isA = j <= 3
                                cs_, sn_, pm_ = (cosA, sinA, permA) if isA else (cosB, sinB, permB)
                                cp("act", zs[:], pz[:, :], [pz], [zs])
                                pr = nps()
                                mm(pr[:, :], pm_[:, :], zs[:], True, True, [pm_, zs], [pr])
                                tt_("dve", t2[:], pr[:, :], sn_[:], ALU.mult, [pr, sn_], [t2])
                                tt_("pool", sg[:], zs[:], cs_[:], ALU.mult, [zs, cs_], [sg])
                                tt_("pool", sg[:], sg[:], t2[:], ALU.add, [sg, t2], [sg])
                            elif j == 20:
                                act(sg[:], pz[:, :], AF.Tanh, [pz], [sg])
                            elif j == 22:
                                act(sg[:], pz[:, :], AF.Sigmoid, [pz], [sg])
                            else:
                                cp("act" if j % 2 == 0 else "dve", sg[:], pz[:, :], [pz], [sg])
                            sl = slice(t0, t0 + 512)
                            if j <= 2:
                                for hh in range(2):
                                    stq(S["QA"][2 * j + hh, :, sl], sg[hh * 64:(hh + 1) * 64, :], [sg])
                            elif j == 3:
                                for hh in range(2):
                                    stq(S["KA"][hh, :, sl], sg[hh * 64:(hh + 1) * 64, :], [sg])
                            elif j in (5, 6):
                                for q4 in range(4):
                                    stq(S["QB"][(j - 5) * 4 + q4, :, sl], sg[q4 * 32:(q4 + 1) * 32, :], [sg])
                            elif j in (7, 8):
                                for q4 in range(4):
                                    stq(S["KB"][(j - 7) * 4 + q4, :, sl], sg[q4 * 32:(q4 + 1) * 32, :], [sg])
                            elif 11 <= j <= 19:
                                for hh in range(2):
                                    stq(S["RKV"][(j - 11) * 2 + hh, :, sl], sg[hh * 64:(hh + 1) * 64, :], [sg])
                            elif j == 20:
                                for hh in range(2):
                                    stq(S["CW"][hh, :, sl], sg[hh * 64:(hh + 1) * 64, :], [sg])
                            elif j == 21:
                                for hh in range(2):
                                    stq(S["CA"][hh, :, sl], sg[hh * 64:(hh + 1) * 64, :], [sg])
                            else:
                                stq(S["CG"][:, sl], sg[:, :], [sg])
                        for i in range(4):
                            vs_ = vst[i % 2]
                            groups = [(512, 128, 0), (1152, 256, 128)]
                            if not latent:
                                groups += [(384, 128, 384), (896, 256, 512)]
                            pv = nps()
                            pk = nps()
                            for (c0, wd, o0) in groups:
                                pp, oo = (pv, o0) if o0 < 384 else (pk, o0 - 384)
                                for k in range(8):
                                    mm(pp[:, oo:oo + wd], h[:, k, i * 128:(i + 1) * 128], win[:, k, c0:c0 + wd], k == 0, k == 7, [h, win], [pp])
                            cp("act", vs_[:, 0:384], pv[:, 0:384], [pv], [vs_])
                            r0 = t0 + i * 128
                            stq(S["VA"][r0:r0 + 128, :], vs_[:, 0:128], [vs_])
                            stq(S["VB"][r0:r0 + 128, :], vs_[:, 128:384], [vs_])
                            if not latent:
                                cp("dve", vs_[:, 384:768], pk[:, 0:384], [pk], [vs_])
                                bl = r0 // TP
                                rr = r0 % TP
                                stq(O["nav"][bl, l, rr:rr + 128, :], vs_[:, 0:128], [vs_])
                                stq(O["nbv"][bl, l, rr:rr + 128, :], vs_[:, 128:384], [vs_])
                                stq(O["nak"][bl, l, rr:rr + 128, :], vs_[:, 384:512], [vs_])
                                stq(O["nbk"][bl, l, rr:rr + 128, :], vs_[:, 512:768], [vs_])
                p.barrier()

            with ExitStack() as st:
                sinkb = bcast_row(st, I["a_sink"][l:l + 1, :], 6, "sinkb")
                act(sinkb[:, 0:6], sinkb[:, 0:6], AF.Exp, [sinkb], [sinkb])
                mprev = sb(st, "mprev", [128, 128]); mnext = sb(st, "mnext", [128, 128])
                ld(mprev[:], I["mprev"][:, :], [mprev]); ld(mnext[:], I["mnext"][:, :], [mnext])
                kT = sb(st, "kT", [64, TS + PAST], BF16)
                qT = sb(st, "qT", [64, 3, TS], BF16)
                vt = sb(st, "vt", [128, (TS + PAST) // 128, 65], BF16)
                p.op("pool", lambda e: e.memset(vt[:, :, 64:65], 1.0), writes=[vt])
                stf = sb(st, "stfa", [64, 3, TS])
                vtf = sb(st, "vtfa", [128, (TS + PAST) // 128, 64])
                cstg = sb(st, "cstg", [128, 4, 64])
                pT = [sb(st, "pT%d" % i, [128, 384], BF16) for i in range(6)]
                ao = [sb(st, "ao%d" % i, [128, 3, 64]) for i in range(2)]
                den = sb(st, "den", [128, 3])
                pti = [0]
                for (S, o, T, latent, bl) in seqs:
                    nb = T // 128
                    for g in range(2):
                        ld(stf[:, 0, 0:T], S["KA"][g, :, o:o + T], [stf])
                        cp("pool", kT[:, 0:T], stf[:, 0, 0:T], [stf], [kT])
                        ld(stf[:, :, 0:T], S["QA"][3 * g:3 * g + 3, :, o:o + T].rearrange("m d t -> d m t"), [stf])
                        cp("pool", qT[:, :, 0:T], stf[:, :, 0:T], [stf], [qT])
                        ld(vtf[:, 0:nb, :], S["VA"][o:o + T, g * 64:(g + 1) * 64].rearrange("(n p) d -> p n d", p=128), [vtf])
                        nctx = 0
                        if latent:
                            nctx = PAST // 128
                            ld(cstg[:], I["cak"][l, :, g * 64:(g + 1) * 64].rearrange("(n p) d -> p n d", p=128), [cstg])
                            pt = nps()
                            for n_ in range(4):
                                tr(pt[0:64, n_ * 128:(n_ + 1) * 128], cstg[:, n_, :], ident[:, :], [cstg, ident], [pt])
                            cp("act", kT[:, T:T + PAST], pt[0:64, :], [pt], [kT])
                            ld(vtf[:, nb:nb + 4, :], I["cav"][l, :, g * 64:(g + 1) * 64].rearrange("(n p) d -> p n d", p=128), [vtf])
                        cp("pool", vt[:, 0:nb + nctx, 0:64], vtf[:, 0:nb + nctx, :], [vtf], [vt])
                        for b in range(nb):
                            if latent:
                                tiles = [(kb, (mprev if kb == b - 1 else (mnext if kb == b + 1 else None)))
                                         for kb in (b - 1, b, b + 1) if 0 <= kb < nb]
                                tiles += [(nb + c_, None) for c_ in range(nctx)]
                            else:
                                tiles = [(kb, None) for kb in range(nb)]
                            po = ps[b % 2]
                            def st1(ti, kb, msk):
                                pss = nps()
                                mm(pss[:, 0:384], kT[:, kb * 128:(kb + 1) * 128], qT[:, :, b * 128:(b + 1) * 128], True, True, [kT, qT], [pss])
                                pt_ = pT[pti[0] % 6]
                                pti[0] += 1
                                act(pt_[:], pss[:, 0:384], AF.Exp, [pss], [pt_], scale=0.125)
                                if msk is not None:
                                    tt_("pool", pt_[:, :].rearrange("p (m q) -> p m q", q=128), pt_[:, :].rearrange("p (m q) -> p m q", q=128),
                                        msk[:, :].unsqueeze(1).to_broadcast([128, 3, 128]), ALU.mult, [pt_, msk], [pt_])
                                return pt_

                            def st2(ti, kb, pt_):
                                for m in range(3):
                                    mm(po[:, m * 65:(m + 1) * 65], pt_[:, m * 128:(m + 1) * 128], vt[:, kb, :], ti == 0 and m == 0, ti == len(tiles) - 1 and m == 2, [pt_, vt], [po])

                            pend = []
                            for ti, (kb, msk) in enumerate(tiles):
                                pend.append((ti, kb, st1(ti, kb, msk)))
                                if len(pend) > 2:
                                    st2(*pend.pop(0))
                            for it_ in pend:
                                st2(*it_)
                            pov = po[:, 0:195].rearrange("p (m d) -> p m d", d=65)
                            tt_("dve", den[:, :].unsqueeze(2), pov[:, :, 64:65], sinkb[:, 3 * g:3 * g + 3].unsqueeze(2), ALU.add, [po, sinkb], [den])
                            p.op("dve", lambda e: e.reciprocal(out=den[:, :], in_=den[:, :]), reads=[den], writes=[den])
                            a_ = ao[b % 2]
                            tt_("dve", a_[:], pov[:, :, 0:64], den[:, :].unsqueeze(2).to_broadcast([128, 3, 64]), ALU.mult, [po, den], [a_])
                            r0 = o + b * 128
                            stq(S["M"][r0:r0 + 128, g * 192:(g + 1) * 192], a_[:].rearrange("p m d -> p (m d)"), [a_])
                p.barrier()

            with ExitStack() as st:
                lamr = bcast_row(st, I["b_lambda"][l:l + 1].rearrange("o a d -> o (a d)"), 128, "lamr")
                lam2 = sb(st, "lam2", [128, 2, 32])
                lv = lamr[:, :].rearrange("p (a b d) -> p a b d", a=2, b=2)
                tt_("dve", lam2[:], lv[:, :, 0, :], lv[:, :, 1, :], ALU.mult, [lamr], [lam2])
                lam1 = sb(st, "lam1", [128, 2])
                p.op("dve", lambda e: e.tensor_reduce(out=lam1[:, :], in_=lam2[:], axis=AX.X, op=ALU.add), reads=[lam2], writes=[lam1])
                act(lam1[:, :], lam1[:, :], AF.Exp, [lam1], [lam1])
                nlam = sb(st, "nlam", [128, 1])
                tt_("dve", nlam[:, :], lam1[:, 1:2], lam1[:, 0:1], ALU.subtract, [lam1], [nlam])
                ts_("dve", nlam[:, :], nlam[:, :], -lam_init, None, ALU.add, ALU.bypass, [nlam], [nlam])
                gsub = bcast_row(st, I["b_subln_g"][l:l + 1, :], 64, "gsub")
                ts_("dve", gsub[:, :], gsub[:, :], 1.0 - lam_init, None, ALU.mult, ALU.bypass, [gsub], [gsub])
                kT = sb(st, "kTb", [32, 2, TS + PAST], BF16)
                qT = sb(st, "qTb", [32, 2, TS], BF16)
                vt = sb(st, "vtb", [128, (TS + PAST) // 128, 65], BF16)
                p.op("pool", lambda e: e.memset(vt[:, :, 64:65], 1.0), writes=[vt])
                stf = sb(st, "stfb", [32, 2, TS])
                vtf = sb(st, "vtfb", [128, (TS + PAST) // 128, 64])
                cstg = sb(st, "cstgb", [128, 4, 32])
                pT = [sb(st, "pTb%d" % i, [128, 512], BF16) for i in range(6)]
                o1 = sb(st, "o1", [128, 4, 64]); o2 = sb(st, "o2", [128, 4, 64]); o3 = sb(st, "o3", [128, 4, 64])
                rd = sb(st, "rd", [128, 2, 4]); ssq = sb(st, "ssq", [128, 4])
                bo = [sb(st, "bo%d" % i, [128, 4, 64]) for i in range(2)]
                pti = [0]
                boi = [0]
                for (S, o, T, latent, bl) in seqs:
                    nk = T // 128 + (PAST // 128 if latent else 0)
                    nown = T // 128
                    qn = min(512, T)
                    for hd in range(4):
                        ld(vtf[:, 0:nown, :], S["VB"][o:o + T, hd * 64:(hd + 1) * 64].rearrange("(n p) d -> p n d", p=128), [vtf])
                        if latent:
                            ld(vtf[:, nown:nk, :], I["cbv"][l, :, hd * 64:(hd + 1) * 64].rearrange("(n p) d -> p n d", p=128), [vtf])
                        cp("pool", vt[:, 0:nk, 0:64], vtf[:, 0:nk, :], [vtf], [vt])
                        ld(stf[:, :, 0:T], S["KB"][2 * hd:2 * hd + 2, :, o:o + T].rearrange("m d t -> d m t"), [stf])
                        cp("pool", kT[:, :, 0:T], stf[:, :, 0:T], [stf], [kT])
                        ld(stf[:, :, 0:T], S["QB"][2 * hd:2 * hd + 2, :, o:o + T].rearrange("m d t -> d m t"), [stf])
                        cp("pool", qT[:, :, 0:T], stf[:, :, 0:T], [stf], [qT])
                        if latent:
                            for mp in range(2):
                                c0 = hd * 64 + mp * 32
                                ld(cstg[:], I["cbk"][l, :, c0:c0 + 32].rearrange("(n p) d -> p n d", p=128), [cstg])
                                pt = nps()
                                for n_ in range(4):
                                    tr(pt[0:32, n_ * 128:(n_ + 1) * 128], cstg[:, n_, :], ident[:, :], [cstg, ident], [pt])
                                cp("act", kT[:, mp, T:T + PAST], pt[0:32, :], [pt], [kT])
                        for q0 in range(0, T, qn):
                            nqb = qn // 128
                            pos = [ps[0], ps[1]]
                            def st1(mp, kt):
                                pss = nps()
                                mm(pss[:, 0:qn], kT[:, mp, kt * 128:(kt + 1) * 128], qT[:, mp, q0:q0 + qn], True, True, [kT, qT], [pss])
                                pt_ = pT[pti[0] % 6]
                                pti[0] += 1
                                act(pt_[:, 0:qn], pss[:, 0:qn], AF.Exp, [pss], [pt_], scale=32.0 ** -0.5)
                                return pt_

                            def st2(mp, kt, pt_):
                                for qb in range(nqb):
                                    mm(pos[mp][:, qb * 65:(qb + 1) * 65], pt_[:, qb * 128:(qb + 1) * 128], vt[:, kt, :], kt == 0 and qb == 0, kt == nk - 1 and qb == nqb - 1, [pt_, vt], [pos[mp]])

                            pend = []
                            for mp in range(2):
                                for kt in range(nk):
                                    pend.append((mp, kt, st1(mp, kt)))
                                    if len(pend) > 2:
                                        st2(*pend.pop(0))
                            for it_ in pend:
                                st2(*it_)
                            v1 = pos[0][:, 0:nqb * 65].rearrange("p (q d) -> p q d", d=65)
                            v2 = pos[1][:, 0:nqb * 65].rearrange("p (q d) -> p q d", d=65)
                            p.op("dve", lambda e: e.reciprocal(out=rd[:, 0, 0:nqb].unsqueeze(2), in_=v1[:, :, 64:65]), reads=[pos[0]], writes=[rd])
                            p.op("dve", lambda e: e.reciprocal(out=rd[:, 1, 0:nqb].unsqueeze(2), in_=v2[:, :, 64:65]), reads=[pos[1]], writes=[rd])
                            tt_("dve", o1[:, 0:nqb, :], v1[:, :, 0:64], rd[:, 0, 0:nqb].unsqueeze(2).to_broadcast([128, nqb, 64]), ALU.mult, [pos[0], rd], [o1])
                            tt_("dve", o2[:, 0:nqb, :], v2[:, :, 0:64], rd[:, 1, 0:nqb].unsqueeze(2).to_broadcast([128, nqb, 64]), ALU.mult, [pos[1], rd], [o2])
                            stt(o1[:, 0:nqb, :], o2[:, 0:nqb, :], nlam[:, 0:1], o1[:, 0:nqb, :], ALU.mult, ALU.add, [o1, o2, nlam], [o1])
                            tt_("pool", o3[:, 0:nqb, :], o1[:, 0:nqb, :], o1[:, 0:nqb, :], ALU.mult, [o1], [o3])
                            p.op("dve", lambda e: e.tensor_reduce(out=ssq[:, 0:nqb], in_=o3[:, 0:nqb, :], axis=AX.X, op=ALU.add), reads=[o3], writes=[ssq])
                            rsqrt_(ssq[:, 0:nqb], ssq[:, 0:nqb], 1.0 / 64, epsc[:, 0:1], [ssq, epsc], [ssq], None)
                            b_ = bo[boi[0] % 2]
                            boi[0] += 1
                            tt_("dve", b_[:, 0:nqb, :], o1[:, 0:nqb, :], ssq[:, 0:nqb].unsqueeze(2).to_broadcast([128, nqb, 64]), ALU.mult, [o1, ssq], [b_])
                            tt_("pool", b_[:, 0:nqb, :], b_[:, 0:nqb, :], gsub[:, 0:64].unsqueeze(1).to_broadcast([128, nqb, 64]), ALU.mult, [b_, gsub], [b_])
                            r0 = o + q0
                            stq(S["M"][r0:r0 + qn, 384 + hd * 64:384 + (hd + 1) * 64].rearrange("(q p) d -> p q d", p=128), b_[:, 0:nqb, :], [b_])
                p.barrier()

            with ExitStack() as st:
                CT = 512
                rx = [sb(st, "rx%d" % i, [64, 18, CT + 2]) for i in range(2)]
                ro = [sb(st, "ro%d" % i, [64, 18, CT]) for i in range(2)]
                ti_ = 0
                for (S, o, T, latent, bl) in seqs:
                    for ts0 in range(0, T, CT):
                        n = min(CT, T - ts0)
                        x_, o_ = rx[ti_ % 2], ro[ti_ % 2]
                        ti_ += 1
                        lo = max(ts0 - 1, 0)
                        hi = min(ts0 + n + 1, T)
                        if ts0 == 0:
                            p.op("pool", lambda e: e.memset(x_[:, :, 0:1], 0.0), writes=[x_])
                        if ts0 + n == T:
                            p.op("pool", lambda e: e.memset(x_[:, :, n + 1:n + 2], 0.0), writes=[x_])
                        ld(x_[:, :, lo - (ts0 - 1):hi - (ts0 - 1)], S["RKV"][:, :, o + lo:o + hi].rearrange("c d t -> d c t"), [x_])
                        for c in range(18):
                            cb = (l * 3) * 18 + c
                            act(o_[:, c, 0:n], x_[:, c, 1:n + 1], AF.Copy, [x_, convc], [o_], scale=convc[:, cb + 18:cb + 19])
                            stt(o_[:, c, 0:n], x_[:, c, 0:n], convc[:, cb:cb + 1], o_[:, c, 0:n], ALU.mult, ALU.add, [x_, convc, o_], [o_])
                            stt(o_[:, c, 0:n], x_[:, c, 2:n + 2], convc[:, cb + 36:cb + 37], o_[:, c, 0:n], ALU.mult, ALU.add, [x_, convc, o_], [o_])
                        stq(S["RKVC"][:, :, o + ts0:o + ts0 + n].rearrange("c d t -> d c t"), o_[:, :, 0:n], [o_])
                p.barrier()

            with ExitStack() as st:
                cw2 = sb(st, "cw2", [64, 2, 384]); ca2 = sb(st, "ca2", [64, 2, 384])
                ld(cw2[:], I["c_w2"][l].rearrange("d r c -> r d c"), [cw2])
                ld(ca2[:], I["c_a2"][l].rearrange("d r c -> r d c"), [ca2])
                MK = {}
                for nm in ("m_lt", "m_le", "m_gt", "m_ge", "n_lt", "n_le", "n_gt", "n_ge", "i6"):
                    MK[nm] = sb(st, nm, [64, 384])
                    ld(MK[nm][:], I[nm][:, :], [MK[nm]])
                rstart = sb(st, "rstart", [64, 768])
                ld(rstart[:], I["rstart"][:, 0:768], [rstart])
                NSG = 128
                onesb = sb(st, "onesb", [64, 2], BF16)
                p.op("pool", lambda e: e.memset(onesb[:], 1.0), writes=[onesb])
                KKB = sb(st, "KKB", [64, 6, NSG]); KAB = sb(st, "KAB", [64, 6, NSG]); RKB = sb(st, "RKB", [64, 6, NSG])
                for (dst_, src_) in ((KKB, kkc), (KAB, kac), (RKB, rkc_)):
                    cp("pool", dst_[:], src_[:, l * 6:(l + 1) * 6].unsqueeze(2).to_broadcast([64, 6, NSG]), [src_], [dst_])
                sig = sb(st, "sig", [64, 6, NSG]); a_ = sb(st, "a_", [64, 6, NSG])
                kk = sb(st, "kk", [64, 6, NSG]); kd = sb(st, "kd", [64, 6, NSG]); bb = sb(st, "bb", [64, 6, NSG])
                Lc = sb(st, "Lc", [64, 6, NSG]); Lx = sb(st, "Lx", [64, 6, NSG]); Ld = sb(st, "Ld", [64, 6, NSG])
                E = sb(st, "E", [64, 6, NSG]); tmp = sb(st, "tmpc", [64, 6, NSG])
                hv = lambda t, h_: t[:, h_ * 64:(h_ + 1) * 64]
                v3 = lambda t: t[:, :].rearrange("p (h i) -> p h i", i=64)
                CB = []
                for ci_ in range(2):
                    Bd = {}
                    for nm in ("rkv",):
                        Bd[nm] = sb(st, "%s%d" % (nm, ci_), [64, 18, NSG])
                    for nm in ("cwt", "cat"):
                        Bd[nm] = sb(st, "%s%d" % (nm, ci_), [64, NSG])
                    for nm in ("rkc", "Kt_", "Rt_", "Dh", "Bh"):
                        Bd[nm] = sb(st, "%s%d" % (nm, ci_), [64, 6, NSG], BF16)
                    for nm in ("Dg", "nBg"):
                        Bd[nm] = sb(st, "%s%d" % (nm, ci_), [64, 6, NSG])
                    Bd["Hb"] = sb(st, "Hb%d" % ci_, [64, 6, 64], BF16)
                    Bd["gC"] = sb(st, "gC%d" % ci_, [64, 6, NSG // 64])
                    Bd["H"] = sb(st, "H%d" % ci_, [64, 6, 64]); Bd["hst"] = sb(st, "hst%d" % ci_, [64, 6, 64])
                    for nm in ("Vt", "Dgt", "nBgt", "Pa", "Pb", "PTa", "PTb", "Xa", "Xb", "AdT", "BdT", "nBbT", "Wb", "Zb"):
                        Bd[nm] = sb(st, "%s%d" % (nm, ci_), [64, 384], BF16)
                    for nm in ("yb", "y2", "Vf"):
                        Bd[nm] = sb(st, "%s%d" % (nm, ci_), [64, 384])
                    Bd["coef"] = sb(st, "coef%d" % ci_, [64, 6])
                    CB.append(Bd)

                def chain(S, o, T, latent, bl, d, Bd):
                    rkv, cwt, cat, rkc = Bd["rkv"], Bd["cwt"], Bd["cat"], Bd["rkc"]
                    Kt_, Rt_, Dh, Bh, Dg, nBg, gC, H, hst = (Bd[k] for k in ("Kt_", "Rt_", "Dh", "Bh", "Dg", "nBg", "gC", "H", "hst"))
                    Hb, Vf = Bd["Hb"], Bd["Vf"]
                    Vt, Dgt, nBgt, AdT, BdT, nBbT, Wb, Zb, yb, y2, coef = (Bd[k] for k in ("Vt", "Dgt", "nBgt", "AdT", "BdT", "nBbT", "Wb", "Zb", "yb", "y2", "coef"))
                    Pq = [Bd["Pa"], Bd["Pb"]]; PTq = [Bd["PTa"], Bd["PTb"]]; Xq = [Bd["Xa"], Bd["Xb"]]
                    nsg = (T + NSG - 1) // NSG
                    YD = S["YF" if d == 0 else "YB"]
                    if latent:
                        src = I["scf" if d == 0 else "scb"][l]
                        ld(hst[:], src.rearrange("h i j -> i h j"), [hst])
                        pt = nps()
                        for h_ in range(6):
                            tr(pt[0:64, h_ * 64:(h_ + 1) * 64], hst[:, h_, :], ident[0:64, 0:64], [hst, ident], [pt])
                        cp("act", H[:].rearrange("p h i -> p (h i)"), pt[0:64, 0:384], [pt], [H])
                    else:
                        p.op("pool", lambda e: e.memset(H[:], 0.0), writes=[H])
                    cp("pool", Hb[:], H[:], [H], [Hb])
                    segs = list(range(nsg)) if d == 0 else list(range(nsg - 1, -1, -1))
                    for sg_ in segs:
                        ts0 = sg_ * NSG
                        n = min(NSG, T - ts0)
                        nch = n // 64
                        assert n == NSG
                        ld(rkv[:, :, 0:n], S["RKVC"][:, :, o + ts0:o + ts0 + n].rearrange("c d t -> d c t"), [rkv])
                        ld(cwt[:, 0:n], S["CW"][d, :, o + ts0:o + ts0 + n], [cwt])
                        ld(cat[:, 0:n], S["CA"][d, :, o + ts0:o + ts0 + n], [cat])
                        for h_ in range(6):
                            ci = (l * 2 + d) * 6 + h_
                            pw = nps()
                            mm(pw[0:64, 0:n], cw2[:, d, h_ * 64:(h_ + 1) * 64], cwt[:, 0:n], True, True, [cw2, cwt], [pw])
                            act(sig[:, h_, 0:n], pw[0:64, 0:n], AF.Sigmoid, [pw, w0c], [sig], bias=w0c[:, ci:ci + 1], scale=1.0)
                            pa = nps()
                            mm(pa[0:64, 0:n], ca2[:, d, h_ * 64:(h_ + 1) * 64], cat[:, 0:n], True, True, [ca2, cat], [pa])
                            act(a_[:, h_, 0:n], pa[0:64, 0:n], AF.Sigmoid, [pa, a0c], [a_], bias=a0c[:, ci:ci + 1], scale=1.0)
                        fl = lambda t: t[:].rearrange("p h t -> p (h t)")
                        tt_("dve", kk[:], rkv[:, 6:12, :], KKB[:], ALU.mult, [rkv, KKB], [kk])
                        tt_("pool", tmp[:], kk[:], kk[:], ALU.mult, [kk], [tmp])
                        for hf in range(2):
                            pk_ = nps()
                            mm(pk_[0:64, 0:384], ones[0:64, 0:64], fl(tmp)[:, hf * 384:(hf + 1) * 384], True, True, [ones, tmp], [pk_])
                            act(fl(Ld)[:, hf * 384:(hf + 1) * 384], pk_[0:64, 0:384], AF.Sqrt, [pk_], [Ld])
                        ts_("pool", Ld[:], Ld[:], 1e-12, None, ALU.max, ALU.bypass, [Ld], [Ld])
                        p.op("dve", lambda e: e.reciprocal(out=Ld[:], in_=Ld[:]), reads=[Ld], writes=[Ld])
                        tt_("dve", kk[:], kk[:], Ld[:], ALU.mult, [kk, Ld], [kk])
                        stt(kd[:], a_[:], -1.0, KAB[:], ALU.add, ALU.mult, [a_, KAB], [kd])
                        stt(kd[:], kd[:], 1.0, rkv[:, 6:12, :], ALU.add, ALU.mult, [kd, rkv], [kd])
                        tt_("pool", tmp[:], rkv[:, 0:6, :], RKB[:], ALU.mult, [rkv, RKB], [tmp])
                        tt_("pool", rkc[:], tmp[:], kd[:], ALU.mult, [tmp, kd], [rkc])
                        tt_("pool", bb[:, :, 0:n], kk[:, :, 0:n], a_[:, :, 0:n], ALU.mult, [kk, a_], [bb])
                        if n == NSG:
                            p.op("dve", lambda e: e.tensor_tensor_scan(out=Lc[:].rearrange("p h t -> p (h t)"), data0=rstart[:, :],
                                                                        data1=sig[:].rearrange("p h t -> p (h t)"), initial=0.0, op0=ALU.mult, op1=ALU.add),
                                 reads=[rstart, sig], writes=[Lc])
                        else:
                            for h_ in range(6):
                                p.op("dve", lambda e: e.tensor_tensor_scan(out=Lc[:, h_, 0:n], data0=rstart[:, 0:n], data1=sig[:, h_, 0:n],
                                                                            initial=0.0, op0=ALU.mult, op1=ALU.add), reads=[rstart, sig], writes=[Lc])
                        tt_("pool", Lx[:, :, 0:n], Lc[:, :, 0:n], sig[:, :, 0:n], ALU.subtract, [Lc, sig], [Lx])
                        c4 = lambda t: t[:, :, 0:n].rearrange("p h (c s) -> p h c s", s=64)
                        Ltot = c4(Lc)[:, :, :, 63:64]
                        tt_("dve", c4(Ld), Ltot.to_broadcast([64, 6, nch, 64]), c4(Lc), ALU.subtract, [Lc], [Ld])
                        act(gC[:, :, 0:nch].unsqueeze(3), Ltot, AF.Exp, [Lc], [gC], scale=-DEC)
                        if d == 0:
                            act(E[:, :, 0:n], Lx[:, :, 0:n], AF.Exp, [Lx], [E], scale=-DEC)
                            tt_("dve", Kt_[:, :, 0:n], kk[:, :, 0:n], E[:, :, 0:n], ALU.mult, [kk, E], [Kt_])
                            act(E[:, :, 0:n], Lc[:, :, 0:n], AF.Exp, [Lc], [E], scale=-DEC)
                            tt_("dve", Rt_[:, :, 0:n], rkv[:, 0:6, 0:n], E[:, :, 0:n], ALU.mult, [rkv, E], [Rt_])
                            act(E[:, :, 0:n], Lc[:, :, 0:n], AF.Exp, [Lc], [E], scale=DEC)
                        else:
                            act(E[:, :, 0:n], Ld[:, :, 0:n], AF.Exp, [Ld], [E], scale=-DEC)
                            tt_("dve", Kt_[:, :, 0:n], kk[:, :, 0:n], E[:, :, 0:n], ALU.mult, [kk, E], [Kt_])
                            tt_("dve", c4(tmp), Ltot.to_broadcast([64, 6, nch, 64]), c4(Lx), ALU.subtract, [Lc, Lx], [tmp])
                            act(E[:, :, 0:n], tmp[:, :, 0:n], AF.Exp, [tmp], [E], scale=-DEC)
                            tt_("dve", Rt_[:, :, 0:n], rkv[:, 0:6, 0:n], E[:, :, 0:n], ALU.mult, [rkv, E], [Rt_])
                            act(E[:, :, 0:n], tmp[:, :, 0:n], AF.Exp, [tmp], [E], scale=DEC)
                        tt_("dve", Dh[:, :, 0:n], kd[:, :, 0:n], E[:, :, 0:n], ALU.mult, [kd, E], [Dh])
                        tt_("pool", Bh[:, :, 0:n], bb[:, :, 0:n], E[:, :, 0:n], ALU.mult, [bb, E], [Bh])
                        act(E[:, :, 0:n], (Ld if d == 0 else Lx)[:, :, 0:n], AF.Exp, [Ld, Lx], [E], scale=-DEC)
                        tt_("dve", Dg[:, :, 0:n], kd[:, :, 0:n], E[:, :, 0:n], ALU.mult, [kd, E], [Dg])
                        stt(nBg[:, :, 0:n], bb[:, :, 0:n], -1.0, E[:, :, 0:n], ALU.mult, ALU.mult, [bb, E], [nBg])
                        if d == 0:
                            nmAT, mAT, nmA, mBT, nmBT = MK["n_lt"], MK["m_lt"], MK["n_gt"], MK["m_le"], MK["n_le"]
                        else:
                            nmAT, mAT, nmA, mBT, nmBT = MK["n_gt"], MK["m_gt"], MK["n_lt"], MK["m_ge"], MK["n_ge"]
                        yield
                        chunks = list(range(nch)) if d == 0 else list(range(nch - 1, -1, -1))
                        for c in chunks:
                            cs = slice(c * 64, (c + 1) * 64)
                            for (src_t, srcidx, dst_t) in ((rkv, 12, Vt), (Dg, 0, Dgt), (nBg, 0, nBgt)):
                                pt = nps()
                                for h_ in range(6):
                                    tr(pt[0:64, h_ * 64:(h_ + 1) * 64], src_t[:, srcidx + h_, cs], ident[0:64, 0:64], [src_t, ident], [pt])
                                cp("act", dst_t[:, :], pt[0:64, 0:384], [pt], [dst_t])
                                if dst_t is Vt:
                                    cp("dve", Vf[:, :], pt[0:64, 0:384], [pt], [Vf])
                            pAbT, pBbT, pAdT, pBdT, pAb = nps(), nps(), nps(), nps(), nps()
                            for h_ in range(6):
                                hs = slice(h_ * 64, (h_ + 1) * 64)
                                mm(pAbT[0:64, hs], Bh[:, h_, cs], Kt_[:, h_, cs], True, True, [Bh, Kt_], [pAbT])
                                mm(pBbT[0:64, hs], Bh[:, h_, cs], Rt_[:, h_, cs], True, True, [Bh, Rt_], [pBbT])
                                mm(pAdT[0:64, hs], Dh[:, h_, cs], Kt_[:, h_, cs], True, True, [Dh, Kt_], [pAdT])
                                mm(pBdT[0:64, hs], Dh[:, h_, cs], Rt_[:, h_, cs], True, True, [Dh, Rt_], [pBdT])
                                mm(pAb[0:64, hs], Kt_[:, h_, cs], Bh[:, h_, cs], True, True, [Kt_, Bh], [pAb])
                            tt_("dve", PTq[0][:, :], pAbT[0:64, 0:384], nmAT[:, :], ALU.mult, [pAbT, nmAT], [PTq[0]])
                            tt_("dve", Pq[0][:, :], pAb[0:64, 0:384], nmA[:, :], ALU.mult, [pAb, nmA], [Pq[0]])
                            tt_("pool", Xq[0][:, :], PTq[0][:, :], MK["i6"][:, :], ALU.add, [PTq[0], MK["i6"]], [Xq[0]])
                            tt_("dve", AdT[:, :], pAdT[0:64, 0:384], mAT[:, :], ALU.mult, [pAdT, mAT], [AdT])
                            tt_("dve", BdT[:, :], pBdT[0:64, 0:384], mBT[:, :], ALU.mult, [pBdT, mBT], [BdT])
                            tt_("dve", nBbT[:, :], pBbT[0:64, 0:384], nmBT[:, :], ALU.mult, [pBbT, nmBT], [nBbT])
                            yield
                            for k in range(1, 6):
                                Pc, PTc = Pq[(k - 1) % 2], PTq[(k - 1) % 2]
                                Pn, PTn = Pq[k % 2], PTq[k % 2]
                                pP = nps()
                                for h_ in range(6):
                                    hs = slice(h_ * 64, (h_ + 1) * 64)
                                    mm(pP[0:64, hs], hv(PTc, h_), hv(Pc, h_), True, True, [PTc, Pc], [pP])
                                if k < 5:
                                    pPT = nps()
                                    for h_ in range(6):
                                        hs = slice(h_ * 64, (h_ + 1) * 64)
                                        mm(pPT[0:64, hs], hv(Pc, h_), hv(PTc, h_), True, True, [PTc, Pc], [pPT])
                                if k >= 2:
                                    Xo, Xn = Xq[k % 2], Xq[(k - 1) % 2]
                                    pX = nps()
                                    for h_ in range(6):
                                        hs = slice(h_ * 64, (h_ + 1) * 64)
                                        mm(pX[0:64, hs], hv(Pc, h_), hv(Xo, h_), True, True, [Pc, Xo], [pX])
                                    tt_("dve", Xn[:, :], pX[0:64, 0:384], Xo[:, :], ALU.add, [pX, Xo], [Xn])
                                cp("act", Pn[:, :], pP[0:64, 0:384], [pP], [Pn])
                                if k < 5:
                                    cp("dve", PTn[:, :], pPT[0:64, 0:384], [pPT], [PTn])
                                yield
                            P5, X4, X5 = Pq[1], Xq[0], Xq[1]
                            pX = nps()
                            for h_ in range(6):
                                hs = slice(h_ * 64, (h_ + 1) * 64)
                                mm(pX[0:64, hs], hv(P5, h_), hv(X4, h_), True, True, [P5, X4], [pX])
                            pW = nps()
                            for h_ in range(6):
                                hs = slice(h_ * 64, (h_ + 1) * 64)
                                mm(pW[0:64, hs], Kt_[:, h_, cs], Hb[:, h_, :], True, False, [Kt_, Hb], [pW])
                                mm(pW[0:64, hs], hv(AdT, h_), hv(Vt, h_), False, True, [AdT, Vt], [pW])
                            tt_("dve", X5[:, :], pX[0:64, 0:384], X4[:, :], ALU.add, [pX, X4], [X5])
                            cp("act", Wb[:, :], pW[0:64, 0:384], [pW], [Wb])
                            yield
                            pZ = nps()
                            for h_ in range(6):
                                hs = slice(h_ * 64, (h_ + 1) * 64)
                                mm(pZ[0:64, hs], hv(X5, h_), hv(Wb, h_), True, True, [X5, Wb], [pZ])
                            cp("act", Zb[:, :], pZ[0:64, 0:384], [pZ], [Zb])
                            yield
                            Hf = H[:].rearrange("p h i -> p (h i)")
                            pY, pC, pH = nps(), nps(), nps()
                            for h_ in range(6):
                                hs = slice(h_ * 64, (h_ + 1) * 64)
                                mm(pY[0:64, hs], Rt_[:, h_, cs], Hb[:, h_, :], True, False, [Rt_, Hb], [pY])
                                mm(pY[0:64, hs], hv(BdT, h_), hv(Vt, h_), False, False, [BdT, Vt], [pY])
                                mm(pY[0:64, hs], hv(nBbT, h_), hv(Zb, h_), False, True, [nBbT, Zb], [pY])
                                mm(pC[0:64, h_:h_ + 1], rkc[:, h_, cs], onesb[:, 0:1], True, True, [rkc, onesb], [pC])
                                mm(pH[0:64, hs], hv(Dgt, h_), hv(Vt, h_), True, False, [Dgt, Vt], [pH])
                                mm(pH[0:64, hs], hv(nBgt, h_), hv(Zb, h_), False, True, [nBgt, Zb], [pH])
                            cp("act", coef[:, :], pC[0:64, 0:6], [pC], [coef])
                            tt_("pool", v3(y2), v3(Vf), coef[:, :].unsqueeze(2).to_broadcast([64, 6, 64]), ALU.mult, [Vf, coef], [y2])
                            tt_("dve", yb[:, :], pY[0:64, 0:384], y2[:, :], ALU.add, [pY, y2], [yb])
                            tt_("pool", H[:], H[:], gC[:, :, c:c + 1].to_broadcast([64, 6, 64]), ALU.mult, [H, gC], [H])
                            tt_("dve", Hf, Hf, pH[0:64, 0:384], ALU.add, [H, pH], [H])
                            cp("pool", Hb[:], H[:], [H], [Hb])
                            r0 = o + ts0 + c * 64
                            stq(YD[r0:r0 + 64, :], yb[:, :], [yb])
                            yield
                    if not latent:
                        pt = nps()
                        for h_ in range(6):
                            tr(pt[0:64, h_ * 64:(h_ + 1) * 64], H[:, h_, :], ident[0:64, 0:64], [H, ident], [pt])
                        cp("act", hst[:].rearrange("p h j -> p (h j)"), pt[0:64, 0:384], [pt], [hst])
                        stq(O["ncf" if d == 0 else "ncb"][bl, l].rearrange("h i j -> i h j"), hst[:], [hst])

                for (S, o, T, latent, bl) in seqs:
                    alive = [chain(S, o, T, latent, bl, 0, CB[0]), chain(S, o, T, latent, bl, 1, CB[1])]
                    while alive:
                        for g_ in list(alive):
                            try:
                                next(g_)
                            except StopIteration:
                                alive.remove(g_)
                p.barrier()

            with ExitStack() as st:
                cg2 = sb(st, "cg2", [128, 384])
                ld(cg2[:], I["c_g2"][l], [cg2])
                lnxg = bcast_row(st, I["c_lnx_g"][l:l + 1, :], 384, "lnxg")
                lnxb = bcast_row(st, I["c_lnx_b"][l:l + 1, :], 384, "lnxb")
                yfb = [sb(st, "yfb%d" % i, [128, 384]) for i in range(2)]
                ybb = [sb(st, "ybb%d" % i, [128, 384]) for i in range(2)]
                y2b = [sb(st, "y2b%d" % i, [128, 384]) for i in range(2)]
                cgb = [sb(st, "cgb%d" % i, [128, 128]) for i in range(2)]
                gsa = sb(st, "gsa", [128, 6]); gsb = sb(st, "gsb", [128, 6])
                w3 = lambda t: t[:, :].rearrange("p (h i) -> p h i", i=64)
                ti_ = 0
                for S in streams:
                    for r0 in range(0, S["T"], 128):
                        yf_, yb_, y2_, cg_ = yfb[ti_ % 2], ybb[ti_ % 2], y2b[ti_ % 2], cgb[ti_ % 2]
                        ti_ += 1
                        ld(yf_[:], S["YF"][r0:r0 + 128, :], [yf_])
                        ld(yb_[:], S["YB"][r0:r0 + 128, :], [yb_])
                        ld(cg_[:], S["CG"][:, r0:r0 + 128], [cg_])
                        tt_("pool", yb_[:, :], yb_[:, :], yf_[:, :], ALU.add, [yb_, yf_], [yb_])
                        p.op("dve", lambda e: e.tensor_reduce(out=gsa[:, :], in_=w3(yb_), axis=AX.X, op=ALU.add), reads=[yb_], writes=[gsa])
                        ts_("dve", gsa[:, :], gsa[:, :], -1.0 / 64, None, ALU.mult, ALU.bypass, [gsa], [gsa])
                        tt_("dve", w3(yb_), w3(yb_), gsa[:, :].unsqueeze(2).to_broadcast([128, 6, 64]), ALU.add, [yb_, gsa], [yb_])
                        tt_("pool", y2_[:, :], yb_[:, :], yb_[:, :], ALU.mult, [yb_], [y2_])
                        p.op("dve", lambda e: e.tensor_reduce(out=gsb[:, :], in_=w3(y2_), axis=AX.X, op=ALU.add), reads=[y2_], writes=[gsb])
                        rsqrt_(gsb[:, :], gsb[:, :], 1.0 / 64, epsc[:, 1:2], [gsb, epsc], [gsb], None)
                        tt_("dve", w3(yb_), w3(yb_), gsb[:, :].unsqueeze(2).to_broadcast([128, 6, 64]), ALU.mult, [yb_, gsb], [yb_])
                        tt_("pool", yb_[:, :], yb_[:, :], lnxg[:, :], ALU.mult, [yb_, lnxg], [yb_])
                        tt_("pool", yb_[:, :], yb_[:, :], lnxb[:, :], ALU.add, [yb_, lnxb], [yb_])
                        pg = nps()
                        mm(pg[:, 0:384], cg_[:, :], cg2[:, :], True, True, [cg_, cg2], [pg])
                        tt_("dve", y2_[:, :], yb_[:, :], pg[:, 0:384], ALU.mult, [yb_, pg], [y2_])
                        stq(S["M"][r0:r0 + 128, 640:1024], y2_[:, :], [y2_])
                p.barrier()

            with ExitStack() as st:
                wout = sb(st, "wout", [128, 8, D], BF16)
                wostg = [sb(st, "wostg%d" % i, [128, D]) for i in range(2)]
                wv = I["w_out"][l].rearrange("(c p) f -> p c f", p=128)
                for k in range(8):
                    ws_ = wostg[k % 2]
                    ld(ws_[:], wv[:, k, :], [ws_])
                    cp("pool", wout[:, k, :], ws_[:], [ws_], [wout])
                xt = sb(st, "xt2", [128, 8, 512])
                mT = sb(st, "mT", [128, 8, 512], BF16)
                sq = sb(st, "sq2", [128, 8, 512])
                rstd = sb(st, "rstd2", [128, 512])
                mtok = [sb(st, "mtok%d" % i, [128, D]) for i in range(2)]
                actb = sb(st, "actb", [128, 22, 512], BF16)
                w13f = [sb(st, "w13f_%d" % i, [128, 8, 2, 256]) for i in range(2)]
                w2f = [sb(st, "w2f_%d" % i, [128, 22, 128]) for i in range(2)]
                w13 = [sb(st, "w13_%d" % i, [128, 8, 2, 256], BF16) for i in range(2)]
                w2b = [sb(st, "w2b_%d" % i, [128, 22, 128], BF16) for i in range(2)]
                sgl = sb(st, "sgl", [128, 512])
                w1v = I["ffn_w1"][l].rearrange("(c p) f -> p c f", p=128)
                w3v = I["ffn_w3"][l].rearrange("(c p) f -> p c f", p=128)
                w2v = I["ffn_w2"][l].rearrange("(c p) f -> p c f", p=128)
                wi = [0]
                for S in streams:
                    latent = S is SS_
                    cv = 1 if latent else 0
                    T = S["T"]
                    XTv = S["XT"].rearrange("(c p) t -> p c t", p=128)
                    for t0 in range(0, T, 512):
                        ld(xt[:], XTv[:, :, t0:t0 + 512], [xt])
                        for i in range(4):
                            mk = mtok[i % 2]
                            ld(mk[:], S["M"][t0 + i * 128:t0 + (i + 1) * 128, :], [mk])
                            for half in range(2):
                                pt = nps()
                                for c in range(4):
                                    cc = half * 4 + c
                                    tr(pt[:, c * 128:(c + 1) * 128], mk[:, cc * 128:(cc + 1) * 128], ident[:, :], [mk, ident], [pt])
                                cp("act" if half == 0 else "dve", mT[:, half * 4:(half + 1) * 4, i * 128:(i + 1) * 128],
                                   pt[:, :].rearrange("p (c t) -> p c t", t=128), [pt], [mT])
                        for dc in range(8):
                            po = nps()
                            for k in range(8):
                                mm(po[:, :], wout[:, k, dc * 128:(dc + 1) * 128], mT[:, k, :], k == 0, k == 7, [wout, mT], [po])
                            stt(xt[:, dc, :], po[:, :], MOD(l, 2, dc, cv), xt[:, dc, :], ALU.mult, ALU.add, [po, mod, xt], [xt])
                        act(sq[:], xt[:], AF.Square, [xt], [sq])
                        pss = nps()
                        for c in range(8):
                            mm(pss[:, :], ones[:, 0:128], sq[:, c, :], c == 0, c == 7, [ones, sq], [pss])
                        rsqrt_(rstd[:], pss[:, :], 1.0 / D, epsc[:, 0:1], [pss, epsc], [rstd], None)
                        hh = mT
                        for c in range(8):
                            stt(sq[:, c, :], xt[:, c, :], modA2[:, l * 8 + c, cv:cv + 1], rstd[:], ALU.mult, ALU.mult, [xt, modA2, rstd], [sq])
                            act(hh[:, c, :], sq[:, c, :], AF.Identity, [sq, mod], [hh], bias=MOD(l, 3, c, cv), scale=1.0)
                        for fp in range(11):
                            wt = w13[wi[0] % 2]
                            wtf = w13f[wi[0] % 2]
                            wi[0] += 1
                            ld(wtf[:, :, 0, :], w1v[:, :, fp * 256:(fp + 1) * 256], [wtf])
                            ld(wtf[:, :, 1, :], w3v[:, :, fp * 256:(fp + 1) * 256], [wtf])
                            cp("pool", wt[:], wtf[:], [wtf], [wt])
                            for f2 in range(2):
                                fc = fp * 2 + f2
                                p1, p3 = nps(), nps()
                                for k in range(8):
                                    mm(p1[:, :], wt[:, k, 0, f2 * 128:(f2 + 1) * 128], hh[:, k, :], k == 0, k == 7, [wt, hh], [p1])
                                for k in range(8):
                                    mm(p3[:, :], wt[:, k, 1, f2 * 128:(f2 + 1) * 128], hh[:, k, :], k == 0, k == 7, [wt, hh], [p3])
                                act(sgl[:], p1[:, :], AF.Silu, [p1], [sgl])
                                tt_("dve", actb[:, fc, :], sgl[:], p3[:, :], ALU.mult, [sgl, p3], [actb])
                        for dc in range(8):
                            w2t = w2b[dc % 2]
                            w2tf = w2f[dc % 2]
                            ld(w2tf[:], w2v[:, :, dc * 128:(dc + 1) * 128], [w2tf])
                            cp("pool", w2t[:], w2tf[:], [w2tf], [w2t])
                            po = nps()
                            for fc in range(22):
                                mm(po[:, :], w2t[:, fc, :], actb[:, fc, :], fc == 0, fc == 21, [w2t, actb], [po])
                            stt(xt[:, dc, :], po[:, :], MOD(l, 5, dc, cv), xt[:, dc, :], ALU.mult, ALU.add, [po, mod, xt], [xt])
                        stq(XTv[:, :, t0:t0 + 512], xt[:], [xt])
                p.barrier()

        with ExitStack() as st:
            xt = sb(st, "xtf", [128, 8, 512])
            sq = sb(st, "sqf", [128, 8, 512])
            rstd = sb(st, "rstdf", [128, 512])
            yo = [sb(st, "yo%d" % i, [128, D]) for i in range(2)]
            for S, dst in ((SP_, O["yp"]), (SS_, O["ys"])):
                if S is SS_ and not do_sample:
                    continue
                T = S["T"]
                XTv = S["XT"].rearrange("(c p) t -> p c t", p=128)
                for t0 in range(0, T, 512):
                    ld(xt[:], XTv[:, :, t0:t0 + 512], [xt])
                    act(sq[:], xt[:], AF.Square, [xt], [sq])
                    pss = nps()
                    for c in range(8):
                        mm(pss[:, :], ones[:, 0:128], sq[:, c, :], c == 0, c == 7, [ones, sq], [pss])
                    rsqrt_(rstd[:], pss[:, :], 1.0 / D, epsc[:, 0:1], [pss, epsc], [rstd], None)
                    for c in range(8):
                        stt(sq[:, c, :], xt[:, c, :], gfc[:, c:c + 1], rstd[:], ALU.mult, ALU.mult, [xt, gfc, rstd], [sq])
                    for i in range(4):
                        y_ = yo[i % 2]
                        for half in range(2):
                            pt = nps()
                            for c in range(4):
                                cc = half * 4 + c
                                tr(pt[:, c * 128:(c + 1) * 128], sq[:, cc, i * 128:(i + 1) * 128], ident[:, :], [sq, ident], [pt])
                            cp("act" if half == 0 else "dve", y_[:, half * 512:(half + 1) * 512], pt[:, :], [pt], [y_])
                        stq(dst[t0 + i * 128:t0 + (i + 1) * 128, :], y_[:], [y_])
            p.barrier()
    return nc


def make_in_maps(inputs):
    f = lambda a: np.ascontiguousarray(np.asarray(a, dtype=np.float32))
    consts = _consts()
    shared = {k: f(inputs[k]) for k in WEIGHT_SHAPES}
    shared.update(consts)
    maps = []
    for c in range(NCORES):
        b = c % 2
        m = dict(shared)
        m["xp"] = f(inputs["x_prompt"][NPL * c:NPL * (c + 1)]).reshape(NPL * TP, D)
        m["xs"] = f(inputs["x_sample"][b])
        m["cak"] = f(inputs["cache_a_k"][b]).reshape(L, PAST, 128)
        m["cav"] = f(inputs["cache_a_v"][b]).reshape(L, PAST, 128)
        m["cbk"] = f(inputs["cache_b_k"][b]).reshape(L, PAST, 256)
        m["cbv"] = f(inputs["cache_b_v"][b]).reshape(L, PAST, 256)
        m["scf"] = f(inputs["state_c_fwd"][b])
        m["scb"] = f(inputs["state_c_bwd"][b])
        m["cs"] = f(inputs["c"][b])
        maps.append(m)
    return maps


def kernel(**inputs):
    nc = build()
    maps = make_in_maps(inputs)
    res = run_bass_kernel_spmd(nc, maps, core_ids=list(range(NCORES))).results
    yp = np.concatenate([r["yp"].reshape(NPL, TP, D) for r in res], axis=0)
    ys = np.stack([res[0]["ys"], res[1]["ys"]], axis=0)
    nak = np.concatenate([r["nak"].reshape(NPL, L, TP, 2, 64) for r in res], axis=0)
    nav = np.concatenate([r["nav"].reshape(NPL, L, TP, 2, 64) for r in res], axis=0)
    nbk = np.concatenate([r["nbk"].reshape(NPL, L, TP, 4, 2, 32) for r in res], axis=0)
    nbv = np.concatenate([r["nbv"].reshape(NPL, L, TP, 4, 64) for r in res], axis=0)
    ncf = np.concatenate([r["ncf"] for r in res], axis=0)
    ncb = np.concatenate([r["ncb"] for r in res], axis=0)
    return tuple(np.ascontiguousarray(a.astype(np.float32)) for a in (yp, ys, nak, nav, nbk, nbv, ncf, ncb))
```

```python
import math
from contextlib import ExitStack
import numpy as np
import concourse.bass as bass
import concourse.mybir as mybir
from concourse.bass_utils import run_bass_kernel_spmd

F32 = mybir.dt.float32
BF16 = mybir.dt.bfloat16
AF = mybir.ActivationFunctionType
ALU = mybir.AluOpType
AX = mybir.AxisListType

D = 1024
L = 4
TP = 256
TS = 4096
PAST = 512
NPL = 2
DFF = 2816
INC = 2944
DEC = 0.606531
EPS = 1e-6
GN_EPS = 64e-5
NCORES = 8


class Buf:
    __slots__ = ("w", "r")

    def __init__(self):
        self.w = []
        self.r = []


class TT:
    def __init__(self, h):
        self.h = h
        self.b = Buf()

    def __getitem__(self, k):
        return self.h[k]


class Prog:
    RING = 12

    def __init__(self, nc, stack):
        self.nc = nc
        self.eng = {"pe": nc.tensor, "act": nc.scalar, "dve": nc.vector, "pool": nc.gpsimd, "sp": nc.sync}
        self.semh = {}
        self.cnt = {}
        self.seen = {e: {} for e in self.eng}
        for e in ("pe", "act", "dve", "pool"):
            self.semh["S_" + e] = stack.enter_context(nc.semaphore("S_" + e))
            self.cnt[e] = 0
        self.dq = {}
        for q in ("sp", "pool"):
            ring = []
            for k in range(self.RING):
                key = "D_%s_%d" % (q, k)
                self.semh[key] = stack.enter_context(nc.semaphore(key))
                ring.append([key, 0])
            self.dq[q] = [ring, 0]
        self.n = 0

    def _waits(self, e, reads, writes):
        need = {}
        for b in reads:
            for (key, val, te) in b.w:
                if need.get(key, 0) < val:
                    need[key] = val
        for b in writes:
            for (key, val, te) in b.w:
                if te != e and need.get(key, 0) < val:
                    need[key] = val
            for (key, val, te) in b.r:
                if te != e and need.get(key, 0) < val:
                    need[key] = val
        seen = self.seen[e]
        for key, val in need.items():
            if seen.get(key, 0) < val:
                self.eng[e].wait_ge(self.semh[key], val)
                seen[key] = val
                self.n += 1

    def _record(self, tok, reads, writes):
        for b in writes:
            b.w = [tok]
            b.r = []
        for b in reads:
            b.r = [t for t in b.r if t[0] != tok[0]]
            b.r.append(tok)

    def op(self, e, fn, reads=(), writes=()):
        reads = [t.b for t in reads]
        writes = [t.b for t in writes]
        self._waits(e, reads, writes)
        ins = fn(self.eng[e])
        self.cnt[e] += 1
        ins.then_inc(self.semh["S_" + e], 1)
        self._record(("S_" + e, self.cnt[e], e), reads, writes)
        self.n += 1

    def dma(self, q, out_ap, in_ap, reads=(), writes=()):
        reads = [t.b for t in reads]
        writes = [t.b for t in writes]
        self._waits(q, reads, writes)
        ring, rr = self.dq[q]
        slot = ring[rr % self.RING]
        self.dq[q][1] = rr + 1
        key, prev = slot
        if prev > 0 and self.seen[q].get(key, 0) < prev:
            self.eng[q].wait_ge(self.semh[key], prev)
            self.seen[q][key] = prev
        ins = self.eng[q].dma_start(out=out_ap, in_=in_ap)
        ins.then_inc(self.semh[key], 16)
        slot[1] = prev + 16
        self._record((key, prev + 16, None), reads, writes)
        self.n += 2

    def barrier(self):
        targets = {}
        for e in ("pe", "act", "dve", "pool"):
            if self.cnt[e] > 0:
                targets["S_" + e] = self.cnt[e]
        for q in self.dq:
            for key, val in self.dq[q][0]:
                if val > 0:
                    targets[key] = val
        for e in self.eng:
            seen = self.seen[e]
            for key, val in targets.items():
                if key == "S_" + e:
                    continue
                if seen.get(key, 0) < val:
                    self.eng[e].wait_ge(self.semh[key], val)
                    seen[key] = val
                    self.n += 1


def _rope_tables(dh, T, grid_w=64):
    half = dh // 2
    t = np.arange(T)
    rows = t // grid_w
    cols = t % grid_w
    inv = 10000.0 ** (-np.arange(0, half, 2, dtype=np.float32) / half)
    cos = np.zeros((dh, T), np.float32)
    sin = np.zeros((dh, T), np.float32)
    perm = np.zeros((dh, dh), np.float32)
    q = half // 2
    for ax, pos in enumerate((rows, cols)):
        ang = pos[None, :].astype(np.float32) * inv[:, None]
        base = ax * half
        for i in range(q):
            cos[base + i] = np.cos(ang[i])
            cos[base + q + i] = np.cos(ang[i])
            sin[base + i] = -np.sin(ang[i])
            sin[base + q + i] = np.sin(ang[i])
            perm[base + q + i, base + i] = 1.0
            perm[base + i, base + q + i] = 1.0
    return cos, sin, perm


def _consts():
    c = {}
    c["ident"] = np.eye(128, dtype=np.float32)
    c["ones"] = np.ones((128, 512), np.float32)
    cosA, sinA, pA = _rope_tables(64, TS)
    cosB, sinB, pB = _rope_tables(32, TS)
    c["cosA"] = np.tile(cosA, (2, 1))
    c["sinA"] = np.tile(sinA, (2, 1))
    c["cosB"] = np.tile(cosB, (4, 1))
    c["sinB"] = np.tile(sinB, (4, 1))
    PA = np.zeros((128, 128), np.float32)
    PB = np.zeros((128, 128), np.float32)
    for i in range(2):
        PA[i * 64:(i + 1) * 64, i * 64:(i + 1) * 64] = pA
    for i in range(4):
        PB[i * 32:(i + 1) * 32, i * 32:(i + 1) * 32] = pB
    c["permA"] = PA
    c["permB"] = PB
    p = np.arange(128)[:, None]
    f = np.arange(128)[None, :]
    c["mprev"] = (p >= f).astype(np.float32)
    c["mnext"] = (p <= f).astype(np.float32)
    p = np.arange(64)[:, None]
    f = np.arange(64)[None, :]
    rep = lambda m: np.tile(m.astype(np.float32), (1, 6))
    c["m_lt"] = rep(p < f)
    c["m_le"] = rep(p <= f)
    c["m_gt"] = rep(p > f)
    c["m_ge"] = rep(p >= f)
    c["n_lt"] = -rep(p < f)
    c["n_le"] = -rep(p <= f)
    c["n_gt"] = -rep(p > f)
    c["n_ge"] = -rep(p >= f)
    c["i6"] = rep(p == f)
    rs = np.ones((64, 6 * 512), np.float32)
    rs[:, ::64] = 0.0
    c["rstart"] = rs
    rs2 = np.ones((64, 6 * 512), np.float32)
    rs2[:, ::128] = 0.0
    c["rstart2"] = rs2
    p = np.arange(128)[:, None]
    f = np.arange(128)[None, :]
    rep3 = lambda m: np.tile(m.astype(np.float32), (1, 3))
    c["bm_lt"] = rep3(p < f); c["bm_le"] = rep3(p <= f); c["bm_gt"] = rep3(p > f); c["bm_ge"] = rep3(p >= f)
    c["bn_lt"] = -rep3(p < f); c["bn_le"] = -rep3(p <= f); c["bn_gt"] = -rep3(p > f); c["bn_ge"] = -rep3(p >= f)
    c["bi6"] = rep3(p == f)
    return c


CONST_SHAPES = {k: v.shape for k, v in _consts().items()}

WEIGHT_SHAPES = dict(
    ada_w=(L, D, 6 * D), ada_b=(L, 6 * D), norm1_g=(L, D), norm2_g=(L, D), w_in=(L, D, INC), a_sink=(L, 6),
    b_lambda=(L, 4, 32), b_subln_g=(L, 64), c_conv=(L, 3, 1152), c_w0=(L, 2, 384), c_w2=(L, 2, 64, 384),
    c_a0=(L, 2, 384), c_a2=(L, 2, 64, 384), c_g2=(L, 128, 384), c_kk=(L, 384), c_ka=(L, 384), c_rk=(L, 6, 64),
    c_lnx_g=(L, 384), c_lnx_b=(L, 384), w_out=(L, D, D), ffn_w1=(L, D, DFF), ffn_w3=(L, D, DFF),
    ffn_w2=(L, DFF, D), final_g=(D,), c_ctx=(D,))

CORE_IN_SHAPES = dict(
    xp=(NPL * TP, D), xs=(TS, D), cak=(L, PAST, 128), cav=(L, PAST, 128), cbk=(L, PAST, 256), cbv=(L, PAST, 256),
    scf=(L, 6, 64, 64), scb=(L, 6, 64, 64), cs=(D,))

OUT_SHAPES = dict(
    yp=(NPL * TP, D), ys=(TS, D), nak=(NPL, L, TP, 128), nav=(NPL, L, TP, 128), nbk=(NPL, L, TP, 256),
    nbv=(NPL, L, TP, 256), ncf=(NPL, L, 6, 64, 64), ncb=(NPL, L, 6, 64, 64))


def build(nlayers=L, debug=False, do_sample=True):
    nc = bass.Bass("TRN2", target_bir_lowering=False)
    I = {}
    for k, s in list(WEIGHT_SHAPES.items()) + list(CORE_IN_SHAPES.items()) + list(CONST_SHAPES.items()):
        I[k] = nc.dram_tensor(k, list(s), F32, kind="ExternalInput").ap()
    O = {k: nc.dram_tensor(k, list(s), F32, kind="ExternalOutput").ap() for k, s in OUT_SHAPES.items()}
    skind = "ExternalOutput" if debug else "Internal"
    streams = []
    for sname, tt in (("p", NPL * TP), ("s", TS)):
        S = {"T": tt, "name": sname}
        for nm, shp in (("XT", [D, tt]), ("QA", [6, 64, tt]), ("KA", [2, 64, tt]), ("QB", [8, 32, tt]),
                        ("KB", [8, 32, tt]), ("VA", [tt, 128]), ("VB", [tt, 256]), ("RKV", [18, 64, tt]),
                        ("CW", [2, 64, tt]), ("CA", [2, 64, tt]), ("CG", [128, tt]), ("M", [tt, D]),
                        ("YF", [tt, 384]), ("YB", [tt, 384]), ("RKVC", [18, 64, tt])):
            S[nm] = nc.dram_tensor("%s_%s" % (nm, sname), shp, F32, kind=skind).ap()
        streams.append(S)
    SP_, SS_ = streams
    if not do_sample:
        streams = [SP_]
    seqs = [(SP_, 0, TP, False, 0), (SP_, TP, TP, False, 1)]
    if do_sample:
        seqs.append((SS_, 0, TS, True, -1))

    with ExitStack() as glob:
        p = Prog(nc, glob)

        uid = [0]

        def sb(st, name, shape, dt=F32):
            uid[0] += 1
            return TT(st.enter_context(nc.sbuf_tensor("s%d_%s" % (uid[0], name), list(shape), dt)))

        ps = [TT(glob.enter_context(nc.psum_tensor("ps%d" % i, [128, 512], F32))) for i in range(8)]
        psi = [0]

        def nps():
            t = ps[2 + psi[0] % 6]
            psi[0] += 1
            return t

        def mm(out, lhsT, rhs, start, stop, r, w):
            p.op("pe", lambda e: e.matmul(out, lhsT, rhs, start=start, stop=stop), reads=r, writes=w)

        def tr(out, in_, idn, r, w):
            p.op("pe", lambda e: e.transpose(out, in_, idn), reads=r, writes=w)

        def ld(out, in_, w, r=()):
            p.dma("sp", out, in_, reads=r, writes=w)

        def stq(out, in_, r):
            p.dma("pool", out, in_, reads=r)

        def act(out, in_, func, r, w, **kw):
            p.op("act", lambda e: e.activation(out=out, in_=in_, func=func, **kw), reads=r, writes=w)

        def tt_(eng, out, in0, in1, op, r, w):
            p.op(eng, lambda e: e.tensor_tensor(out=out, in0=in0, in1=in1, op=op), reads=r, writes=w)

        def ts_(eng, out, in0, s1, s2, op0, op1, r, w):
            if s2 is None:
                p.op(eng, lambda e: e.tensor_single_scalar(out=out, in_=in0, scalar=s1, op=op0), reads=r, writes=w)
            else:
                p.op(eng, lambda e: e.tensor_scalar(out=out, in0=in0, scalar1=s1, scalar2=s2, op0=op0, op1=op1), reads=r, writes=w)

        def stt(out, in0, scalar, in1, op0, op1, r, w):
            p.op("dve", lambda e: e.scalar_tensor_tensor(out=out, in0=in0, scalar=scalar, in1=in1, op0=op0, op1=op1), reads=r, writes=w)

        def cp(eng, out, in_, r, w):
            if eng == "act":
                p.op("act", lambda e: e.copy(out=out, in_=in_), reads=r, writes=w)
            else:
                p.op(eng, lambda e: e.tensor_copy(out=out, in_=in_), reads=r, writes=w)

        def rsqrt_(out, in_, scale, bias, r, w, tmpw):
            act(out, in_, AF.Sqrt, r, w, scale=scale, bias=bias)
            p.op("dve", lambda e: e.reciprocal(out=out, in_=out), reads=w, writes=w)

        ident = sb(glob, "ident", [128, 128])
        ones = sb(glob, "ones", [128, 512])
        epsc = sb(glob, "epsc", [128, 2])
        ld(ident[:], I["ident"][:, :], [ident])
        ld(ones[:], I["ones"][:, :], [ones])
        p.op("pool", lambda e: e.memset(epsc[:, 0:1], EPS), writes=[epsc])
        p.op("pool", lambda e: e.memset(epsc[:, 1:2], GN_EPS), writes=[epsc])

        stgc = sb(glob, "stgc", [128, 512])

        def load_cols(st, src2d, rows, w, name):
            dst = sb(st, name, [w, rows])
            for r0 in range(0, rows, 128):
                rr = min(128, rows - r0)
                ld(stgc[0:rr, 0:w], src2d[r0:r0 + rr, :], [stgc])
                pt = nps()
                tr(pt[0:w, 0:rr], stgc[0:rr, 0:w], ident[0:rr, 0:rr], [stgc, ident], [pt])
                cp("act", dst[:, r0:r0 + rr], pt[0:w, 0:rr], [pt], [dst])
            return dst

        def bcast_row(st, src_row, n, name, dst=None):
            if dst is None:
                dst = sb(st, name, [128, n])
            ld(stgc[0:1, 0:n], src_row, [stgc])
            pt = nps()
            mm(pt[:, 0:n], ones[0:1, 0:128], stgc[0:1, 0:n], True, True, [ones, stgc], [pt])
            cp("act", dst[:, 0:n], pt[:, 0:n], [pt], [dst])
            return dst

        adab = load_cols(glob, I["ada_b"].rearrange("l (j p) -> (l j) p", p=128), L * 48, 128, "adab")
        g1c = load_cols(glob, I["norm1_g"].rearrange("l (j p) -> (l j) p", p=128), L * 8, 128, "g1c")
        g2c = load_cols(glob, I["norm2_g"].rearrange("l (j p) -> (l j) p", p=128), L * 8, 128, "g2c")
        gfc = load_cols(glob, I["final_g"].rearrange("(j p) -> j p", p=128), 8, 128, "gfc")
        cct = load_cols(glob, I["c_ctx"].rearrange("(j p) -> j p", p=128), 8, 128, "cct")
        cst = load_cols(glob, I["cs"].rearrange("(j p) -> j p", p=128), 8, 128, "cst")
        convc = load_cols(glob, I["c_conv"].rearrange("l k (t c) -> (l k t) c", c=64), L * 3 * 18, 64, "convc")
        w0c = load_cols(glob, I["c_w0"].rearrange("l d (h c) -> (l d h) c", c=64), L * 12, 64, "w0c")
        a0c = load_cols(glob, I["c_a0"].rearrange("l d (h c) -> (l d h) c", c=64), L * 12, 64, "a0c")
        kkc = load_cols(glob, I["c_kk"].rearrange("l (h c) -> (l h) c", c=64), L * 6, 64, "kkc")
        kac = load_cols(glob, I["c_ka"].rearrange("l (h c) -> (l h) c", c=64), L * 6, 64, "kac")
        rkc_ = load_cols(glob, I["c_rk"].rearrange("l h c -> (l h) c"), L * 6, 64, "rkc")

        mod = sb(glob, "mod", [128, L * 48, 2])
        modA1 = sb(glob, "modA1", [128, L * 8, 2])
        modA2 = sb(glob, "modA2", [128, L * 8, 2])
        with ExitStack() as st:
            silc = sb(st, "silc", [128, 8, 2])
            act(silc[:, :, 0], cct[:, :], AF.Silu, [cct], [silc])
            act(silc[:, :, 1], cst[:, :], AF.Silu, [cst], [silc])
            pcs = [sb(st, "adapc%d" % i, [128, 8, 512]) for i in range(2)]
            k_ = 0
            for l in range(nlayers):
                wv = I["ada_w"][l].rearrange("(c p) f -> p c f", p=128)
                for pc in range(12):
                    pt_ = pcs[k_ % 2]
                    k_ += 1
                    ld(pt_[:], wv[:, :, pc * 512:(pc + 1) * 512], [pt_])
                    for jb in range(4):
                        pq = nps()
                        for k in range(8):
                            mm(pq[:, 0:2], pt_[:, k, jb * 128:(jb + 1) * 128], silc[:, k, :], k == 0, k == 7, [pt_, silc], [pq])
                        j = l * 48 + pc * 4 + jb
                        ts_("dve", mod[:, j, :], pq[:, 0:2], adab[:, j:j + 1], None, ALU.add, ALU.bypass, [pq, adab], [mod])
                stt(modA1[:, l * 8:(l + 1) * 8, :], mod[:, l * 48 + 8:l * 48 + 16, :], 1.0,
                    g1c[:, l * 8:(l + 1) * 8].unsqueeze(2).to_broadcast([128, 8, 2]), ALU.add, ALU.mult, [mod, g1c], [modA1])
                stt(modA2[:, l * 8:(l + 1) * 8, :], mod[:, l * 48 + 32:l * 48 + 40, :], 1.0,
                    g2c[:, l * 8:(l + 1) * 8].unsqueeze(2).to_broadcast([128, 8, 2]), ALU.add, ALU.mult, [mod, g2c], [modA2])
            p.barrier()

        def MOD(l, which, c, cv):
            j = l * 48 + which * 8 + c
            return mod[:, j, cv:cv + 1]

        with ExitStack() as st:
            xin = [sb(st, "xin%d" % i, [128, D]) for i in range(2)]
            xo = [sb(st, "xo%d" % i, [128, 8, 128]) for i in range(2)]
            k_ = 0
            for S, src in ((SP_, I["xp"]), (SS_, I["xs"])):
                if S is SS_ and not do_sample:
                    continue
                for b in range(S["T"] // 128):
                    a_ = xin[k_ % 2]
                    o_ = xo[k_ % 2]
                    k_ += 1
                    ld(a_[:], src[b * 128:(b + 1) * 128, :], [a_])
                    for half in range(2):
                        pt = nps()
                        for c in range(4):
                            cc = half * 4 + c
                            tr(pt[:, c * 128:(c + 1) * 128], a_[:, cc * 128:(cc + 1) * 128], ident[:, :], [a_, ident], [pt])
                        cp("act" if half == 0 else "dve", o_[:, half * 4:(half + 1) * 4, :],
                           pt[:, :].rearrange("p (c t) -> p c t", t=128), [pt], [o_])
                    stq(S["XT"].rearrange("(c p) t -> p c t", p=128)[:, :, b * 128:(b + 1) * 128], o_[:], [o_])
            p.barrier()

        for l in range(nlayers):
            lam_init = 0.8 - 0.6 * math.exp(-0.3 * l)
            with ExitStack() as st:
                win = sb(st, "win", [128, 8, INC], BF16)
                wstg = [sb(st, "wstg%d" % i, [128, INC]) for i in range(2)]
                wv = I["w_in"][l].rearrange("(c p) f -> p c f", p=128)
                for k in range(8):
                    ws_ = wstg[k % 2]
                    ld(ws_[:], wv[:, k, :], [ws_])
                    cp("pool" if k % 2 == 0 else "dve", win[:, k, :], ws_[:], [ws_], [win])
                xt = sb(st, "xt", [128, 8, 512])
                sq = sb(st, "sq", [128, 8, 512])
                h = sb(st, "h", [128, 8, 512], BF16)
                hf = sb(st, "hf", [128, 8, 512])
                rstd = sb(st, "rstd", [128, 512])
                stg = [sb(st, "stg%d" % i, [128, 512]) for i in range(3)]
                zs = sb(st, "zs", [128, 512])
                t2 = sb(st, "t2", [128, 512])
                cosA = sb(st, "cosA", [128, 512]); sinA = sb(st, "sinA", [128, 512])
                cosB = sb(st, "cosB", [128, 512]); sinB = sb(st, "sinB", [128, 512])
                permA = sb(st, "permA", [128, 128]); permB = sb(st, "permB", [128, 128])
                ld(permA[:], I["permA"][:, :], [permA]); ld(permB[:], I["permB"][:, :], [permB])
                vst = [sb(st, "vst%d" % i, [128, 768]) for i in range(2)]
                sgi = [0]
                for S in streams:
                    latent = S is SS_
                    cv = 1 if latent else 0
                    T = S["T"]
                    XTv = S["XT"].rearrange("(c p) t -> p c t", p=128)
                    for t0 in range(0, T, 512):
                        ld(xt[:], XTv[:, :, t0:t0 + 512], [xt])
                        if latent:
                            ld(cosA[:], I["cosA"][:, t0:t0 + 512], [cosA]); ld(sinA[:], I["sinA"][:, t0:t0 + 512], [sinA])
                            ld(cosB[:], I["cosB"][:, t0:t0 + 512], [cosB]); ld(sinB[:], I["sinB"][:, t0:t0 + 512], [sinB])
                        act(sq[:], xt[:], AF.Square, [xt], [sq])
                        pss = nps()
                        for c in range(8):
                            mm(pss[:, :], ones[:, 0:128], sq[:, c, :], c == 0, c == 7, [ones, sq], [pss])
                        rsqrt_(rstd[:], pss[:, :], 1.0 / D, epsc[:, 0:1], [pss, epsc], [rstd], None)
                        for c in range(8):
                            stt(hf[:, c, :], xt[:, c, :], modA1[:, l * 8 + c, cv:cv + 1], rstd[:], ALU.mult, ALU.mult, [xt, modA1, rstd], [hf])
                            act(h[:, c, :], hf[:, c, :], AF.Identity, [hf, mod], [h], bias=MOD(l, 0, c, cv), scale=1.0)
                        for j in range(23):
                            if j in (4, 9, 10):
                                continue
                            pz = nps()
                            for k in range(8):
                                mm(pz[:, :], win[:, k, j * 128:(j + 1) * 128], h[:, k, :], k == 0, k == 7, [win, h], [pz])
                            sg = stg[sgi[0] % 3]
                            sgi[0] += 1
                            rope = latent and j in (0, 1, 2, 3, 5, 6, 7, 8)
                            if rope:
                                isA = j <= 3
                                cs_, sn_, pm_ = (cosA, sinA, permA) if isA else (cosB, sinB, permB)
                                cp("act", zs[:], pz[:, :], [pz], [zs])
                                pr = nps()
                                mm(pr[:, :], pm_[:, :], zs[:], True, True, [pm_, zs], [pr])
                                tt_("dve", t2[:], pr[:, :], sn_[:], ALU.mult, [pr, sn_], [t2])
                                tt_("pool", sg[:], zs[:], cs_[:], ALU.mult, [zs, cs_], [sg])
                                tt_("pool", sg[:], sg[:], t2[:], ALU.add, [sg, t2], [sg])
                            elif j == 20:
                                act(sg[:], pz[:, :], AF.Tanh, [pz], [sg])
                            elif j == 22:
                                act(sg[:], pz[:, :], AF.Sigmoid, [pz], [sg])
                            else:
                                cp("act" if j % 2 == 0 else "dve", sg[:], pz[:, :], [pz], [sg])
                            sl = slice(t0, t0 + 512)
                            if j <= 2:
                                for hh in range(2):
                                    stq(S["QA"][2 * j + hh, :, sl], sg[hh * 64:(hh + 1) * 64, :], [sg])
                            elif j == 3:
                                for hh in range(2):
                                    stq(S["KA"][hh, :, sl], sg[hh * 64:(hh + 1) * 64, :], [sg])
                            elif j in (5, 6):
                                for q4 in range(4):
                                    stq(S["QB"][(j - 5) * 4 + q4, :, sl], sg[q4 * 32:(q4 + 1) * 32, :], [sg])
                            elif j in (7, 8):
                                for q4 in range(4):
                                    stq(S["KB"][(j - 7) * 4 + q4, :, sl], sg[q4 * 32:(q4 + 1) * 32, :], [sg])
                            elif 11 <= j <= 19:
                                for hh in range(2):
                                    stq(S["RKV"][(j - 11) * 2 + hh, :, sl], sg[hh * 64:(hh + 1) * 64, :], [sg])
                            elif j == 20:
                                for hh in range(2):
                                    stq(S["CW"][hh, :, sl], sg[hh * 64:(hh + 1) * 64, :], [sg])
                            elif j == 21:
                                for hh in range(2):
                                    stq(S["CA"][hh, :, sl], sg[hh * 64:(hh + 1) * 64, :], [sg])
                            else:
                                stq(S["CG"][:, sl], sg[:, :], [sg])
                        for i in range(4):
                            vs_ = vst[i % 2]
                            groups = [(512, 128, 0), (1152, 256, 128)]
                            if not latent:
                                groups += [(384, 128, 384), (896, 256, 512)]
                            pv = nps()
                            pk = nps()
                            for (c0, wd, o0) in groups:
                                pp, oo = (pv, o0) if o0 < 384 else (pk, o0 - 384)
                                for k in range(8):
                                    mm(pp[:, oo:oo + wd], h[:, k, i * 128:(i + 1) * 128], win[:, k, c0:c0 + wd], k == 0, k == 7, [h, win], [pp])
                            cp("act", vs_[:, 0:384], pv[:, 0:384], [pv], [vs_])
                            r0 = t0 + i * 128
                            stq(S["VA"][r0:r0 + 128, :], vs_[:, 0:128], [vs_])
                            stq(S["VB"][r0:r0 + 128, :], vs_[:, 128:384], [vs_])
                            if not latent:
                                cp("dve", vs_[:, 384:768], pk[:, 0:384], [pk], [vs_])
                                bl = r0 // TP
                                rr = r0 % TP
                                stq(O["nav"][bl, l, rr:rr + 128, :], vs_[:, 0:128], [vs_])
                                stq(O["nbv"][bl, l, rr:rr + 128, :], vs_[:, 128:384], [vs_])
                                stq(O["nak"][bl, l, rr:rr + 128, :], vs_[:, 384:512], [vs_])
                                stq(O["nbk"][bl, l, rr:rr + 128, :], vs_[:, 512:768], [vs_])
                p.barrier()

            with ExitStack() as st:
                sinkb = bcast_row(st, I["a_sink"][l:l + 1, :], 6, "sinkb")
                act(sinkb[:, 0:6], sinkb[:, 0:6], AF.Exp, [sinkb], [sinkb])
                mprev = sb(st, "mprev", [128, 128]); mnext = sb(st, "mnext", [128, 128])
                ld(mprev[:], I["mprev"][:, :], [mprev]); ld(mnext[:], I["mnext"][:, :], [mnext])
                kT = sb(st, "kT", [64, TS + PAST], BF16)
                qT = sb(st, "qT", [64, 3, TS], BF16)
                vt = sb(st, "vt", [128, (TS + PAST) // 128, 65], BF16)
                p.op("pool", lambda e: e.memset(vt[:, :, 64:65], 1.0), writes=[vt])
                stf = sb(st, "stfa", [64, 3, TS])
                vtf = sb(st, "vtfa", [128, (TS + PAST) // 128, 64])
                cstg = sb(st, "cstg", [128, 4, 64])
                pT = [sb(st, "pT%d" % i, [128, 384], BF16) for i in range(6)]
                ao = [sb(st, "ao%d" % i, [128, 3, 64]) for i in range(2)]
                den = sb(st, "den", [128, 3])
                pti = [0]
                for (S, o, T, latent, bl) in seqs:
                    nb = T // 128
                    for g in range(2):
                        ld(stf[:, 0, 0:T], S["KA"][g, :, o:o + T], [stf])
                        cp("pool", kT[:, 0:T], stf[:, 0, 0:T], [stf], [kT])
                        ld(stf[:, :, 0:T], S["QA"][3 * g:3 * g + 3, :, o:o + T].rearrange("m d t -> d m t"), [stf])
                        cp("pool", qT[:, :, 0:T], stf[:, :, 0:T], [stf], [qT])
                        ld(vtf[:, 0:nb, :], S["VA"][o:o + T, g * 64:(g + 1) * 64].rearrange("(n p) d -> p n d", p=128), [vtf])
                        nctx = 0
                        if latent:
                            nctx = PAST // 128
                            ld(cstg[:], I["cak"][l, :, g * 64:(g + 1) * 64].rearrange("(n p) d -> p n d", p=128), [cstg])
                            pt = nps()
                            for n_ in range(4):
                                tr(pt[0:64, n_ * 128:(n_ + 1) * 128], cstg[:, n_, :], ident[:, :], [cstg, ident], [pt])
                            cp("act", kT[:, T:T + PAST], pt[0:64, :], [pt], [kT])
                            ld(vtf[:, nb:nb + 4, :], I["cav"][l, :, g * 64:(g + 1) * 64].rearrange("(n p) d -> p n d", p=128), [vtf])
                        cp("pool", vt[:, 0:nb + nctx, 0:64], vtf[:, 0:nb + nctx, :], [vtf], [vt])
                        for b in range(nb):
                            if latent:
                                tiles = [(kb, (mprev if kb == b - 1 else (mnext if kb == b + 1 else None)))
                                         for kb in (b - 1, b, b + 1) if 0 <= kb < nb]
                                tiles += [(nb + c_, None) for c_ in range(nctx)]
                            else:
                                tiles = [(kb, None) for kb in range(nb)]
                            po = ps[b % 2]
                            def st1(ti, kb, msk):
                                pss = nps()
                                mm(pss[:, 0:384], kT[:, kb * 128:(kb + 1) * 128], qT[:, :, b * 128:(b + 1) * 128], True, True, [kT, qT], [pss])
                                pt_ = pT[pti[0] % 6]
                                pti[0] += 1
                                act(pt_[:], pss[:, 0:384], AF.Exp, [pss], [pt_], scale=0.125)
                                if msk is not None:
                                    tt_("pool", pt_[:, :].rearrange("p (m q) -> p m q", q=128), pt_[:, :].rearrange("p (m q) -> p m q", q=128),
                                        msk[:, :].unsqueeze(1).to_broadcast([128, 3, 128]), ALU.mult, [pt_, msk], [pt_])
                                return pt_

                            def st2(ti, kb, pt_):
                                for m in range(3):
                                    mm(po[:, m * 65:(m + 1) * 65], pt_[:, m * 128:(m + 1) * 128], vt[:, kb, :], ti == 0 and m == 0, ti == len(tiles) - 1 and m == 2, [pt_, vt], [po])

                            pend = []
                            for ti, (kb, msk) in enumerate(tiles):
                                pend.append((ti, kb, st1(ti, kb, msk)))
                                if len(pend) > 2:
                                    st2(*pend.pop(0))
                            for it_ in pend:
                                st2(*it_)
                            pov = po[:, 0:195].rearrange("p (m d) -> p m d", d=65)
                            tt_("dve", den[:, :].unsqueeze(2), pov[:, :, 64:65], sinkb[:, 3 * g:3 * g + 3].unsqueeze(2), ALU.add, [po, sinkb], [den])
                            p.op("dve", lambda e: e.reciprocal(out=den[:, :], in_=den[:, :]), reads=[den], writes=[den])
                            a_ = ao[b % 2]
                            tt_("dve", a_[:], pov[:, :, 0:64], den[:, :].unsqueeze(2).to_broadcast([128, 3, 64]), ALU.mult, [po, den], [a_])
                            r0 = o + b * 128
                            stq(S["M"][r0:r0 + 128, g * 192:(g + 1) * 192], a_[:].rearrange("p m d -> p (m d)"), [a_])
                p.barrier()

            with ExitStack() as st:
                lamr = bcast_row(st, I["b_lambda"][l:l + 1].rearrange("o a d -> o (a d)"), 128, "lamr")
                lam2 = sb(st, "lam2", [128, 2, 32])
                lv = lamr[:, :].rearrange("p (a b d) -> p a b d", a=2, b=2)
                tt_("dve", lam2[:], lv[:, :, 0, :], lv[:, :, 1, :], ALU.mult, [lamr], [lam2])
                lam1 = sb(st, "lam1", [128, 2])
                p.op("dve", lambda e: e.tensor_reduce(out=lam1[:, :], in_=lam2[:], axis=AX.X, op=ALU.add), reads=[lam2], writes=[lam1])
                act(lam1[:, :], lam1[:, :], AF.Exp, [lam1], [lam1])
                nlam = sb(st, "nlam", [128, 1])
                tt_("dve", nlam[:, :], lam1[:, 1:2], lam1[:, 0:1], ALU.subtract, [lam1], [nlam])
                ts_("dve", nlam[:, :], nlam[:, :], -lam_init, None, ALU.add, ALU.bypass, [nlam], [nlam])
                gsub = bcast_row(st, I["b_subln_g"][l:l + 1, :], 64, "gsub")
                ts_("dve", gsub[:, :], gsub[:, :], 1.0 - lam_init, None, ALU.mult, ALU.bypass, [gsub], [gsub])
                kT = sb(st, "kTb", [32, 2, TS + PAST], BF16)
                qT = sb(st, "qTb", [32, 2, TS], BF16)
                vt = sb(st, "vtb", [128, (TS + PAST) // 128, 65], BF16)
                p.op("pool", lambda e: e.memset(vt[:, :, 64:65], 1.0), writes=[vt])
                stf = sb(st, "stfb", [32, 2, TS])
                vtf = sb(st, "vtfb", [128, (TS + PAST) // 128, 64])
                cstg = sb(st, "cstgb", [128, 4, 32])
                pT = [sb(st, "pTb%d" % i, [128, 512], BF16) for i in range(6)]
                o1 = sb(st, "o1", [128, 4, 64]); o2 = sb(st, "o2", [128, 4, 64]); o3 = sb(st, "o3", [128, 4, 64])
                rd = sb(st, "rd", [128, 2, 4]); ssq = sb(st, "ssq", [128, 4])
                bo = [sb(st, "bo%d" % i, [128, 4, 64]) for i in range(2)]
                pti = [0]
                boi = [0]
                for (S, o, T, latent, bl) in seqs:
                    nk = T // 128 + (PAST // 128 if latent else 0)
                    nown = T // 128
                    qn = min(512, T)
                    for hd in range(4):
                        ld(vtf[:, 0:nown, :], S["VB"][o:o + T, hd * 64:(hd + 1) * 64].rearrange("(n p) d -> p n d", p=128), [vtf])
                        if latent:
                            ld(vtf[:, nown:nk, :], I["cbv"][l, :, hd * 64:(hd + 1) * 64].rearrange("(n p) d -> p n d", p=128), [vtf])
                        cp("pool", vt[:, 0:nk, 0:64], vtf[:, 0:nk, :], [vtf], [vt])
                        ld(stf[:, :, 0:T], S["KB"][2 * hd:2 * hd + 2, :, o:o + T].rearrange("m d t -> d m t"), [stf])
                        cp("pool", kT[:, :, 0:T], stf[:, :, 0:T], [stf], [kT])
                        ld(stf[:, :, 0:T], S["QB"][2 * hd:2 * hd + 2, :, o:o + T].rearrange("m d t -> d m t"), [stf])
                        cp("pool", qT[:, :, 0:T], stf[:, :, 0:T], [stf], [qT])
                        if latent:
                            for mp in range(2):
                                c0 = hd * 64 + mp * 32
                                ld(cstg[:], I["cbk"][l, :, c0:c0 + 32].rearrange("(n p) d -> p n d", p=128), [cstg])
                                pt = nps()
                                for n_ in range(4):
                                    tr(pt[0:32, n_ * 128:(n_ + 1) * 128], cstg[:, n_, :], ident[:, :], [cstg, ident], [pt])
                                cp("act", kT[:, mp, T:T + PAST], pt[0:32, :], [pt], [kT])
                        for q0 in range(0, T, qn):
                            nqb = qn // 128
                            pos = [ps[0], ps[1]]
                            def st1(mp, kt):
                                pss = nps()
                                mm(pss[:, 0:qn], kT[:, mp, kt * 128:(kt + 1) * 128], qT[:, mp, q0:q0 + qn], True, True, [kT, qT], [pss])
                                pt_ = pT[pti[0] % 6]
                                pti[0] += 1
                                act(pt_[:, 0:qn], pss[:, 0:qn], AF.Exp, [pss], [pt_], scale=32.0 ** -0.5)
                                return pt_

                            def st2(mp, kt, pt_):
                                for qb in range(nqb):
                                    mm(pos[mp][:, qb * 65:(qb + 1) * 65], pt_[:, qb * 128:(qb + 1) * 128], vt[:, kt, :], kt == 0 and qb == 0, kt == nk - 1 and qb == nqb - 1, [pt_, vt], [pos[mp]])

                            pend = []
                            for mp in range(2):
                                for kt in range(nk):
                                    pend.append((mp, kt, st1(mp, kt)))
                                    if len(pend) > 2:
                                        st2(*pend.pop(0))
                            for it_ in pend:
                                st2(*it_)
                            v1 = pos[0][:, 0:nqb * 65].rearrange("p (q d) -> p q d", d=65)
                            v2 = pos[1][:, 0:nqb * 65].rearrange("p (q d) -> p q d", d=65)
                            p.op("dve", lambda e: e.reciprocal(out=rd[:, 0, 0:nqb].unsqueeze(2), in_=v1[:, :, 64:65]), reads=[pos[0]], writes=[rd])
                            p.op("dve", lambda e: e.reciprocal(out=rd[:, 1, 0:nqb].unsqueeze(2), in_=v2[:, :, 64:65]), reads=[pos[1]], writes=[rd])
                            tt_("dve", o1[:, 0:nqb, :], v1[:, :, 0:64], rd[:, 0, 0:nqb].unsqueeze(2).to_broadcast([128, nqb, 64]), ALU.mult, [pos[0], rd], [o1])
                            tt_("dve", o2[:, 0:nqb, :], v2[:, :, 0:64], rd[:, 1, 0:nqb].unsqueeze(2).to_broadcast([128, nqb, 64]), ALU.mult, [pos[1], rd], [o2])
                            stt(o1[:, 0:nqb, :], o2[:, 0:nqb, :], nlam[:, 0:1], o1[:, 0:nqb, :], ALU.mult, ALU.add, [o1, o2, nlam], [o1])
                            tt_("pool", o3[:, 0:nqb, :], o1[:, 0:nqb, :], o1[:, 0:nqb, :], ALU.mult, [o1], [o3])
                            p.op("dve", lambda e: e.tensor_reduce(out=ssq[:, 0:nqb], in_=o3[:, 0:nqb, :], axis=AX.X, op=ALU.add), reads=[o3], writes=[ssq])
                            rsqrt_(ssq[:, 0:nqb], ssq[:, 0:nqb], 1.0 / 64, epsc[:, 0:1], [ssq, epsc], [ssq], None)
                            b_ = bo[boi[0] % 2]
                            boi[0] += 1
                            tt_("dve", b_[:, 0:nqb, :], o1[:, 0:nqb, :], ssq[:, 0:nqb].unsqueeze(2).to_broadcast([128, nqb, 64]), ALU.mult, [o1, ssq], [b_])
                            tt_("pool", b_[:, 0:nqb, :], b_[:, 0:nqb, :], gsub[:, 0:64].unsqueeze(1).to_broadcast([128, nqb, 64]), ALU.mult, [b_, gsub], [b_])
                            r0 = o + q0
                            stq(S["M"][r0:r0 + qn, 384 + hd * 64:384 + (hd + 1) * 64].rearrange("(q p) d -> p q d", p=128), b_[:, 0:nqb, :], [b_])
                p.barrier()

            with ExitStack() as st:
                CT = 512
                rx = [sb(st, "rx%d" % i, [64, 18, CT + 2]) for i in range(2)]
                ro = [sb(st, "ro%d" % i, [64, 18, CT]) for i in range(2)]
                ti_ = 0
                for (S, o, T, latent, bl) in seqs:
                    for ts0 in range(0, T, CT):
                        n = min(CT, T - ts0)
                        x_, o_ = rx[ti_ % 2], ro[ti_ % 2]
                        ti_ += 1
                        lo = max(ts0 - 1, 0)
                        hi = min(ts0 + n + 1, T)
                        if ts0 == 0:
                            p.op("pool", lambda e: e.memset(x_[:, :, 0:1], 0.0), writes=[x_])
                        if ts0 + n == T:
                            p.op("pool", lambda e: e.memset(x_[:, :, n + 1:n + 2], 0.0), writes=[x_])
                        ld(x_[:, :, lo - (ts0 - 1):hi - (ts0 - 1)], S["RKV"][:, :, o + lo:o + hi].rearrange("c d t -> d c t"), [x_])
                        for c in range(18):
                            cb = (l * 3) * 18 + c
                            act(o_[:, c, 0:n], x_[:, c, 1:n + 1], AF.Copy, [x_, convc], [o_], scale=convc[:, cb + 18:cb + 19])
                            stt(o_[:, c, 0:n], x_[:, c, 0:n], convc[:, cb:cb + 1], o_[:, c, 0:n], ALU.mult, ALU.add, [x_, convc, o_], [o_])
                            stt(o_[:, c, 0:n], x_[:, c, 2:n + 2], convc[:, cb + 36:cb + 37], o_[:, c, 0:n], ALU.mult, ALU.add, [x_, convc, o_], [o_])
                        stq(S["RKVC"][:, :, o + ts0:o + ts0 + n].rearrange("c d t -> d c t"), o_[:, :, 0:n], [o_])
                p.barrier()

            with ExitStack() as st:
                CH = 128
                NSG = 128
                NH = 3
                W3 = NH * CH
                V3 = NH * 64
                NLEV = 6
                cw2 = sb(st, "cw2", [64, 2, 384]); ca2 = sb(st, "ca2", [64, 2, 384])
                ld(cw2[:], I["c_w2"][l].rearrange("d r c -> r d c"), [cw2])
                ld(ca2[:], I["c_a2"][l].rearrange("d r c -> r d c"), [ca2])
                MK = {}
                for nm in ("m_lt", "m_le", "m_gt", "m_ge", "n_lt", "n_le", "n_gt", "n_ge", "i6"):
                    MK[nm] = sb(st, nm, [CH, W3])
                    ld(MK[nm][:], I["b" + nm][:, :], [MK[nm]])
                rstart = sb(st, "rstart", [64, NH * NSG])
                ld(rstart[:], I["rstart2"][:, 0:NH * NSG], [rstart])
                onesb = sb(st, "onesb", [CH, 2], BF16)
                p.op("pool", lambda e: e.memset(onesb[:], 1.0), writes=[onesb])
                KKB = sb(st, "KKB", [64, 6, NSG]); KAB = sb(st, "KAB", [64, 6, NSG]); RKB = sb(st, "RKB", [64, 6, NSG])
                for (dst_, src_) in ((KKB, kkc), (KAB, kac), (RKB, rkc_)):
                    cp("pool", dst_[:], src_[:, l * 6:(l + 1) * 6].unsqueeze(2).to_broadcast([64, 6, NSG]), [src_], [dst_])
                v3 = lambda t: t[:, :].rearrange("p (h i) -> p h i", i=64)
                hw = lambda t, hi_: t[:, hi_ * CH:(hi_ + 1) * CH]
                hv = lambda t, hi_: t[:, hi_ * 64:(hi_ + 1) * 64]
                SHT = {nm: sb(st, "sh_" + nm, [64, NH, NSG]) for nm in ("sig", "a_", "kk", "kd", "bb", "Lc", "Lx", "Ld", "E", "tmp")}
                CB = []
                for ci_ in range(4):
                    Bd = dict(SHT)
                    Bd["rkv2"] = [sb(st, "rkv%d_%d" % (ci_, i_), [64, 3 * NH, NSG]) for i_ in range(2)]
                    Bd["cwt2"] = [sb(st, "cwt%d_%d" % (ci_, i_), [64, NSG]) for i_ in range(2)]
                    Bd["cat2"] = [sb(st, "cat%d_%d" % (ci_, i_), [64, NSG]) for i_ in range(2)]
                    Bd["yb2"] = [sb(st, "yb%d_%d" % (ci_, i_), [CH, V3]) for i_ in range(2)]
                    for nm in ("Dg", "nBg"):
                        Bd[nm] = sb(st, "%s%d" % (nm, ci_), [64, NH, NSG])
                    for nm in ("rkc", "Kt_", "Rt_", "Dh", "Bh"):
                        Bd[nm] = sb(st, "%s%d" % (nm, ci_), [64, NH, NSG], BF16)
                    Bd["gC"] = sb(st, "gC%d" % ci_, [64, NH, NSG // CH])
                    Bd["H"] = sb(st, "H%d" % ci_, [64, NH, 64]); Bd["hst"] = sb(st, "hst%d" % ci_, [64, NH, 64])
                    Bd["Hb"] = sb(st, "Hb%d" % ci_, [64, NH, 64], BF16)
                    for nm in ("Pa", "Pb", "PTa", "PTb", "Xa", "Xb", "AdT", "BdT", "nBbT"):
                        Bd[nm] = sb(st, "%s%d" % (nm, ci_), [CH, W3], BF16)
                    for nm in ("Vt", "Dgt", "nBgt", "Wb", "Zb"):
                        Bd[nm] = sb(st, "%s%d" % (nm, ci_), [CH, V3], BF16)
                    for nm in ("y2", "Vf"):
                        Bd[nm] = sb(st, "%s%d" % (nm, ci_), [CH, V3])
                    Bd["yb"] = None
                    Bd["coef"] = sb(st, "coef%d" % ci_, [CH, NH])
                    CB.append(Bd)

                def chain(S, o, T, latent, bl, d, h0, Bd):
                    rkc = Bd["rkc"]
                    ybi = [0]
                    pst = []
                    sig, a_, kk, kd, bb, Lc, Lx, Ld, E, tmp = (Bd[k] for k in ("sig", "a_", "kk", "kd", "bb", "Lc", "Lx", "Ld", "E", "tmp"))
                    Kt_, Rt_, Dh, Bh, Dg, nBg, gC, H, hst = (Bd[k] for k in ("Kt_", "Rt_", "Dh", "Bh", "Dg", "nBg", "gC", "H", "hst"))
                    Hb, Vf = Bd["Hb"], Bd["Vf"]
                    Vt, Dgt, nBgt, AdT, BdT, nBbT, Wb, Zb, yb, y2, coef = (Bd[k] for k in ("Vt", "Dgt", "nBgt", "AdT", "BdT", "nBbT", "Wb", "Zb", "yb", "y2", "coef"))
                    Pq = [Bd["Pa"], Bd["Pb"]]; PTq = [Bd["PTa"], Bd["PTb"]]; Xq = [Bd["Xa"], Bd["Xb"]]
                    nsg = T // NSG
                    YD = S["YF" if d == 0 else "YB"]
                    HR = range(NH)
                    if latent:
                        src = I["scf" if d == 0 else "scb"][l]
                        ld(hst[:], src[h0:h0 + NH].rearrange("h i j -> i h j"), [hst])
                        pt = nps()
                        for hi_ in HR:
                            tr(pt[0:64, hi_ * 64:(hi_ + 1) * 64], hst[:, hi_, :], ident[0:64, 0:64], [hst, ident], [pt])
                        cp("act", H[:].rearrange("p h i -> p (h i)"), pt[0:64, 0:V3], [pt], [H])
                    else:
                        p.op("pool", lambda e: e.memset(H[:], 0.0), writes=[H])
                    cp("pool", Hb[:], H[:], [H], [Hb])
                    segs = list(range(nsg)) if d == 0 else list(range(nsg - 1, -1, -1))
                    if d == 0:
                        nmAT, mAT, nmA, mBT, nmBT = MK["n_lt"], MK["m_lt"], MK["n_gt"], MK["m_le"], MK["n_le"]
                    else:
                        nmAT, mAT, nmA, mBT, nmBT = MK["n_gt"], MK["m_gt"], MK["n_lt"], MK["m_ge"], MK["n_ge"]
                    for si_, sg_ in enumerate(segs):
                        rkv, cwt, cat = Bd["rkv2"][si_ % 2], Bd["cwt2"][si_ % 2], Bd["cat2"][si_ % 2]
                        ts0 = sg_ * NSG
                        n = NSG
                        nch = n // CH
                        for part in range(3):
                            ld(rkv[:, part * NH:(part + 1) * NH, :], S["RKVC"][part * 6 + h0:part * 6 + h0 + NH, :, o + ts0:o + ts0 + n].rearrange("c d t -> d c t"), [rkv])
                        ld(cwt[:, :], S["CW"][d, :, o + ts0:o + ts0 + n], [cwt])
                        ld(cat[:, :], S["CA"][d, :, o + ts0:o + ts0 + n], [cat])
                        rr_ = rkv[:, 0:NH, :]
                        kr_ = rkv[:, NH:2 * NH, :]
                        pw = nps(); pa = nps()
                        for hi_ in HR:
                            h_ = h0 + hi_
                            ci = (l * 2 + d) * 6 + h_
                            mm(pw[0:64, hi_ * n:(hi_ + 1) * n], cw2[:, d, h_ * 64:(h_ + 1) * 64], cwt[:, :], True, True, [cw2, cwt], [pw])
                            act(sig[:, hi_, :], pw[0:64, hi_ * n:(hi_ + 1) * n], AF.Sigmoid, [pw, w0c], [sig], bias=w0c[:, ci:ci + 1], scale=1.0)
                            mm(pa[0:64, hi_ * n:(hi_ + 1) * n], ca2[:, d, h_ * 64:(h_ + 1) * 64], cat[:, :], True, True, [ca2, cat], [pa])
                            act(a_[:, hi_, :], pa[0:64, hi_ * n:(hi_ + 1) * n], AF.Sigmoid, [pa, a0c], [a_], bias=a0c[:, ci:ci + 1], scale=1.0)
                        fl = lambda t: t[:].rearrange("p h t -> p (h t)")
                        tt_("dve", kk[:], kr_, KKB[:, h0:h0 + NH, :], ALU.mult, [rkv, KKB], [kk])
                        tt_("pool", tmp[:], kk[:], kk[:], ALU.mult, [kk], [tmp])
                        pk_ = nps()
                        mm(pk_[0:64, 0:NH * n], ones[0:64, 0:64], fl(tmp), True, True, [ones, tmp], [pk_])
                        act(fl(Ld), pk_[0:64, 0:NH * n], AF.Sqrt, [pk_], [Ld])
                        ts_("pool", Ld[:], Ld[:], 1e-12, None, ALU.max, ALU.bypass, [Ld], [Ld])
                        p.op("dve", lambda e: e.reciprocal(out=Ld[:], in_=Ld[:]), reads=[Ld], writes=[Ld])
                        tt_("dve", kk[:], kk[:], Ld[:], ALU.mult, [kk, Ld], [kk])
                        stt(kd[:], a_[:], -1.0, KAB[:, h0:h0 + NH, :], ALU.add, ALU.mult, [a_, KAB], [kd])
                        stt(kd[:], kd[:], 1.0, kr_, ALU.add, ALU.mult, [kd, rkv], [kd])
                        tt_("pool", tmp[:], rr_, RKB[:, h0:h0 + NH, :], ALU.mult, [rkv, RKB], [tmp])
                        tt_("pool", rkc[:], tmp[:], kd[:], ALU.mult, [tmp, kd], [rkc])
                        tt_("pool", bb[:], kk[:], a_[:], ALU.mult, [kk, a_], [bb])
                        p.op("dve", lambda e: e.tensor_tensor_scan(out=fl(Lc), data0=rstart[:, :], data1=fl(sig), initial=0.0, op0=ALU.mult, op1=ALU.add),
                             reads=[rstart, sig], writes=[Lc])
                        tt_("pool", Lx[:], Lc[:], sig[:], ALU.subtract, [Lc, sig], [Lx])
                        c4 = lambda t: t[:].rearrange("p h (c s) -> p h c s", s=CH)
                        Ltot = c4(Lc)[:, :, :, CH - 1:CH]
                        tt_("dve", c4(Ld), Ltot.to_broadcast([64, NH, nch, CH]), c4(Lc), ALU.subtract, [Lc], [Ld])
                        act(gC[:, :, 0:nch].unsqueeze(3), Ltot, AF.Exp, [Lc], [gC], scale=-DEC)
                        if d == 0:
                            act(E[:], Lx[:], AF.Exp, [Lx], [E], scale=-DEC)
                            tt_("dve", Kt_[:], kk[:], E[:], ALU.mult, [kk, E], [Kt_])
                            act(E[:], Lc[:], AF.Exp, [Lc], [E], scale=-DEC)
                            tt_("dve", Rt_[:], rr_, E[:], ALU.mult, [rkv, E], [Rt_])
                            act(E[:], Lc[:], AF.Exp, [Lc], [E], scale=DEC)
                        else:
                            act(E[:], Ld[:], AF.Exp, [Ld], [E], scale=-DEC)
                            tt_("dve", Kt_[:], kk[:], E[:], ALU.mult, [kk, E], [Kt_])
                            tt_("dve", c4(tmp), Ltot.to_broadcast([64, NH, nch, CH]), c4(Lx), ALU.subtract, [Lc, Lx], [tmp])
                            act(E[:], tmp[:], AF.Exp, [tmp], [E], scale=-DEC)
                            tt_("dve", Rt_[:], rr_, E[:], ALU.mult, [rkv, E], [Rt_])
                            act(E[:], tmp[:], AF.Exp, [tmp], [E], scale=DEC)
                        tt_("dve", Dh[:], kd[:], E[:], ALU.mult, [kd, E], [Dh])
                        tt_("pool", Bh[:], bb[:], E[:], ALU.mult, [bb, E], [Bh])
                        act(E[:], (Ld if d == 0 else Lx)[:], AF.Exp, [Ld, Lx], [E], scale=-DEC)
                        tt_("dve", Dg[:], kd[:], E[:], ALU.mult, [kd, E], [Dg])
                        stt(nBg[:], bb[:], -1.0, E[:], ALU.mult, ALU.mult, [bb, E], [nBg])
                        yield
                        chunks = list(range(nch)) if d == 0 else list(range(nch - 1, -1, -1))
                        for c in chunks:
                            cs = slice(c * CH, (c + 1) * CH)
                            pt = nps()
                            for hi_ in HR:
                                tr(pt[0:CH, hi_ * 64:(hi_ + 1) * 64], rkv[:, 2 * NH + hi_, cs], ident[0:64, 0:64], [rkv, ident], [pt])
                            cp("act", Vt[:, :], pt[0:CH, 0:V3], [pt], [Vt])
                            cp("dve", Vf[:, :], pt[0:CH, 0:V3], [pt], [Vf])
                            for (src_t, dst_t) in ((Dg, Dgt), (nBg, nBgt)):
                                pt = nps()
                                for hi_ in HR:
                                    tr(pt[0:CH, hi_ * 64:(hi_ + 1) * 64], src_t[:, hi_, cs], ident[0:64, 0:64], [src_t, ident], [pt])
                                cp("act", dst_t[:, :], pt[0:CH, 0:V3], [pt], [dst_t])
                            pAbT, pBbT, pAdT, pBdT, pAb = nps(), nps(), nps(), nps(), nps()
                            for hi_ in HR:
                                hs = slice(hi_ * CH, (hi_ + 1) * CH)
                                mm(pAbT[0:CH, hs], Bh[:, hi_, cs], Kt_[:, hi_, cs], True, True, [Bh, Kt_], [pAbT])
                                mm(pBbT[0:CH, hs], Bh[:, hi_, cs], Rt_[:, hi_, cs], True, True, [Bh, Rt_], [pBbT])
                                mm(pAdT[0:CH, hs], Dh[:, hi_, cs], Kt_[:, hi_, cs], True, True, [Dh, Kt_], [pAdT])
                                mm(pBdT[0:CH, hs], Dh[:, hi_, cs], Rt_[:, hi_, cs], True, True, [Dh, Rt_], [pBdT])
                                mm(pAb[0:CH, hs], Kt_[:, hi_, cs], Bh[:, hi_, cs], True, True, [Kt_, Bh], [pAb])
                            tt_("dve", PTq[0][:, :], pAbT[0:CH, 0:W3], nmAT[:, :], ALU.mult, [pAbT, nmAT], [PTq[0]])
                            tt_("dve", Pq[0][:, :], pAb[0:CH, 0:W3], nmA[:, :], ALU.mult, [pAb, nmA], [Pq[0]])
                            tt_("pool", Xq[0][:, :], PTq[0][:, :], MK["i6"][:, :], ALU.add, [PTq[0], MK["i6"]], [Xq[0]])
                            tt_("dve", AdT[:, :], pAdT[0:CH, 0:W3], mAT[:, :], ALU.mult, [pAdT, mAT], [AdT])
                            tt_("dve", BdT[:, :], pBdT[0:CH, 0:W3], mBT[:, :], ALU.mult, [pBdT, mBT], [BdT])
                            tt_("dve", nBbT[:, :], pBbT[0:CH, 0:W3], nmBT[:, :], ALU.mult, [pBbT, nmBT], [nBbT])
                            yield
                            for k in range(1, NLEV + 1):
                                Pc, PTc = Pq[(k - 1) % 2], PTq[(k - 1) % 2]
                                Pn, PTn = Pq[k % 2], PTq[k % 2]
                                pP = nps()
                                for hi_ in HR:
                                    hs = slice(hi_ * CH, (hi_ + 1) * CH)
                                    mm(pP[0:CH, hs], hw(PTc, hi_), hw(Pc, hi_), True, True, [PTc, Pc], [pP])
                                if k < NLEV:
                                    pPT = nps()
                                    for hi_ in HR:
                                        hs = slice(hi_ * CH, (hi_ + 1) * CH)
                                        mm(pPT[0:CH, hs], hw(Pc, hi_), hw(PTc, hi_), True, True, [PTc, Pc], [pPT])
                                if k >= 2:
                                    Xo, Xn = Xq[k % 2], Xq[(k - 1) % 2]
                                    pX = nps()
                                    for hi_ in HR:
                                        hs = slice(hi_ * CH, (hi_ + 1) * CH)
                                        mm(pX[0:CH, hs], hw(Pc, hi_), hw(Xo, hi_), True, True, [Pc, Xo], [pX])
                                    tt_("dve", Xn[:, :], pX[0:CH, 0:W3], Xo[:, :], ALU.add, [pX, Xo], [Xn])
                                cp("act", Pn[:, :], pP[0:CH, 0:W3], [pP], [Pn])
                                if k < NLEV:
                                    cp("dve", PTn[:, :], pPT[0:CH, 0:W3], [pPT], [PTn])
                                yield
                            PL, XL0, XL1 = Pq[NLEV % 2], Xq[(NLEV - 1) % 2], Xq[NLEV % 2]
                            pX = nps()
                            for hi_ in HR:
                                hs = slice(hi_ * CH, (hi_ + 1) * CH)
                                mm(pX[0:CH, hs], hw(PL, hi_), hw(XL0, hi_), True, True, [PL, XL0], [pX])
                            pW = nps()
                            for hi_ in HR:
                                hs = slice(hi_ * 64, (hi_ + 1) * 64)
                                mm(pW[0:CH, hs], Kt_[:, hi_, cs], Hb[:, hi_, :], True, False, [Kt_, Hb], [pW])
                                mm(pW[0:CH, hs], hw(AdT, hi_), hv(Vt, hi_), False, True, [AdT, Vt], [pW])
                            tt_("dve", XL1[:, :], pX[0:CH, 0:W3], XL0[:, :], ALU.add, [pX, XL0], [XL1])
                            cp("act", Wb[:, :], pW[0:CH, 0:V3], [pW], [Wb])
                            yield
                            pZ = nps()
                            for hi_ in HR:
                                hs = slice(hi_ * 64, (hi_ + 1) * 64)
                                mm(pZ[0:CH, hs], hw(XL1, hi_), hv(Wb, hi_), True, True, [XL1, Wb], [pZ])
                            cp("act", Zb[:, :], pZ[0:CH, 0:V3], [pZ], [Zb])
                            yield
                            Hf = H[:].rearrange("p h i -> p (h i)")
                            pY, pC, pH = nps(), nps(), nps()
                            for hi_ in HR:
                                hs = slice(hi_ * 64, (hi_ + 1) * 64)
                                mm(pY[0:CH, hs], Rt_[:, hi_, cs], Hb[:, hi_, :], True, False, [Rt_, Hb], [pY])
                                mm(pY[0:CH, hs], hw(BdT, hi_), hv(Vt, hi_), False, False, [BdT, Vt], [pY])
                                mm(pY[0:CH, hs], hw(nBbT, hi_), hv(Zb, hi_), False, True, [nBbT, Zb], [pY])
                                mm(pC[0:CH, hi_:hi_ + 1], rkc[:, hi_, cs], onesb[0:64, 0:1], True, True, [rkc, onesb], [pC])
                                mm(pH[0:64, hs], hv(Dgt, hi_), hv(Vt, hi_), True, False, [Dgt, Vt], [pH])
                                mm(pH[0:64, hs], hv(nBgt, hi_), hv(Zb, hi_), False, True, [nBgt, Zb], [pH])
                            cp("act", coef[:, :], pC[0:CH, 0:NH], [pC], [coef])
                            tt_("pool", v3(y2), v3(Vf), coef[:, :].unsqueeze(2).to_broadcast([CH, NH, 64]), ALU.mult, [Vf, coef], [y2])
                            yb = Bd["yb2"][ybi[0] % 2]
                            ybi[0] += 1
                            tt_("dve", yb[:, :], pY[0:CH, 0:V3], y2[:, :], ALU.add, [pY, y2], [yb])
                            tt_("pool", H[:], H[:], gC[:, :, c:c + 1].to_broadcast([64, NH, 64]), ALU.mult, [H, gC], [H])
                            tt_("dve", Hf, Hf, pH[0:64, 0:V3], ALU.add, [H, pH], [H])
                            cp("pool", Hb[:], H[:], [H], [Hb])
                            r0 = o + ts0 + c * CH
                            pst.append((r0, yb))
                            yield
                            while pst:
                                r0_, yb_ = pst.pop(0)
                                ld(YD[r0_:r0_ + CH, h0 * 64:(h0 + NH) * 64], yb_[:, :], [], [yb_])
                    if not latent:
                        pt = nps()
                        for hi_ in HR:
                            tr(pt[0:64, hi_ * 64:(hi_ + 1) * 64], H[:, hi_, :], ident[0:64, 0:64], [H, ident], [pt])
                        cp("act", hst[:].rearrange("p h j -> p (h j)"), pt[0:64, 0:V3], [pt], [hst])
                        stq(O["ncf" if d == 0 else "ncb"][bl, l, h0:h0 + NH].rearrange("h i j -> i h j"), hst[:], [hst])

                for (S, o, T, latent, bl) in seqs:
                    alive = [chain(S, o, T, latent, bl, d_, h0_, CB[d_ * 2 + h0_ // 3]) for d_ in range(2) for h0_ in (0, 3)]
                    while alive:
                        for g_ in list(alive):
                            try:
                                next(g_)
                            except StopIteration:
                                alive.remove(g_)
                p.barrier()

            with ExitStack() as st:
                cg2 = sb(st, "cg2", [128, 384])
                ld(cg2[:], I["c_g2"][l], [cg2])
                lnxg = bcast_row(st, I["c_lnx_g"][l:l + 1, :], 384, "lnxg")
                lnxb = bcast_row(st, I["c_lnx_b"][l:l + 1, :], 384, "lnxb")
                yfb = [sb(st, "yfb%d" % i, [128, 384]) for i in range(2)]
                ybb = [sb(st, "ybb%d" % i, [128, 384]) for i in range(2)]
                y2b = [sb(st, "y2b%d" % i, [128, 384]) for i in range(2)]
                cgb = [sb(st, "cgb%d" % i, [128, 128]) for i in range(2)]
                gsa = sb(st, "gsa", [128, 6]); gsb = sb(st, "gsb", [128, 6])
                w3 = lambda t: t[:, :].rearrange("p (h i) -> p h i", i=64)
                ti_ = 0
                for S in streams:
                    for r0 in range(0, S["T"], 128):
                        yf_, yb_, y2_, cg_ = yfb[ti_ % 2], ybb[ti_ % 2], y2b[ti_ % 2], cgb[ti_ % 2]
                        ti_ += 1
                        ld(yf_[:], S["YF"][r0:r0 + 128, :], [yf_])
                        ld(yb_[:], S["YB"][r0:r0 + 128, :], [yb_])
                        ld(cg_[:], S["CG"][:, r0:r0 + 128], [cg_])
                        tt_("pool", yb_[:, :], yb_[:, :], yf_[:, :], ALU.add, [yb_, yf_], [yb_])
                        p.op("dve", lambda e: e.tensor_reduce(out=gsa[:, :], in_=w3(yb_), axis=AX.X, op=ALU.add), reads=[yb_], writes=[gsa])
                        ts_("dve", gsa[:, :], gsa[:, :], -1.0 / 64, None, ALU.mult, ALU.bypass, [gsa], [gsa])
                        tt_("dve", w3(yb_), w3(yb_), gsa[:, :].unsqueeze(2).to_broadcast([128, 6, 64]), ALU.add, [yb_, gsa], [yb_])
                        tt_("pool", y2_[:, :], yb_[:, :], yb_[:, :], ALU.mult, [yb_], [y2_])
                        p.op("dve", lambda e: e.tensor_reduce(out=gsb[:, :], in_=w3(y2_), axis=AX.X, op=ALU.add), reads=[y2_], writes=[gsb])
                        rsqrt_(gsb[:, :], gsb[:, :], 1.0 / 64, epsc[:, 1:2], [gsb, epsc], [gsb], None)
                        tt_("dve", w3(yb_), w3(yb_), gsb[:, :].unsqueeze(2).to_broadcast([128, 6, 64]), ALU.mult, [yb_, gsb], [yb_])
                        tt_("pool", yb_[:, :], yb_[:, :], lnxg[:, :], ALU.mult, [yb_, lnxg], [yb_])
                        tt_("pool", yb_[:, :], yb_[:, :], lnxb[:, :], ALU.add, [yb_, lnxb], [yb_])
                        pg = nps()
                        mm(pg[:, 0:384], cg_[:, :], cg2[:, :], True, True, [cg_, cg2], [pg])
                        tt_("dve", y2_[:, :], yb_[:, :], pg[:, 0:384], ALU.mult, [yb_, pg], [y2_])
                        stq(S["M"][r0:r0 + 128, 640:1024], y2_[:, :], [y2_])
                p.barrier()

            with ExitStack() as st:
                NT = 256
                wout = sb(st, "wout", [128, 8, D], BF16)
                w1b = sb(st, "w1b", [128, 8, DFF], BF16)
                w3b = sb(st, "w3b", [128, 8, DFF], BF16)
                w2b = sb(st, "w2b", [128, 22, D], BF16)
                wstg = [sb(st, "wstgo%d" % i, [128, 704]) for i in range(2)]
                wsi = [0]
                cengs = ("pool", "dve", "act")

                def load_cast(dst_ap, src_ap, ncols, dst_t):
                    for c0 in range(0, ncols, 704):
                        c1 = min(ncols, c0 + 704)
                        ws_ = wstg[wsi[0] % 2]
                        e_ = cengs[wsi[0] % 3]
                        wsi[0] += 1
                        ld(ws_[:, 0:c1 - c0], src_ap[:, c0:c1], [ws_])
                        cp(e_, dst_ap[:, c0:c1], ws_[:, 0:c1 - c0], [ws_], [dst_t])

                wv = I["w_out"][l].rearrange("(c p) f -> p c f", p=128)
                w1v = I["ffn_w1"][l].rearrange("(c p) f -> p c f", p=128)
                w3v = I["ffn_w3"][l].rearrange("(c p) f -> p c f", p=128)
                w2v = I["ffn_w2"][l].rearrange("(c p) f -> p c f", p=128)
                for k in range(8):
                    load_cast(wout[:, k, :], wv[:, k, :], D, wout)
                for k in range(8):
                    for hf in range(2):
                        load_cast(w1b[:, k, hf * 1408:(hf + 1) * 1408], w1v[:, k, hf * 1408:(hf + 1) * 1408], 1408, w1b)
                        load_cast(w3b[:, k, hf * 1408:(hf + 1) * 1408], w3v[:, k, hf * 1408:(hf + 1) * 1408], 1408, w3b)
                for fc in range(22):
                    load_cast(w2b[:, fc, :], w2v[:, fc, :], D, w2b)
                xt = sb(st, "xt2", [128, 8, NT])
                mT = sb(st, "mT", [128, 8, NT], BF16)
                sq = sb(st, "sq2", [128, 8, NT])
                rstd = sb(st, "rstd2", [128, NT])
                mtok = [sb(st, "mtok%d" % i, [128, D]) for i in range(2)]
                actb = sb(st, "actb", [128, 22, NT], BF16)
                sgl = [sb(st, "sgl%d" % i, [128, NT]) for i in range(2)]
                for S in streams:
                    latent = S is SS_
                    cv = 1 if latent else 0
                    T = S["T"]
                    XTv = S["XT"].rearrange("(c p) t -> p c t", p=128)
                    for t0 in range(0, T, NT):
                        ld(xt[:], XTv[:, :, t0:t0 + NT], [xt])
                        for i in range(NT // 128):
                            mk = mtok[i % 2]
                            ld(mk[:], S["M"][t0 + i * 128:t0 + (i + 1) * 128, :], [mk])
                            for half in range(2):
                                pt = nps()
                                for c in range(4):
                                    cc = half * 4 + c
                                    tr(pt[:, c * 128:(c + 1) * 128], mk[:, cc * 128:(cc + 1) * 128], ident[:, :], [mk, ident], [pt])
                                cp("act" if half == 0 else "dve", mT[:, half * 4:(half + 1) * 4, i * 128:(i + 1) * 128],
                                   pt[:, :].rearrange("p (c t) -> p c t", t=128), [pt], [mT])
                        for dc in range(8):
                            po = nps()
                            for k in range(8):
                                mm(po[:, 0:NT], wout[:, k, dc * 128:(dc + 1) * 128], mT[:, k, :], k == 0, k == 7, [wout, mT], [po])
                            stt(xt[:, dc, :], po[:, 0:NT], MOD(l, 2, dc, cv), xt[:, dc, :], ALU.mult, ALU.add, [po, mod, xt], [xt])
                        act(sq[:], xt[:], AF.Square, [xt], [sq])
                        pss = nps()
                        for c in range(8):
                            mm(pss[:, 0:NT], ones[:, 0:128], sq[:, c, :], c == 0, c == 7, [ones, sq], [pss])
                        rsqrt_(rstd[:], pss[:, 0:NT], 1.0 / D, epsc[:, 0:1], [pss, epsc], [rstd], None)
                        hh = mT
                        for c in range(8):
                            stt(sq[:, c, :], xt[:, c, :], modA2[:, l * 8 + c, cv:cv + 1], rstd[:], ALU.mult, ALU.mult, [xt, modA2, rstd], [sq])
                            act(hh[:, c, :], sq[:, c, :], AF.Identity, [sq, mod], [hh], bias=MOD(l, 3, c, cv), scale=1.0)
                        for fc in range(22):
                            p1, p3 = nps(), nps()
                            for k in range(8):
                                mm(p1[:, 0:NT], w1b[:, k, fc * 128:(fc + 1) * 128], hh[:, k, :], k == 0, k == 7, [w1b, hh], [p1])
                            for k in range(8):
                                mm(p3[:, 0:NT], w3b[:, k, fc * 128:(fc + 1) * 128], hh[:, k, :], k == 0, k == 7, [w3b, hh], [p3])
                            sg_ = sgl[fc % 2]
                            act(sg_[:], p1[:, 0:NT], AF.Silu, [p1], [sg_])
                            tt_("dve", actb[:, fc, :], sg_[:], p3[:, 0:NT], ALU.mult, [sg_, p3], [actb])
                        for dc in range(8):
                            po = nps()
                            for fc in range(22):
                                mm(po[:, 0:NT], w2b[:, fc, dc * 128:(dc + 1) * 128], actb[:, fc, :], fc == 0, fc == 21, [w2b, actb], [po])
                            stt(xt[:, dc, :], po[:, 0:NT], MOD(l, 5, dc, cv), xt[:, dc, :], ALU.mult, ALU.add, [po, mod, xt], [xt])
                        stq(XTv[:, :, t0:t0 + NT], xt[:], [xt])
                p.barrier()

        with ExitStack() as st:
            xt = sb(st, "xtf", [128, 8, 512])
            sq = sb(st, "sqf", [128, 8, 512])
            rstd = sb(st, "rstdf", [128, 512])
            yo = [sb(st, "yo%d" % i, [128, D]) for i in range(2)]
            for S, dst in ((SP_, O["yp"]), (SS_, O["ys"])):
                if S is SS_ and not do_sample:
                    continue
                T = S["T"]
                XTv = S["XT"].rearrange("(c p) t -> p c t", p=128)
                for t0 in range(0, T, 512):
                    ld(xt[:], XTv[:, :, t0:t0 + 512], [xt])
                    act(sq[:], xt[:], AF.Square, [xt], [sq])
                    pss = nps()
                    for c in range(8):
                        mm(pss[:, :], ones[:, 0:128], sq[:, c, :], c == 0, c == 7, [ones, sq], [pss])
                    rsqrt_(rstd[:], pss[:, :], 1.0 / D, epsc[:, 0:1], [pss, epsc], [rstd], None)
                    for c in range(8):
                        stt(sq[:, c, :], xt[:, c, :], gfc[:, c:c + 1], rstd[:], ALU.mult, ALU.mult, [xt, gfc, rstd], [sq])
                    for i in range(4):
                        y_ = yo[i % 2]
                        for half in range(2):
                            pt = nps()
                            for c in range(4):
                                cc = half * 4 + c
                                tr(pt[:, c * 128:(c + 1) * 128], sq[:, cc, i * 128:(i + 1) * 128], ident[:, :], [sq, ident], [pt])
                            cp("act" if half == 0 else "dve", y_[:, half * 512:(half + 1) * 512], pt[:, :], [pt], [y_])
                        stq(dst[t0 + i * 128:t0 + (i + 1) * 128, :], y_[:], [y_])
            p.barrier()
    return nc


def make_in_maps(inputs):
    f = lambda a: np.ascontiguousarray(np.asarray(a, dtype=np.float32))
    consts = _consts()
    shared = {k: f(inputs[k]) for k in WEIGHT_SHAPES}
    shared.update(consts)
    maps = []
    for c in range(NCORES):
        b = c % 2
        m = dict(shared)
        m["xp"] = f(inputs["x_prompt"][NPL * c:NPL * (c + 1)]).reshape(NPL * TP, D)
        m["xs"] = f(inputs["x_sample"][b])
        m["cak"] = f(inputs["cache_a_k"][b]).reshape(L, PAST, 128)
        m["cav"] = f(inputs["cache_a_v"][b]).reshape(L, PAST, 128)
        m["cbk"] = f(inputs["cache_b_k"][b]).reshape(L, PAST, 256)
        m["cbv"] = f(inputs["cache_b_v"][b]).reshape(L, PAST, 256)
        m["scf"] = f(inputs["state_c_fwd"][b])
        m["scb"] = f(inputs["state_c_bwd"][b])
        m["cs"] = f(inputs["c"][b])
        maps.append(m)
    return maps


def kernel(**inputs):
    nc = build()
    maps = make_in_maps(inputs)
    res = run_bass_kernel_spmd(nc, maps, core_ids=list(range(NCORES))).results
    yp = np.concatenate([r["yp"].reshape(NPL, TP, D) for r in res], axis=0)
    ys = np.stack([res[0]["ys"], res[1]["ys"]], axis=0)
    nak = np.concatenate([r["nak"].reshape(NPL, L, TP, 2, 64) for r in res], axis=0)
    nav = np.concatenate([r["nav"].reshape(NPL, L, TP, 2, 64) for r in res], axis=0)
    nbk = np.concatenate([r["nbk"].reshape(NPL, L, TP, 4, 2, 32) for r in res], axis=0)
    nbv = np.concatenate([r["nbv"].reshape(NPL, L, TP, 4, 64) for r in res], axis=0)
    ncf = np.concatenate([r["ncf"] for r in res], axis=0)
    ncb = np.concatenate([r["ncb"] for r in res], axis=0)
    return tuple(np.ascontiguousarray(a.astype(np.float32)) for a in (yp, ys, nak, nav, nbk, nbv, ncf, ncb))
```

```python
import math
from contextlib import ExitStack
import numpy as np
import concourse.bass as bass
import concourse.mybir as mybir
from concourse.bass_utils import run_bass_kernel_spmd

F32 = mybir.dt.float32
BF16 = mybir.dt.bfloat16
AF = mybir.ActivationFunctionType
ALU = mybir.AluOpType
AX = mybir.AxisListType

D = 1024
L = 4
TP = 256
TS = 4096
PAST = 512
NPL = 2
DFF = 2816
INC = 2944
DEC = 0.606531
EPS = 1e-6
GN_EPS = 64e-5
NCORES = 8


class Buf:
    __slots__ = ("w", "r")

    def __init__(self):
        self.w = []
        self.r = []


class TT:
    def __init__(self, h):
        self.h = h
        self.b = Buf()

    def __getitem__(self, k):
        return self.h[k]


class Prog:
    RING = 12

    def __init__(self, nc, stack):
        self.nc = nc
        self.eng = {"pe": nc.tensor, "act": nc.scalar, "dve": nc.vector, "pool": nc.gpsimd, "sp": nc.sync}
        self.semh = {}
        self.cnt = {}
        self.seen = {e: {} for e in self.eng}
        for e in ("pe", "act", "dve", "pool"):
            self.semh["S_" + e] = stack.enter_context(nc.semaphore("S_" + e))
            self.cnt[e] = 0
        self.dq = {}
        for q in ("sp", "pool"):
            ring = []
            for k in range(self.RING):
                key = "D_%s_%d" % (q, k)
                self.semh[key] = stack.enter_context(nc.semaphore(key))
                ring.append([key, 0])
            self.dq[q] = [ring, 0]
        self.n = 0

    def _waits(self, e, reads, writes):
        need = {}
        for b in reads:
            for (key, val, te) in b.w:
                if need.get(key, 0) < val:
                    need[key] = val
        for b in writes:
            for (key, val, te) in b.w:
                if te != e and need.get(key, 0) < val:
                    need[key] = val
            for (key, val, te) in b.r:
                if te != e and need.get(key, 0) < val:
                    need[key] = val
        seen = self.seen[e]
        for key, val in need.items():
            if seen.get(key, 0) < val:
                self.eng[e].wait_ge(self.semh[key], val)
                seen[key] = val
                self.n += 1

    def _record(self, tok, reads, writes):
        for b in writes:
            b.w = [tok]
            b.r = []
        for b in reads:
            b.r = [t for t in b.r if t[0] != tok[0]]
            b.r.append(tok)

    def op(self, e, fn, reads=(), writes=()):
        reads = [t.b for t in reads]
        writes = [t.b for t in writes]
        self._waits(e, reads, writes)
        ins = fn(self.eng[e])
        self.cnt[e] += 1
        ins.then_inc(self.semh["S_" + e], 1)
        self._record(("S_" + e, self.cnt[e], e), reads, writes)
        self.n += 1

    def dma(self, q, out_ap, in_ap, reads=(), writes=()):
        reads = [t.b for t in reads]
        writes = [t.b for t in writes]
        self._waits(q, reads, writes)
        ring, rr = self.dq[q]
        slot = ring[rr % self.RING]
        self.dq[q][1] = rr + 1
        key, prev = slot
        if prev > 0 and self.seen[q].get(key, 0) < prev:
            self.eng[q].wait_ge(self.semh[key], prev)
            self.seen[q][key] = prev
        ins = self.eng[q].dma_start(out=out_ap, in_=in_ap)
        ins.then_inc(self.semh[key], 16)
        slot[1] = prev + 16
        self._record((key, prev + 16, None), reads, writes)
        self.n += 2

    def barrier(self):
        targets = {}
        for e in ("pe", "act", "dve", "pool"):
            if self.cnt[e] > 0:
                targets["S_" + e] = self.cnt[e]
        for q in self.dq:
            for key, val in self.dq[q][0]:
                if val > 0:
                    targets[key] = val
        for e in self.eng:
            seen = self.seen[e]
            for key, val in targets.items():
                if key == "S_" + e:
                    continue
                if seen.get(key, 0) < val:
                    self.eng[e].wait_ge(self.semh[key], val)
                    seen[key] = val
                    self.n += 1


def _rope_tables(dh, T, grid_w=64):
    half = dh // 2
    t = np.arange(T)
    rows = t // grid_w
    cols = t % grid_w
    inv = 10000.0 ** (-np.arange(0, half, 2, dtype=np.float32) / half)
    cos = np.zeros((dh, T), np.float32)
    sin = np.zeros((dh, T), np.float32)
    perm = np.zeros((dh, dh), np.float32)
    q = half // 2
    for ax, pos in enumerate((rows, cols)):
        ang = pos[None, :].astype(np.float32) * inv[:, None]
        base = ax * half
        for i in range(q):
            cos[base + i] = np.cos(ang[i])
            cos[base + q + i] = np.cos(ang[i])
            sin[base + i] = -np.sin(ang[i])
            sin[base + q + i] = np.sin(ang[i])
            perm[base + q + i, base + i] = 1.0
            perm[base + i, base + q + i] = 1.0
    return cos, sin, perm


def _consts():
    c = {}
    c["ident"] = np.eye(128, dtype=np.float32)
    c["ones"] = np.ones((128, 512), np.float32)
    cosA, sinA, pA = _rope_tables(64, TS)
    cosB, sinB, pB = _rope_tables(32, TS)
    c["cosA"] = np.tile(cosA, (2, 1))
    c["sinA"] = np.tile(sinA, (2, 1))
    c["cosB"] = np.tile(cosB, (4, 1))
    c["sinB"] = np.tile(sinB, (4, 1))
    PA = np.zeros((128, 128), np.float32)
    PB = np.zeros((128, 128), np.float32)
    for i in range(2):
        PA[i * 64:(i + 1) * 64, i * 64:(i + 1) * 64] = pA
    for i in range(4):
        PB[i * 32:(i + 1) * 32, i * 32:(i + 1) * 32] = pB
    c["permA"] = PA
    c["permB"] = PB
    p = np.arange(128)[:, None]
    f = np.arange(128)[None, :]
    c["mprev"] = (p >= f).astype(np.float32)
    c["mnext"] = (p <= f).astype(np.float32)
    p = np.arange(64)[:, None]
    f = np.arange(64)[None, :]
    rep = lambda m: np.tile(m.astype(np.float32), (1, 6))
    c["m_lt"] = rep(p < f)
    c["m_le"] = rep(p <= f)
    c["m_gt"] = rep(p > f)
    c["m_ge"] = rep(p >= f)
    c["n_lt"] = -rep(p < f)
    c["n_le"] = -rep(p <= f)
    c["n_gt"] = -rep(p > f)
    c["n_ge"] = -rep(p >= f)
    c["i6"] = rep(p == f)
    rs = np.ones((64, 6 * 512), np.float32)
    rs[:, ::64] = 0.0
    c["rstart"] = rs
    rs2 = np.ones((64, 6 * 512), np.float32)
    rs2[:, ::128] = 0.0
    c["rstart2"] = rs2
    p = np.arange(128)[:, None]
    f = np.arange(128)[None, :]
    rep3 = lambda m: np.tile(m.astype(np.float32), (1, 3))
    c["bm_lt"] = rep3(p < f); c["bm_le"] = rep3(p <= f); c["bm_gt"] = rep3(p > f); c["bm_ge"] = rep3(p >= f)
    c["bn_lt"] = -rep3(p < f); c["bn_le"] = -rep3(p <= f); c["bn_gt"] = -rep3(p > f); c["bn_ge"] = -rep3(p >= f)
    c["bi6"] = rep3(p == f)
    return c


CONST_SHAPES = {k: v.shape for k, v in _consts().items()}

WEIGHT_SHAPES = dict(
    ada_w=(L, D, 6 * D), ada_b=(L, 6 * D), norm1_g=(L, D), norm2_g=(L, D), w_in=(L, D, INC), a_sink=(L, 6),
    b_lambda=(L, 4, 32), b_subln_g=(L, 64), c_conv=(L, 3, 1152), c_w0=(L, 2, 384), c_w2=(L, 2, 64, 384),
    c_a0=(L, 2, 384), c_a2=(L, 2, 64, 384), c_g2=(L, 128, 384), c_kk=(L, 384), c_ka=(L, 384), c_rk=(L, 6, 64),
    c_lnx_g=(L, 384), c_lnx_b=(L, 384), w_out=(L, D, D), ffn_w1=(L, D, DFF), ffn_w3=(L, D, DFF),
    ffn_w2=(L, DFF, D), final_g=(D,), c_ctx=(D,))

CORE_IN_SHAPES = dict(
    xp=(NPL * TP, D), xs=(TS, D), cak=(L, PAST, 128), cav=(L, PAST, 128), cbk=(L, PAST, 256), cbv=(L, PAST, 256),
    scf=(L, 6, 64, 64), scb=(L, 6, 64, 64), cs=(D,))

OUT_SHAPES = dict(
    yp=(NPL * TP, D), ys=(TS, D), nak=(NPL, L, TP, 128), nav=(NPL, L, TP, 128), nbk=(NPL, L, TP, 256),
    nbv=(NPL, L, TP, 256), ncf=(NPL, L, 6, 64, 64), ncb=(NPL, L, 6, 64, 64))


def build(nlayers=L, debug=False, do_sample=True):
    nc = bass.Bass("TRN2", target_bir_lowering=False)
    I = {}
    for k, s in list(WEIGHT_SHAPES.items()) + list(CORE_IN_SHAPES.items()) + list(CONST_SHAPES.items()):
        I[k] = nc.dram_tensor(k, list(s), F32, kind="ExternalInput").ap()
    O = {k: nc.dram_tensor(k, list(s), F32, kind="ExternalOutput").ap() for k, s in OUT_SHAPES.items()}
    skind = "ExternalOutput" if debug else "Internal"
    streams = []
    for sname, tt in (("p", NPL * TP), ("s", TS)):
        S = {"T": tt, "name": sname}
        for nm, shp in (("XT", [D, tt]), ("QA", [6, 64, tt]), ("KA", [2, 64, tt]), ("QB", [8, 32, tt]),
                        ("KB", [8, 32, tt]), ("VA", [tt, 128]), ("VB", [tt, 256]), ("RKV", [18, 64, tt]),
                        ("CW", [2, 64, tt]), ("CA", [2, 64, tt]), ("CG", [128, tt]), ("M", [tt, D]),
                        ("YF", [tt, 384]), ("YB", [tt, 384]), ("RKVC", [18, 64, tt])):
            S[nm] = nc.dram_tensor("%s_%s" % (nm, sname), shp, F32, kind=skind).ap()
        streams.append(S)
    SP_, SS_ = streams
    if not do_sample:
        streams = [SP_]
    seqs = [(SP_, 0, TP, False, 0), (SP_, TP, TP, False, 1)]
    if do_sample:
        seqs.append((SS_, 0, TS, True, -1))

    with ExitStack() as glob:
        p = Prog(nc, glob)

        uid = [0]

        def sb(st, name, shape, dt=F32):
            uid[0] += 1
            return TT(st.enter_context(nc.sbuf_tensor("s%d_%s" % (uid[0], name), list(shape), dt)))

        ps = [TT(glob.enter_context(nc.psum_tensor("ps%d" % i, [128, 512], F32))) for i in range(8)]
        psi = [0]

        def nps():
            t = ps[2 + psi[0] % 6]
            psi[0] += 1
            return t

        def mm(out, lhsT, rhs, start, stop, r, w):
            p.op("pe", lambda e: e.matmul(out, lhsT, rhs, start=start, stop=stop), reads=r, writes=w)

        def tr(out, in_, idn, r, w):
            p.op("pe", lambda e: e.transpose(out, in_, idn), reads=r, writes=w)

        def ld(out, in_, w, r=()):
            p.dma("sp", out, in_, reads=r, writes=w)

        def stq(out, in_, r):
            p.dma("pool", out, in_, reads=r)

        def act(out, in_, func, r, w, **kw):
            p.op("act", lambda e: e.activation(out=out, in_=in_, func=func, **kw), reads=r, writes=w)

        def tt_(eng, out, in0, in1, op, r, w):
            p.op(eng, lambda e: e.tensor_tensor(out=out, in0=in0, in1=in1, op=op), reads=r, writes=w)

        def ts_(eng, out, in0, s1, s2, op0, op1, r, w):
            if s2 is None:
                p.op(eng, lambda e: e.tensor_single_scalar(out=out, in_=in0, scalar=s1, op=op0), reads=r, writes=w)
            else:
                p.op(eng, lambda e: e.tensor_scalar(out=out, in0=in0, scalar1=s1, scalar2=s2, op0=op0, op1=op1), reads=r, writes=w)

        def stt(out, in0, scalar, in1, op0, op1, r, w):
            p.op("dve", lambda e: e.scalar_tensor_tensor(out=out, in0=in0, scalar=scalar, in1=in1, op0=op0, op1=op1), reads=r, writes=w)

        def cp(eng, out, in_, r, w):
            if eng == "act":
                p.op("act", lambda e: e.copy(out=out, in_=in_), reads=r, writes=w)
            else:
                p.op(eng, lambda e: e.tensor_copy(out=out, in_=in_), reads=r, writes=w)

        def rsqrt_(out, in_, scale, bias, r, w, tmpw):
            act(out, in_, AF.Sqrt, r, w, scale=scale, bias=bias)
            p.op("dve", lambda e: e.reciprocal(out=out, in_=out), reads=w, writes=w)

        ident = sb(glob, "ident", [128, 128])
        ones = sb(glob, "ones", [128, 512])
        epsc = sb(glob, "epsc", [128, 2])
        ld(ident[:], I["ident"][:, :], [ident])
        ld(ones[:], I["ones"][:, :], [ones])
        p.op("pool", lambda e: e.memset(epsc[:, 0:1], EPS), writes=[epsc])
        p.op("pool", lambda e: e.memset(epsc[:, 1:2], GN_EPS), writes=[epsc])

        stgc = sb(glob, "stgc", [128, 512])

        def load_cols(st, src2d, rows, w, name):
            dst = sb(st, name, [w, rows])
            for r0 in range(0, rows, 128):
                rr = min(128, rows - r0)
                ld(stgc[0:rr, 0:w], src2d[r0:r0 + rr, :], [stgc])
                pt = nps()
                tr(pt[0:w, 0:rr], stgc[0:rr, 0:w], ident[0:rr, 0:rr], [stgc, ident], [pt])
                cp("act", dst[:, r0:r0 + rr], pt[0:w, 0:rr], [pt], [dst])
            return dst

        def bcast_row(st, src_row, n, name, dst=None):
            if dst is None:
                dst = sb(st, name, [128, n])
            ld(stgc[0:1, 0:n], src_row, [stgc])
            pt = nps()
            mm(pt[:, 0:n], ones[0:1, 0:128], stgc[0:1, 0:n], True, True, [ones, stgc], [pt])
            cp("act", dst[:, 0:n], pt[:, 0:n], [pt], [dst])
            return dst

        adab = load_cols(glob, I["ada_b"].rearrange("l (j p) -> (l j) p", p=128), L * 48, 128, "adab")
        g1c = load_cols(glob, I["norm1_g"].rearrange("l (j p) -> (l j) p", p=128), L * 8, 128, "g1c")
        g2c = load_cols(glob, I["norm2_g"].rearrange("l (j p) -> (l j) p", p=128), L * 8, 128, "g2c")
        gfc = load_cols(glob, I["final_g"].rearrange("(j p) -> j p", p=128), 8, 128, "gfc")
        cct = load_cols(glob, I["c_ctx"].rearrange("(j p) -> j p", p=128), 8, 128, "cct")
        cst = load_cols(glob, I["cs"].rearrange("(j p) -> j p", p=128), 8, 128, "cst")
        convc = load_cols(glob, I["c_conv"].rearrange("l k (t c) -> (l k t) c", c=64), L * 3 * 18, 64, "convc")
        w0c = load_cols(glob, I["c_w0"].rearrange("l d (h c) -> (l d h) c", c=64), L * 12, 64, "w0c")
        a0c = load_cols(glob, I["c_a0"].rearrange("l d (h c) -> (l d h) c", c=64), L * 12, 64, "a0c")
        kkc = load_cols(glob, I["c_kk"].rearrange("l (h c) -> (l h) c", c=64), L * 6, 64, "kkc")
        kac = load_cols(glob, I["c_ka"].rearrange("l (h c) -> (l h) c", c=64), L * 6, 64, "kac")
        rkc_ = load_cols(glob, I["c_rk"].rearrange("l h c -> (l h) c"), L * 6, 64, "rkc")

        mod = sb(glob, "mod", [128, L * 48, 2])
        modA1 = sb(glob, "modA1", [128, L * 8, 2])
        modA2 = sb(glob, "modA2", [128, L * 8, 2])
        with ExitStack() as st:
            silc = sb(st, "silc", [128, 8, 2])
            act(silc[:, :, 0], cct[:, :], AF.Silu, [cct], [silc])
            act(silc[:, :, 1], cst[:, :], AF.Silu, [cst], [silc])
            pcs = [sb(st, "adapc%d" % i, [128, 8, 512]) for i in range(2)]
            k_ = 0
            for l in range(nlayers):
                wv = I["ada_w"][l].rearrange("(c p) f -> p c f", p=128)
                for pc in range(12):
                    pt_ = pcs[k_ % 2]
                    k_ += 1
                    ld(pt_[:], wv[:, :, pc * 512:(pc + 1) * 512], [pt_])
                    for jb in range(4):
                        pq = nps()
                        for k in range(8):
                            mm(pq[:, 0:2], pt_[:, k, jb * 128:(jb + 1) * 128], silc[:, k, :], k == 0, k == 7, [pt_, silc], [pq])
                        j = l * 48 + pc * 4 + jb
                        ts_("dve", mod[:, j, :], pq[:, 0:2], adab[:, j:j + 1], None, ALU.add, ALU.bypass, [pq, adab], [mod])
                stt(modA1[:, l * 8:(l + 1) * 8, :], mod[:, l * 48 + 8:l * 48 + 16, :], 1.0,
                    g1c[:, l * 8:(l + 1) * 8].unsqueeze(2).to_broadcast([128, 8, 2]), ALU.add, ALU.mult, [mod, g1c], [modA1])
                stt(modA2[:, l * 8:(l + 1) * 8, :], mod[:, l * 48 + 32:l * 48 + 40, :], 1.0,
                    g2c[:, l * 8:(l + 1) * 8].unsqueeze(2).to_broadcast([128, 8, 2]), ALU.add, ALU.mult, [mod, g2c], [modA2])
            p.barrier()

        def MOD(l, which, c, cv):
            j = l * 48 + which * 8 + c
            return mod[:, j, cv:cv + 1]

        with ExitStack() as st:
            xin = [sb(st, "xin%d" % i, [128, D]) for i in range(2)]
            xo = [sb(st, "xo%d" % i, [128, 8, 128]) for i in range(2)]
            k_ = 0
            for S, src in ((SP_, I["xp"]), (SS_, I["xs"])):
                if S is SS_ and not do_sample:
                    continue
                for b in range(S["T"] // 128):
                    a_ = xin[k_ % 2]
                    o_ = xo[k_ % 2]
                    k_ += 1
                    ld(a_[:], src[b * 128:(b + 1) * 128, :], [a_])
                    for half in range(2):
                        pt = nps()
                        for c in range(4):
                            cc = half * 4 + c
                            tr(pt[:, c * 128:(c + 1) * 128], a_[:, cc * 128:(cc + 1) * 128], ident[:, :], [a_, ident], [pt])
                        cp("act" if half == 0 else "dve", o_[:, half * 4:(half + 1) * 4, :],
                           pt[:, :].rearrange("p (c t) -> p c t", t=128), [pt], [o_])
                    stq(S["XT"].rearrange("(c p) t -> p c t", p=128)[:, :, b * 128:(b + 1) * 128], o_[:], [o_])
            p.barrier()

        for l in range(nlayers):
            lam_init = 0.8 - 0.6 * math.exp(-0.3 * l)
            with ExitStack() as st:
                win = sb(st, "win", [128, 8, INC], BF16)
                wstg = [sb(st, "wstg%d" % i, [128, INC]) for i in range(2)]
                wv = I["w_in"][l].rearrange("(c p) f -> p c f", p=128)
                for k in range(8):
                    ws_ = wstg[k % 2]
                    ld(ws_[:], wv[:, k, :], [ws_])
                    cp("pool" if k % 2 == 0 else "dve", win[:, k, :], ws_[:], [ws_], [win])
                xt = sb(st, "xt", [128, 8, 512])
                sq = sb(st, "sq", [128, 8, 512])
                h = sb(st, "h", [128, 8, 512], BF16)
                hf = sb(st, "hf", [128, 8, 512])
                rstd = sb(st, "rstd", [128, 512])
                stg = [sb(st, "stg%d" % i, [128, 512]) for i in range(3)]
                zs = sb(st, "zs", [128, 512])
                t2 = sb(st, "t2", [128, 512])
                cosA = sb(st, "cosA", [128, 512]); sinA = sb(st, "sinA", [128, 512])
                cosB = sb(st, "cosB", [128, 512]); sinB = sb(st, "sinB", [128, 512])
                permA = sb(st, "permA", [128, 128]); permB = sb(st, "permB", [128, 128])
                ld(permA[:], I["permA"][:, :], [permA]); ld(permB[:], I["permB"][:, :], [permB])
                vst = [sb(st, "vst%d" % i, [128, 768]) for i in range(2)]
                sgi = [0]
                for S in streams:
                    latent = S is SS_
                    cv = 1 if latent else 0
                    T = S["T"]
                    XTv = S["XT"].rearrange("(c p) t -> p c t", p=128)
                    for t0 in range(0, T, 512):
                        ld(xt[:], XTv[:, :, t0:t0 + 512], [xt])
                        if latent:
                            ld(cosA[:], I["cosA"][:, t0:t0 + 512], [cosA]); ld(sinA[:], I["sinA"][:, t0:t0 + 512], [sinA])
                            ld(cosB[:], I["cosB"][:, t0:t0 + 512], [cosB]); ld(sinB[:], I["sinB"][:, t0:t0 + 512], [sinB])
                        act(sq[:], xt[:], AF.Square, [xt], [sq])
                        pss = nps()
                        for c in range(8):
                            mm(pss[:, :], ones[:, 0:128], sq[:, c, :], c == 0, c == 7, [ones, sq], [pss])
                        rsqrt_(rstd[:], pss[:, :], 1.0 / D, epsc[:, 0:1], [pss, epsc], [rstd], None)
                        for c in range(8):
                            stt(hf[:, c, :], xt[:, c, :], modA1[:, l * 8 + c, cv:cv + 1], rstd[:], ALU.mult, ALU.mult, [xt, modA1, rstd], [hf])
                            act(h[:, c, :], hf[:, c, :], AF.Identity, [hf, mod], [h], bias=MOD(l, 0, c, cv), scale=1.0)
                        for j in range(23):
                            if j in (4, 9, 10):
                                continue
                            pz = nps()
                            for k in range(8):
                                mm(pz[:, :], win[:, k, j * 128:(j + 1) * 128], h[:, k, :], k == 0, k == 7, [win, h], [pz])
                            sg = stg[sgi[0] % 3]
                            sgi[0] += 1
                            rope = latent and j in (0, 1, 2, 3, 5, 6, 7, 8)
                            if rope:
                                isA = j <= 3
                                cs_, sn_, pm_ = (cosA, sinA, permA) if isA else (cosB, sinB, permB)
                                cp("act", zs[:], pz[:, :], [pz], [zs])
                                pr = nps()
                                mm(pr[:, :], pm_[:, :], zs[:], True, True, [pm_, zs], [pr])
                                tt_("dve", t2[:], pr[:, :], sn_[:], ALU.mult, [pr, sn_], [t2])
                                tt_("pool", sg[:], zs[:], cs_[:], ALU.mult, [zs, cs_], [sg])
                                tt_("pool", sg[:], sg[:], t2[:], ALU.add, [sg, t2], [sg])
                            elif j == 20:
                                act(sg[:], pz[:, :], AF.Tanh, [pz], [sg])
                            elif j == 22:
                                act(sg[:], pz[:, :], AF.Sigmoid, [pz], [sg])
                            else:
                                cp("act" if j % 2 == 0 else "dve", sg[:], pz[:, :], [pz], [sg])
                            sl = slice(t0, t0 + 512)
                            if j <= 2:
                                for hh in range(2):
                                    stq(S["QA"][2 * j + hh, :, sl], sg[hh * 64:(hh + 1) * 64, :], [sg])
                            elif j == 3:
                                for hh in range(2):
                                    stq(S["KA"][hh, :, sl], sg[hh * 64:(hh + 1) * 64, :], [sg])
                            elif j in (5, 6):
                                for q4 in range(4):
                                    stq(S["QB"][(j - 5) * 4 + q4, :, sl], sg[q4 * 32:(q4 + 1) * 32, :], [sg])
                            elif j in (7, 8):
                                for q4 in range(4):
                                    stq(S["KB"][(j - 7) * 4 + q4, :, sl], sg[q4 * 32:(q4 + 1) * 32, :], [sg])
                            elif 11 <= j <= 19:
                                for hh in range(2):
                                    stq(S["RKV"][(j - 11) * 2 + hh, :, sl], sg[hh * 64:(hh + 1) * 64, :], [sg])
                            elif j == 20:
                                for hh in range(2):
                                    stq(S["CW"][hh, :, sl], sg[hh * 64:(hh + 1) * 64, :], [sg])
                            elif j == 21:
                                for hh in range(2):
                                    stq(S["CA"][hh, :, sl], sg[hh * 64:(hh + 1) * 64, :], [sg])
                            else:
                                stq(S["CG"][:, sl], sg[:, :], [sg])
                        for i in range(4):
                            vs_ = vst[i % 2]
                            groups = [(512, 128, 0), (1152, 256, 128)]
                            if not latent:
                                groups += [(384, 128, 384), (896, 256, 512)]
                            pv = nps()
                            pk = nps()
                            for (c0, wd, o0) in groups:
                                pp, oo = (pv, o0) if o0 < 384 else (pk, o0 - 384)
                                for k in range(8):
                                    mm(pp[:, oo:oo + wd], h[:, k, i * 128:(i + 1) * 128], win[:, k, c0:c0 + wd], k == 0, k == 7, [h, win], [pp])
                            cp("act", vs_[:, 0:384], pv[:, 0:384], [pv], [vs_])
                            r0 = t0 + i * 128
                            stq(S["VA"][r0:r0 + 128, :], vs_[:, 0:128], [vs_])
                            stq(S["VB"][r0:r0 + 128, :], vs_[:, 128:384], [vs_])
                            if not latent:
                                cp("dve", vs_[:, 384:768], pk[:, 0:384], [pk], [vs_])
                                bl = r0 // TP
                                rr = r0 % TP
                                stq(O["nav"][bl, l, rr:rr + 128, :], vs_[:, 0:128], [vs_])
                                stq(O["nbv"][bl, l, rr:rr + 128, :], vs_[:, 128:384], [vs_])
                                stq(O["nak"][bl, l, rr:rr + 128, :], vs_[:, 384:512], [vs_])
                                stq(O["nbk"][bl, l, rr:rr + 128, :], vs_[:, 512:768], [vs_])
                p.barrier()

            with ExitStack() as st:
                sinkb = bcast_row(st, I["a_sink"][l:l + 1, :], 6, "sinkb")
                act(sinkb[:, 0:6], sinkb[:, 0:6], AF.Exp, [sinkb], [sinkb])
                mprev = sb(st, "mprev", [128, 128]); mnext = sb(st, "mnext", [128, 128])
                ld(mprev[:], I["mprev"][:, :], [mprev]); ld(mnext[:], I["mnext"][:, :], [mnext])
                kT = sb(st, "kT", [64, TS + PAST], BF16)
                qT = sb(st, "qT", [64, 3, TS], BF16)
                vt = sb(st, "vt", [128, (TS + PAST) // 128, 65], BF16)
                p.op("pool", lambda e: e.memset(vt[:, :, 64:65], 1.0), writes=[vt])
                stf = sb(st, "stfa", [64, 3, TS])
                vtf = sb(st, "vtfa", [128, (TS + PAST) // 128, 64])
                cstg = sb(st, "cstg", [128, 4, 64])
                pT = [sb(st, "pT%d" % i, [128, 384], BF16) for i in range(6)]
                ao = [sb(st, "ao%d" % i, [128, 3, 64]) for i in range(2)]
                den = sb(st, "den", [128, 3])
                pti = [0]
                for (S, o, T, latent, bl) in seqs:
                    nb = T // 128
                    for g in range(2):
                        ld(stf[:, 0, 0:T], S["KA"][g, :, o:o + T], [stf])
                        cp("pool", kT[:, 0:T], stf[:, 0, 0:T], [stf], [kT])
                        ld(stf[:, :, 0:T], S["QA"][3 * g:3 * g + 3, :, o:o + T].rearrange("m d t -> d m t"), [stf])
                        cp("pool", qT[:, :, 0:T], stf[:, :, 0:T], [stf], [qT])
                        ld(vtf[:, 0:nb, :], S["VA"][o:o + T, g * 64:(g + 1) * 64].rearrange("(n p) d -> p n d", p=128), [vtf])
                        nctx = 0
                        if latent:
                            nctx = PAST // 128
                            ld(cstg[:], I["cak"][l, :, g * 64:(g + 1) * 64].rearrange("(n p) d -> p n d", p=128), [cstg])
                            pt = nps()
                            for n_ in range(4):
                                tr(pt[0:64, n_ * 128:(n_ + 1) * 128], cstg[:, n_, :], ident[:, :], [cstg, ident], [pt])
                            cp("act", kT[:, T:T + PAST], pt[0:64, :], [pt], [kT])
                            ld(vtf[:, nb:nb + 4, :], I["cav"][l, :, g * 64:(g + 1) * 64].rearrange("(n p) d -> p n d", p=128), [vtf])
                        cp("pool", vt[:, 0:nb + nctx, 0:64], vtf[:, 0:nb + nctx, :], [vtf], [vt])
                        for b in range(nb):
                            if latent:
                                tiles = [(kb, (mprev if kb == b - 1 else (mnext if kb == b + 1 else None)))
                                         for kb in (b - 1, b, b + 1) if 0 <= kb < nb]
                                tiles += [(nb + c_, None) for c_ in range(nctx)]
                            else:
                                tiles = [(kb, None) for kb in range(nb)]
                            po = ps[b % 2]
                            def st1(ti, kb, msk):
                                pss = nps()
                                mm(pss[:, 0:384], kT[:, kb * 128:(kb + 1) * 128], qT[:, :, b * 128:(b + 1) * 128], True, True, [kT, qT], [pss])
                                pt_ = pT[pti[0] % 6]
                                pti[0] += 1
                                act(pt_[:], pss[:, 0:384], AF.Exp, [pss], [pt_], scale=0.125)
                                if msk is not None:
                                    tt_("pool", pt_[:, :].rearrange("p (m q) -> p m q", q=128), pt_[:, :].rearrange("p (m q) -> p m q", q=128),
                                        msk[:, :].unsqueeze(1).to_broadcast([128, 3, 128]), ALU.mult, [pt_, msk], [pt_])
                                return pt_

                            def st2(ti, kb, pt_):
                                for m in range(3):
                                    mm(po[:, m * 65:(m + 1) * 65], pt_[:, m * 128:(m + 1) * 128], vt[:, kb, :], ti == 0 and m == 0, ti == len(tiles) - 1 and m == 2, [pt_, vt], [po])

                            pend = []
                            for ti, (kb, msk) in enumerate(tiles):
                                pend.append((ti, kb, st1(ti, kb, msk)))
                                if len(pend) > 2:
                                    st2(*pend.pop(0))
                            for it_ in pend:
                                st2(*it_)
                            pov = po[:, 0:195].rearrange("p (m d) -> p m d", d=65)
                            tt_("dve", den[:, :].unsqueeze(2), pov[:, :, 64:65], sinkb[:, 3 * g:3 * g + 3].unsqueeze(2), ALU.add, [po, sinkb], [den])
                            p.op("dve", lambda e: e.reciprocal(out=den[:, :], in_=den[:, :]), reads=[den], writes=[den])
                            a_ = ao[b % 2]
                            tt_("dve", a_[:], pov[:, :, 0:64], den[:, :].unsqueeze(2).to_broadcast([128, 3, 64]), ALU.mult, [po, den], [a_])
                            r0 = o + b * 128
                            stq(S["M"][r0:r0 + 128, g * 192:(g + 1) * 192], a_[:].rearrange("p m d -> p (m d)"), [a_])
                p.barrier()

            with ExitStack() as st:
                lamr = bcast_row(st, I["b_lambda"][l:l + 1].rearrange("o a d -> o (a d)"), 128, "lamr")
                lam2 = sb(st, "lam2", [128, 2, 32])
                lv = lamr[:, :].rearrange("p (a b d) -> p a b d", a=2, b=2)
                tt_("dve", lam2[:], lv[:, :, 0, :], lv[:, :, 1, :], ALU.mult, [lamr], [lam2])
                lam1 = sb(st, "lam1", [128, 2])
                p.op("dve", lambda e: e.tensor_reduce(out=lam1[:, :], in_=lam2[:], axis=AX.X, op=ALU.add), reads=[lam2], writes=[lam1])
                act(lam1[:, :], lam1[:, :], AF.Exp, [lam1], [lam1])
                nlam = sb(st, "nlam", [128, 1])
                tt_("dve", nlam[:, :], lam1[:, 1:2], lam1[:, 0:1], ALU.subtract, [lam1], [nlam])
                ts_("dve", nlam[:, :], nlam[:, :], -lam_init, None, ALU.add, ALU.bypass, [nlam], [nlam])
                gsub = bcast_row(st, I["b_subln_g"][l:l + 1, :], 64, "gsub")
                ts_("dve", gsub[:, :], gsub[:, :], 1.0 - lam_init, None, ALU.mult, ALU.bypass, [gsub], [gsub])
                kT = sb(st, "kTb", [32, 2, TS + PAST], BF16)
                qT = sb(st, "qTb", [32, 2, TS], BF16)
                vt = sb(st, "vtb", [128, (TS + PAST) // 128, 65], BF16)
                p.op("pool", lambda e: e.memset(vt[:, :, 64:65], 1.0), writes=[vt])
                stf = sb(st, "stfb", [32, 2, TS])
                vtf = sb(st, "vtfb", [128, (TS + PAST) // 128, 64])
                cstg = sb(st, "cstgb", [128, 4, 32])
                pT = [sb(st, "pTb%d" % i, [128, 512], BF16) for i in range(6)]
                o1 = sb(st, "o1", [128, 4, 64]); o2 = sb(st, "o2", [128, 4, 64]); o3 = sb(st, "o3", [128, 4, 64])
                rd = sb(st, "rd", [128, 2, 4]); ssq = sb(st, "ssq", [128, 4])
                bo = [sb(st, "bo%d" % i, [128, 4, 64]) for i in range(2)]
                pti = [0]
                boi = [0]
                qti = [0]
                pbi = [0]

                def npsb():
                    t_ = ps[4 + pbi[0] % 4]
                    pbi[0] += 1
                    return t_
                for (S, o, T, latent, bl) in seqs:
                    nk = T // 128 + (PAST // 128 if latent else 0)
                    nown = T // 128
                    qn = min(512, T)
                    for hd in range(4):
                        ld(vtf[:, 0:nown, :], S["VB"][o:o + T, hd * 64:(hd + 1) * 64].rearrange("(n p) d -> p n d", p=128), [vtf])
                        if latent:
                            ld(vtf[:, nown:nk, :], I["cbv"][l, :, hd * 64:(hd + 1) * 64].rearrange("(n p) d -> p n d", p=128), [vtf])
                        cp("pool", vt[:, 0:nk, 0:64], vtf[:, 0:nk, :], [vtf], [vt])
                        ld(stf[:, :, 0:T], S["KB"][2 * hd:2 * hd + 2, :, o:o + T].rearrange("m d t -> d m t"), [stf])
                        cp("pool", kT[:, :, 0:T], stf[:, :, 0:T], [stf], [kT])
                        ld(stf[:, :, 0:T], S["QB"][2 * hd:2 * hd + 2, :, o:o + T].rearrange("m d t -> d m t"), [stf])
                        cp("pool", qT[:, :, 0:T], stf[:, :, 0:T], [stf], [qT])
                        if latent:
                            for mp in range(2):
                                c0 = hd * 64 + mp * 32
                                ld(cstg[:], I["cbk"][l, :, c0:c0 + 32].rearrange("(n p) d -> p n d", p=128), [cstg])
                                pt = npsb()
                                for n_ in range(4):
                                    tr(pt[0:32, n_ * 128:(n_ + 1) * 128], cstg[:, n_, :], ident[:, :], [cstg, ident], [pt])
                                cp("act", kT[:, mp, T:T + PAST], pt[0:32, :], [pt], [kT])
                        for q0 in range(0, T, qn):
                            nqb = qn // 128
                            pos = [ps[2 * (qti[0] % 2)], ps[2 * (qti[0] % 2) + 1]]
                            qti[0] += 1
                            def st1(mp, kt):
                                pss = npsb()
                                mm(pss[:, 0:qn], kT[:, mp, kt * 128:(kt + 1) * 128], qT[:, mp, q0:q0 + qn], True, True, [kT, qT], [pss])
                                pt_ = pT[pti[0] % 6]
                                pti[0] += 1
                                act(pt_[:, 0:qn], pss[:, 0:qn], AF.Exp, [pss], [pt_], scale=32.0 ** -0.5)
                                return pt_

                            def st2(mp, kt, pt_):
                                for qb in range(nqb):
                                    mm(pos[mp][:, qb * 65:(qb + 1) * 65], pt_[:, qb * 128:(qb + 1) * 128], vt[:, kt, :], kt == 0 and qb == 0, kt == nk - 1 and qb == nqb - 1, [pt_, vt], [pos[mp]])

                            pend = []
                            for mp in range(2):
                                for kt in range(nk):
                                    pend.append((mp, kt, st1(mp, kt)))
                                    if len(pend) > 2:
                                        st2(*pend.pop(0))
                            for it_ in pend:
                                st2(*it_)
                            v1 = pos[0][:, 0:nqb * 65].rearrange("p (q d) -> p q d", d=65)
                            v2 = pos[1][:, 0:nqb * 65].rearrange("p (q d) -> p q d", d=65)
                            p.op("dve", lambda e: e.reciprocal(out=rd[:, 0, 0:nqb].unsqueeze(2), in_=v1[:, :, 64:65]), reads=[pos[0]], writes=[rd])
                            p.op("dve", lambda e: e.reciprocal(out=rd[:, 1, 0:nqb].unsqueeze(2), in_=v2[:, :, 64:65]), reads=[pos[1]], writes=[rd])
                            tt_("dve", o1[:, 0:nqb, :], v1[:, :, 0:64], rd[:, 0, 0:nqb].unsqueeze(2).to_broadcast([128, nqb, 64]), ALU.mult, [pos[0], rd], [o1])
                            tt_("dve", o2[:, 0:nqb, :], v2[:, :, 0:64], rd[:, 1, 0:nqb].unsqueeze(2).to_broadcast([128, nqb, 64]), ALU.mult, [pos[1], rd], [o2])
                            stt(o1[:, 0:nqb, :], o2[:, 0:nqb, :], nlam[:, 0:1], o1[:, 0:nqb, :], ALU.mult, ALU.add, [o1, o2, nlam], [o1])
                            tt_("pool", o3[:, 0:nqb, :], o1[:, 0:nqb, :], o1[:, 0:nqb, :], ALU.mult, [o1], [o3])
                            p.op("dve", lambda e: e.tensor_reduce(out=ssq[:, 0:nqb], in_=o3[:, 0:nqb, :], axis=AX.X, op=ALU.add), reads=[o3], writes=[ssq])
                            rsqrt_(ssq[:, 0:nqb], ssq[:, 0:nqb], 1.0 / 64, epsc[:, 0:1], [ssq, epsc], [ssq], None)
                            b_ = bo[boi[0] % 2]
                            boi[0] += 1
                            tt_("dve", b_[:, 0:nqb, :], o1[:, 0:nqb, :], ssq[:, 0:nqb].unsqueeze(2).to_broadcast([128, nqb, 64]), ALU.mult, [o1, ssq], [b_])
                            tt_("pool", b_[:, 0:nqb, :], b_[:, 0:nqb, :], gsub[:, 0:64].unsqueeze(1).to_broadcast([128, nqb, 64]), ALU.mult, [b_, gsub], [b_])
                            r0 = o + q0
                            stq(S["M"][r0:r0 + qn, 384 + hd * 64:384 + (hd + 1) * 64].rearrange("(q p) d -> p q d", p=128), b_[:, 0:nqb, :], [b_])
                p.barrier()

            with ExitStack() as st:
                CT = 512
                rx = [sb(st, "rx%d" % i, [64, 18, CT + 2]) for i in range(2)]
                ro = [sb(st, "ro%d" % i, [64, 18, CT]) for i in range(2)]
                ti_ = 0
                for (S, o, T, latent, bl) in seqs:
                    for ts0 in range(0, T, CT):
                        n = min(CT, T - ts0)
                        x_, o_ = rx[ti_ % 2], ro[ti_ % 2]
                        ti_ += 1
                        lo = max(ts0 - 1, 0)
                        hi = min(ts0 + n + 1, T)
                        if ts0 == 0:
                            p.op("pool", lambda e: e.memset(x_[:, :, 0:1], 0.0), writes=[x_])
                        if ts0 + n == T:
                            p.op("pool", lambda e: e.memset(x_[:, :, n + 1:n + 2], 0.0), writes=[x_])
                        ld(x_[:, :, lo - (ts0 - 1):hi - (ts0 - 1)], S["RKV"][:, :, o + lo:o + hi].rearrange("c d t -> d c t"), [x_])
                        for c in range(18):
                            cb = (l * 3) * 18 + c
                            act(o_[:, c, 0:n], x_[:, c, 1:n + 1], AF.Copy, [x_, convc], [o_], scale=convc[:, cb + 18:cb + 19])
                            stt(o_[:, c, 0:n], x_[:, c, 0:n], convc[:, cb:cb + 1], o_[:, c, 0:n], ALU.mult, ALU.add, [x_, convc, o_], [o_])
                            stt(o_[:, c, 0:n], x_[:, c, 2:n + 2], convc[:, cb + 36:cb + 37], o_[:, c, 0:n], ALU.mult, ALU.add, [x_, convc, o_], [o_])
                        stq(S["RKVC"][:, :, o + ts0:o + ts0 + n].rearrange("c d t -> d c t"), o_[:, :, 0:n], [o_])
                p.barrier()

            with ExitStack() as st:
                CH = 128
                NSG = 128
                NH = 3
                W3 = NH * CH
                V3 = NH * 64
                NLEV = 6
                cw2 = sb(st, "cw2", [64, 2, 384]); ca2 = sb(st, "ca2", [64, 2, 384])
                ld(cw2[:], I["c_w2"][l].rearrange("d r c -> r d c"), [cw2])
                ld(ca2[:], I["c_a2"][l].rearrange("d r c -> r d c"), [ca2])
                MK = {}
                for nm in ("m_lt", "m_le", "m_gt", "m_ge", "n_lt", "n_le", "n_gt", "n_ge", "i6"):
                    MK[nm] = sb(st, nm, [CH, W3])
                    ld(MK[nm][:], I["b" + nm][:, :], [MK[nm]])
                rstart = sb(st, "rstart", [64, NH * NSG])
                ld(rstart[:], I["rstart2"][:, 0:NH * NSG], [rstart])
                onesb = sb(st, "onesb", [CH, 2], BF16)
                p.op("pool", lambda e: e.memset(onesb[:], 1.0), writes=[onesb])
                KKB = sb(st, "KKB", [64, 6, NSG]); KAB = sb(st, "KAB", [64, 6, NSG]); RKB = sb(st, "RKB", [64, 6, NSG])
                for (dst_, src_) in ((KKB, kkc), (KAB, kac), (RKB, rkc_)):
                    cp("pool", dst_[:], src_[:, l * 6:(l + 1) * 6].unsqueeze(2).to_broadcast([64, 6, NSG]), [src_], [dst_])
                v3 = lambda t: t[:, :].rearrange("p (h i) -> p h i", i=64)
                hw = lambda t, hi_: t[:, hi_ * CH:(hi_ + 1) * CH]
                hv = lambda t, hi_: t[:, hi_ * 64:(hi_ + 1) * 64]
                SHT = {nm: sb(st, "sh_" + nm, [64, NH, NSG]) for nm in ("sig", "a_", "kk", "kd", "bb", "Lc", "Lx", "Ld", "E", "tmp")}
                CB = []
                for ci_ in range(4):
                    Bd = dict(SHT)
                    Bd["rkv2"] = [sb(st, "rkv%d_%d" % (ci_, i_), [64, 3 * NH, NSG]) for i_ in range(2)]
                    Bd["cwt2"] = [sb(st, "cwt%d_%d" % (ci_, i_), [64, NSG]) for i_ in range(2)]
                    Bd["cat2"] = [sb(st, "cat%d_%d" % (ci_, i_), [64, NSG]) for i_ in range(2)]
                    Bd["yb2"] = [sb(st, "yb%d_%d" % (ci_, i_), [CH, V3]) for i_ in range(2)]
                    for nm in ("Dg", "nBg"):
                        Bd[nm] = sb(st, "%s%d" % (nm, ci_), [64, NH, NSG])
                    for nm in ("rkc", "Kt_", "Rt_", "Dh", "Bh"):
                        Bd[nm] = sb(st, "%s%d" % (nm, ci_), [64, NH, NSG], BF16)
                    Bd["gC"] = sb(st, "gC%d" % ci_, [64, NH, NSG // CH])
                    Bd["H"] = sb(st, "H%d" % ci_, [64, NH, 64]); Bd["hst"] = sb(st, "hst%d" % ci_, [64, NH, 64])
                    Bd["Hb"] = sb(st, "Hb%d" % ci_, [64, NH, 64], BF16)
                    for nm in ("Pa", "Pb", "PTa", "PTb", "Xa", "Xb", "AdT", "BdT", "nBbT"):
                        Bd[nm] = sb(st, "%s%d" % (nm, ci_), [CH, W3], BF16)
                    for nm in ("Vt", "Dgt", "nBgt", "Wb", "Zb"):
                        Bd[nm] = sb(st, "%s%d" % (nm, ci_), [CH, V3], BF16)
                    for nm in ("y2", "Vf"):
                        Bd[nm] = sb(st, "%s%d" % (nm, ci_), [CH, V3])
                    Bd["yb"] = None
                    Bd["coef"] = sb(st, "coef%d" % ci_, [CH, NH])
                    CB.append(Bd)

                def chain(S, o, T, latent, bl, d, h0, Bd):
                    rkc = Bd["rkc"]
                    ybi = [0]
                    pst = []
                    sig, a_, kk, kd, bb, Lc, Lx, Ld, E, tmp = (Bd[k] for k in ("sig", "a_", "kk", "kd", "bb", "Lc", "Lx", "Ld", "E", "tmp"))
                    Kt_, Rt_, Dh, Bh, Dg, nBg, gC, H, hst = (Bd[k] for k in ("Kt_", "Rt_", "Dh", "Bh", "Dg", "nBg", "gC", "H", "hst"))
                    Hb, Vf = Bd["Hb"], Bd["Vf"]
                    Vt, Dgt, nBgt, AdT, BdT, nBbT, Wb, Zb, yb, y2, coef = (Bd[k] for k in ("Vt", "Dgt", "nBgt", "AdT", "BdT", "nBbT", "Wb", "Zb", "yb", "y2", "coef"))
                    Pq = [Bd["Pa"], Bd["Pb"]]; PTq = [Bd["PTa"], Bd["PTb"]]; Xq = [Bd["Xa"], Bd["Xb"]]
                    nsg = T // NSG
                    YD = S["YF" if d == 0 else "YB"]
                    HR = range(NH)
                    if latent:
                        src = I["scf" if d == 0 else "scb"][l]
                        ld(hst[:], src[h0:h0 + NH].rearrange("h i j -> i h j"), [hst])
                        pt = nps()
                        for hi_ in HR:
                            tr(pt[0:64, hi_ * 64:(hi_ + 1) * 64], hst[:, hi_, :], ident[0:64, 0:64], [hst, ident], [pt])
                        cp("act", H[:].rearrange("p h i -> p (h i)"), pt[0:64, 0:V3], [pt], [H])
                    else:
                        p.op("pool", lambda e: e.memset(H[:], 0.0), writes=[H])
                    cp("pool", Hb[:], H[:], [H], [Hb])
                    segs = list(range(nsg)) if d == 0 else list(range(nsg - 1, -1, -1))
                    if d == 0:
                        nmAT, mAT, nmA, mBT, nmBT = MK["n_lt"], MK["m_lt"], MK["n_gt"], MK["m_le"], MK["n_le"]
                    else:
                        nmAT, mAT, nmA, mBT, nmBT = MK["n_gt"], MK["m_gt"], MK["n_lt"], MK["m_ge"], MK["n_ge"]
                    for si_, sg_ in enumerate(segs):
                        rkv, cwt, cat = Bd["rkv2"][si_ % 2], Bd["cwt2"][si_ % 2], Bd["cat2"][si_ % 2]
                        ts0 = sg_ * NSG
                        n = NSG
                        nch = n // CH
                        for part in range(3):
                            ld(rkv[:, part * NH:(part + 1) * NH, :], S["RKVC"][part * 6 + h0:part * 6 + h0 + NH, :, o + ts0:o + ts0 + n].rearrange("c d t -> d c t"), [rkv])
                        ld(cwt[:, :], S["CW"][d, :, o + ts0:o + ts0 + n], [cwt])
                        ld(cat[:, :], S["CA"][d, :, o + ts0:o + ts0 + n], [cat])
                        rr_ = rkv[:, 0:NH, :]
                        kr_ = rkv[:, NH:2 * NH, :]
                        pw = nps(); pa = nps()
                        for hi_ in HR:
                            h_ = h0 + hi_
                            ci = (l * 2 + d) * 6 + h_
                            mm(pw[0:64, hi_ * n:(hi_ + 1) * n], cw2[:, d, h_ * 64:(h_ + 1) * 64], cwt[:, :], True, True, [cw2, cwt], [pw])
                            act(sig[:, hi_, :], pw[0:64, hi_ * n:(hi_ + 1) * n], AF.Sigmoid, [pw, w0c], [sig], bias=w0c[:, ci:ci + 1], scale=1.0)
                            mm(pa[0:64, hi_ * n:(hi_ + 1) * n], ca2[:, d, h_ * 64:(h_ + 1) * 64], cat[:, :], True, True, [ca2, cat], [pa])
                            act(a_[:, hi_, :], pa[0:64, hi_ * n:(hi_ + 1) * n], AF.Sigmoid, [pa, a0c], [a_], bias=a0c[:, ci:ci + 1], scale=1.0)
                        fl = lambda t: t[:].rearrange("p h t -> p (h t)")
                        tt_("dve", kk[:], kr_, KKB[:, h0:h0 + NH, :], ALU.mult, [rkv, KKB], [kk])
                        tt_("pool", tmp[:], kk[:], kk[:], ALU.mult, [kk], [tmp])
                        pk_ = nps()
                        mm(pk_[0:64, 0:NH * n], ones[0:64, 0:64], fl(tmp), True, True, [ones, tmp], [pk_])
                        act(fl(Ld), pk_[0:64, 0:NH * n], AF.Sqrt, [pk_], [Ld])
                        ts_("pool", Ld[:], Ld[:], 1e-12, None, ALU.max, ALU.bypass, [Ld], [Ld])
                        p.op("dve", lambda e: e.reciprocal(out=Ld[:], in_=Ld[:]), reads=[Ld], writes=[Ld])
                        tt_("dve", kk[:], kk[:], Ld[:], ALU.mult, [kk, Ld], [kk])
                        stt(kd[:], a_[:], -1.0, KAB[:, h0:h0 + NH, :], ALU.add, ALU.mult, [a_, KAB], [kd])
                        stt(kd[:], kd[:], 1.0, kr_, ALU.add, ALU.mult, [kd, rkv], [kd])
                        tt_("pool", tmp[:], rr_, RKB[:, h0:h0 + NH, :], ALU.mult, [rkv, RKB], [tmp])
                        tt_("pool", rkc[:], tmp[:], kd[:], ALU.mult, [tmp, kd], [rkc])
                        tt_("pool", bb[:], kk[:], a_[:], ALU.mult, [kk, a_], [bb])
                        p.op("dve", lambda e: e.tensor_tensor_scan(out=fl(Lc), data0=rstart[:, :], data1=fl(sig), initial=0.0, op0=ALU.mult, op1=ALU.add),
                             reads=[rstart, sig], writes=[Lc])
                        tt_("pool", Lx[:], Lc[:], sig[:], ALU.subtract, [Lc, sig], [Lx])
                        c4 = lambda t: t[:].rearrange("p h (c s) -> p h c s", s=CH)
                        Ltot = c4(Lc)[:, :, :, CH - 1:CH]
                        tt_("dve", c4(Ld), Ltot.to_broadcast([64, NH, nch, CH]), c4(Lc), ALU.subtract, [Lc], [Ld])
                        act(gC[:, :, 0:nch].unsqueeze(3), Ltot, AF.Exp, [Lc], [gC], scale=-DEC)
                        if d == 0:
                            act(E[:], Lx[:], AF.Exp, [Lx], [E], scale=-DEC)
                            tt_("dve", Kt_[:], kk[:], E[:], ALU.mult, [kk, E], [Kt_])
                            act(E[:], Lc[:], AF.Exp, [Lc], [E], scale=-DEC)
                            tt_("dve", Rt_[:], rr_, E[:], ALU.mult, [rkv, E], [Rt_])
                            act(E[:], Lc[:], AF.Exp, [Lc], [E], scale=DEC)
                        else:
                            act(E[:], Ld[:], AF.Exp, [Ld], [E], scale=-DEC)
                            tt_("dve", Kt_[:], kk[:], E[:], ALU.mult, [kk, E], [Kt_])
                            tt_("dve", c4(tmp), Ltot.to_broadcast([64, NH, nch, CH]), c4(Lx), ALU.subtract, [Lc, Lx], [tmp])
                            act(E[:], tmp[:], AF.Exp, [tmp], [E], scale=-DEC)
                            tt_("dve", Rt_[:], rr_, E[:], ALU.mult, [rkv, E], [Rt_])
                            act(E[:], tmp[:], AF.Exp, [tmp], [E], scale=DEC)
                        tt_("dve", Dh[:], kd[:], E[:], ALU.mult, [kd, E], [Dh])
                        tt_("pool", Bh[:], bb[:], E[:], ALU.mult, [bb, E], [Bh])
                        act(E[:], (Ld if d == 0 else Lx)[:], AF.Exp, [Ld, Lx], [E], scale=-DEC)
                        tt_("dve", Dg[:], kd[:], E[:], ALU.mult, [kd, E], [Dg])
                        stt(nBg[:], bb[:], -1.0, E[:], ALU.mult, ALU.mult, [bb, E], [nBg])
                        yield
                        chunks = list(range(nch)) if d == 0 else list(range(nch - 1, -1, -1))
                        for c in chunks:
                            cs = slice(c * CH, (c + 1) * CH)
                            pt = nps()
                            for hi_ in HR:
                                tr(pt[0:CH, hi_ * 64:(hi_ + 1) * 64], rkv[:, 2 * NH + hi_, cs], ident[0:64, 0:64], [rkv, ident], [pt])
                            cp("act", Vt[:, :], pt[0:CH, 0:V3], [pt], [Vt])
                            cp("dve", Vf[:, :], pt[0:CH, 0:V3], [pt], [Vf])
                            for (src_t, dst_t) in ((Dg, Dgt), (nBg, nBgt)):
                                pt = nps()
                                for hi_ in HR:
                                    tr(pt[0:CH, hi_ * 64:(hi_ + 1) * 64], src_t[:, hi_, cs], ident[0:64, 0:64], [src_t, ident], [pt])
                                cp("act", dst_t[:, :], pt[0:CH, 0:V3], [pt], [dst_t])
                            pAbT, pBbT, pAdT, pBdT, pAb = nps(), nps(), nps(), nps(), nps()
                            for hi_ in HR:
                                hs = slice(hi_ * CH, (hi_ + 1) * CH)
                                mm(pAbT[0:CH, hs], Bh[:, hi_, cs], Kt_[:, hi_, cs], True, True, [Bh, Kt_], [pAbT])
                                mm(pBbT[0:CH, hs], Bh[:, hi_, cs], Rt_[:, hi_, cs], True, True, [Bh, Rt_], [pBbT])
                                mm(pAdT[0:CH, hs], Dh[:, hi_, cs], Kt_[:, hi_, cs], True, True, [Dh, Kt_], [pAdT])
                                mm(pBdT[0:CH, hs], Dh[:, hi_, cs], Rt_[:, hi_, cs], True, True, [Dh, Rt_], [pBdT])
                                mm(pAb[0:CH, hs], Kt_[:, hi_, cs], Bh[:, hi_, cs], True, True, [Kt_, Bh], [pAb])
                            tt_("dve", PTq[0][:, :], pAbT[0:CH, 0:W3], nmAT[:, :], ALU.mult, [pAbT, nmAT], [PTq[0]])
                            tt_("dve", Pq[0][:, :], pAb[0:CH, 0:W3], nmA[:, :], ALU.mult, [pAb, nmA], [Pq[0]])
                            tt_("pool", Xq[0][:, :], PTq[0][:, :], MK["i6"][:, :], ALU.add, [PTq[0], MK["i6"]], [Xq[0]])
                            tt_("dve", AdT[:, :], pAdT[0:CH, 0:W3], mAT[:, :], ALU.mult, [pAdT, mAT], [AdT])
                            tt_("dve", BdT[:, :], pBdT[0:CH, 0:W3], mBT[:, :], ALU.mult, [pBdT, mBT], [BdT])
                            tt_("dve", nBbT[:, :], pBbT[0:CH, 0:W3], nmBT[:, :], ALU.mult, [pBbT, nmBT], [nBbT])
                            yield
                            for k in range(1, NLEV + 1):
                                Pc, PTc = Pq[(k - 1) % 2], PTq[(k - 1) % 2]
                                Pn, PTn = Pq[k % 2], PTq[k % 2]
                                pP = nps()
                                for hi_ in HR:
                                    hs = slice(hi_ * CH, (hi_ + 1) * CH)
                                    mm(pP[0:CH, hs], hw(PTc, hi_), hw(Pc, hi_), True, True, [PTc, Pc], [pP])
                                if k < NLEV:
                                    pPT = nps()
                                    for hi_ in HR:
                                        hs = slice(hi_ * CH, (hi_ + 1) * CH)
                                        mm(pPT[0:CH, hs], hw(Pc, hi_), hw(PTc, hi_), True, True, [PTc, Pc], [pPT])
                                if k >= 2:
                                    Xo, Xn = Xq[k % 2], Xq[(k - 1) % 2]
                                    pX = nps()
                                    for hi_ in HR:
                                        hs = slice(hi_ * CH, (hi_ + 1) * CH)
                                        mm(pX[0:CH, hs], hw(Pc, hi_), hw(Xo, hi_), True, True, [Pc, Xo], [pX])
                                    tt_("dve", Xn[:, :], pX[0:CH, 0:W3], Xo[:, :], ALU.add, [pX, Xo], [Xn])
                                cp("act", Pn[:, :], pP[0:CH, 0:W3], [pP], [Pn])
                                if k < NLEV:
                                    cp("dve", PTn[:, :], pPT[0:CH, 0:W3], [pPT], [PTn])
                                yield
                            PL, XL0, XL1 = Pq[NLEV % 2], Xq[(NLEV - 1) % 2], Xq[NLEV % 2]
                            pX = nps()
                            for hi_ in HR:
                                hs = slice(hi_ * CH, (hi_ + 1) * CH)
                                mm(pX[0:CH, hs], hw(PL, hi_), hw(XL0, hi_), True, True, [PL, XL0], [pX])
                            pW = nps()
                            for hi_ in HR:
                                hs = slice(hi_ * 64, (hi_ + 1) * 64)
                                mm(pW[0:CH, hs], Kt_[:, hi_, cs], Hb[:, hi_, :], True, False, [Kt_, Hb], [pW])
                                mm(pW[0:CH, hs], hw(AdT, hi_), hv(Vt, hi_), False, True, [AdT, Vt], [pW])
                            tt_("dve", XL1[:, :], pX[0:CH, 0:W3], XL0[:, :], ALU.add, [pX, XL0], [XL1])
                            cp("act", Wb[:, :], pW[0:CH, 0:V3], [pW], [Wb])
                            yield
                            pZ = nps()
                            for hi_ in HR:
                                hs = slice(hi_ * 64, (hi_ + 1) * 64)
                                mm(pZ[0:CH, hs], hw(XL1, hi_), hv(Wb, hi_), True, True, [XL1, Wb], [pZ])
                            cp("act", Zb[:, :], pZ[0:CH, 0:V3], [pZ], [Zb])
                            yield
                            Hf = H[:].rearrange("p h i -> p (h i)")
                            pY, pC, pH = nps(), nps(), nps()
                            for hi_ in HR:
                                hs = slice(hi_ * 64, (hi_ + 1) * 64)
                                mm(pY[0:CH, hs], Rt_[:, hi_, cs], Hb[:, hi_, :], True, False, [Rt_, Hb], [pY])
                                mm(pY[0:CH, hs], hw(BdT, hi_), hv(Vt, hi_), False, False, [BdT, Vt], [pY])
                                mm(pY[0:CH, hs], hw(nBbT, hi_), hv(Zb, hi_), False, True, [nBbT, Zb], [pY])
                                mm(pC[0:CH, hi_:hi_ + 1], rkc[:, hi_, cs], onesb[0:64, 0:1], True, True, [rkc, onesb], [pC])
                                mm(pH[0:64, hs], hv(Dgt, hi_), hv(Vt, hi_), True, False, [Dgt, Vt], [pH])
                                mm(pH[0:64, hs], hv(nBgt, hi_), hv(Zb, hi_), False, True, [nBgt, Zb], [pH])
                            cp("act", coef[:, :], pC[0:CH, 0:NH], [pC], [coef])
                            tt_("pool", v3(y2), v3(Vf), coef[:, :].unsqueeze(2).to_broadcast([CH, NH, 64]), ALU.mult, [Vf, coef], [y2])
                            yb = Bd["yb2"][ybi[0] % 2]
                            ybi[0] += 1
                            tt_("dve", yb[:, :], pY[0:CH, 0:V3], y2[:, :], ALU.add, [pY, y2], [yb])
                            tt_("pool", H[:], H[:], gC[:, :, c:c + 1].to_broadcast([64, NH, 64]), ALU.mult, [H, gC], [H])
                            tt_("dve", Hf, Hf, pH[0:64, 0:V3], ALU.add, [H, pH], [H])
                            cp("pool", Hb[:], H[:], [H], [Hb])
                            r0 = o + ts0 + c * CH
                            pst.append((r0, yb))
                            yield
                            while pst:
                                r0_, yb_ = pst.pop(0)
                                ld(YD[r0_:r0_ + CH, h0 * 64:(h0 + NH) * 64], yb_[:, :], [], [yb_])
                    if not latent:
                        pt = nps()
                        for hi_ in HR:
                            tr(pt[0:64, hi_ * 64:(hi_ + 1) * 64], H[:, hi_, :], ident[0:64, 0:64], [H, ident], [pt])
                        cp("act", hst[:].rearrange("p h j -> p (h j)"), pt[0:64, 0:V3], [pt], [hst])
                        stq(O["ncf" if d == 0 else "ncb"][bl, l, h0:h0 + NH].rearrange("h i j -> i h j"), hst[:], [hst])

                for (S, o, T, latent, bl) in seqs:
                    alive = [chain(S, o, T, latent, bl, d_, h0_, CB[d_ * 2 + h0_ // 3]) for d_ in range(2) for h0_ in (0, 3)]
                    while alive:
                        for g_ in list(alive):
                            try:
                                next(g_)
                            except StopIteration:
                                alive.remove(g_)
                p.barrier()

            with ExitStack() as st:
                cg2 = sb(st, "cg2", [128, 384])
                ld(cg2[:], I["c_g2"][l], [cg2])
                lnxg = bcast_row(st, I["c_lnx_g"][l:l + 1, :], 384, "lnxg")
                lnxb = bcast_row(st, I["c_lnx_b"][l:l + 1, :], 384, "lnxb")
                yfb = [sb(st, "yfb%d" % i, [128, 384]) for i in range(2)]
                ybb = [sb(st, "ybb%d" % i, [128, 384]) for i in range(2)]
                y2b = [sb(st, "y2b%d" % i, [128, 384]) for i in range(2)]
                cgb = [sb(st, "cgb%d" % i, [128, 128]) for i in range(2)]
                gsa = sb(st, "gsa", [128, 6]); gsb = sb(st, "gsb", [128, 6])
                w3 = lambda t: t[:, :].rearrange("p (h i) -> p h i", i=64)
                ti_ = 0
                for S in streams:
                    for r0 in range(0, S["T"], 128):
                        yf_, yb_, y2_, cg_ = yfb[ti_ % 2], ybb[ti_ % 2], y2b[ti_ % 2], cgb[ti_ % 2]
                        ti_ += 1
                        ld(yf_[:], S["YF"][r0:r0 + 128, :], [yf_])
                        ld(yb_[:], S["YB"][r0:r0 + 128, :], [yb_])
                        ld(cg_[:], S["CG"][:, r0:r0 + 128], [cg_])
                        tt_("pool", yb_[:, :], yb_[:, :], yf_[:, :], ALU.add, [yb_, yf_], [yb_])
                        p.op("dve", lambda e: e.tensor_reduce(out=gsa[:, :], in_=w3(yb_), axis=AX.X, op=ALU.add), reads=[yb_], writes=[gsa])
                        ts_("dve", gsa[:, :], gsa[:, :], -1.0 / 64, None, ALU.mult, ALU.bypass, [gsa], [gsa])
                        tt_("dve", w3(yb_), w3(yb_), gsa[:, :].unsqueeze(2).to_broadcast([128, 6, 64]), ALU.add, [yb_, gsa], [yb_])
                        tt_("pool", y2_[:, :], yb_[:, :], yb_[:, :], ALU.mult, [yb_], [y2_])
                        p.op("dve", lambda e: e.tensor_reduce(out=gsb[:, :], in_=w3(y2_), axis=AX.X, op=ALU.add), reads=[y2_], writes=[gsb])
                        rsqrt_(gsb[:, :], gsb[:, :], 1.0 / 64, epsc[:, 1:2], [gsb, epsc], [gsb], None)
                        tt_("dve", w3(yb_), w3(yb_), gsb[:, :].unsqueeze(2).to_broadcast([128, 6, 64]), ALU.mult, [yb_, gsb], [yb_])
                        tt_("pool", yb_[:, :], yb_[:, :], lnxg[:, :], ALU.mult, [yb_, lnxg], [yb_])
                        tt_("pool", yb_[:, :], yb_[:, :], lnxb[:, :], ALU.add, [yb_, lnxb], [yb_])
                        pg = nps()
                        mm(pg[:, 0:384], cg_[:, :], cg2[:, :], True, True, [cg_, cg2], [pg])
                        tt_("dve", y2_[:, :], yb_[:, :], pg[:, 0:384], ALU.mult, [yb_, pg], [y2_])
                        stq(S["M"][r0:r0 + 128, 640:1024], y2_[:, :], [y2_])
                p.barrier()

            with ExitStack() as st:
                NT = 256
                wout = sb(st, "wout", [128, 8, D], BF16)
                w1b = sb(st, "w1b", [128, 8, DFF], BF16)
                w3b = sb(st, "w3b", [128, 8, DFF], BF16)
                w2b = sb(st, "w2b", [128, 22, D], BF16)
                wstg = [sb(st, "wstgo%d" % i, [128, 704]) for i in range(2)]
                wsi = [0]
                cengs = ("pool", "dve", "act")

                def load_cast(dst_ap, src_ap, ncols, dst_t):
                    for c0 in range(0, ncols, 704):
                        c1 = min(ncols, c0 + 704)
                        ws_ = wstg[wsi[0] % 2]
                        e_ = cengs[wsi[0] % 3]
                        wsi[0] += 1
                        ld(ws_[:, 0:c1 - c0], src_ap[:, c0:c1], [ws_])
                        cp(e_, dst_ap[:, c0:c1], ws_[:, 0:c1 - c0], [ws_], [dst_t])

                wv = I["w_out"][l].rearrange("(c p) f -> p c f", p=128)
                w1v = I["ffn_w1"][l].rearrange("(c p) f -> p c f", p=128)
                w3v = I["ffn_w3"][l].rearrange("(c p) f -> p c f", p=128)
                w2v = I["ffn_w2"][l].rearrange("(c p) f -> p c f", p=128)
                for k in range(8):
                    load_cast(wout[:, k, :], wv[:, k, :], D, wout)
                for k in range(8):
                    for hf in range(2):
                        load_cast(w1b[:, k, hf * 1408:(hf + 1) * 1408], w1v[:, k, hf * 1408:(hf + 1) * 1408], 1408, w1b)
                        load_cast(w3b[:, k, hf * 1408:(hf + 1) * 1408], w3v[:, k, hf * 1408:(hf + 1) * 1408], 1408, w3b)
                for fc in range(22):
                    load_cast(w2b[:, fc, :], w2v[:, fc, :], D, w2b)
                xt = sb(st, "xt2", [128, 8, NT])
                mT = sb(st, "mT", [128, 8, NT], BF16)
                sq = sb(st, "sq2", [128, 8, NT])
                rstd = sb(st, "rstd2", [128, NT])
                mtok = [sb(st, "mtok%d" % i, [128, D]) for i in range(2)]
                actb = sb(st, "actb", [128, 22, NT], BF16)
                sgl = [sb(st, "sgl%d" % i, [128, NT]) for i in range(2)]
                for S in streams:
                    latent = S is SS_
                    cv = 1 if latent else 0
                    T = S["T"]
                    XTv = S["XT"].rearrange("(c p) t -> p c t", p=128)
                    for t0 in range(0, T, NT):
                        ld(xt[:], XTv[:, :, t0:t0 + NT], [xt])
                        for i in range(NT // 128):
                            mk = mtok[i % 2]
                            ld(mk[:], S["M"][t0 + i * 128:t0 + (i + 1) * 128, :], [mk])
                            for half in range(2):
                                pt = nps()
                                for c in range(4):
                                    cc = half * 4 + c
                                    tr(pt[:, c * 128:(c + 1) * 128], mk[:, cc * 128:(cc + 1) * 128], ident[:, :], [mk, ident], [pt])
                                cp("act" if half == 0 else "dve", mT[:, half * 4:(half + 1) * 4, i * 128:(i + 1) * 128],
                                   pt[:, :].rearrange("p (c t) -> p c t", t=128), [pt], [mT])
                        for dc in range(8):
                            po = nps()
                            for k in range(8):
                                mm(po[:, 0:NT], wout[:, k, dc * 128:(dc + 1) * 128], mT[:, k, :], k == 0, k == 7, [wout, mT], [po])
                            stt(xt[:, dc, :], po[:, 0:NT], MOD(l, 2, dc, cv), xt[:, dc, :], ALU.mult, ALU.add, [po, mod, xt], [xt])
                        act(sq[:], xt[:], AF.Square, [xt], [sq])
                        pss = nps()
                        for c in range(8):
                            mm(pss[:, 0:NT], ones[:, 0:128], sq[:, c, :], c == 0, c == 7, [ones, sq], [pss])
                        rsqrt_(rstd[:], pss[:, 0:NT], 1.0 / D, epsc[:, 0:1], [pss, epsc], [rstd], None)
                        hh = mT
                        for c in range(8):
                            stt(sq[:, c, :], xt[:, c, :], modA2[:, l * 8 + c, cv:cv + 1], rstd[:], ALU.mult, ALU.mult, [xt, modA2, rstd], [sq])
                            act(hh[:, c, :], sq[:, c, :], AF.Identity, [sq, mod], [hh], bias=MOD(l, 3, c, cv), scale=1.0)
                        for fc in range(22):
                            p1, p3 = nps(), nps()
                            for k in range(8):
                                mm(p1[:, 0:NT], w1b[:, k, fc * 128:(fc + 1) * 128], hh[:, k, :], k == 0, k == 7, [w1b, hh], [p1])
                            for k in range(8):
                                mm(p3[:, 0:NT], w3b[:, k, fc * 128:(fc + 1) * 128], hh[:, k, :], k == 0, k == 7, [w3b, hh], [p3])
                            sg_ = sgl[fc % 2]
                            act(sg_[:], p1[:, 0:NT], AF.Silu, [p1], [sg_])
                            tt_("dve", actb[:, fc, :], sg_[:], p3[:, 0:NT], ALU.mult, [sg_, p3], [actb])
                        for dc in range(8):
                            po = nps()
                            for fc in range(22):
                                mm(po[:, 0:NT], w2b[:, fc, dc * 128:(dc + 1) * 128], actb[:, fc, :], fc == 0, fc == 21, [w2b, actb], [po])
                            stt(xt[:, dc, :], po[:, 0:NT], MOD(l, 5, dc, cv), xt[:, dc, :], ALU.mult, ALU.add, [po, mod, xt], [xt])
                        stq(XTv[:, :, t0:t0 + NT], xt[:], [xt])
                p.barrier()

        with ExitStack() as st:
            xt = sb(st, "xtf", [128, 8, 512])
            sq = sb(st, "sqf", [128, 8, 512])
            rstd = sb(st, "rstdf", [128, 512])
            yo = [sb(st, "yo%d" % i, [128, D]) for i in range(2)]
            for S, dst in ((SP_, O["yp"]), (SS_, O["ys"])):
                if S is SS_ and not do_sample:
                    continue
                T = S["T"]
                XTv = S["XT"].rearrange("(c p) t -> p c t", p=128)
                for t0 in range(0, T, 512):
                    ld(xt[:], XTv[:, :, t0:t0 + 512], [xt])
                    act(sq[:], xt[:], AF.Square, [xt], [sq])
                    pss = nps()
                    for c in range(8):
                        mm(pss[:, :], ones[:, 0:128], sq[:, c, :], c == 0, c == 7, [ones, sq], [pss])
                    rsqrt_(rstd[:], pss[:, :], 1.0 / D, epsc[:, 0:1], [pss, epsc], [rstd], None)
                    for c in range(8):
                        stt(sq[:, c, :], xt[:, c, :], gfc[:, c:c + 1], rstd[:], ALU.mult, ALU.mult, [xt, gfc, rstd], [sq])
                    for i in range(4):
                        y_ = yo[i % 2]
                        for half in range(2):
                            pt = nps()
                            for c in range(4):
                                cc = half * 4 + c
                                tr(pt[:, c * 128:(c + 1) * 128], sq[:, cc, i * 128:(i + 1) * 128], ident[:, :], [sq, ident], [pt])
                            cp("act" if half == 0 else "dve", y_[:, half * 512:(half + 1) * 512], pt[:, :], [pt], [y_])
                        stq(dst[t0 + i * 128:t0 + (i + 1) * 128, :], y_[:], [y_])
            p.barrier()
    return nc


def make_in_maps(inputs):
    f = lambda a: np.ascontiguousarray(np.asarray(a, dtype=np.float32))
    consts = _consts()
    shared = {k: f(inputs[k]) for k in WEIGHT_SHAPES}
    shared.update(consts)
    maps = []
    for c in range(NCORES):
        b = c % 2
        m = dict(shared)
        m["xp"] = f(inputs["x_prompt"][NPL * c:NPL * (c + 1)]).reshape(NPL * TP, D)
        m["xs"] = f(inputs["x_sample"][b])
        m["cak"] = f(inputs["cache_a_k"][b]).reshape(L, PAST, 128)
        m["cav"] = f(inputs["cache_a_v"][b]).reshape(L, PAST, 128)
        m["cbk"] = f(inputs["cache_b_k"][b]).reshape(L, PAST, 256)
        m["cbv"] = f(inputs["cache_b_v"][b]).reshape(L, PAST, 256)
        m["scf"] = f(inputs["state_c_fwd"][b])
        m["scb"] = f(inputs["state_c_bwd"][b])
        m["cs"] = f(inputs["c"][b])
        maps.append(m)
    return maps


def kernel(**inputs):
    nc = build()
    maps = make_in_maps(inputs)
    res = run_bass_kernel_spmd(nc, maps, core_ids=list(range(NCORES))).results
    yp = np.concatenate([r["yp"].reshape(NPL, TP, D) for r in res], axis=0)
    ys = np.stack([res[0]["ys"], res[1]["ys"]], axis=0)
    nak = np.concatenate([r["nak"].reshape(NPL, L, TP, 2, 64) for r in res], axis=0)
    nav = np.concatenate([r["nav"].reshape(NPL, L, TP, 2, 64) for r in res], axis=0)
    nbk = np.concatenate([r["nbk"].reshape(NPL, L, TP, 4, 2, 32) for r in res], axis=0)
    nbv = np.concatenate([r["nbv"].reshape(NPL, L, TP, 4, 64) for r in res], axis=0)
    ncf = np.concatenate([r["ncf"] for r in res], axis=0)
    ncb = np.concatenate([r["ncb"] for r in res], axis=0)
    return tuple(np.ascontiguousarray(a.astype(np.float32)) for a in (yp, ys, nak, nav, nbk, nbv, ncf, ncb))
```
